# Optimizing a Trainium2 kernel written in Bass

```python
import jax, jax.numpy as jnp
from jax import lax
import numpy as np

D_MODEL = 1024
BATCH = 4
SEQ = 4096
DEPTH = 1
DEC_BATCH = 32
DEC_SEQ = 32
PAST_LEN = 1024

CHUNK = 64
PLE_DIM = 256
M_HEADS = 4
M_INNER = 2 * D_MODEL
M_HD = M_INNER // M_HEADS
CONV_W = 4
H_EXPAND = 128
H_WIDTH = D_MODEL
H_HEADS = H_WIDTH // H_EXPAND
IN_SIZES = (M_INNER, M_INNER, M_INNER, M_HEADS, M_HEADS, H_WIDTH, H_WIDTH, H_WIDTH, H_WIDTH, D_MODEL, D_MODEL)
N_IN = 3 * M_INNER + 2 * M_HEADS + 4 * H_WIDTH + 2 * D_MODEL
EPS = 1e-6

kernel_name = 'mlstm_hgrn2_gated_stream_step'


def _rmsnorm(x, g):
    xf = x.astype(jnp.float32)
    y = xf * lax.rsqrt(jnp.mean(xf * xf, axis=-1, keepdims=True) + EPS)
    return (y * g.astype(jnp.float32)).astype(x.dtype)


def _head_layernorm(h, g):
    mu = jnp.mean(h, axis=-1, keepdims=True)
    d = h - mu
    y = d * lax.rsqrt(jnp.mean(d * d, axis=-1, keepdims=True) + EPS)
    return y.reshape(h.shape[0], h.shape[1], -1) * g


def _head_rmsnorm(h, g):
    y = h * lax.rsqrt(jnp.mean(h * h, axis=-1, keepdims=True) + EPS)
    return y.reshape(h.shape[0], h.shape[1], -1) * g


def _causal_conv(u, buf, w, b):
    T = u.shape[1]
    xp = jnp.concatenate([buf, u], axis=1)
    y = b + xp[:, 0:T] * w[0]
    for j in range(1, CONV_W):
        y = y + xp[:, j:j + T] * w[j]
    return y, xp[:, -(CONV_W - 1):]


def _to_chunks(a, L):
    a = a.reshape(a.shape[:2] + (a.shape[2] // L, L) + a.shape[3:])
    return jnp.moveaxis(a, 2, 0)


def _from_chunks(a):
    a = jnp.moveaxis(a, 0, 2)
    return a.reshape(a.shape[:2] + (a.shape[2] * a.shape[3],) + a.shape[4:])


def _mlstm_chunk(carry, inp):
    C, n, m = carry
    q, k, v, ig, lf = inp
    L = q.shape[2]
    causal = jnp.tril(jnp.ones((L, L), dtype=bool))
    b = jnp.cumsum(lf, axis=-1)
    logd = jnp.where(causal, b[..., :, None] - b[..., None, :] + ig[..., None, :], -jnp.inf)
    m_prev = b + m[..., None]
    m_t = jnp.maximum(m_prev, jnp.max(logd, axis=-1))
    dmat = jnp.exp(logd - m_t[..., None])
    w_inter = jnp.exp(m_prev - m_t)
    s = jnp.einsum('bhtd,bhsd->bhts', q, k) * dmat
    num = w_inter[..., None] * jnp.einsum('bhed,bhtd->bhte', C, q) + jnp.einsum('bhts,bhse->bhte', s, v)
    den = w_inter * jnp.einsum('bhd,bhtd->bht', n, q) + jnp.sum(s, axis=-1)
    h = num / jnp.maximum(jnp.abs(den), jnp.exp(-m_t))[..., None]
    m_last = m_t[..., -1]
    w_last = jnp.exp(b[..., -1:] - b + ig - m_last[..., None])
    decay = jnp.exp(m_prev[..., -1] - m_last)
    C_new = decay[..., None, None] * C + jnp.einsum('bhs,bhse,bhsd->bhed', w_last, v, k)
    n_new = decay[..., None] * n + jnp.einsum('bhs,bhsd->bhd', w_last, k)
    return (C_new, n_new, m_last), h


def _hgrn_chunk(S, inp):
    q, k, v, lf = inp
    L = q.shape[2]
    causal = jnp.tril(jnp.ones((L, L), dtype=bool))[:, :, None]
    g = jnp.cumsum(lf, axis=2)
    dec = jnp.exp(jnp.where(causal, g[:, :, :, None, :] - g[:, :, None, :, :], -jnp.inf))
    a = jnp.einsum('bhtc,bhtsc,bhsc->bhts', q, dec, k)
    o = jnp.einsum('bhtc,bhce->bhte', q * jnp.exp(g), S) + jnp.einsum('bhts,bhse->bhte', a, v)
    g_last = g[:, :, -1:, :]
    S_new = jnp.exp(g_last[:, :, 0, :])[..., None] * S + jnp.einsum('bhsc,bhse->bhce', k * jnp.exp(g_last - g), v)
    return S_new, o


def _mlstm(q, k, v, ig, lf, C, n, m):
    L = min(CHUNK, q.shape[2])
    xs = (_to_chunks(q, L), _to_chunks(k, L), _to_chunks(v, L), _to_chunks(ig, L), _to_chunks(lf, L))
    (C, n, m), h = lax.scan(_mlstm_chunk, (C, n, m), xs)
    return _from_chunks(h), C, n, m


def _hgrn(q, k, v, lf, S):
    L = min(CHUNK, q.shape[2])
    xs = (_to_chunks(q, L), _to_chunks(k, L), _to_chunks(v, L), _to_chunks(lf, L))
    S, o = lax.scan(_hgrn_chunk, S, xs)
    return _from_chunks(o), S


def _split_points():
    pts, acc = [], 0
    for s in IN_SIZES[:-1]:
        acc += s
        pts.append(acc)
    return pts


def _trunk(x, p, conv0, C0, n0, m0, S0, norm_g, w_in, b_ig, b_fg, conv_w, conv_b, w_qm, w_km,
           mnorm_g, m_skip, w_brm, hgrn_lb, hnorm_g, w_brh, w_out, w_ple, w_pg, final_g):
    f32 = jnp.float32
    B_, T, _ = x.shape
    lb_all = jnp.cumsum(jax.nn.softmax(hgrn_lb.astype(f32), axis=0), axis=0)
    h = x
    convs, Cs, ns, ms, Ss = [], [], [], [], []
    for l in range(DEPTH):
        xn = _rmsnorm(h, norm_g[l])
        proj = xn @ w_in[l]
        u, v_m, z_m, ig_pre, fg_pre, q_h, f_h, i_h, g_h, ga, gb = jnp.split(proj, _split_points(), axis=-1)
        c, conv_new = _causal_conv(u, conv0[l].astype(u.dtype), conv_w[l], conv_b[l])
        c = jax.nn.silu(c).astype(f32)
        ch = c.reshape(B_, T, M_HEADS, M_HD)
        qm = jnp.einsum('bthd,hde->bhte', ch, w_qm[l].astype(f32))
        km = jnp.einsum('bthd,hde->bhte', ch, w_km[l].astype(f32)) * (M_HD ** -0.5)
        vm = v_m.astype(f32).reshape(B_, T, M_HEADS, M_HD).transpose(0, 2, 1, 3)
        ig = (ig_pre.astype(f32) + b_ig[l].astype(f32)).transpose(0, 2, 1)
        lf = jax.nn.log_sigmoid(fg_pre.astype(f32) + b_fg[l].astype(f32)).transpose(0, 2, 1)
        hm, C_new, n_new, m_new = _mlstm(qm, km, vm, ig, lf, C0[l].astype(f32), n0[l].astype(f32), m0[l].astype(f32))
        hm = _head_layernorm(hm.transpose(0, 2, 1, 3), mnorm_g[l].astype(f32))
        hm = (hm + m_skip[l].astype(f32) * c) * jax.nn.silu(z_m.astype(f32))
        ya = hm.astype(x.dtype) @ w_brm[l]
        lb = lb_all[l]
        f_lin = f_h.astype(f32)
        lfh = jnp.log(lb + (1.0 - lb) * jax.nn.sigmoid(f_lin))
        kh = (1.0 - lb) * jax.nn.sigmoid(-f_lin)
        qh = jax.nn.silu(q_h.astype(f32))
        to_heads = lambda a: a.reshape(B_, T, H_HEADS, H_EXPAND).transpose(0, 2, 1, 3)
        oh, S_new = _hgrn(to_heads(qh), to_heads(kh), to_heads(i_h.astype(f32)), to_heads(lfh), S0[l].astype(f32))
        oh = _head_rmsnorm(oh.transpose(0, 2, 1, 3), hnorm_g[l].astype(f32)) * jax.nn.silu(g_h.astype(f32))
        yb = oh.astype(x.dtype) @ w_brh[l]
        y = jax.nn.sigmoid(ga) * ya + jax.nn.sigmoid(gb) * yb
        h = h + y @ w_out[l]
        h = h + jax.nn.sigmoid(h @ w_pg[l]) * (p[l] @ w_ple[l])
        convs.append(conv_new); Cs.append(C_new); ns.append(n_new); ms.append(m_new); Ss.append(S_new)
    out = _rmsnorm(h, final_g)
    return out, jnp.stack(convs), jnp.stack(Cs), jnp.stack(ns), jnp.stack(ms), jnp.stack(Ss)


def setup_inputs(seed: int = 0) -> dict:
    key = jax.random.key(seed)
    ks = jax.random.split(key, 32)
    nrm = lambda k, s, sc: jax.random.normal(k, s, jnp.float32) * sc
    return {
        'x_prompt': nrm(ks[0], (BATCH, SEQ, D_MODEL), 1.0),
        'x_sample': nrm(ks[1], (DEC_BATCH, DEC_SEQ, D_MODEL), 1.0),
        'state_conv': nrm(ks[2], (DEPTH, DEC_BATCH, CONV_W - 1, M_INNER), 1.0),
        'state_mlstm_C': nrm(ks[3], (DEPTH, DEC_BATCH, M_HEADS, M_HD, M_HD), 0.05),
        'state_mlstm_n': nrm(ks[4], (DEPTH, DEC_BATCH, M_HEADS, M_HD), 0.05),
        'state_mlstm_m': nrm(ks[5], (DEPTH, DEC_BATCH, M_HEADS), 1.0),
        'state_hgrn': nrm(ks[6], (DEPTH, DEC_BATCH, H_HEADS, H_EXPAND, H_EXPAND), 0.1),
        'p_prompt': nrm(ks[7], (DEPTH, BATCH, SEQ, PLE_DIM), 1.0),
        'p_sample': nrm(ks[8], (DEPTH, DEC_BATCH, DEC_SEQ, PLE_DIM), 1.0),
        'norm_g': 1.0 + nrm(ks[9], (DEPTH, D_MODEL), 0.1),
        'w_in': nrm(ks[10], (DEPTH, D_MODEL, N_IN), D_MODEL ** -0.5),
        'b_ig': nrm(ks[11], (DEPTH, M_HEADS), 0.1),
        'b_fg': jnp.linspace(3.0, 6.0, M_HEADS)[None, :] + nrm(ks[12], (DEPTH, M_HEADS), 0.1),
        'conv_w': nrm(ks[13], (DEPTH, CONV_W, M_INNER), CONV_W ** -0.5),
        'conv_b': nrm(ks[14], (DEPTH, M_INNER), 0.02),
        'w_qm': nrm(ks[15], (DEPTH, M_HEADS, M_HD, M_HD), M_HD ** -0.5),
        'w_km': nrm(ks[16], (DEPTH, M_HEADS, M_HD, M_HD), M_HD ** -0.5),
        'mnorm_g': 1.0 + nrm(ks[17], (DEPTH, M_INNER), 0.1),
        'm_skip': 1.0 + nrm(ks[18], (DEPTH, M_INNER), 0.1),
        'w_brm': nrm(ks[19], (DEPTH, M_INNER, D_MODEL), M_INNER ** -0.5),
        'hgrn_lb': nrm(ks[20], (DEPTH + 1, H_WIDTH), 0.1),
        'hnorm_g': 1.0 + nrm(ks[21], (DEPTH, H_WIDTH), 0.1),
        'w_brh': nrm(ks[22], (DEPTH, H_WIDTH, D_MODEL), H_WIDTH ** -0.5),
        'w_out': nrm(ks[23], (DEPTH, D_MODEL, D_MODEL), D_MODEL ** -0.5),
        'w_ple': nrm(ks[24], (DEPTH, PLE_DIM, D_MODEL), PLE_DIM ** -0.5),
        'w_pg': nrm(ks[25], (DEPTH, D_MODEL, D_MODEL), D_MODEL ** -0.5),
        'final_g': 1.0 + nrm(ks[26], (D_MODEL,), 0.1),
    }


def reference(x_prompt, x_sample, state_conv, state_mlstm_C, state_mlstm_n, state_mlstm_m, state_hgrn,
              p_prompt, p_sample, norm_g, w_in, b_ig, b_fg, conv_w, conv_b, w_qm, w_km, mnorm_g, m_skip,
              w_brm, hgrn_lb, hnorm_g, w_brh, w_out, w_ple, w_pg, final_g):
    f32 = jnp.float32
    B_ = x_prompt.shape[0]
    conv0 = jnp.zeros((DEPTH, B_, CONV_W - 1, M_INNER), x_prompt.dtype)
    C0 = jnp.zeros((DEPTH, B_, M_HEADS, M_HD, M_HD), f32)
    n0 = jnp.zeros((DEPTH, B_, M_HEADS, M_HD), f32)
    m0 = jnp.zeros((DEPTH, B_, M_HEADS), f32)
    S0 = jnp.zeros((DEPTH, B_, H_HEADS, H_EXPAND, H_EXPAND), f32)
    y_prompt, conv_p, C_p, n_p, m_p, S_p = _trunk(
        x_prompt, p_prompt, conv0, C0, n0, m0, S0, norm_g, w_in, b_ig, b_fg, conv_w, conv_b, w_qm, w_km,
        mnorm_g, m_skip, w_brm, hgrn_lb, hnorm_g, w_brh, w_out, w_ple, w_pg, final_g)
    y_sample, conv_s, C_s, n_s, m_s, S_s = _trunk(
        x_sample, p_sample, state_conv, state_mlstm_C, state_mlstm_n, state_mlstm_m, state_hgrn,
        norm_g, w_in, b_ig, b_fg, conv_w, conv_b, w_qm, w_km,
        mnorm_g, m_skip, w_brm, hgrn_lb, hnorm_g, w_brh, w_out, w_ple, w_pg, final_g)
    return (y_prompt, y_sample, conv_p, C_p, n_p, m_p, S_p, conv_s, C_s, n_s, m_s, S_s)
```

```python
import numpy as np
from contextlib import ExitStack
import concourse.bass as bass
import concourse.mybir as mybir
from concourse.bass_utils import run_bass_kernel_spmd

F32 = mybir.dt.float32
BF16 = mybir.dt.bfloat16
AF = mybir.ActivationFunctionType
ALU = mybir.AluOpType
AX = mybir.AxisListType

D = 1024
MI = 2048
MH = 4
HD = 512
HH = 8
HE = 128
PLE = 256
NIN = 12296
MHL = 2
MIL = 1024
HGL = 4
HWL = 512
NINL = 7172
EPS = 1e-6
NEG = -30000.0


class V:
    __slots__ = ("t", "ap")

    def __init__(self, t, ap):
        self.t = t
        self.ap = ap

    def __getitem__(self, key):
        return V(self.t, self.ap[key])

    def bitcast(self, dt):
        return V(self.t, self.ap.bitcast(dt))

    def rr(self, pat, **kw):
        return V(self.t, self.ap.rearrange(pat, **kw))

    def bc(self, shape):
        return V(self.t, self.ap.to_broadcast(list(shape)))


class T:
    def __init__(self, h, name):
        self.h = h
        self.name = name
        self.w = None
        self.r = {}
        self.dsem = None
        self.dcnt = 0

    def __getitem__(self, key):
        return V(self, self.h[key])

    @property
    def v(self):
        return V(self, self.h[:])


class Eng:
    def __init__(self, k, name, h, strict=False):
        self.k = k
        self.name = name
        self.h = h
        self.sem = k.new_sem("e_" + name)
        self.cnt = 0
        self.last = None
        self.seen = {}
        self.strict = strict

    def wait(self, ev):
        sem, val, key = ev
        prod = self.k.engs.get(key)
        if prod is not None and val > prod.cnt:
            assert prod.last is not None and val == prod.cnt + 1, (key, val, prod.cnt)
            prod.last.then_inc(prod.sem, 1)
            prod.last = None
            prod.cnt += 1
        if self.seen.get(key, 0) < val:
            self.h.wait_ge(sem, val)
            self.seen[key] = val


class K:
    def __init__(self, nc, es):
        self.nc = nc
        self.es = es
        self.nsem = 0
        self.pe = Eng(self, "pe", nc.tensor)
        self.act = Eng(self, "act", nc.scalar)
        self.dve = Eng(self, "dve", nc.vector)
        self.pool = Eng(self, "pool", nc.gpsimd, strict=True)
        self.sp = Eng(self, "sp", nc.sync)
        self.engs = {e.name: e for e in (self.pe, self.act, self.dve, self.pool, self.sp)}
        self.out_events = []
        self.dsems = {}
        self.uid = 0
        self.dma_last = {}

    def new_sem(self, name):
        self.nsem += 1
        return self.es.enter_context(self.nc.semaphore(name))

    def sb(self, name, shape, dt=F32, es=None):
        self.uid += 1
        h = (es or self.es).enter_context(self.nc.sbuf_tensor("%s_%d" % (name, self.uid), list(shape), dt))
        return T(h, name)

    def ps(self, name, shape, dt=F32):
        h = self.es.enter_context(self.nc.psum_tensor(name, list(shape), dt))
        return T(h, name)

    def dram(self, name, shape, dt):
        h = self.nc.dram_tensor(name, list(shape), dt, kind="Internal")
        return T(h, name)

    def _pre(self, eng, rd, wr):
        for v in rd:
            t = v.t
            if t.w is not None:
                eng.wait(t.w)
        for v in wr:
            t = v.t
            if t.w is not None and (eng.strict or t.w[2] != eng.name):
                eng.wait(t.w)
            for key, ev in t.r.items():
                if eng.strict or key != eng.name:
                    eng.wait(ev)

    def _post(self, ev, rd, wr):
        for v in wr:
            v.t.w = ev
            v.t.r = {}
        for v in rd:
            if v.t.w is ev:
                continue
            v.t.r[ev[2]] = ev

    def op(self, eng, fn, rd, wr):
        rd = [v for v in rd if isinstance(v, V)]
        wr = [v for v in wr if isinstance(v, V)]
        self._pre(eng, rd, wr)
        ins = fn()
        eng.last = ins
        ev = (eng.sem, eng.cnt + 1, eng.name)
        self._post(ev, rd, wr)
        return ins

    def dma(self, q, out, in_, semt=None, **kw):
        rd = [in_] if isinstance(in_, V) else []
        wr = [out] if isinstance(out, V) else []
        self._pre(q, rd, wr)
        o = out.ap if isinstance(out, V) else out
        i = in_.ap if isinstance(in_, V) else in_
        ins = q.h.dma_start(out=o, in_=i, **kw)
        st = semt or (wr[0].t if wr else rd[0].t)
        key = "d_" + st.name
        if key not in self.dsems:
            self.dsems[key] = [self.new_sem(key), 0]
        ent = self.dsems[key]
        ent[1] += 16
        ins.then_inc(ent[0], 16)
        ev = (ent[0], ent[1], key)
        self.dma_last[key] = ev
        self._post(ev, rd, wr)
        if not wr:
            self.out_events.append(ev)
        return ev

    def barrier(self):
        engs = [self.pe, self.act, self.dve, self.pool, self.sp]
        evs = [(e.sem, e.cnt + (1 if e.last is not None else 0), e.name) for e in engs
               if e.cnt > 0 or e.last is not None]
        for e in engs:
            for ev in evs:
                if ev[2] != e.name:
                    e.wait(ev)
            for ev in self.dma_last.values():
                e.wait(ev)

    def mm(self, out, lhsT, rhs, start=True, stop=True, **kw):
        return self.op(self.pe, lambda: self.nc.tensor.matmul(out.ap, lhsT=lhsT.ap, rhs=rhs.ap, start=start, stop=stop, **kw),
                       [lhsT, rhs], [out])

    def tr(self, out, in_, ident):
        return self.op(self.pe, lambda: self.nc.tensor.transpose(out.ap, in_.ap, ident.ap), [in_, ident], [out])

    def actf(self, out, in_, func, bias=None, scale=None, accum=None):
        kw = {}
        if bias is not None:
            kw["bias"] = bias.ap if isinstance(bias, V) else bias
        if scale is not None:
            kw["scale"] = scale.ap if isinstance(scale, V) else scale
        if accum is not None:
            kw["accum_out"] = accum.ap
        return self.op(self.act, lambda: self.nc.scalar.activation(out=out.ap, in_=in_.ap, func=func, **kw),
                       [in_, bias, scale], [out, accum])

    def _e(self, eng):
        return {"dve": self.dve, "pool": self.pool, "act": self.act}[eng] if isinstance(eng, str) else eng

    def tt(self, eng, out, a, b, op):
        e = self._e(eng)
        return self.op(e, lambda: e.h.tensor_tensor(out=out.ap, in0=a.ap, in1=b.ap, op=op), [a, b], [out])

    def ts(self, eng, out, a, s1, op0, s2=None, op1=None, accum=None):
        e = self._e(eng)
        a1 = s1.ap if isinstance(s1, V) else s1
        a2 = s2.ap if isinstance(s2, V) else s2
        kw = {}
        if op1 is not None:
            kw["op1"] = op1
        if accum is not None:
            kw["accum_out"] = accum.ap
        return self.op(e, lambda: e.h.tensor_scalar(out=out.ap, in0=a.ap, scalar1=a1, scalar2=a2, op0=op0, **kw),
                       [a, s1, s2], [out, accum])

    def stt(self, out, a, s, b, op0, op1):
        e = self.dve
        sa = s.ap if isinstance(s, V) else s
        return self.op(e, lambda: e.h.scalar_tensor_tensor(out=out.ap, in0=a.ap, scalar=sa, in1=b.ap, op0=op0, op1=op1),
                       [a, s, b], [out])

    def cp(self, eng, out, in_):
        e = self._e(eng)
        if e is self.act:
            return self.op(e, lambda: self.nc.scalar.copy(out=out.ap, in_=in_.ap), [in_], [out])
        return self.op(e, lambda: e.h.tensor_copy(out=out.ap, in_=in_.ap), [in_], [out])

    def memset(self, eng, out, val):
        e = self._e(eng)
        return self.op(e, lambda: e.h.memset(out.ap, val), [], [out])


def _wplan():
    plan = []
    for h in range(MHL):
        plan += [("U%d" % h, "in", 0 + h * 512), ("Z%d" % h, "in", 1024 + h * 512), ("V%d" % h, "in", 2048 + h * 512),
                 ("QK%d" % h, "qk", h)]
    for h in range(MHL):
        plan += [("BRM%d" % h, "brm", h)]
    plan += [("QH0", "in", 3076), ("FH0", "in", 3588), ("IH0", "in", 4100), ("GH0", "in", 4612)]
    plan += [("GA0", "in", 5124), ("GA1", "in", 5124 + 512)]
    plan += [("BRH", "brh", 0)]
    plan += [("GB0", "in", 6148), ("GB1", "in", 6148 + 512)]
    plan += [("OUT0", "sq", ("w_out", 0)), ("OUT1", "sq", ("w_out", 1)), ("PG0", "sq", ("w_pg", 0)), ("PG1", "sq", ("w_pg", 1)),
             ("PLE", "ple", 0)]
    return plan


def build(TP=4096, NS=8, dbg=None):
    nc = bass.Bass("TRN2", target_bir_lowering=False)
    dbg = dbg or {}

    def din(name, shape):
        return nc.dram_tensor(name, list(shape), F32, kind="ExternalInput").ap()

    def dout(name, shape):
        return nc.dram_tensor(name, list(shape), F32, kind="ExternalOutput").ap()

    I = dict(
        xp=din("xp", [TP, D]), pp=din("pp", [TP, PLE]),
        xs=din("xs", [NS * 32, D]), ps=din("ps", [NS * 32, PLE]),
        sconv=din("sconv", [NS * 3, MIL]), sC=din("sC", [NS, MHL, HD, HD]), sn=din("sn", [NS, MHL * 4, 128]),
        sm=din("sm", [NS, MHL]), sS=din("sS", [NS, HGL, HE, HE]),
        norm_g=din("norm_g", [D]), w_in=din("w_in", [D, NINL]), b_ig=din("b_ig", [MHL]), b_fg=din("b_fg", [MHL]),
        conv_w=din("conv_w", [4, MIL]), conv_b=din("conv_b", [MIL]), w_qm=din("w_qm", [MHL, HD, HD]),
        w_km=din("w_km", [MHL, HD, HD]), mnorm_g=din("mnorm_g", [MIL]), m_skip=din("m_skip", [MIL]),
        w_brm=din("w_brm", [MIL, D]), hgrn_lb=din("hgrn_lb", [2, HWL]), hnorm_g=din("hnorm_g", [HWL]),
        w_brh=din("w_brh", [HWL, D]), w_out=din("w_out", [D, D]), w_ple=din("w_ple", [PLE, D]), w_pg=din("w_pg", [D, D]),
        final_g=din("final_g", [D]),
    )
    O = dict(
        yp=dout("yp", [TP, D]), ys=dout("ys", [NS * 32, D]),
        conv_p=dout("conv_p", [3, MIL]), C_p=dout("C_p", [MHL, HD, HD]), n_p=dout("n_p", [MHL * 4, 128]),
        m_p=dout("m_p", [MHL, 1]), S_p=dout("S_p", [HGL, HE, HE]),
        conv_s=dout("conv_s", [NS * 3, MIL]), C_s=dout("C_s", [NS, MHL, HD, HD]), n_s=dout("n_s", [NS, MHL * 4, 128]),
        m_s=dout("m_s", [NS, MHL, 1]), S_s=dout("S_s", [NS, HGL, HE, HE]),
    )
    with ExitStack() as es:
        k = K(nc, es)
        _program(nc, k, I, O, TP, NS, dbg)
    return nc


def _program(nc, k, I, O, TP, NS, dbg):
    sp, pe, act, dve, pool = k.sp, k.pe, k.act, k.dve, k.pool
    plan = _wplan()
    NW = len(plan)
    do_prompt = dbg.get("prompt", True)
    do_sample = dbg.get("sample", True)

    NRING = 3
    ring = [k.sb("wring%d" % i, [128, 4096], BF16) for i in range(NRING)]
    wstate = dict(next_load=0, total=0)
    seq = []

    def w_issue(upto):
        while wstate["next_load"] < min(upto + 1, len(seq)):
            j = wstate["next_load"]
            pi = seq[j]
            sp.wait(grp_ev[pi // GRP])
            k.dma(sp, ring[j % NRING].v, wscr.h[pi])
            wstate["next_load"] += 1

    def w_get(key, hold=0):
        j = wstate["total"]
        assert plan[seq[j]][0] == key, (plan[seq[j]][0], key)
        w_issue(j + NRING - 1 - hold)
        wstate["total"] += 1
        return ring[j % NRING]

    def w3(w, kk, lo=0, hi=4096):
        return V(w, w.h[:, lo:hi].rearrange("p (k c) -> p k c", k=kk))

    identF = k.sb("identF", [128, 128])
    identB = k.sb("identB", [128, 128], BF16)
    utri = k.sb("utri", [128, 128])
    mnegT = k.sb("mnegT", [128, 128])
    maskbd = k.sb("maskbd", [128, 128])
    sel = k.sb("sel", [2, 2, 128])
    ones4 = k.sb("ones4", [2, 128])
    onesb = k.sb("onesb", [128, 1], BF16)
    m01p = k.sb("m01p", [128, 512])
    m01s = k.sb("m01s", [128, 256])

    def asel(t, pattern, cmp, fill, cm):
        k.op(pool, lambda: nc.gpsimd.affine_select(out=t.h[:], in_=t.h[:], pattern=pattern, compare_op=cmp, fill=fill,
                                                   base=0, channel_multiplier=cm), [t.v], [t.v])
    k.memset(pool, identF.v, 1.0)
    asel(identF, [[-1, 128]], ALU.is_equal, 0.0, 1)
    k.cp(pool, identB.v, identF.v)
    k.memset(pool, utri.v, 1.0)
    asel(utri, [[1, 128]], ALU.is_ge, 0.0, -1)
    k.memset(pool, mnegT.v, 0.0)
    asel(mnegT, [[1, 128]], ALU.is_ge, NEG, -1)
    k.cp(pool, maskbd.v, utri.v)
    k.memset(pool, maskbd[0:64, 64:128], 0.0)
    k.memset(pool, sel.v, 1.0)
    asel(sel, [[-1, 2], [0, 128]], ALU.is_equal, 0.0, 1)
    k.memset(pool, ones4.v, 1.0)
    k.memset(pool, onesb.v, 1.0)
    k.memset(pool, m01p.v, 1.0)
    k.memset(pool, m01p.v.rr("p (c l) -> p c l", l=64)[:, :, 0:1], 0.0)
    k.memset(pool, m01s.v, 1.0)
    k.memset(pool, m01s.v.rr("p (c l) -> p c l", l=32)[:, :, 0:1], 0.0)

    cst = T(None, "cst")

    def cload(name, shape, src, **kw):
        t = k.sb(name, shape)
        k.dma(sp, t.v, src, semt=cst, **kw)
        return t
    slow = dict(allow_slow_non_contiguous=True)
    ng_col = cload("ng_col", [128, 8], I["norm_g"].rearrange("(c p) -> p c", p=128), **slow)
    cw_col = k.sb("cw_col", [128, 8, 4])
    for j in range(4):
        k.dma(sp, cw_col[:, :, j], I["conv_w"][j].rearrange("(c p) -> p c", p=128), semt=cst, **slow)
    cb_col = cload("cb_col", [128, 8], I["conv_b"].rearrange("(c p) -> p c", p=128), **slow)
    mg_col = cload("mg_col", [128, 8], I["mnorm_g"].rearrange("(c p) -> p c", p=128), **slow)
    sk_col = cload("sk_col", [128, 8], I["m_skip"].rearrange("(c p) -> p c", p=128), **slow)
    hg_col = cload("hg_col", [128, 4], I["hnorm_g"].rearrange("(c p) -> p c", p=128), **slow)
    lb_raw = k.sb("lb_raw", [128, 4, 2])
    for j in range(2):
        k.dma(sp, lb_raw[:, :, j], I["hgrn_lb"][j].rearrange("(c p) -> p c", p=128), semt=cst, **slow)
    big_bc = cload("big_bc", [128, 2], I["b_ig"].partition_broadcast(128))
    bfg_bc = cload("bfg_bc", [128, 2], I["b_fg"].partition_broadcast(128))
    fg_bc = cload("fg_bc", [128, D], I["final_g"].partition_broadcast(128))
    for t_ in (ng_col, cw_col, cb_col, mg_col, sk_col, hg_col, lb_raw, big_bc, bfg_bc, fg_bc):
        t_.w = k.dma_last["d_cst"]
    lb_col = k.sb("lb_col", [128, 4])
    oml_col = k.sb("oml_col", [128, 4])
    noml_col = k.sb("noml_col", [128, 4])
    k.tt(dve, lb_col.v, lb_raw[:, :, 0], lb_raw[:, :, 1], ALU.subtract)
    k.actf(lb_col.v, lb_col.v, AF.Sigmoid)
    k.ts(dve, oml_col.v, lb_col.v, -1.0, ALU.mult, 1.0, ALU.add)
    k.ts(dve, noml_col.v, oml_col.v, -1.0, ALU.mult)

    wscr = k.dram("wscr", [NW, 128, 4096], BF16)
    GRP = 3
    wgrp = [T(None, "wg%d" % g) for g in range((NW + GRP - 1) // GRP)]
    wg_sb = k.sb("wg_sb", [128, 8, 4], BF16)
    k.dma(pool, wg_sb.v, I["w_in"][:, 3072:3076].rearrange("(k p) c -> p k c", p=128), allow_slow_non_contiguous=True)
    grp_ev = {}
    for i, (key, kind, arg) in enumerate(plan):
        dst = wscr.h[i]
        g = wgrp[i // GRP]
        if kind == "in":
            src = I["w_in"][:, arg:arg + 512].rearrange("(k p) c -> p k c", p=128)
            ev = k.dma(pool, dst.rearrange("p (k c) -> p k c", k=8), src, semt=g)
        elif kind == "qk":
            for j, wn in enumerate(("w_qm", "w_km")):
                src = I[wn][arg].rearrange("(k p) c -> p k c", p=128)
                ev = k.dma(pool, dst[:, j * 2048:(j + 1) * 2048].rearrange("p (k c) -> p k c", k=4), src, semt=g)
        elif kind == "brm":
            src = I["w_brm"][arg * 512:(arg + 1) * 512, :].rearrange("(k p) c -> p k c", p=128)
            ev = k.dma(pool, dst.rearrange("p (k c) -> p k c", k=4), src, semt=g)
        elif kind == "sq":
            src = I[arg[0]][:, arg[1] * 512:(arg[1] + 1) * 512].rearrange("(k p) c -> p k c", p=128)
            ev = k.dma(pool, dst.rearrange("p (k c) -> p k c", k=8), src, semt=g)
        elif kind == "brh":
            src = I["w_brh"].rearrange("(k p) c -> p k c", p=128)
            ev = k.dma(pool, dst.rearrange("p (k c) -> p k c", k=4), src, semt=g)
        elif kind == "ple":
            src = I["w_ple"].rearrange("(k p) c -> p k c", p=128)
            ev = k.dma(pool, dst[:, 0:2048].rearrange("p (k c) -> p k c", k=2), src, semt=g)
        grp_ev[i // GRP] = ev
    k.out_events = []

    pa = [k.ps("pa%d" % i, [128, 512]) for i in range(2)]
    psm = k.ps("psm", [128, 512])
    pnum = k.ps("pnum", [128, 512])
    pint = k.ps("pint", [128, 512])
    pdc = k.ps("pdc", [128, 512])
    ptr = k.ps("ptr", [128, 512])
    ptrb = k.ps("ptrb", [128, 1024], BF16)
    stt_ = dict(pa=0, dc=0)

    def next_pa():
        stt_["pa"] ^= 1
        return pa[stt_["pa"]]

    def next_dc():
        stt_["dc"] = (stt_["dc"] + 1) % 3
        return [pdc, pa[0], pa[1]][stt_["dc"]]

    C_nat = k.sb("C_nat", [128, MHL, 4, 512])
    CT_pp = [k.sb("CT_bf%d" % i, [128, MHL, 4, 512], BF16) for i in range(2)]
    n_col = k.sb("n_col", [128, 8, 8])
    S_st = k.sb("S_st", [128, HGL, HE])
    S_bf = [k.sb("S_bf%d" % i, [128, HGL, HE], BF16) for i in range(2)]
    hist = k.sb("hist", [128, 8, 8, 3])
    mcar = k.sb("mcar", [2, 8])
    cvo = k.sb("cvo", [24, MIL])
    nout = k.sb("nout", [8, 8, 128])

    xsrc = {(n_, w_): k.dram("xsrc_%d%s" % (n_, w_), [n_, D], F32) for n_ in (512, 256) for w_ in "ab"}
    xdst = {(n_, w_): k.dram("xdst_%d%s" % (n_, w_), [n_, D], F32) for n_ in (512, 256) for w_ in "ab"}

    def exchange(NT, which, src_t, NSB):
        xs_, xd_ = xsrc[(NT, which)].h, xdst[(NT, which)].h
        for b_ in range(NSB):
            pool.wait(k.dma(pool, xs_[b_ * 128:(b_ + 1) * 128, :], src_t[:, b_, :]))
        cci = nc.gpsimd.collective_compute("AllReduce", ALU.add, ins=[xs_[:, :]], outs=[xd_[:, :]], replica_groups=RG)
        ccst["n"] += 1
        cci.then_inc(ccsem)
        return (ccsem, ccst["n"], "cc"), xd_
    ccsem = k.new_sem("ccsem")
    ccst = dict(n=0)
    RG = [[0, 1], [2, 3], [4, 5], [6, 7]]

    def tile(x_dram, p_dram, y_dram, NT, TS, NU, NSEG, first, last, sample, tix=0):
        NSB = NT // 128
        SEG = NT // NSEG
        L = 32 if sample else 64
        NCH = TS // L
        m01 = m01s if sample else m01p
        hmask = utri if sample else maskbd

        def utok(u):
            return slice(u * TS, (u + 1) * TS)

        def blk(b):
            return slice(b * 128, (b + 1) * 128)

        with ExitStack() as tes:
            xnT = k.sb("xnT", [128, 8, NT], BF16, es=tes)
            y_acc = k.sb("y_acc", [128, NSB, D], es=tes)
            CT_cur, CT_nxt = CT_pp[tix % 2], CT_pp[(tix + 1) % 2]
            u_tm = k.sb("u_tm", [128, NU, 2], es=tes)
            negM = k.sb("negM", [2, NU, TS], es=tes)
            wi_tm = k.sb("wi_tm", [128, NU, 2], es=tes)
            cl_tm = k.sb("cl_tm", [128, NU, 2], es=tes)
            dec_bc = k.sb("dec_bc", [128, NU, 2], es=tes)
            wiT_all = k.sb("wiT_all", [2, NU, TS], es=tes)
            wl_all = k.sb("wl_all", [128, NU, 2], es=tes)
            wlb_all = k.sb("wlb_all", [128, NU, 2], BF16, es=tes)

            def run(ga, gb, ra=1, rb=1):
                gens = [g_ for g_ in (ga, gb) if g_ is not None]
                rate = {id(ga): ra, id(gb): rb}
                while gens:
                    for g_ in list(gens):
                        for _ in range(rate[id(g_)]):
                            try:
                                next(g_)
                            except StopIteration:
                                gens.remove(g_)
                                break

            with ExitStack() as pes:
                x_tm = k.sb("x_tm", [128, NSB, D], es=pes)
                xs_b = k.sb("xs_b", [128, D], BF16, es=pes)
                junk = k.sb("junk", [128, D], es=pes)
                ssq = k.sb("ssq", [128, 4], es=pes)
                for b in range(NSB):
                    k.dma(sp, x_tm[:, b, :], x_dram[b * 128:(b + 1) * 128, :])
                if sample:
                    k.dma(sp, mcar[0:2, 0:NU], I["sm"].rearrange("s h -> h s"), allow_slow_non_contiguous=True)
                    sc_tm = k.sb("sc_tm", [24, MIL], es=pes)
                    NS3 = NSEG * 3
                    k.dma(sp, sc_tm[0:NSEG * 3, :], I["sconv"])
                    for grp in range(2):
                        for cc in range(4):
                            c16 = grp * 4 + cc
                            k.tr(ptr[:, cc * NS3:(cc + 1) * NS3], sc_tm[0:NS3, c16 * 128:(c16 + 1) * 128], identF[0:NS3, 0:NS3])
                        k.cp(act, hist[:, grp * 4:(grp + 1) * 4, 0:NSEG, :],
                             ptr[:, 0:4 * NS3].rr("p (c s j) -> p c s j", c=4, j=3))
                    nrow = k.sb("nrow", [8, 128], es=pes)
                    for u in range(NU):
                        k.dma(sp, nrow.v, I["sn"][u])
                        k.tr(psm[:, 256:264], nrow.v, identF[0:8, 0:8])
                        k.cp(act, n_col[:, u, :], psm[:, 256:264])
                elif first:
                    k.memset(pool, mcar.v, 0.0)
                    k.memset(pool, hist.v, 0.0)
                    k.memset(pool, n_col.v, 0.0)
                    k.memset(pool, S_st.v, 0.0)
                    k.memset(pool, S_bf[0].v, 0.0)
                for b in range(NSB):
                    k.actf(junk.v, x_tm[:, b, :], AF.Square, accum=ssq[:, 0:1])
                    k.actf(ssq[:, 1:2], ssq[:, 0:1], AF.Ln, scale=1.0 / D, bias=EPS)
                    k.actf(ssq[:, 2:3], ssq[:, 1:2], AF.Exp, scale=-0.5)
                    k.ts(dve, xs_b.v, x_tm[:, b, :], ssq[:, 2:3], ALU.mult)
                    for c in range(8):
                        k.tr(ptrb[:, c * 128:(c + 1) * 128], xs_b[:, c * 128:(c + 1) * 128], identB.v)
                    k.tt(dve, xnT[:, :, blk(b)], ptrb.v.rr("p (c t) -> p c t", c=8),
                         V(ng_col, ng_col.h[:].unsqueeze(2).to_broadcast([128, 8, 128])), ALU.mult)
            k.barrier()

            with ExitStack() as pes:
                u_exts = [k.sb("u_ext%d" % i, [128, NSEG, SEG + 3], es=pes) for i in range(2)]
                cv = k.sb("cv", [128, NSEG, SEG], es=pes)
                gs = [k.sb("gs%d" % i, [128, 2], es=pes) for i in range(8)]
                g_ig, g_fg, g_e1, g_sp, g_csp, g_d4, g_d4b, g_wl = gs
                gr = [k.sb("gr%d" % i, [2, TS], es=pes) for i in range(5)]
                g_uT, g_M, g_mT, g_wiT, g_clT = gr
                BS = [dict(cT=k.sb("cT%d" % i, [128, 4, NT], BF16, es=pes), szT=k.sb("szT%d" % i, [128, 4, NT], BF16, es=pes),
                           qT=k.sb("qT%d" % i, [128, 4, NT], BF16, es=pes), kT=k.sb("kT%d" % i, [128, 4, NT], BF16, es=pes),
                           k_tm=k.sb("k_tm%d" % i, [128, NU, 512], BF16, es=pes), v_tm=k.sb("v_tm%d" % i, [128, NU, 512], BF16, es=pes),
                           hmT=k.sb("hmT%d" % i, [128, 4, NT], BF16, es=pes)) for i in range(2)]
                pdc2 = V(ptrb, ptrb.h[:].bitcast(F32))
                dcst = dict(i=0)

                def next_dc2():
                    dcst["i"] ^= 1
                    return pdc.v if dcst["i"] else pdc2
                NCT = 2 if sample else NU - 1
                CTtmp = [k.sb("CTtmp%d" % i, [128, 4, 512], BF16, es=pes) for i in range(NCT)]
                n_bfs = k.sb("n_bfs", [128, NU + 1, 4], BF16, es=pes)
                dT_sb = [k.sb("dT_sb%d" % i, [128, 128], es=pes) for i in range(2)]
                sT_sb = [k.sb("sT_sb%d" % i, [128, 128], BF16, es=pes) for i in range(2)]
                qwT = [k.sb("qwT%d" % i, [128, 4, 128], BF16, es=pes) for i in range(2)]
                hraw = [k.sb("hraw%d" % i, [128, 512], es=pes) for i in range(2)]
                hn_sb = [k.sb("hn_sb%d" % i, [128, 512], es=pes) for i in range(2)]
                tmp_sb = [k.sb("tmp_sb%d" % i, [128, 4, 128], es=pes) for i in range(2)]
                wv_all = k.sb("wv_all", [128, NU, 512], BF16, es=pes)
                dd = [[k.sb("dd%d_%d" % (i, j), [128, 8], es=pes) for i in range(6)] for j in range(2)]
                SC = float(HD) ** -0.5
                cnt = dict(u=0)

                tb = dict(i=0)

                def make_CT3(h, dst):
                    for dc in range(4):
                        tb["i"] = (tb["i"] + 1) % 3
                        bank = [ptr, pnum, pint][tb["i"]]
                        for ec in range(4):
                            k.tr(bank[:, ec * 128:(ec + 1) * 128], C_nat[:, h, ec, dc * 128:(dc + 1) * 128], identF.v)
                        k.cp(act, dst[:, dc, :], bank.v)
                        yield

                def gate_gen():
                    for u in range(NU):
                        slot = u if sample else 0
                        G = psm[0:TS, 384:388]
                        for kk in range(8):
                            k.mm(G, xnT[:, kk, utok(u)], wg_sb[:, kk, :], start=(kk == 0), stop=(kk == 7))
                        k.tt(dve, g_ig[0:TS, :], G[:, 0:2], big_bc[0:TS, :], ALU.add)
                        k.tt(dve, g_fg[0:TS, :], G[:, 2:4], bfg_bc[0:TS, :], ALU.add)
                        k.actf(g_e1[0:TS, :], g_fg[0:TS, :], AF.Exp, scale=-1.0)
                        k.actf(g_sp[0:TS, :], g_e1[0:TS, :], AF.Ln, bias=1.0)
                        yield
                        k.mm(psm[0:TS, 392:394], utri[0:TS, 0:TS], g_sp[0:TS, :])
                        k.cp(act, g_csp[0:TS, :], psm[0:TS, 392:394])
                        yield
                        k.tt(dve, u_tm[0:TS, u, :], g_ig[0:TS, :], g_csp[0:TS, :], ALU.add)
                        k.tr(psm[0:2, 256:256 + TS], u_tm[0:TS, u, :], identF[0:TS, 0:TS])
                        k.cp(dve, g_uT.v, psm[0:2, 256:256 + TS])
                        yield
                        k.tr(psm[0:2, 128:128 + TS], g_csp[0:TS, :], identF[0:TS, 0:TS])
                        k.op(dve, lambda: nc.vector.tensor_tensor_scan(out=g_M.h[:], data0=g_uT.h[:], data1=g_uT.h[:],
                                                                        initial=mcar.h[0:2, slot:slot + 1], op0=ALU.max, op1=ALU.max),
                             [g_uT.v, mcar.v], [g_M.v])
                        k.ts(dve, negM[0:2, u, :], g_M.v, -1.0, ALU.mult)
                        yield
                        k.tt(dve, g_mT.v, g_M.v, psm[0:2, 128:128 + TS], ALU.subtract)
                        k.actf(g_wiT.v, g_M.v, AF.Exp, scale=-1.0, bias=mcar[0:2, slot:slot + 1])
                        k.actf(g_clT.v, g_mT.v, AF.Exp, scale=-1.0)
                        yield
                        k.cp(dve, mcar[0:2, slot:slot + 1], g_mT[0:2, TS - 1:TS])
                        k.tr(psm[0:TS, 396:398], g_wiT.v, identF[0:2, 0:2])
                        k.cp(act, wi_tm[0:TS, u, :], psm[0:TS, 396:398])
                        k.tr(psm[0:TS, 400:402], g_clT.v, identF[0:2, 0:2])
                        k.cp(act, cl_tm[0:TS, u, :], psm[0:TS, 400:402])
                        yield
                        k.ts(dve, g_d4[0:2, 0:2], identF[0:2, 0:2], g_wiT[0:2, TS - 1:TS], ALU.mult)
                        k.mm(psm[:, 404:406], ones4.v, g_d4[0:2, 0:2])
                        k.cp(act, dec_bc[:, u, :], psm[:, 404:406])
                        yield
                        k.cp(dve, wiT_all[0:2, u, :], g_wiT.v)
                        k.ts(dve, g_d4b[0:2, 0:2], identF[0:2, 0:2], negM[0:2, u, TS - 1:TS], ALU.mult)
                        k.mm(psm[:, 408:410], ones4.v, g_d4b[0:2, 0:2])
                        k.tt(dve, g_wl[0:TS, :], u_tm[0:TS, u, :], psm[0:TS, 408:410], ALU.add)
                        k.actf(wl_all[0:TS, u, :], g_wl[0:TS, :], AF.Exp)
                        k.cp(dve, wlb_all[0:TS, u, :], wl_all[0:TS, u, :])
                        yield
                        if sample or (last and u == NU - 1):
                            k.dma(pool, O["m_s"][u] if sample else O["m_p"], mcar[0:2, slot:slot + 1])
                def proj_gen(h, B_):
                    cT, szT, qT, kT, k_tm, v_tm, hmT = (B_[n_] for n_ in ('cT', 'szT', 'qT', 'kT', 'k_tm', 'v_tm', 'hmT'))
                    W = w_get("U%d" % h)
                    W3 = w3(W, 8)
                    for cc in range(4):
                        c16 = 4 * h + cc
                        acc = next_pa()
                        for kk in range(8):
                            k.mm(acc[:, 0:NT], W3[:, kk, cc * 128:(cc + 1) * 128], xnT[:, kk, :], start=(kk == 0), stop=(kk == 7))
                        u_ext = u_exts[cc % 2]
                        k.cp(pool, u_ext[:, :, 0:3], hist[:, c16, 0:NSEG, :])
                        k.cp(act, u_ext[:, :, 3:3 + SEG], acc[:, 0:NT].rr("p (s t) -> p s t", s=NSEG))
                        k.cp(pool, hist[:, c16, 0:NSEG, :], u_ext[:, :, SEG:SEG + 3])
                        k.actf(cv.v, u_ext[:, :, 0:SEG], AF.Identity, scale=cw_col[:, c16, 0:1], bias=cb_col[:, c16:c16 + 1])
                        for j in range(1, 4):
                            k.stt(cv.v, u_ext[:, :, j:j + SEG], cw_col[:, c16, j:j + 1], cv.v, ALU.mult, ALU.add)
                        k.actf(cT[:, cc, :].rr("p (s t) -> p s t", s=NSEG), cv.v, AF.Silu)
                        yield
                    W = w_get("Z%d" % h)
                    W3 = w3(W, 8)
                    for cc in range(4):
                        acc = next_pa()
                        for kk in range(8):
                            k.mm(acc[:, 0:NT], W3[:, kk, cc * 128:(cc + 1) * 128], xnT[:, kk, :], start=(kk == 0), stop=(kk == 7))
                        k.actf(szT[:, cc, :], acc[:, 0:NT], AF.Silu)
                        yield
                    W = w_get("V%d" % h)
                    W3 = w3(W, 8)
                    for u in range(NU):
                        acc = next_pa()
                        for kk in range(8):
                            k.mm(acc[0:TS, :], xnT[:, kk, utok(u)], W3[:, kk, :], start=(kk == 0), stop=(kk == 7))
                        k.cp(act if u % 2 else dve, v_tm[0:TS, u, :], acc[0:TS, :])
                        yield
                    W = w_get("QK%d" % h)
                    Wq = w3(W, 4, 0, 2048)
                    Wk = w3(W, 4, 2048, 4096)
                    for ec in range(4):
                        acc = next_pa()
                        for dc in range(4):
                            k.mm(acc[:, 0:NT], Wq[:, dc, ec * 128:(ec + 1) * 128], cT[:, dc, :], start=(dc == 0), stop=(dc == 3))
                        k.cp(act, qT[:, ec, :], acc[:, 0:NT])
                        acc = next_pa()
                        for dc in range(4):
                            k.mm(acc[:, 0:NT], Wk[:, dc, ec * 128:(ec + 1) * 128], cT[:, dc, :], start=(dc == 0), stop=(dc == 3))
                        k.ts(dve, kT[:, ec, :], acc[:, 0:NT], SC, ALU.mult)
                        yield
                    for u in range(NU):
                        acc = next_pa()
                        for dc in range(4):
                            k.mm(acc[0:TS, :], cT[:, dc, utok(u)], Wk[:, dc, :], start=(dc == 0), stop=(dc == 3))
                        k.ts(dve, k_tm[0:TS, u, :], acc[0:TS, :], SC, ALU.mult)
                        yield
                    for ec in range(4):
                        c16 = 4 * h + ec
                        k.stt(cT[:, ec, :], cT[:, ec, :], sk_col[:, c16:c16 + 1], szT[:, ec, :], ALU.mult, ALU.mult)
                        k.actf(szT[:, ec, :], szT[:, ec, :], AF.Copy, scale=mg_col[:, c16:c16 + 1])
                        yield


                def passes_gen(h, B_):
                    cT, szT, qT, kT, k_tm, v_tm, hmT = (B_[n_] for n_ in ('cT', 'szT', 'qT', 'kT', 'k_tm', 'v_tm', 'hmT'))
                    for u in range(NU):
                        k.actf(wv_all[0:TS, u, :], v_tm[0:TS, u, :], AF.Copy, scale=wl_all[0:TS, u, h:h + 1])

                    def state_pass(u):
                        slot = u if sample else 0
                        if sample:
                            k.dma(sp, C_nat[:, h], I["sC"][u, h].rearrange("(ec p) d -> p ec d", p=128))
                            yield from make_CT3(h, CTtmp[u % 2])
                        elif first and u == 0:
                            k.memset(pool, C_nat[:, h], 0.0)
                            k.memset(pool, CT_cur[:, h], 0.0)
                        k.cp(dve, n_bfs[:, u, :], n_col[:, slot, 4 * h:4 * h + 4])
                        for ec in range(4):
                            dcp = next_dc2()
                            k.mm(dcp, wv_all[0:TS, u, ec * 128:(ec + 1) * 128], k_tm[0:TS, u, :])
                            k.stt(C_nat[:, h, ec, :], C_nat[:, h, ec, :], dec_bc[:, u, h:h + 1], dcp, ALU.mult, ALU.add)
                            yield
                        for dc in range(4):
                            k.mm(psm[:, 420 + dc:421 + dc], k_tm[0:TS, u, dc * 128:(dc + 1) * 128], wlb_all[0:TS, u, h:h + 1])
                        k.stt(n_col[:, slot, 4 * h:4 * h + 4], n_col[:, slot, 4 * h:4 * h + 4], dec_bc[:, u, h:h + 1],
                              psm[:, 420:424], ALU.mult, ALU.add)
                        if sample:
                            k.dma(pool, O["C_s"][u, h].rearrange("(ec p) d -> p ec d", p=128), C_nat[:, h])
                        elif last and u == NU - 1:
                            k.dma(pool, O["C_p"][h].rearrange("(ec p) d -> p ec d", p=128), C_nat[:, h])
                        else:
                            yield from make_CT3(h, CTtmp[u] if u < NU - 1 else CT_nxt[:, h])

                    def h_group(us):
                        R_ = [(u_ % 2, u_) for u_ in us]
                        CTs = {u: (CTtmp[u % 2] if sample else (CT_cur[:, h] if u == 0 else CTtmp[u - 1])) for u in us}
                        hb = [ptr, ptr]
                        yield
                        for i2, u in R_:
                            nm = psm[0:TS, 0:TS]
                            k.mm(nm, sel[0:2, h, 0:TS], negM[0:2, u, :], start=True, stop=False)
                            k.mm(nm, identF[0:TS, 0:TS], mnegT[0:TS, 0:TS], start=False, stop=True)
                            kq = psm[0:TS, 128:128 + TS]
                            for ec in range(4):
                                k.mm(kq, kT[:, ec, utok(u)], qT[:, ec, utok(u)], start=(ec == 0), stop=(ec == 3))
                            k.mm(psm[:, 256:256 + TS], sel[0:2, h, :], wiT_all[0:2, u, :])
                        yield
                        for i2, u in R_:
                            k.actf(dT_sb[i2][0:TS, 0:TS], psm[0:TS, 0:TS], AF.Exp, bias=u_tm[0:TS, u, h:h + 1])
                        yield
                        for i2, u in R_:
                            k.tt(dve, sT_sb[i2][0:TS, 0:TS], psm[0:TS, 128:128 + TS], dT_sb[i2][0:TS, 0:TS], ALU.mult)
                            k.tt(dve, qwT[i2][:, :, 0:TS], qT[:, :, utok(u)],
                                 V(psm, psm.h[:, 256:256 + TS].unsqueeze(1).to_broadcast([128, 4, TS])), ALU.mult)
                        yield
                        for i2, u in R_:
                            pn = pnum if i2 == 0 else pint
                            k.mm(pn[0:TS, :], sT_sb[i2][0:TS, 0:TS], v_tm[0:TS, u, :], start=True, stop=False)
                            for dc in range(4):
                                k.mm(pn[0:TS, :], qwT[i2][:, dc, 0:TS], CTs[u][:, dc, :], start=False, stop=(dc == 3))
                            dn_ = psm[0:TS, 416 + i2:417 + i2]
                            k.mm(dn_, sT_sb[i2][0:TS, 0:TS], onesb[0:TS, :], start=True, stop=False)
                            for dc in range(4):
                                k.mm(dn_, qwT[i2][:, dc, 0:TS], n_bfs[:, u, dc:dc + 1], start=False, stop=(dc == 3))
                        yield
                        for i2, u in R_:
                            d_rs, d_den, d_aden, d_rden, d_st, d_mv = dd[i2]
                            dn_ = psm[0:TS, 416 + i2:417 + i2]
                            k.ts(dve, d_den[0:TS, 0:1], dn_, -1.0, ALU.mult)
                            k.tt(dve, d_aden[0:TS, 0:1], d_den[0:TS, 0:1], dn_, ALU.max)
                            k.tt(dve, d_aden[0:TS, 0:1], d_aden[0:TS, 0:1], cl_tm[0:TS, u, h:h + 1], ALU.max)
                            k.op(dve, lambda d_rden=d_rden, d_aden=d_aden: nc.vector.reciprocal(out=d_rden.h[0:TS, 0:1], in_=d_aden.h[0:TS, 0:1]),
                                 [d_aden.v], [d_rden.v])
                        yield
                        for i2, u in R_:
                            pn = pnum if i2 == 0 else pint
                            k.actf(hraw[i2][0:TS, :], pn[0:TS, :], AF.Copy, scale=dd[i2][3][0:TS, 0:1])
                        yield
                        for i2, u in R_:
                            d_rs, d_den, d_aden, d_rden, d_st, d_mv = dd[i2]
                            k.op(dve, lambda d_st=d_st, i2=i2: nc.vector.bn_stats(out=d_st.h[0:TS, 0:6], in_=hraw[i2].h[0:TS, :]), [hraw[i2].v], [d_st.v])
                            k.op(dve, lambda d_st=d_st, d_mv=d_mv: nc.vector.bn_aggr(out=d_mv.h[0:TS, 0:2], in_=d_st.h[0:TS, 0:6]), [d_st.v], [d_mv.v])
                        yield
                        for i2, u in R_:
                            d_rs, d_den, d_aden, d_rden, d_st, d_mv = dd[i2]
                            k.actf(d_rs[0:TS, 0:1], d_mv[0:TS, 1:2], AF.Ln, bias=EPS)
                            k.actf(d_rs[0:TS, 1:2], d_rs[0:TS, 0:1], AF.Exp, scale=-0.5)
                        yield
                        for i2, u in R_:
                            d_rs, d_den, d_aden, d_rden, d_st, d_mv = dd[i2]
                            k.ts(dve, hn_sb[i2][0:TS, :], hraw[i2][0:TS, :], d_mv[0:TS, 0:1], ALU.subtract, d_rs[0:TS, 1:2], ALU.mult)
                        yield
                        for i2, u in R_:
                            for ec in range(4):
                                k.tr(hb[i2][:, ec * TS:(ec + 1) * TS], hn_sb[i2][0:TS, ec * 128:(ec + 1) * 128], identF[0:TS, 0:TS])
                        yield
                        for i2, u in R_:
                            k.tt(dve, tmp_sb[i2][:, :, 0:TS], hb[i2][:, 0:4 * TS].rr("p (e t) -> p e t", e=4), szT[:, :, utok(u)], ALU.mult)
                            k.tt(pool, hmT[:, :, utok(u)], tmp_sb[i2][:, :, 0:TS], cT[:, :, utok(u)], ALU.add)


                    if sample:
                        for u in range(NU):
                            yield from state_pass(u)
                            yield from h_group([u])
                    else:
                        for u in range(NU):
                            yield from state_pass(u)
                        for u in range(NU):
                            yield from h_group([u])
                def brm_gen(h, B_):
                    cT, szT, qT, kT, k_tm, v_tm, hmT = (B_[n_] for n_ in ('cT', 'szT', 'qT', 'kT', 'k_tm', 'v_tm', 'hmT'))
                    W = w_get("BRM%d" % h)
                    W3 = w3(W, 4)
                    for b in range(NSB):
                        for q in range(2):
                            acc = next_pa()
                            for ec in range(4):
                                k.mm(acc.v, hmT[:, ec, blk(b)], W3[:, ec, q * 512:(q + 1) * 512], start=(ec == 0), stop=(ec == 3))
                            dst = y_acc[:, b, q * 512:(q + 1) * 512]
                            if h == 0:
                                k.cp(act, dst, acc.v)
                                yield
                            else:
                                k.tt(dve, dst, dst, acc.v, ALU.add)
                                yield


                run(gate_gen(), proj_gen(0, BS[0]), 1, 1)
                run(passes_gen(0, BS[0]), proj_gen(1, BS[1]), 3, 1)
                run(passes_gen(1, BS[1]), brm_gen(0, BS[0]), 3, 1)
                run(brm_gen(1, BS[1]), None)
                for slot in range(NU if sample else 1):
                    if sample or last:
                        k.tr(psm[0:8, 256:384], n_col[:, slot, :], identF.v)
                        k.cp(act, nout[0:8, slot, :], psm[0:8, 256:384])
                        k.dma(pool, O["n_s"][slot] if sample else O["n_p"], nout[0:8, slot, :])
                if sample or last:
                    for grp in range(2):
                        for cc in range(4):
                            c16 = grp * 4 + cc
                            k.tr(ptr[0:NSEG * 3, cc * 128:(cc + 1) * 128], hist[:, c16, 0:NSEG, :].rr("p s j -> p (s j)"), identF.v)
                        k.cp(act, cvo[0:NSEG * 3, grp * 512:(grp + 1) * 512], ptr[0:NSEG * 3, :])
                    k.dma(pool, O["conv_s"] if sample else O["conv_p"], cvo[0:NSEG * 3, :])
                evA, xdA = exchange(NT, "a", y_acc, NSB)
            k.barrier()

            yb_acc = k.sb("yb_acc", [128, NSB, D], es=tes)
            sgb_t = k.sb("sgb_t", [128, NSB, D], es=tes)
            if True:
                with ExitStack() as pes:
                    ohT = k.sb("ohT", [128, 4, NT], BF16, es=pes)
                    qs_sb = k.sb("qs_sb", [128, 4, NT], es=pes)
                    sigf = k.sb("sigf", [128, NT], es=pes)
                    lfh = k.sb("lfh", [128, NT], es=pes)
                    kh_sb = k.sb("kh_sb", [128, NT], es=pes)
                    g_sb = k.sb("g_sb", [128, NT], es=pes)
                    eg_sb = k.sb("eg_sb", [128, NT], es=pes)
                    eng_sb = k.sb("eng_sb", [128, NT], es=pes)
                    qtT = k.sb("qtT", [128, 4, NT], BF16, es=pes)
                    ktT = k.sb("ktT", [128, 4, NT], BF16, es=pes)
                    sgT = k.sb("sgT", [128, 4, NT], BF16, es=pes)
                    egl = k.sb("egl", [128, 4, NT // L], es=pes)
                    vh_tm = k.sb("vh_tm", [128, NU, 512], BF16, es=pes)
                    kt_tm = k.sb("kt_tm", [128, 512], BF16, es=pes)
                    aT_sb = k.sb("aT_sb", [128, 4, 128], BF16, es=pes)
                    o_sb = k.sb("o_sb", [128, 512], es=pes)
                    sq_sb = k.sb("sq_sb", [128, 512], es=pes)
                    on_sb = k.sb("on_sb", [128, 4, 128], es=pes)
                    hs = k.sb("hs", [128, 12], es=pes)
                    for g in range(1):
                        hsl = slice(4 * g, 4 * g + 4)
                        W = w_get("QH%d" % g)
                        W3 = w3(W, 8)
                        for j in range(4):
                            acc = next_pa()
                            for kk in range(8):
                                k.mm(acc[:, 0:NT], W3[:, kk, j * 128:(j + 1) * 128], xnT[:, kk, :], start=(kk == 0), stop=(kk == 7))
                            k.actf(qs_sb[:, j, :], acc[:, 0:NT], AF.Silu)
                        W = w_get("FH%d" % g)
                        W3 = w3(W, 8)
                        for j in range(4):
                            hh = 4 * g + j
                            acc = next_pa()
                            for kk in range(8):
                                k.mm(acc[:, 0:NT], W3[:, kk, j * 128:(j + 1) * 128], xnT[:, kk, :], start=(kk == 0), stop=(kk == 7))
                            k.actf(sigf.v, acc[:, 0:NT], AF.Sigmoid)
                            k.actf(lfh.v, sigf.v, AF.Ln, scale=oml_col[:, hh:hh + 1], bias=lb_col[:, hh:hh + 1])
                            k.ts(dve, kh_sb.v, sigf.v, noml_col[:, hh:hh + 1], ALU.mult, oml_col[:, hh:hh + 1], ALU.add)
                            k.op(dve, lambda: nc.vector.tensor_tensor_scan(out=g_sb.h[:], data0=m01.h[:, 0:NT], data1=lfh.h[:],
                                                                            initial=0.0, op0=ALU.mult, op1=ALU.add),
                                 [m01.v, lfh.v], [g_sb.v])
                            k.actf(eg_sb.v, g_sb.v, AF.Exp)
                            k.actf(eng_sb.v, g_sb.v, AF.Exp, scale=-1.0)
                            k.tt(pool, qtT[:, j, :], qs_sb[:, j, :], eg_sb.v, ALU.mult)
                            k.tt(dve, ktT[:, j, :], kh_sb.v, eng_sb.v, ALU.mult)
                            k.cp(dve, egl[:, j, :], eg_sb.v.rr("p (c l) -> p c l", l=L)[:, :, L - 1])
                        W = w_get("IH%d" % g)
                        W3 = w3(W, 8)
                        for u in range(NU):
                            acc = next_pa()
                            for kk in range(8):
                                k.mm(acc[0:TS, :], xnT[:, kk, utok(u)], W3[:, kk, :], start=(kk == 0), stop=(kk == 7))
                            k.cp(act if u % 2 else dve, vh_tm[0:TS, u, :], acc[0:TS, :])
                        W = w_get("GH%d" % g)
                        W3 = w3(W, 8)
                        for j in range(4):
                            acc = next_pa()
                            for kk in range(8):
                                k.mm(acc[:, 0:NT], W3[:, kk, j * 128:(j + 1) * 128], xnT[:, kk, :], start=(kk == 0), stop=(kk == 7))
                            k.actf(sgT[:, j, :], acc[:, 0:NT], AF.Silu)
                        def hg_unit(u):
                            if sample:
                                k.dma(sp, S_st[:, hsl, :], I["sS"][u].rearrange("h c e -> c h e"))
                                k.cp(act, S_bf[0][:, hsl, :], S_st[:, hsl, :])
                            for j in range(4):
                                k.tr(ptrb[0:TS, j * 128:(j + 1) * 128], ktT[:, j, utok(u)], identB.v)
                            k.cp(act, kt_tm[0:TS, :], ptrb[0:TS, 0:512])
                            yield
                            for j in range(4):
                                k.mm(pnum[0:TS, j * TS:(j + 1) * TS], ktT[:, j, utok(u)], qtT[:, j, utok(u)])
                            k.tt(dve, aT_sb[0:TS, :, 0:TS], pnum[0:TS, 0:4 * TS].rr("p (j t) -> p j t", j=4),
                                 V(hmask, hmask.h[0:TS, 0:TS].unsqueeze(1).to_broadcast([TS, 4, TS])), ALU.mult)
                            yield

                            def s_update(c, dst_bf):
                                rows = slice(c * L, (c + 1) * L)
                                gci = u * NCH + c
                                for j in range(4):
                                    k.mm(pdc[:, j * 128:(j + 1) * 128], kt_tm[rows, j * 128:(j + 1) * 128],
                                         vh_tm[rows, u, j * 128:(j + 1) * 128])
                                k.tt(dve, S_st[:, hsl, :], S_st[:, hsl, :], pdc.v.rr("p (j e) -> p j e", j=4), ALU.add)
                                k.tt(dve, S_st[:, hsl, :], S_st[:, hsl, :],
                                     V(egl, egl.h[:, :, gci:gci + 1].to_broadcast([128, 4, 128])), ALU.mult)
                                if dst_bf is not None:
                                    k.cp(act, dst_bf[:, hsl, :], S_st[:, hsl, :])
                            for c in range(NCH - 1):
                                s_update(c, S_bf[c + 1])
                                yield
                            for j in range(4):
                                hh = 4 * g + j
                                for c in range(NCH):
                                    rows = slice(c * L, (c + 1) * L)
                                    k.mm(pint[rows, j * 128:(j + 1) * 128], qtT[:, j, u * TS + c * L:u * TS + (c + 1) * L],
                                         S_bf[c][:, hh, :], start=True, stop=False, skip_group_check=True)
                                k.mm(pint[0:TS, j * 128:(j + 1) * 128], aT_sb[0:TS, j, 0:TS], vh_tm[0:TS, u, j * 128:(j + 1) * 128],
                                     start=False, stop=True, skip_group_check=True)
                            yield
                            s_update(NCH - 1, None if sample else S_bf[0])
                            k.cp(act, o_sb[0:TS, :], pint[0:TS, :])
                            yield
                            k.tt(pool, sq_sb[0:TS, :], o_sb[0:TS, :], o_sb[0:TS, :], ALU.mult)
                            k.op(dve, lambda: nc.vector.tensor_reduce(out=hs.h[0:TS, 0:4],
                                                                       in_=sq_sb.h[0:TS, :].rearrange("p (j e) -> p j e", j=4),
                                                                       axis=AX.X, op=ALU.add), [sq_sb.v], [hs.v])
                            k.actf(hs[0:TS, 4:8], hs[0:TS, 0:4], AF.Ln, scale=1.0 / HE, bias=EPS)
                            k.actf(hs[0:TS, 8:12], hs[0:TS, 4:8], AF.Exp, scale=-0.5)
                            yield
                            k.tt(dve, on_sb[0:TS, :, :], o_sb[0:TS, :].rr("p (j e) -> p j e", j=4),
                                 V(hs, hs.h[0:TS, 8:12].unsqueeze(2).to_broadcast([TS, 4, 128])), ALU.mult)
                            for j in range(4):
                                k.tr(ptr[:, j * TS:(j + 1) * TS], on_sb[0:TS, j, :], identF[0:TS, 0:TS])
                            for j in range(4):
                                hh = 4 * g + j
                                k.stt(ohT[:, hh, utok(u)], ptr[:, j * TS:(j + 1) * TS], hg_col[:, hh:hh + 1], sgT[:, j, utok(u)],
                                      ALU.mult, ALU.mult)
                            if sample:
                                k.dma(pool, O["S_s"][u].rearrange("h c e -> c h e"), S_st[:, hsl, :])
                            elif last and u == NU - 1:
                                k.dma(pool, O["S_p"].rearrange("h c e -> c h e"), S_st[:, hsl, :])
                            yield

                        def hg_units_gen():
                            for u in range(NU):
                                yield from hg_unit(u)

                        def gates_gen():
                            for nm_, sgt in (("GA", y_acc),):
                                for q in range(2):
                                    W = w_get("%s%d" % (nm_, q))
                                    W3 = w3(W, 8)
                                    for b in range(NSB):
                                        acc = next_pa()
                                        for kk in range(8):
                                            k.mm(acc.v, xnT[:, kk, blk(b)], W3[:, kk, :], start=(kk == 0), stop=(kk == 7))
                                        k.actf(sgt[:, b, q * 512:(q + 1) * 512], acc.v, AF.Sigmoid)
                                        yield
                        run(hg_units_gen(), gates_gen(), 2, 1)
                    W = w_get("BRH")
                    W3 = w3(W, 4)
                    for b in range(NSB):
                        for q in range(2):
                            acc = next_pa()
                            for hh in range(4):
                                k.mm(acc.v, ohT[:, hh, blk(b)], W3[:, hh, q * 512:(q + 1) * 512], start=(hh == 0), stop=(hh == 3))
                            k.cp(act, yb_acc[:, b, q * 512:(q + 1) * 512], acc.v)
                    evB, xdB = exchange(NT, "b", yb_acc, NSB)
            k.barrier()

            with ExitStack() as pes:
                x2 = k.sb("x2", [128, NSB, D], es=pes)
                tT = k.sb("tT", [128, 8, NT], BF16, es=pes)
                pT = k.sb("pT", [128, 2, NT], BF16, es=pes)
                p_tm = k.sb("p_tm", [128, NSB, PLE], es=pes)
                bf_sb = k.sb("bf_sb", [128, D], BF16, es=pes)
                sg_sb = k.sb("sg2_sb", [128, 512], es=pes)
                ya_ld = k.sb("ya_ld", [128, NSB, D], es=pes)
                outb = [k.sb("outb%d" % i, [128, D], es=pes) for i in range(2)]
                ssq = k.sb("ssq2", [128, 4], es=pes)
                for b in range(NSB):
                    k.dma(sp, x2[:, b, :], x_dram[b * 128:(b + 1) * 128, :])
                    k.dma(sp, p_tm[:, b, :], p_dram[b * 128:(b + 1) * 128, :])

                def to_T(dst, src_fn, nchunk):
                    for b in range(NSB):
                        k.cp(act, bf_sb[:, 0:nchunk * 128], src_fn(b))
                        for c in range(nchunk):
                            k.tr(ptrb[:, c * 128:(c + 1) * 128], bf_sb[:, c * 128:(c + 1) * 128], identB.v)
                        k.cp(dve, dst[:, 0:nchunk, blk(b)], ptrb[:, 0:nchunk * 128].rr("p (c t) -> p c t", c=nchunk))
                to_T(pT, lambda b: p_tm[:, b, :], 2)
                for q in range(2):
                    W = w_get("GB%d" % q)
                    W3 = w3(W, 8)
                    for b in range(NSB):
                        acc = next_pa()
                        for kk in range(8):
                            k.mm(acc.v, xnT[:, kk, blk(b)], W3[:, kk, :], start=(kk == 0), stop=(kk == 7))
                        k.actf(sgb_t[:, b, q * 512:(q + 1) * 512], acc.v, AF.Sigmoid)
                sp.wait(evA)
                sp.wait(evB)
                for b in range(NSB):
                    k.dma(sp, ya_ld[:, b, :], xdA[b * 128:(b + 1) * 128, :])
                    k.dma(sp, yb_acc[:, b, :], xdB[b * 128:(b + 1) * 128, :])
                    k.tt(dve, ya_ld[:, b, :], ya_ld[:, b, :], y_acc[:, b, :], ALU.mult)
                    k.tt(pool, yb_acc[:, b, :], yb_acc[:, b, :], sgb_t[:, b, :], ALU.mult)
                    k.tt(dve, ya_ld[:, b, :], ya_ld[:, b, :], yb_acc[:, b, :], ALU.add)
                to_T(tT, lambda b: ya_ld[:, b, :], 8)
                for q in range(2):
                    W = w_get("OUT%d" % q)
                    W3 = w3(W, 8)
                    for b in range(NSB):
                        acc = next_pa()
                        for kk in range(8):
                            k.mm(acc.v, tT[:, kk, blk(b)], W3[:, kk, :], start=(kk == 0), stop=(kk == 7))
                        dst = x2[:, b, q * 512:(q + 1) * 512]
                        k.tt(dve, dst, dst, acc.v, ALU.add)
                to_T(tT, lambda b: x2[:, b, :], 8)
                Wp = [w_get("PG0"), w_get("PG1", hold=1), w_get("PLE", hold=2)]
                Wple = w3(Wp[2], 2, 0, 2048)
                for b in range(NSB):
                    for q in range(2):
                        W3 = w3(Wp[q], 8)
                        acc = next_pa()
                        for kk in range(8):
                            k.mm(acc.v, tT[:, kk, blk(b)], W3[:, kk, :], start=(kk == 0), stop=(kk == 7))
                        k.actf(sg_sb.v, acc.v, AF.Sigmoid)
                        acc2 = next_pa()
                        for kk in range(2):
                            k.mm(acc2.v, pT[:, kk, blk(b)], Wple[:, kk, q * 512:(q + 1) * 512], start=(kk == 0), stop=(kk == 1))
                        k.tt(dve, sg_sb.v, sg_sb.v, acc2.v, ALU.mult)
                        dst = x2[:, b, q * 512:(q + 1) * 512]
                        k.tt(pool, dst, dst, sg_sb.v, ALU.add)
                    ob = outb[b % 2]
                    k.actf(ob.v, x2[:, b, :], AF.Square, accum=ssq[:, 0:1])
                    k.actf(ssq[:, 1:2], ssq[:, 0:1], AF.Ln, scale=1.0 / D, bias=EPS)
                    k.actf(ssq[:, 2:3], ssq[:, 1:2], AF.Exp, scale=-0.5)
                    k.stt(ob.v, x2[:, b, :], ssq[:, 2:3], fg_bc.v, ALU.mult, ALU.mult)
                    k.dma(pool, y_dram[b * 128:(b + 1) * 128, :], ob.v)
            k.barrier()

    NTP = min(512, TP)
    ntile = TP // NTP if do_prompt else 0
    for ti in range(ntile):
        seq.extend(range(NW))
    if do_sample:
        seq.extend(range(NW))
    for ti in range(ntile):
        tile(I["xp"][ti * NTP:(ti + 1) * NTP, :], I["pp"][ti * NTP:(ti + 1) * NTP, :], O["yp"][ti * NTP:(ti + 1) * NTP, :],
             NTP, 128, NTP // 128, 1, ti == 0, ti == ntile - 1, False, tix=ti)
    if do_sample:
        tile(I["xs"], I["ps"], O["ys"], NS * 32, 32, NS, NS, True, True, True)
    for ev in k.out_events:
        sp.wait(ev)


_NC_CACHE = {}


def _in_map(core, inputs, NS):
    b = core // 2
    hf = core % 2
    s0 = (core // 2) * NS
    f = lambda a: np.ascontiguousarray(a, dtype=np.float32)
    hs = slice(2 * hf, 2 * hf + 2)
    cs = slice(hf * MIL, (hf + 1) * MIL)
    gs = slice(hf * HWL, (hf + 1) * HWL)
    w = inputs["w_in"][0]
    w_in = np.concatenate([
        w[:, 0 + hf * MIL:0 + (hf + 1) * MIL], w[:, 4096 + hf * MIL:4096 + (hf + 1) * MIL], w[:, 2048 + hf * MIL:2048 + (hf + 1) * MIL],
        w[:, 6144 + 2 * hf:6144 + 2 * hf + 2], w[:, 6148 + 2 * hf:6148 + 2 * hf + 2],
        w[:, 6152 + hf * HWL:6152 + (hf + 1) * HWL], w[:, 7176 + hf * HWL:7176 + (hf + 1) * HWL],
        w[:, 8200 + hf * HWL:8200 + (hf + 1) * HWL], w[:, 9224 + hf * HWL:9224 + (hf + 1) * HWL],
        w[:, 10248:12296]], axis=1)
    assert w_in.shape[1] == NINL
    m = dict(
        xp=f(inputs["x_prompt"][b]), pp=f(inputs["p_prompt"][0, b]),
        xs=f(inputs["x_sample"][s0:s0 + NS].reshape(NS * 32, D)), ps=f(inputs["p_sample"][0, s0:s0 + NS].reshape(NS * 32, PLE)),
        sconv=f(inputs["state_conv"][0, s0:s0 + NS][:, :, cs].reshape(NS * 3, MIL)), sC=f(inputs["state_mlstm_C"][0, s0:s0 + NS, hs]),
        sn=f(inputs["state_mlstm_n"][0, s0:s0 + NS, hs].reshape(NS, MHL * 4, 128)), sm=f(inputs["state_mlstm_m"][0, s0:s0 + NS, hs]),
        sS=f(inputs["state_hgrn"][0, s0:s0 + NS, 4 * hf:4 * hf + 4]),
        norm_g=f(inputs["norm_g"][0]), w_in=f(w_in), b_ig=f(inputs["b_ig"][0, hs]), b_fg=f(inputs["b_fg"][0, hs]),
        conv_w=f(inputs["conv_w"][0][:, cs]), conv_b=f(inputs["conv_b"][0, cs]), w_qm=f(inputs["w_qm"][0, hs]), w_km=f(inputs["w_km"][0, hs]),
        mnorm_g=f(inputs["mnorm_g"][0, cs]), m_skip=f(inputs["m_skip"][0, cs]), w_brm=f(inputs["w_brm"][0, cs]),
        hgrn_lb=f(inputs["hgrn_lb"][:, gs]), hnorm_g=f(inputs["hnorm_g"][0, gs]), w_brh=f(inputs["w_brh"][0, gs]), w_out=f(inputs["w_out"][0]),
        w_ple=f(inputs["w_ple"][0]), w_pg=f(inputs["w_pg"][0]), final_g=f(inputs["final_g"]),
    )
    return m


def kernel(**inputs):
    inputs = {k_: np.asarray(v) for k_, v in inputs.items()}
    B, TP = inputs["x_prompt"].shape[:2]
    NSEQ = inputs["x_sample"].shape[0]
    NCORE = 8
    NS = NSEQ // (NCORE // 2)
    key = (TP, NS)
    if key not in _NC_CACHE:
        _NC_CACHE[key] = build(TP, NS)
    nc = _NC_CACHE[key]
    in_maps = [_in_map(c, inputs, NS) for c in range(NCORE)]
    res = run_bass_kernel_spmd(nc, in_maps, core_ids=list(range(NCORE)))
    R = res.results
    f32 = np.float32
    NP = NCORE // 2
    cat2 = lambda name, fn, ax: [np.concatenate([fn(R[2 * p][name]), fn(R[2 * p + 1][name])], axis=ax) for p in range(NP)]
    y_prompt = np.stack([R[2 * b]["yp"] for b in range(B)]).astype(f32)
    y_sample = np.concatenate([R[2 * p]["ys"].reshape(NS, 32, D) for p in range(NP)]).astype(f32)
    conv_p = np.stack(cat2("conv_p", lambda a: a, 1))[None].astype(f32)
    C_p = np.stack(cat2("C_p", lambda a: a, 0))[None].astype(f32)
    n_p = np.stack(cat2("n_p", lambda a: a.reshape(MHL, HD), 0))[None].astype(f32)
    m_p = np.stack(cat2("m_p", lambda a: a.reshape(MHL), 0))[None].astype(f32)
    S_p = np.stack(cat2("S_p", lambda a: a, 0))[None].astype(f32)
    conv_s = np.concatenate(cat2("conv_s", lambda a: a.reshape(NS, 3, MIL), 2))[None].astype(f32)
    C_s = np.concatenate(cat2("C_s", lambda a: a, 1))[None].astype(f32)
    n_s = np.concatenate(cat2("n_s", lambda a: a.reshape(NS, MHL, HD), 1))[None].astype(f32)
    m_s = np.concatenate(cat2("m_s", lambda a: a.reshape(NS, MHL), 1))[None].astype(f32)
    S_s = np.concatenate(cat2("S_s", lambda a: a, 1))[None].astype(f32)
    return (y_prompt, y_sample, conv_p, C_p, n_p, m_p, S_p, conv_s, C_s, n_s, m_s, S_s)
```

```python
import numpy as np
from contextlib import ExitStack
import concourse.bass as bass
import concourse.mybir as mybir
from concourse.bass_utils import run_bass_kernel_spmd

F32 = mybir.dt.float32
BF16 = mybir.dt.bfloat16
AF = mybir.ActivationFunctionType
ALU = mybir.AluOpType
AX = mybir.AxisListType

D = 1024
MI = 2048
MH = 4
HD = 512
HH = 8
HE = 128
PLE = 256
NIN = 12296
MHL = 2
MIL = 1024
HGL = 4
HWL = 512
NINL = 7172
EPS = 1e-6
NEG = -30000.0


class V:
    __slots__ = ("t", "ap")

    def __init__(self, t, ap):
        self.t = t
        self.ap = ap

    def __getitem__(self, key):
        return V(self.t, self.ap[key])

    def bitcast(self, dt):
        return V(self.t, self.ap.bitcast(dt))

    def rr(self, pat, **kw):
        return V(self.t, self.ap.rearrange(pat, **kw))

    def bc(self, shape):
        return V(self.t, self.ap.to_broadcast(list(shape)))


class T:
    def __init__(self, h, name):
        self.h = h
        self.name = name
        self.w = None
        self.r = {}
        self.dsem = None
        self.dcnt = 0

    def __getitem__(self, key):
        return V(self, self.h[key])

    @property
    def v(self):
        return V(self, self.h[:])


class Eng:
    def __init__(self, k, name, h, strict=False):
        self.k = k
        self.name = name
        self.h = h
        self.sem = k.new_sem("e_" + name)
        self.cnt = 0
        self.last = None
        self.seen = {}
        self.strict = strict

    def wait(self, ev):
        sem, val, key = ev
        prod = self.k.engs.get(key)
        if prod is not None and val > prod.cnt:
            assert prod.last is not None and val == prod.cnt + 1, (key, val, prod.cnt)
            prod.last.then_inc(prod.sem, 1)
            prod.last = None
            prod.cnt += 1
        if self.seen.get(key, 0) < val:
            self.h.wait_ge(sem, val)
            self.seen[key] = val


class K:
    def __init__(self, nc, es):
        self.nc = nc
        self.es = es
        self.nsem = 0
        self.pe = Eng(self, "pe", nc.tensor)
        self.act = Eng(self, "act", nc.scalar)
        self.dve = Eng(self, "dve", nc.vector)
        self.pool = Eng(self, "pool", nc.gpsimd, strict=True)
        self.sp = Eng(self, "sp", nc.sync)
        self.engs = {e.name: e for e in (self.pe, self.act, self.dve, self.pool, self.sp)}
        self.out_events = []
        self.dsems = {}
        self.uid = 0
        self.dma_last = {}

    def new_sem(self, name):
        self.nsem += 1
        return self.es.enter_context(self.nc.semaphore(name))

    def sb(self, name, shape, dt=F32, es=None):
        self.uid += 1
        h = (es or self.es).enter_context(self.nc.sbuf_tensor("%s_%d" % (name, self.uid), list(shape), dt))
        return T(h, name)

    def ps(self, name, shape, dt=F32):
        h = self.es.enter_context(self.nc.psum_tensor(name, list(shape), dt))
        return T(h, name)

    def dram(self, name, shape, dt):
        h = self.nc.dram_tensor(name, list(shape), dt, kind="Internal")
        return T(h, name)

    def _pre(self, eng, rd, wr):
        for v in rd:
            t = v.t
            if t.w is not None:
                eng.wait(t.w)
        for v in wr:
            t = v.t
            if t.w is not None and (eng.strict or t.w[2] != eng.name):
                eng.wait(t.w)
            for key, ev in t.r.items():
                if eng.strict or key != eng.name:
                    eng.wait(ev)

    def _post(self, ev, rd, wr):
        for v in wr:
            v.t.w = ev
            v.t.r = {}
        for v in rd:
            if v.t.w is ev:
                continue
            v.t.r[ev[2]] = ev

    def op(self, eng, fn, rd, wr):
        rd = [v for v in rd if isinstance(v, V)]
        wr = [v for v in wr if isinstance(v, V)]
        self._pre(eng, rd, wr)
        ins = fn()
        eng.last = ins
        ev = (eng.sem, eng.cnt + 1, eng.name)
        self._post(ev, rd, wr)
        return ins

    def dma(self, q, out, in_, semt=None, **kw):
        rd = [in_] if isinstance(in_, V) else []
        wr = [out] if isinstance(out, V) else []
        self._pre(q, rd, wr)
        o = out.ap if isinstance(out, V) else out
        i = in_.ap if isinstance(in_, V) else in_
        ins = q.h.dma_start(out=o, in_=i, **kw)
        st = semt or (wr[0].t if wr else rd[0].t)
        key = "d_" + st.name
        if key not in self.dsems:
            self.dsems[key] = [self.new_sem(key), 0]
        ent = self.dsems[key]
        ent[1] += 16
        ins.then_inc(ent[0], 16)
        ev = (ent[0], ent[1], key)
        self.dma_last[key] = ev
        self._post(ev, rd, wr)
        if not wr:
            self.out_events.append(ev)
        return ev

    def barrier(self):
        engs = [self.pe, self.act, self.dve, self.pool, self.sp]
        evs = [(e.sem, e.cnt + (1 if e.last is not None else 0), e.name) for e in engs
               if e.cnt > 0 or e.last is not None]
        for e in engs:
            for ev in evs:
                if ev[2] != e.name:
                    e.wait(ev)
            for ev in self.dma_last.values():
                e.wait(ev)

    def mm(self, out, lhsT, rhs, start=True, stop=True, **kw):
        return self.op(self.pe, lambda: self.nc.tensor.matmul(out.ap, lhsT=lhsT.ap, rhs=rhs.ap, start=start, stop=stop, **kw),
                       [lhsT, rhs], [out])

    def tr(self, out, in_, ident):
        return self.op(self.pe, lambda: self.nc.tensor.transpose(out.ap, in_.ap, ident.ap), [in_, ident], [out])

    def actf(self, out, in_, func, bias=None, scale=None, accum=None):
        kw = {}
        if bias is not None:
            kw["bias"] = bias.ap if isinstance(bias, V) else bias
        if scale is not None:
            kw["scale"] = scale.ap if isinstance(scale, V) else scale
        if accum is not None:
            kw["accum_out"] = accum.ap
        return self.op(self.act, lambda: self.nc.scalar.activation(out=out.ap, in_=in_.ap, func=func, **kw),
                       [in_, bias, scale], [out, accum])

    def _e(self, eng):
        return {"dve": self.dve, "pool": self.pool, "act": self.act}[eng] if isinstance(eng, str) else eng

    def tt(self, eng, out, a, b, op):
        e = self._e(eng)
        return self.op(e, lambda: e.h.tensor_tensor(out=out.ap, in0=a.ap, in1=b.ap, op=op), [a, b], [out])

    def ts(self, eng, out, a, s1, op0, s2=None, op1=None, accum=None):
        e = self._e(eng)
        a1 = s1.ap if isinstance(s1, V) else s1
        a2 = s2.ap if isinstance(s2, V) else s2
        kw = {}
        if op1 is not None:
            kw["op1"] = op1
        if accum is not None:
            kw["accum_out"] = accum.ap
        return self.op(e, lambda: e.h.tensor_scalar(out=out.ap, in0=a.ap, scalar1=a1, scalar2=a2, op0=op0, **kw),
                       [a, s1, s2], [out, accum])

    def stt(self, out, a, s, b, op0, op1):
        e = self.dve
        sa = s.ap if isinstance(s, V) else s
        return self.op(e, lambda: e.h.scalar_tensor_tensor(out=out.ap, in0=a.ap, scalar=sa, in1=b.ap, op0=op0, op1=op1),
                       [a, s, b], [out])

    def cp(self, eng, out, in_):
        e = self._e(eng)
        if e is self.act:
            return self.op(e, lambda: self.nc.scalar.copy(out=out.ap, in_=in_.ap), [in_], [out])
        return self.op(e, lambda: e.h.tensor_copy(out=out.ap, in_=in_.ap), [in_], [out])

    def memset(self, eng, out, val):
        e = self._e(eng)
        return self.op(e, lambda: e.h.memset(out.ap, val), [], [out])


def _wplan():
    plan = []
    for h in range(MHL):
        plan += [("U%d" % h, "in", 0 + h * 512), ("Z%d" % h, "in", 1024 + h * 512), ("V%d" % h, "in", 2048 + h * 512),
                 ("QK%d" % h, "qk", h)]
    for h in range(MHL):
        plan += [("BRM%d" % h, "brm", h)]
    plan += [("QH0", "in", 3076), ("FH0", "in", 3588), ("IH0", "in", 4100), ("GH0", "in", 4612)]
    plan += [("BRH", "brh", 0)]
    plan += [("GA0", "in", 5124), ("GA1", "in", 5124 + 512), ("GB0", "in", 6148), ("GB1", "in", 6148 + 512)]
    plan += [("OUT0", "sq", ("w_out", 0)), ("OUT1", "sq", ("w_out", 1)), ("PG0", "sq", ("w_pg", 0)), ("PG1", "sq", ("w_pg", 1)),
             ("PLE", "ple", 0)]
    return plan


def build(TP=4096, NS=8, dbg=None):
    nc = bass.Bass("TRN2", target_bir_lowering=False)
    dbg = dbg or {}

    def din(name, shape):
        return nc.dram_tensor(name, list(shape), F32, kind="ExternalInput").ap()

    def dout(name, shape):
        return nc.dram_tensor(name, list(shape), F32, kind="ExternalOutput").ap()

    I = dict(
        xp=din("xp", [TP, D]), pp=din("pp", [TP, PLE]),
        xs=din("xs", [NS * 32, D]), ps=din("ps", [NS * 32, PLE]),
        sconv=din("sconv", [NS * 3, MIL]), sC=din("sC", [NS, MHL, HD, HD]), sn=din("sn", [NS, MHL * 4, 128]),
        sm=din("sm", [NS, MHL]), sS=din("sS", [NS, HGL, HE, HE]),
        norm_g=din("norm_g", [D]), w_in=din("w_in", [D, NINL]), b_ig=din("b_ig", [MHL]), b_fg=din("b_fg", [MHL]),
        conv_w=din("conv_w", [4, MIL]), conv_b=din("conv_b", [MIL]), w_qm=din("w_qm", [MHL, HD, HD]),
        w_km=din("w_km", [MHL, HD, HD]), mnorm_g=din("mnorm_g", [MIL]), m_skip=din("m_skip", [MIL]),
        w_brm=din("w_brm", [MIL, D]), hgrn_lb=din("hgrn_lb", [2, HWL]), hnorm_g=din("hnorm_g", [HWL]),
        w_brh=din("w_brh", [HWL, D]), w_out=din("w_out", [D, D]), w_ple=din("w_ple", [PLE, D]), w_pg=din("w_pg", [D, D]),
        final_g=din("final_g", [D]),
    )
    O = dict(
        yp=dout("yp", [TP, D]), ys=dout("ys", [NS * 32, D]),
        conv_p=dout("conv_p", [3, MIL]), C_p=dout("C_p", [MHL, HD, HD]), n_p=dout("n_p", [MHL * 4, 128]),
        m_p=dout("m_p", [MHL, 1]), S_p=dout("S_p", [HGL, HE, HE]),
        conv_s=dout("conv_s", [NS * 3, MIL]), C_s=dout("C_s", [NS, MHL, HD, HD]), n_s=dout("n_s", [NS, MHL * 4, 128]),
        m_s=dout("m_s", [NS, MHL, 1]), S_s=dout("S_s", [NS, HGL, HE, HE]),
    )
    with ExitStack() as es:
        k = K(nc, es)
        _program(nc, k, I, O, TP, NS, dbg)
    return nc


def _program(nc, k, I, O, TP, NS, dbg):
    sp, pe, act, dve, pool = k.sp, k.pe, k.act, k.dve, k.pool
    plan = _wplan()
    NW = len(plan)
    do_prompt = dbg.get("prompt", True)
    do_sample = dbg.get("sample", True)

    NRING = 3
    ring = [k.sb("wring%d" % i, [128, 4096], BF16) for i in range(NRING)]
    wstate = dict(next_load=0, total=0)
    seq = []

    def w_issue(upto):
        while wstate["next_load"] < min(upto + 1, len(seq)):
            j = wstate["next_load"]
            pi = seq[j]
            sp.wait(grp_ev[pi // GRP])
            k.dma(sp, ring[j % NRING].v, wscr.h[pi])
            wstate["next_load"] += 1

    def w_get(key, hold=0):
        j = wstate["total"]
        assert plan[seq[j]][0] == key, (plan[seq[j]][0], key)
        w_issue(j + NRING - 1 - hold)
        wstate["total"] += 1
        return ring[j % NRING]

    def w3(w, kk, lo=0, hi=4096):
        return V(w, w.h[:, lo:hi].rearrange("p (k c) -> p k c", k=kk))

    identF = k.sb("identF", [128, 128])
    identB = k.sb("identB", [128, 128], BF16)
    utri = k.sb("utri", [128, 128])
    mnegT = k.sb("mnegT", [128, 128])
    maskbd = k.sb("maskbd", [128, 128])
    sel = k.sb("sel", [2, 2, 128])
    ones4 = k.sb("ones4", [2, 128])
    onesb = k.sb("onesb", [128, 1], BF16)
    m01p = k.sb("m01p", [128, 512])
    m01s = k.sb("m01s", [128, 256])

    def asel(t, pattern, cmp, fill, cm):
        k.op(pool, lambda: nc.gpsimd.affine_select(out=t.h[:], in_=t.h[:], pattern=pattern, compare_op=cmp, fill=fill,
                                                   base=0, channel_multiplier=cm), [t.v], [t.v])
    k.memset(pool, identF.v, 1.0)
    asel(identF, [[-1, 128]], ALU.is_equal, 0.0, 1)
    k.cp(pool, identB.v, identF.v)
    k.memset(pool, utri.v, 1.0)
    asel(utri, [[1, 128]], ALU.is_ge, 0.0, -1)
    k.memset(pool, mnegT.v, 0.0)
    asel(mnegT, [[1, 128]], ALU.is_ge, NEG, -1)
    k.cp(pool, maskbd.v, utri.v)
    k.memset(pool, maskbd[0:64, 64:128], 0.0)
    k.memset(pool, sel.v, 1.0)
    asel(sel, [[-1, 2], [0, 128]], ALU.is_equal, 0.0, 1)
    k.memset(pool, ones4.v, 1.0)
    k.memset(pool, onesb.v, 1.0)
    k.memset(pool, m01p.v, 1.0)
    k.memset(pool, m01p.v.rr("p (c l) -> p c l", l=64)[:, :, 0:1], 0.0)
    k.memset(pool, m01s.v, 1.0)
    k.memset(pool, m01s.v.rr("p (c l) -> p c l", l=32)[:, :, 0:1], 0.0)

    cst = T(None, "cst")

    def cload(name, shape, src, **kw):
        t = k.sb(name, shape)
        k.dma(act, t.v, src, semt=cst, **kw)
        return t
    slow = dict(allow_slow_non_contiguous=True)
    ng_col = cload("ng_col", [128, 8], I["norm_g"].rearrange("(c p) -> p c", p=128), **slow)
    cw_col = k.sb("cw_col", [128, 8, 4])
    for j in range(4):
        k.dma(act, cw_col[:, :, j], I["conv_w"][j].rearrange("(c p) -> p c", p=128), semt=cst, **slow)
    cb_col = cload("cb_col", [128, 8], I["conv_b"].rearrange("(c p) -> p c", p=128), **slow)
    mg_col = cload("mg_col", [128, 8], I["mnorm_g"].rearrange("(c p) -> p c", p=128), **slow)
    sk_col = cload("sk_col", [128, 8], I["m_skip"].rearrange("(c p) -> p c", p=128), **slow)
    hg_col = cload("hg_col", [128, 4], I["hnorm_g"].rearrange("(c p) -> p c", p=128), **slow)
    lb_raw = k.sb("lb_raw", [128, 4, 2])
    for j in range(2):
        k.dma(act, lb_raw[:, :, j], I["hgrn_lb"][j].rearrange("(c p) -> p c", p=128), semt=cst, **slow)
    big_bc = cload("big_bc", [128, 2], I["b_ig"].partition_broadcast(128))
    bfg_bc = cload("bfg_bc", [128, 2], I["b_fg"].partition_broadcast(128))
    fg_bc = cload("fg_bc", [128, D], I["final_g"].partition_broadcast(128))
    for t_ in (ng_col, cw_col, cb_col, mg_col, sk_col, hg_col, lb_raw, big_bc, bfg_bc, fg_bc):
        t_.w = k.dma_last["d_cst"]
    lb_col = k.sb("lb_col", [128, 4])
    oml_col = k.sb("oml_col", [128, 4])
    noml_col = k.sb("noml_col", [128, 4])
    k.tt(dve, lb_col.v, lb_raw[:, :, 0], lb_raw[:, :, 1], ALU.subtract)
    k.actf(lb_col.v, lb_col.v, AF.Sigmoid)
    k.ts(dve, oml_col.v, lb_col.v, -1.0, ALU.mult, 1.0, ALU.add)
    k.ts(dve, noml_col.v, oml_col.v, -1.0, ALU.mult)

    wscr = k.dram("wscr", [NW, 128, 4096], BF16)
    GRP = 3
    wgrp = [T(None, "wg%d" % g) for g in range((NW + GRP - 1) // GRP)]
    wg_sb = k.sb("wg_sb", [128, 8, 4], BF16)
    k.dma(pool, wg_sb.v, I["w_in"][:, 3072:3076].rearrange("(k p) c -> p k c", p=128), allow_slow_non_contiguous=True)
    grp_ev = {}
    for i, (key, kind, arg) in enumerate(plan):
        dst = wscr.h[i]
        g = wgrp[i // GRP]
        if kind == "in":
            src = I["w_in"][:, arg:arg + 512].rearrange("(k p) c -> p k c", p=128)
            ev = k.dma(pool, dst.rearrange("p (k c) -> p k c", k=8), src, semt=g)
        elif kind == "qk":
            for j, wn in enumerate(("w_qm", "w_km")):
                src = I[wn][arg].rearrange("(k p) c -> p k c", p=128)
                ev = k.dma(pool, dst[:, j * 2048:(j + 1) * 2048].rearrange("p (k c) -> p k c", k=4), src, semt=g)
        elif kind == "brm":
            src = I["w_brm"][arg * 512:(arg + 1) * 512, :].rearrange("(k p) c -> p k c", p=128)
            ev = k.dma(pool, dst.rearrange("p (k c) -> p k c", k=4), src, semt=g)
        elif kind == "sq":
            src = I[arg[0]][:, arg[1] * 512:(arg[1] + 1) * 512].rearrange("(k p) c -> p k c", p=128)
            ev = k.dma(pool, dst.rearrange("p (k c) -> p k c", k=8), src, semt=g)
        elif kind == "brh":
            src = I["w_brh"].rearrange("(k p) c -> p k c", p=128)
            ev = k.dma(pool, dst.rearrange("p (k c) -> p k c", k=4), src, semt=g)
        elif kind == "ple":
            src = I["w_ple"].rearrange("(k p) c -> p k c", p=128)
            ev = k.dma(pool, dst[:, 0:2048].rearrange("p (k c) -> p k c", k=2), src, semt=g)
        grp_ev[i // GRP] = ev
    k.out_events = []

    pa = [k.ps("pa%d" % i, [128, 512]) for i in range(2)]
    psm = k.ps("psm", [128, 512])
    pnum = k.ps("pnum", [128, 512])
    pint = k.ps("pint", [128, 512])
    pdc = k.ps("pdc", [128, 512])
    ptr = k.ps("ptr", [128, 512])
    ptrb = k.ps("ptrb", [128, 1024], BF16)
    stt_ = dict(pa=0, dc=0)

    pa_list = [pa[0], pa[1]]

    def next_pa():
        stt_["pa"] = (stt_["pa"] + 1) % len(pa_list)
        return pa_list[stt_["pa"]]

    def set_pa(lst):
        pa_list[:] = lst
        stt_["pa"] = 0

    def next_dc():
        stt_["dc"] = (stt_["dc"] + 1) % 3
        return [pdc, pa[0], pa[1]][stt_["dc"]]

    C_nat = k.sb("C_nat", [128, MHL, 4, 512])
    CT_pp = [k.sb("CT_bf%d" % i, [128, MHL, 4, 512], BF16) for i in range(2)]
    n_col = k.sb("n_col", [128, 8, 8])
    S_st = k.sb("S_st", [128, HGL, HE])
    S_bf = [k.sb("S_bf%d" % i, [128, HGL, HE], BF16) for i in range(2)]
    hist = k.sb("hist", [128, 8, 8, 3])
    mcar = k.sb("mcar", [2, 8])
    cvo = k.sb("cvo", [24, MIL])
    nout = k.sb("nout", [8, 8, 128])

    xsrc = {(n_, w_): k.dram("xsrc_%d%s" % (n_, w_), [n_, D], F32) for n_ in (512, 256) for w_ in "ab"}
    xdst = {(n_, w_): k.dram("xdst_%d%s" % (n_, w_), [n_, D], F32) for n_ in (512, 256) for w_ in "ab"}

    def exchange(NT, which, src_t, NSB):
        xs_, xd_ = xsrc[(NT, which)].h, xdst[(NT, which)].h
        for b_ in range(NSB):
            pool.wait(k.dma(pool, xs_[b_ * 128:(b_ + 1) * 128, :], src_t[:, b_, :]))
        cci = nc.gpsimd.collective_compute("AllReduce", ALU.add, ins=[xs_[:, :]], outs=[xd_[:, :]], replica_groups=RG)
        ccst["n"] += 1
        cci.then_inc(ccsem)
        return (ccsem, ccst["n"], "cc"), xd_
    ccsem = k.new_sem("ccsem")
    ccst = dict(n=0)
    RG = [[0, 1], [2, 3], [4, 5], [6, 7]]

    def tile(x_dram, p_dram, y_dram, NT, TS, NU, NSEG, first, last, sample, tix=0):
        NSB = NT // 128
        SEG = NT // NSEG
        L = 32 if sample else 64
        NCH = TS // L
        m01 = m01s if sample else m01p
        hmask = utri if sample else maskbd

        def utok(u):
            return slice(u * TS, (u + 1) * TS)

        def blk(b):
            return slice(b * 128, (b + 1) * 128)

        with ExitStack() as tes:
            xnT = k.sb("xnT", [128, 8, NT], BF16, es=tes)
            y_acc = k.sb("y_acc", [128, NSB, D], es=tes)
            CT_cur, CT_nxt = CT_pp[tix % 2], CT_pp[(tix + 1) % 2]
            u_tm = k.sb("u_tm", [128, NU, 2], es=tes)
            negM = k.sb("negM", [2, NU, TS], es=tes)
            wi_tm = k.sb("wi_tm", [128, NU, 2], es=tes)
            cl_tm = k.sb("cl_tm", [128, NU, 2], es=tes)
            dec_bc = k.sb("dec_bc", [128, NU, 2], es=tes)
            wiT_all = k.sb("wiT_all", [2, NU, TS], es=tes)
            wl_all = k.sb("wl_all", [128, NU, 2], es=tes)
            wlb_all = k.sb("wlb_all", [128, NU, 2], BF16, es=tes)

            with ExitStack() as pes:
                x_tm = k.sb("x_tm", [128, NSB, D], es=pes)
                xs_b = k.sb("xs_b", [128, D], BF16, es=pes)
                junk = k.sb("junk", [128, D], es=pes)
                ssq = k.sb("ssq", [128, 4], es=pes)
                for b in range(NSB):
                    k.dma(sp, x_tm[:, b, :], x_dram[b * 128:(b + 1) * 128, :])
                if sample:
                    k.dma(sp, mcar[0:2, 0:NU], I["sm"].rearrange("s h -> h s"), allow_slow_non_contiguous=True)
                    sc_tm = k.sb("sc_tm", [24, MIL], es=pes)
                    NS3 = NSEG * 3
                    k.dma(sp, sc_tm[0:NSEG * 3, :], I["sconv"])
                    for grp in range(2):
                        for cc in range(4):
                            c16 = grp * 4 + cc
                            k.tr(ptr[:, cc * NS3:(cc + 1) * NS3], sc_tm[0:NS3, c16 * 128:(c16 + 1) * 128], identF[0:NS3, 0:NS3])
                        k.cp(act, hist[:, grp * 4:(grp + 1) * 4, 0:NSEG, :],
                             ptr[:, 0:4 * NS3].rr("p (c s j) -> p c s j", c=4, j=3))
                    nrow = k.sb("nrow", [8, 128], es=pes)
                    for u in range(NU):
                        k.dma(sp, nrow.v, I["sn"][u])
                        k.tr(psm[:, 256:264], nrow.v, identF[0:8, 0:8])
                        k.cp(act, n_col[:, u, :], psm[:, 256:264])
                elif first:
                    k.memset(pool, mcar.v, 0.0)
                    k.memset(pool, hist.v, 0.0)
                    k.memset(pool, n_col.v, 0.0)
                    k.memset(pool, S_st.v, 0.0)
                    k.memset(pool, S_bf[0].v, 0.0)
                for b in range(NSB):
                    k.actf(junk.v, x_tm[:, b, :], AF.Square, accum=ssq[:, 0:1])
                    k.actf(ssq[:, 1:2], ssq[:, 0:1], AF.Ln, scale=1.0 / D, bias=EPS)
                    k.actf(ssq[:, 2:3], ssq[:, 1:2], AF.Exp, scale=-0.5)
                    k.ts(dve, xs_b.v, x_tm[:, b, :], ssq[:, 2:3], ALU.mult)
                    for c in range(8):
                        k.tr(ptrb[:, c * 128:(c + 1) * 128], xs_b[:, c * 128:(c + 1) * 128], identB.v)
                    k.tt(dve, xnT[:, :, blk(b)], ptrb.v.rr("p (c t) -> p c t", c=8),
                         V(ng_col, ng_col.h[:].unsqueeze(2).to_broadcast([128, 8, 128])), ALU.mult)
            k.barrier()

            with ExitStack() as pes:
                u_exts = [k.sb("u_ext%d" % i, [128, NSEG, SEG + 3], es=pes) for i in range(2)]
                cv = k.sb("cv", [128, NSEG, SEG], es=pes)
                gs = [k.sb("gs%d" % i, [128, 2], es=pes) for i in range(8)]
                g_ig, g_fg, g_e1, g_sp, g_csp, g_d4, g_d4b, g_wl = gs
                gr = [k.sb("gr%d" % i, [2, TS], es=pes) for i in range(5)]
                g_uT, g_M, g_mT, g_wiT, g_clT = gr
                BS = [dict(cT=k.sb("cT%d" % i, [128, 4, NT], BF16, es=pes), szT=k.sb("szT%d" % i, [128, 4, NT], BF16, es=pes),
                           qT=k.sb("qT%d" % i, [128, 4, NT], BF16, es=pes), kT=k.sb("kT%d" % i, [128, 4, NT], BF16, es=pes),
                           k_tm=k.sb("k_tm%d" % i, [128, NU, 512], BF16, es=pes), v_tm=k.sb("v_tm%d" % i, [128, NU, 512], BF16, es=pes),
                           hmT=k.sb("hmT%d" % i, [128, 4, NT], BF16, es=pes)) for i in range(2)]
                pdc2 = V(ptrb, ptrb.h[:].bitcast(F32))
                dcst = dict(i=0)

                def next_dc2():
                    dcst["i"] ^= 1
                    return pdc.v if dcst["i"] else pdc2
                NCT = 2 if sample else NU - 1
                CTtmp = [k.sb("CTtmp%d" % i, [128, 4, 512], BF16, es=pes) for i in range(NCT)]
                n_bfs = k.sb("n_bfs", [128, NU + 1, 4], BF16, es=pes)
                dT_sb = [k.sb("dT_sb%d" % i, [128, 128], es=pes) for i in range(2)]
                sT_sb = [k.sb("sT_sb%d" % i, [128, 128], BF16, es=pes) for i in range(2)]
                qwT = [k.sb("qwT%d" % i, [128, 4, 128], BF16, es=pes) for i in range(2)]
                hraw = [k.sb("hraw%d" % i, [128, 512], es=pes) for i in range(2)]
                hn_sb = [k.sb("hn_sb%d" % i, [128, 512], es=pes) for i in range(2)]
                tmp_sb = [k.sb("tmp_sb%d" % i, [128, 4, 128], es=pes) for i in range(2)]
                wv_all = k.sb("wv_all", [128, NU, 512], BF16, es=pes)
                dd = [[k.sb("dd%d_%d" % (i, j), [128, 8], es=pes) for i in range(6)] for j in range(2)]
                SC = float(HD) ** -0.5
                cnt = dict(u=0)

                tb = dict(i=0)

                def make_CT3(h, dst):
                    for dc in range(4):
                        tb["i"] = (tb["i"] + 1) % 3
                        bank = [ptr, pnum, pint][tb["i"]]
                        for ec in range(4):
                            k.tr(bank[:, ec * 128:(ec + 1) * 128], C_nat[:, h, ec, dc * 128:(dc + 1) * 128], identF.v)
                        k.cp(act, dst[:, dc, :], bank.v)
                        yield

                def gate_gen():
                    for u in range(NU):
                        slot = u if sample else 0
                        G = psm[0:TS, 384:388]
                        for kk in range(8):
                            k.mm(G, xnT[:, kk, utok(u)], wg_sb[:, kk, :], start=(kk == 0), stop=(kk == 7))
                        k.tt(dve, g_ig[0:TS, :], G[:, 0:2], big_bc[0:TS, :], ALU.add)
                        k.tt(dve, g_fg[0:TS, :], G[:, 2:4], bfg_bc[0:TS, :], ALU.add)
                        k.actf(g_e1[0:TS, :], g_fg[0:TS, :], AF.Exp, scale=-1.0)
                        k.actf(g_sp[0:TS, :], g_e1[0:TS, :], AF.Ln, bias=1.0)
                        yield
                        k.mm(psm[0:TS, 392:394], utri[0:TS, 0:TS], g_sp[0:TS, :])
                        k.cp(act, g_csp[0:TS, :], psm[0:TS, 392:394])
                        yield
                        k.tt(dve, u_tm[0:TS, u, :], g_ig[0:TS, :], g_csp[0:TS, :], ALU.add)
                        k.tr(psm[0:2, 256:256 + TS], u_tm[0:TS, u, :], identF[0:TS, 0:TS])
                        k.cp(dve, g_uT.v, psm[0:2, 256:256 + TS])
                        yield
                        k.tr(psm[0:2, 128:128 + TS], g_csp[0:TS, :], identF[0:TS, 0:TS])
                        k.op(dve, lambda: nc.vector.tensor_tensor_scan(out=g_M.h[:], data0=g_uT.h[:], data1=g_uT.h[:],
                                                                        initial=mcar.h[0:2, slot:slot + 1], op0=ALU.max, op1=ALU.max),
                             [g_uT.v, mcar.v], [g_M.v])
                        k.ts(dve, negM[0:2, u, :], g_M.v, -1.0, ALU.mult)
                        yield
                        k.tt(dve, g_mT.v, g_M.v, psm[0:2, 128:128 + TS], ALU.subtract)
                        k.actf(g_wiT.v, g_M.v, AF.Exp, scale=-1.0, bias=mcar[0:2, slot:slot + 1])
                        k.actf(g_clT.v, g_mT.v, AF.Exp, scale=-1.0)
                        yield
                        k.cp(dve, mcar[0:2, slot:slot + 1], g_mT[0:2, TS - 1:TS])
                        k.tr(psm[0:TS, 396:398], g_wiT.v, identF[0:2, 0:2])
                        k.cp(act, wi_tm[0:TS, u, :], psm[0:TS, 396:398])
                        k.tr(psm[0:TS, 400:402], g_clT.v, identF[0:2, 0:2])
                        k.cp(act, cl_tm[0:TS, u, :], psm[0:TS, 400:402])
                        yield
                        k.ts(dve, g_d4[0:2, 0:2], identF[0:2, 0:2], g_wiT[0:2, TS - 1:TS], ALU.mult)
                        k.mm(psm[:, 404:406], ones4.v, g_d4[0:2, 0:2])
                        k.cp(act, dec_bc[:, u, :], psm[:, 404:406])
                        yield
                        k.cp(dve, wiT_all[0:2, u, :], g_wiT.v)
                        k.ts(dve, g_d4b[0:2, 0:2], identF[0:2, 0:2], negM[0:2, u, TS - 1:TS], ALU.mult)
                        k.mm(psm[:, 408:410], ones4.v, g_d4b[0:2, 0:2])
                        k.tt(dve, g_wl[0:TS, :], u_tm[0:TS, u, :], psm[0:TS, 408:410], ALU.add)
                        k.actf(wl_all[0:TS, u, :], g_wl[0:TS, :], AF.Exp)
                        k.cp(dve, wlb_all[0:TS, u, :], wl_all[0:TS, u, :])
                        yield
                        if sample or (last and u == NU - 1):
                            k.dma(pool, O["m_s"][u] if sample else O["m_p"], mcar[0:2, slot:slot + 1])
                def proj_gen(h, B_):
                    cT, szT, qT, kT, k_tm, v_tm, hmT = (B_[n_] for n_ in ('cT', 'szT', 'qT', 'kT', 'k_tm', 'v_tm', 'hmT'))
                    W = w_get("U%d" % h)
                    W3 = w3(W, 8)
                    for cc in range(4):
                        c16 = 4 * h + cc
                        acc = next_pa()
                        for kk in range(8):
                            k.mm(acc[:, 0:NT], W3[:, kk, cc * 128:(cc + 1) * 128], xnT[:, kk, :], start=(kk == 0), stop=(kk == 7))
                        u_ext = u_exts[cc % 2]
                        k.cp(pool, u_ext[:, :, 0:3], hist[:, c16, 0:NSEG, :])
                        k.cp(act, u_ext[:, :, 3:3 + SEG], acc[:, 0:NT].rr("p (s t) -> p s t", s=NSEG))
                        k.cp(pool, hist[:, c16, 0:NSEG, :], u_ext[:, :, SEG:SEG + 3])
                        k.actf(cv.v, u_ext[:, :, 0:SEG], AF.Identity, scale=cw_col[:, c16, 0:1], bias=cb_col[:, c16:c16 + 1])
                        for j in range(1, 4):
                            k.stt(cv.v, u_ext[:, :, j:j + SEG], cw_col[:, c16, j:j + 1], cv.v, ALU.mult, ALU.add)
                        k.actf(cT[:, cc, :].rr("p (s t) -> p s t", s=NSEG), cv.v, AF.Silu)
                        yield
                    W = w_get("Z%d" % h)
                    W3 = w3(W, 8)
                    for cc in range(4):
                        acc = next_pa()
                        for kk in range(8):
                            k.mm(acc[:, 0:NT], W3[:, kk, cc * 128:(cc + 1) * 128], xnT[:, kk, :], start=(kk == 0), stop=(kk == 7))
                        k.actf(szT[:, cc, :], acc[:, 0:NT], AF.Silu)
                        yield
                    W = w_get("V%d" % h)
                    W3 = w3(W, 8)
                    for u in range(NU):
                        acc = next_pa()
                        for kk in range(8):
                            k.mm(acc[0:TS, :], xnT[:, kk, utok(u)], W3[:, kk, :], start=(kk == 0), stop=(kk == 7))
                        k.cp(act if u % 2 else dve, v_tm[0:TS, u, :], acc[0:TS, :])
                        yield
                    W = w_get("QK%d" % h)
                    Wq = w3(W, 4, 0, 2048)
                    Wk = w3(W, 4, 2048, 4096)
                    for ec in range(4):
                        acc = next_pa()
                        for dc in range(4):
                            k.mm(acc[:, 0:NT], Wq[:, dc, ec * 128:(ec + 1) * 128], cT[:, dc, :], start=(dc == 0), stop=(dc == 3))
                        k.cp(act, qT[:, ec, :], acc[:, 0:NT])
                        acc = next_pa()
                        for dc in range(4):
                            k.mm(acc[:, 0:NT], Wk[:, dc, ec * 128:(ec + 1) * 128], cT[:, dc, :], start=(dc == 0), stop=(dc == 3))
                        k.ts(dve, kT[:, ec, :], acc[:, 0:NT], SC, ALU.mult)
                        yield
                    for u in range(NU):
                        acc = next_pa()
                        for dc in range(4):
                            k.mm(acc[0:TS, :], cT[:, dc, utok(u)], Wk[:, dc, :], start=(dc == 0), stop=(dc == 3))
                        k.ts(dve, k_tm[0:TS, u, :], acc[0:TS, :], SC, ALU.mult)
                        yield
                    for ec in range(4):
                        c16 = 4 * h + ec
                        k.stt(cT[:, ec, :], cT[:, ec, :], sk_col[:, c16:c16 + 1], szT[:, ec, :], ALU.mult, ALU.mult)
                        k.actf(szT[:, ec, :], szT[:, ec, :], AF.Copy, scale=mg_col[:, c16:c16 + 1])
                        yield


                def passes_gen(h, B_):
                    cT, szT, qT, kT, k_tm, v_tm, hmT = (B_[n_] for n_ in ('cT', 'szT', 'qT', 'kT', 'k_tm', 'v_tm', 'hmT'))
                    for u in range(NU):
                        k.actf(wv_all[0:TS, u, :], v_tm[0:TS, u, :], AF.Copy, scale=wl_all[0:TS, u, h:h + 1])

                    def state_pass(u):
                        slot = u if sample else 0
                        if sample:
                            k.dma(sp, C_nat[:, h], I["sC"][u, h].rearrange("(ec p) d -> p ec d", p=128))
                            yield from make_CT3(h, CTtmp[u % 2])
                        elif first and u == 0:
                            k.memset(pool, C_nat[:, h], 0.0)
                            k.memset(pool, CT_cur[:, h], 0.0)
                        k.cp(dve, n_bfs[:, u, :], n_col[:, slot, 4 * h:4 * h + 4])
                        for ec in range(4):
                            dcp = next_dc2()
                            k.mm(dcp, wv_all[0:TS, u, ec * 128:(ec + 1) * 128], k_tm[0:TS, u, :])
                            k.stt(C_nat[:, h, ec, :], C_nat[:, h, ec, :], dec_bc[:, u, h:h + 1], dcp, ALU.mult, ALU.add)
                            yield
                        for dc in range(4):
                            k.mm(psm[:, 420 + dc:421 + dc], k_tm[0:TS, u, dc * 128:(dc + 1) * 128], wlb_all[0:TS, u, h:h + 1])
                        k.stt(n_col[:, slot, 4 * h:4 * h + 4], n_col[:, slot, 4 * h:4 * h + 4], dec_bc[:, u, h:h + 1],
                              psm[:, 420:424], ALU.mult, ALU.add)
                        if sample:
                            k.dma(pool, O["C_s"][u, h].rearrange("(ec p) d -> p ec d", p=128), C_nat[:, h])
                        elif last and u == NU - 1:
                            k.dma(pool, O["C_p"][h].rearrange("(ec p) d -> p ec d", p=128), C_nat[:, h])
                        else:
                            yield from make_CT3(h, CTtmp[u] if u < NU - 1 else CT_nxt[:, h])

                    def h_group(us):
                        R_ = [(u_ % 2, u_) for u_ in us]
                        CTs = {u: (CTtmp[u % 2] if sample else (CT_cur[:, h] if u == 0 else CTtmp[u - 1])) for u in us}
                        hb = [ptr, ptr]
                        yield
                        for i2, u in R_:
                            nm = psm[0:TS, 0:TS]
                            k.mm(nm, sel[0:2, h, 0:TS], negM[0:2, u, :], start=True, stop=False)
                            k.mm(nm, identF[0:TS, 0:TS], mnegT[0:TS, 0:TS], start=False, stop=True)
                            kq = psm[0:TS, 128:128 + TS]
                            for ec in range(4):
                                k.mm(kq, kT[:, ec, utok(u)], qT[:, ec, utok(u)], start=(ec == 0), stop=(ec == 3))
                            k.mm(psm[:, 256:256 + TS], sel[0:2, h, :], wiT_all[0:2, u, :])
                        yield
                        for i2, u in R_:
                            k.actf(dT_sb[i2][0:TS, 0:TS], psm[0:TS, 0:TS], AF.Exp, bias=u_tm[0:TS, u, h:h + 1])
                        yield
                        for i2, u in R_:
                            k.tt(dve, sT_sb[i2][0:TS, 0:TS], psm[0:TS, 128:128 + TS], dT_sb[i2][0:TS, 0:TS], ALU.mult)
                            k.tt(dve, qwT[i2][:, :, 0:TS], qT[:, :, utok(u)],
                                 V(psm, psm.h[:, 256:256 + TS].unsqueeze(1).to_broadcast([128, 4, TS])), ALU.mult)
                        yield
                        for i2, u in R_:
                            pn = pnum if i2 == 0 else pint
                            k.mm(pn[0:TS, :], sT_sb[i2][0:TS, 0:TS], v_tm[0:TS, u, :], start=True, stop=False)
                            for dc in range(4):
                                k.mm(pn[0:TS, :], qwT[i2][:, dc, 0:TS], CTs[u][:, dc, :], start=False, stop=(dc == 3))
                            dn_ = psm[0:TS, 416 + i2:417 + i2]
                            k.mm(dn_, sT_sb[i2][0:TS, 0:TS], onesb[0:TS, :], start=True, stop=False)
                            for dc in range(4):
                                k.mm(dn_, qwT[i2][:, dc, 0:TS], n_bfs[:, u, dc:dc + 1], start=False, stop=(dc == 3))
                        yield
                        for i2, u in R_:
                            d_rs, d_den, d_aden, d_rden, d_st, d_mv = dd[i2]
                            dn_ = psm[0:TS, 416 + i2:417 + i2]
                            k.ts(dve, d_den[0:TS, 0:1], dn_, -1.0, ALU.mult)
                            k.tt(dve, d_aden[0:TS, 0:1], d_den[0:TS, 0:1], dn_, ALU.max)
                            k.tt(dve, d_aden[0:TS, 0:1], d_aden[0:TS, 0:1], cl_tm[0:TS, u, h:h + 1], ALU.max)
                            k.op(dve, lambda d_rden=d_rden, d_aden=d_aden: nc.vector.reciprocal(out=d_rden.h[0:TS, 0:1], in_=d_aden.h[0:TS, 0:1]),
                                 [d_aden.v], [d_rden.v])
                        yield
                        for i2, u in R_:
                            pn = pnum if i2 == 0 else pint
                            k.actf(hraw[i2][0:TS, :], pn[0:TS, :], AF.Copy, scale=dd[i2][3][0:TS, 0:1])
                        yield
                        for i2, u in R_:
                            d_rs, d_den, d_aden, d_rden, d_st, d_mv = dd[i2]
                            k.op(dve, lambda d_st=d_st, i2=i2: nc.vector.bn_stats(out=d_st.h[0:TS, 0:6], in_=hraw[i2].h[0:TS, :]), [hraw[i2].v], [d_st.v])
                            k.op(dve, lambda d_st=d_st, d_mv=d_mv: nc.vector.bn_aggr(out=d_mv.h[0:TS, 0:2], in_=d_st.h[0:TS, 0:6]), [d_st.v], [d_mv.v])
                        yield
                        for i2, u in R_:
                            d_rs, d_den, d_aden, d_rden, d_st, d_mv = dd[i2]
                            k.actf(d_rs[0:TS, 0:1], d_mv[0:TS, 1:2], AF.Ln, bias=EPS)
                            k.actf(d_rs[0:TS, 1:2], d_rs[0:TS, 0:1], AF.Exp, scale=-0.5)
                        yield
                        for i2, u in R_:
                            d_rs, d_den, d_aden, d_rden, d_st, d_mv = dd[i2]
                            k.ts(dve, hn_sb[i2][0:TS, :], hraw[i2][0:TS, :], d_mv[0:TS, 0:1], ALU.subtract, d_rs[0:TS, 1:2], ALU.mult)
                        yield
                        for i2, u in R_:
                            for ec in range(4):
                                k.tr(hb[i2][:, ec * TS:(ec + 1) * TS], hn_sb[i2][0:TS, ec * 128:(ec + 1) * 128], identF[0:TS, 0:TS])
                        yield
                        for i2, u in R_:
                            k.tt(dve, tmp_sb[i2][:, :, 0:TS], hb[i2][:, 0:4 * TS].rr("p (e t) -> p e t", e=4), szT[:, :, utok(u)], ALU.mult)
                            k.tt(pool, hmT[:, :, utok(u)], tmp_sb[i2][:, :, 0:TS], cT[:, :, utok(u)], ALU.add)


                    if sample:
                        for u in range(NU):
                            yield from state_pass(u)
                            yield from h_group([u])
                    else:
                        for u in range(NU):
                            yield from state_pass(u)
                        for u in range(NU):
                            yield from h_group([u])
                def brm_gen(h, B_):
                    cT, szT, qT, kT, k_tm, v_tm, hmT = (B_[n_] for n_ in ('cT', 'szT', 'qT', 'kT', 'k_tm', 'v_tm', 'hmT'))
                    W = w_get("BRM%d" % h)
                    W3 = w3(W, 4)
                    for b in range(NSB):
                        for q in range(2):
                            acc = next_pa()
                            for ec in range(4):
                                k.mm(acc.v, hmT[:, ec, blk(b)], W3[:, ec, q * 512:(q + 1) * 512], start=(ec == 0), stop=(ec == 3))
                            dst = y_acc[:, b, q * 512:(q + 1) * 512]
                            if h == 0:
                                k.cp(act, dst, acc.v)
                                yield
                            else:
                                k.tt(dve, dst, dst, acc.v, ALU.add)
                                yield


                def run(ga, gb, ra=1, rb=1):
                    gens = [g_ for g_ in (ga, gb) if g_ is not None]
                    rate = {id(ga): ra, id(gb): rb}
                    while gens:
                        for g_ in list(gens):
                            for _ in range(rate[id(g_)]):
                                try:
                                    next(g_)
                                except StopIteration:
                                    gens.remove(g_)
                                    break
                run(gate_gen(), proj_gen(0, BS[0]), 1, 1)
                run(passes_gen(0, BS[0]), proj_gen(1, BS[1]), 3, 1)
                run(passes_gen(1, BS[1]), brm_gen(0, BS[0]), 3, 1)
                run(brm_gen(1, BS[1]), None)
                for slot in range(NU if sample else 1):
                    if sample or last:
                        k.tr(psm[0:8, 256:384], n_col[:, slot, :], identF.v)
                        k.cp(act, nout[0:8, slot, :], psm[0:8, 256:384])
                        k.dma(pool, O["n_s"][slot] if sample else O["n_p"], nout[0:8, slot, :])
                if sample or last:
                    for grp in range(2):
                        for cc in range(4):
                            c16 = grp * 4 + cc
                            k.tr(ptr[0:NSEG * 3, cc * 128:(cc + 1) * 128], hist[:, c16, 0:NSEG, :].rr("p s j -> p (s j)"), identF.v)
                        k.cp(act, cvo[0:NSEG * 3, grp * 512:(grp + 1) * 512], ptr[0:NSEG * 3, :])
                    k.dma(pool, O["conv_s"] if sample else O["conv_p"], cvo[0:NSEG * 3, :])
                evA, xdA = exchange(NT, "a", y_acc, NSB)
            k.barrier()

            yb_acc = k.sb("yb_acc", [128, NSB, D], es=tes)
            if True:
                with ExitStack() as pes:
                    ohT = k.sb("ohT", [128, 4, NT], BF16, es=pes)
                    qs_sb = k.sb("qs_sb", [128, 4, NT], es=pes)
                    sigf = k.sb("sigf", [128, NT], es=pes)
                    lfh = k.sb("lfh", [128, NT], es=pes)
                    kh_sb = k.sb("kh_sb", [128, NT], es=pes)
                    g_sb = k.sb("g_sb", [128, NT], es=pes)
                    eg_sb = k.sb("eg_sb", [128, NT], es=pes)
                    eng_sb = k.sb("eng_sb", [128, NT], es=pes)
                    qtT = k.sb("qtT", [128, 4, NT], BF16, es=pes)
                    ktT = k.sb("ktT", [128, 4, NT], BF16, es=pes)
                    sgT = k.sb("sgT", [128, 4, NT], BF16, es=pes)
                    egl = k.sb("egl", [128, 4, NT // L], es=pes)
                    vh_tm = k.sb("vh_tm", [128, NU, 512], BF16, es=pes)
                    kt_tm = k.sb("kt_tm", [128, 512], BF16, es=pes)
                    aT_sb = k.sb("aT_sb", [128, 4, 128], BF16, es=pes)
                    o_sb = k.sb("o_sb", [128, 512], es=pes)
                    sq_sb = k.sb("sq_sb", [128, 512], es=pes)
                    on_sb = k.sb("on_sb", [128, 4, 128], es=pes)
                    hs = k.sb("hs", [128, 12], es=pes)
                    for g in range(1):
                        hsl = slice(4 * g, 4 * g + 4)
                        set_pa([pa[0], pa[1], pdc, ptr])
                        W = w_get("QH%d" % g)
                        W3 = w3(W, 8)
                        for j in range(4):
                            acc = next_pa()
                            for kk in range(8):
                                k.mm(acc[:, 0:NT], W3[:, kk, j * 128:(j + 1) * 128], xnT[:, kk, :], start=(kk == 0), stop=(kk == 7))
                            k.actf(qs_sb[:, j, :], acc[:, 0:NT], AF.Silu)
                        W = w_get("FH%d" % g)
                        W3 = w3(W, 8)
                        for j in range(4):
                            hh = 4 * g + j
                            acc = next_pa()
                            for kk in range(8):
                                k.mm(acc[:, 0:NT], W3[:, kk, j * 128:(j + 1) * 128], xnT[:, kk, :], start=(kk == 0), stop=(kk == 7))
                            k.actf(sigf.v, acc[:, 0:NT], AF.Sigmoid)
                            k.actf(lfh.v, sigf.v, AF.Ln, scale=oml_col[:, hh:hh + 1], bias=lb_col[:, hh:hh + 1])
                            k.ts(dve, kh_sb.v, sigf.v, noml_col[:, hh:hh + 1], ALU.mult, oml_col[:, hh:hh + 1], ALU.add)
                            k.op(dve, lambda: nc.vector.tensor_tensor_scan(out=g_sb.h[:], data0=m01.h[:, 0:NT], data1=lfh.h[:],
                                                                            initial=0.0, op0=ALU.mult, op1=ALU.add),
                                 [m01.v, lfh.v], [g_sb.v])
                            k.actf(eg_sb.v, g_sb.v, AF.Exp)
                            k.actf(eng_sb.v, g_sb.v, AF.Exp, scale=-1.0)
                            k.tt(pool, qtT[:, j, :], qs_sb[:, j, :], eg_sb.v, ALU.mult)
                            k.tt(dve, ktT[:, j, :], kh_sb.v, eng_sb.v, ALU.mult)
                            k.cp(dve, egl[:, j, :], eg_sb.v.rr("p (c l) -> p c l", l=L)[:, :, L - 1])
                        W = w_get("IH%d" % g)
                        W3 = w3(W, 8)
                        for u in range(NU):
                            acc = next_pa()
                            for kk in range(8):
                                k.mm(acc[0:TS, :], xnT[:, kk, utok(u)], W3[:, kk, :], start=(kk == 0), stop=(kk == 7))
                            k.cp(act if u % 2 else dve, vh_tm[0:TS, u, :], acc[0:TS, :])
                        W = w_get("GH%d" % g)
                        W3 = w3(W, 8)
                        for j in range(4):
                            acc = next_pa()
                            for kk in range(8):
                                k.mm(acc[:, 0:NT], W3[:, kk, j * 128:(j + 1) * 128], xnT[:, kk, :], start=(kk == 0), stop=(kk == 7))
                            k.actf(sgT[:, j, :], acc[:, 0:NT], AF.Silu)
                        set_pa([pa[0], pa[1]])
                        for u in range(NU):
                            if sample:
                                k.dma(sp, S_st[:, hsl, :], I["sS"][u].rearrange("h c e -> c h e"))
                                k.cp(act, S_bf[0][:, hsl, :], S_st[:, hsl, :])
                            for j in range(4):
                                k.tr(ptrb[0:TS, j * 128:(j + 1) * 128], ktT[:, j, utok(u)], identB.v)
                            k.cp(act, kt_tm[0:TS, :], ptrb[0:TS, 0:512])
                            for j in range(4):
                                k.mm(pnum[0:TS, j * TS:(j + 1) * TS], ktT[:, j, utok(u)], qtT[:, j, utok(u)])
                            k.tt(dve, aT_sb[0:TS, :, 0:TS], pnum[0:TS, 0:4 * TS].rr("p (j t) -> p j t", j=4),
                                 V(hmask, hmask.h[0:TS, 0:TS].unsqueeze(1).to_broadcast([TS, 4, TS])), ALU.mult)
                            def s_update(c, dst_bf):
                                rows = slice(c * L, (c + 1) * L)
                                gci = u * NCH + c
                                for j in range(4):
                                    k.mm(pdc[:, j * 128:(j + 1) * 128], kt_tm[rows, j * 128:(j + 1) * 128],
                                         vh_tm[rows, u, j * 128:(j + 1) * 128])
                                k.tt(dve, S_st[:, hsl, :], S_st[:, hsl, :], pdc.v.rr("p (j e) -> p j e", j=4), ALU.add)
                                k.tt(dve, S_st[:, hsl, :], S_st[:, hsl, :],
                                     V(egl, egl.h[:, :, gci:gci + 1].to_broadcast([128, 4, 128])), ALU.mult)
                                if dst_bf is not None:
                                    k.cp(act, dst_bf[:, hsl, :], S_st[:, hsl, :])
                            for c in range(NCH - 1):
                                s_update(c, S_bf[c + 1])
                            for j in range(4):
                                hh = 4 * g + j
                                for c in range(NCH):
                                    rows = slice(c * L, (c + 1) * L)
                                    k.mm(pint[rows, j * 128:(j + 1) * 128], qtT[:, j, u * TS + c * L:u * TS + (c + 1) * L],
                                         S_bf[c][:, hh, :], start=True, stop=False, skip_group_check=True)
                                k.mm(pint[0:TS, j * 128:(j + 1) * 128], aT_sb[0:TS, j, 0:TS], vh_tm[0:TS, u, j * 128:(j + 1) * 128],
                                     start=False, stop=True, skip_group_check=True)
                            s_update(NCH - 1, None if sample else S_bf[0])
                            k.cp(act, o_sb[0:TS, :], pint[0:TS, :])
                            k.tt(pool, sq_sb[0:TS, :], o_sb[0:TS, :], o_sb[0:TS, :], ALU.mult)
                            k.op(dve, lambda: nc.vector.tensor_reduce(out=hs.h[0:TS, 0:4],
                                                                       in_=sq_sb.h[0:TS, :].rearrange("p (j e) -> p j e", j=4),
                                                                       axis=AX.X, op=ALU.add), [sq_sb.v], [hs.v])
                            k.actf(hs[0:TS, 4:8], hs[0:TS, 0:4], AF.Ln, scale=1.0 / HE, bias=EPS)
                            k.actf(hs[0:TS, 8:12], hs[0:TS, 4:8], AF.Exp, scale=-0.5)
                            k.tt(dve, on_sb[0:TS, :, :], o_sb[0:TS, :].rr("p (j e) -> p j e", j=4),
                                 V(hs, hs.h[0:TS, 8:12].unsqueeze(2).to_broadcast([TS, 4, 128])), ALU.mult)
                            for j in range(4):
                                k.tr(ptr[:, j * TS:(j + 1) * TS], on_sb[0:TS, j, :], identF[0:TS, 0:TS])
                            for j in range(4):
                                hh = 4 * g + j
                                k.stt(ohT[:, hh, utok(u)], ptr[:, j * TS:(j + 1) * TS], hg_col[:, hh:hh + 1], sgT[:, j, utok(u)],
                                      ALU.mult, ALU.mult)
                            if sample:
                                k.dma(pool, O["S_s"][u].rearrange("h c e -> c h e"), S_st[:, hsl, :])
                            elif last and u == NU - 1:
                                k.dma(pool, O["S_p"].rearrange("h c e -> c h e"), S_st[:, hsl, :])
                    W = w_get("BRH")
                    W3 = w3(W, 4)
                    for b in range(NSB):
                        for q in range(2):
                            acc = next_pa()
                            for hh in range(4):
                                k.mm(acc.v, ohT[:, hh, blk(b)], W3[:, hh, q * 512:(q + 1) * 512], start=(hh == 0), stop=(hh == 3))
                            k.cp(act, yb_acc[:, b, q * 512:(q + 1) * 512], acc.v)
                    evB, xdB = exchange(NT, "b", yb_acc, NSB)
            k.barrier()

            with ExitStack() as pes:
                x2 = k.sb("x2", [128, NSB, D], es=pes)
                tT = k.sb("tT", [128, 8, NT], BF16, es=pes)
                pT = k.sb("pT", [128, 2, NT], BF16, es=pes)
                p_tm = k.sb("p_tm", [128, NSB, PLE], es=pes)
                bf_sb = k.sb("bf_sb", [128, D], BF16, es=pes)
                sg_sb = k.sb("sg2_sb", [128, 512], es=pes)
                sga = k.sb("sga", [128, NSB, D], es=pes)
                sgb = k.sb("sgb", [128, NSB, D], es=pes)
                outb = [k.sb("outb%d" % i, [128, D], es=pes) for i in range(2)]
                ssq = k.sb("ssq2", [128, 4], es=pes)
                set_pa([pa[0], pa[1], pnum, pint])
                for b in range(NSB):
                    k.dma(sp, x2[:, b, :], x_dram[b * 128:(b + 1) * 128, :])
                    k.dma(sp, p_tm[:, b, :], p_dram[b * 128:(b + 1) * 128, :])

                def to_T(dst, src_fn, nchunk):
                    for b in range(NSB):
                        k.cp(act, bf_sb[:, 0:nchunk * 128], src_fn(b))
                        for c in range(nchunk):
                            k.tr(ptrb[:, c * 128:(c + 1) * 128], bf_sb[:, c * 128:(c + 1) * 128], identB.v)
                        k.cp(dve, dst[:, 0:nchunk, blk(b)], ptrb[:, 0:nchunk * 128].rr("p (c t) -> p c t", c=nchunk))
                to_T(pT, lambda b: p_tm[:, b, :], 2)
                for nm_, sgt in (("GA", sga), ("GB", sgb)):
                    for q in range(2):
                        W = w_get("%s%d" % (nm_, q))
                        W3 = w3(W, 8)
                        for b in range(NSB):
                            acc = next_pa()
                            for kk in range(8):
                                k.mm(acc.v, xnT[:, kk, blk(b)], W3[:, kk, :], start=(kk == 0), stop=(kk == 7))
                            k.actf(sgt[:, b, q * 512:(q + 1) * 512], acc.v, AF.Sigmoid)
                sp.wait(evA)
                sp.wait(evB)
                for b in range(NSB):
                    k.dma(sp, y_acc[:, b, :], xdA[b * 128:(b + 1) * 128, :])
                    k.dma(sp, yb_acc[:, b, :], xdB[b * 128:(b + 1) * 128, :])
                    k.tt(dve, y_acc[:, b, :], y_acc[:, b, :], sga[:, b, :], ALU.mult)
                    k.tt(pool, yb_acc[:, b, :], yb_acc[:, b, :], sgb[:, b, :], ALU.mult)
                    k.tt(dve, y_acc[:, b, :], y_acc[:, b, :], yb_acc[:, b, :], ALU.add)
                to_T(tT, lambda b: y_acc[:, b, :], 8)
                for q in range(2):
                    W = w_get("OUT%d" % q)
                    W3 = w3(W, 8)
                    for b in range(NSB):
                        acc = next_pa()
                        for kk in range(8):
                            k.mm(acc.v, tT[:, kk, blk(b)], W3[:, kk, :], start=(kk == 0), stop=(kk == 7))
                        dst = x2[:, b, q * 512:(q + 1) * 512]
                        k.tt(dve, dst, dst, acc.v, ALU.add)
                to_T(tT, lambda b: x2[:, b, :], 8)
                Wp = [w_get("PG0"), w_get("PG1", hold=1), w_get("PLE", hold=2)]
                Wple = w3(Wp[2], 2, 0, 2048)
                for b in range(NSB):
                    for q in range(2):
                        W3 = w3(Wp[q], 8)
                        acc = next_pa()
                        for kk in range(8):
                            k.mm(acc.v, tT[:, kk, blk(b)], W3[:, kk, :], start=(kk == 0), stop=(kk == 7))
                        k.actf(sg_sb.v, acc.v, AF.Sigmoid)
                        acc2 = next_pa()
                        for kk in range(2):
                            k.mm(acc2.v, pT[:, kk, blk(b)], Wple[:, kk, q * 512:(q + 1) * 512], start=(kk == 0), stop=(kk == 1))
                        k.tt(dve, sg_sb.v, sg_sb.v, acc2.v, ALU.mult)
                        dst = x2[:, b, q * 512:(q + 1) * 512]
                        k.tt(pool, dst, dst, sg_sb.v, ALU.add)
                    ob = outb[b % 2]
                    k.actf(ob.v, x2[:, b, :], AF.Square, accum=ssq[:, 0:1])
                    k.actf(ssq[:, 1:2], ssq[:, 0:1], AF.Ln, scale=1.0 / D, bias=EPS)
                    k.actf(ssq[:, 2:3], ssq[:, 1:2], AF.Exp, scale=-0.5)
                    k.stt(ob.v, x2[:, b, :], ssq[:, 2:3], fg_bc.v, ALU.mult, ALU.mult)
                    k.dma(pool, y_dram[b * 128:(b + 1) * 128, :], ob.v)
                set_pa([pa[0], pa[1]])
            k.barrier()

    NTP = min(512, TP)
    ntile = TP // NTP if do_prompt else 0
    for ti in range(ntile):
        seq.extend(range(NW))
    if do_sample:
        seq.extend(range(NW))
    for ti in range(ntile):
        tile(I["xp"][ti * NTP:(ti + 1) * NTP, :], I["pp"][ti * NTP:(ti + 1) * NTP, :], O["yp"][ti * NTP:(ti + 1) * NTP, :],
             NTP, 128, NTP // 128, 1, ti == 0, ti == ntile - 1, False, tix=ti)
    if do_sample:
        tile(I["xs"], I["ps"], O["ys"], NS * 32, 32, NS, NS, True, True, True)
    for ev in k.out_events:
        sp.wait(ev)


_NC_CACHE = {}


def _in_map(core, inputs, NS):
    b = core // 2
    hf = core % 2
    s0 = (core // 2) * NS
    f = lambda a: np.ascontiguousarray(a, dtype=np.float32)
    hs = slice(2 * hf, 2 * hf + 2)
    cs = slice(hf * MIL, (hf + 1) * MIL)
    gs = slice(hf * HWL, (hf + 1) * HWL)
    w = inputs["w_in"][0]
    w_in = np.concatenate([
        w[:, 0 + hf * MIL:0 + (hf + 1) * MIL], w[:, 4096 + hf * MIL:4096 + (hf + 1) * MIL], w[:, 2048 + hf * MIL:2048 + (hf + 1) * MIL],
        w[:, 6144 + 2 * hf:6144 + 2 * hf + 2], w[:, 6148 + 2 * hf:6148 + 2 * hf + 2],
        w[:, 6152 + hf * HWL:6152 + (hf + 1) * HWL], w[:, 7176 + hf * HWL:7176 + (hf + 1) * HWL],
        w[:, 8200 + hf * HWL:8200 + (hf + 1) * HWL], w[:, 9224 + hf * HWL:9224 + (hf + 1) * HWL],
        w[:, 10248:12296]], axis=1)
    assert w_in.shape[1] == NINL
    m = dict(
        xp=f(inputs["x_prompt"][b]), pp=f(inputs["p_prompt"][0, b]),
        xs=f(inputs["x_sample"][s0:s0 + NS].reshape(NS * 32, D)), ps=f(inputs["p_sample"][0, s0:s0 + NS].reshape(NS * 32, PLE)),
        sconv=f(inputs["state_conv"][0, s0:s0 + NS][:, :, cs].reshape(NS * 3, MIL)), sC=f(inputs["state_mlstm_C"][0, s0:s0 + NS, hs]),
        sn=f(inputs["state_mlstm_n"][0, s0:s0 + NS, hs].reshape(NS, MHL * 4, 128)), sm=f(inputs["state_mlstm_m"][0, s0:s0 + NS, hs]),
        sS=f(inputs["state_hgrn"][0, s0:s0 + NS, 4 * hf:4 * hf + 4]),
        norm_g=f(inputs["norm_g"][0]), w_in=f(w_in), b_ig=f(inputs["b_ig"][0, hs]), b_fg=f(inputs["b_fg"][0, hs]),
        conv_w=f(inputs["conv_w"][0][:, cs]), conv_b=f(inputs["conv_b"][0, cs]), w_qm=f(inputs["w_qm"][0, hs]), w_km=f(inputs["w_km"][0, hs]),
        mnorm_g=f(inputs["mnorm_g"][0, cs]), m_skip=f(inputs["m_skip"][0, cs]), w_brm=f(inputs["w_brm"][0, cs]),
        hgrn_lb=f(inputs["hgrn_lb"][:, gs]), hnorm_g=f(inputs["hnorm_g"][0, gs]), w_brh=f(inputs["w_brh"][0, gs]), w_out=f(inputs["w_out"][0]),
        w_ple=f(inputs["w_ple"][0]), w_pg=f(inputs["w_pg"][0]), final_g=f(inputs["final_g"]),
    )
    return m


def kernel(**inputs):
    inputs = {k_: np.asarray(v) for k_, v in inputs.items()}
    B, TP = inputs["x_prompt"].shape[:2]
    NSEQ = inputs["x_sample"].shape[0]
    NCORE = 8
    NS = NSEQ // (NCORE // 2)
    key = (TP, NS)
    if key not in _NC_CACHE:
        _NC_CACHE[key] = build(TP, NS)
    nc = _NC_CACHE[key]
    in_maps = [_in_map(c, inputs, NS) for c in range(NCORE)]
    res = run_bass_kernel_spmd(nc, in_maps, core_ids=list(range(NCORE)))
    R = res.results
    f32 = np.float32
    NP = NCORE // 2
    cat2 = lambda name, fn, ax: [np.concatenate([fn(R[2 * p][name]), fn(R[2 * p + 1][name])], axis=ax) for p in range(NP)]
    y_prompt = np.stack([R[2 * b]["yp"] for b in range(B)]).astype(f32)
    y_sample = np.concatenate([R[2 * p]["ys"].reshape(NS, 32, D) for p in range(NP)]).astype(f32)
    conv_p = np.stack(cat2("conv_p", lambda a: a, 1))[None].astype(f32)
    C_p = np.stack(cat2("C_p", lambda a: a, 0))[None].astype(f32)
    n_p = np.stack(cat2("n_p", lambda a: a.reshape(MHL, HD), 0))[None].astype(f32)
    m_p = np.stack(cat2("m_p", lambda a: a.reshape(MHL), 0))[None].astype(f32)
    S_p = np.stack(cat2("S_p", lambda a: a, 0))[None].astype(f32)
    conv_s = np.concatenate(cat2("conv_s", lambda a: a.reshape(NS, 3, MIL), 2))[None].astype(f32)
    C_s = np.concatenate(cat2("C_s", lambda a: a, 1))[None].astype(f32)
    n_s = np.concatenate(cat2("n_s", lambda a: a.reshape(NS, MHL, HD), 1))[None].astype(f32)
    m_s = np.concatenate(cat2("m_s", lambda a: a.reshape(NS, MHL), 1))[None].astype(f32)
    S_s = np.concatenate(cat2("S_s", lambda a: a, 1))[None].astype(f32)
    return (y_prompt, y_sample, conv_p, C_p, n_p, m_p, S_p, conv_s, C_s, n_s, m_s, S_s)
```

```python
import numpy as np
from contextlib import ExitStack
import concourse.bass as bass
import concourse.mybir as mybir
from concourse.bass_utils import run_bass_kernel_spmd

F32 = mybir.dt.float32
BF16 = mybir.dt.bfloat16
AF = mybir.ActivationFunctionType
ALU = mybir.AluOpType
AX = mybir.AxisListType

D = 1024
MI = 2048
MH = 4
HD = 512
HH = 8
HE = 128
PLE = 256
NIN = 12296
MHL = 2
MIL = 1024
HGL = 4
HWL = 512
NINL = 7172
EPS = 1e-6
NEG = -30000.0


class V:
    __slots__ = ("t", "ap")

    def __init__(self, t, ap):
        self.t = t
        self.ap = ap

    def __getitem__(self, key):
        return V(self.t, self.ap[key])

    def bitcast(self, dt):
        return V(self.t, self.ap.bitcast(dt))

    def rr(self, pat, **kw):
        return V(self.t, self.ap.rearrange(pat, **kw))

    def bc(self, shape):
        return V(self.t, self.ap.to_broadcast(list(shape)))


class T:
    def __init__(self, h, name):
        self.h = h
        self.name = name
        self.w = None
        self.r = {}
        self.dsem = None
        self.dcnt = 0

    def __getitem__(self, key):
        return V(self, self.h[key])

    @property
    def v(self):
        return V(self, self.h[:])


class TA:
    def __init__(self, t, ap, name):
        self.t = t
        self.h = ap
        self.name = name

    def __getitem__(self, key):
        return V(self.t, self.h[key])

    @property
    def v(self):
        return V(self.t, self.h)


class Eng:
    def __init__(self, k, name, h, strict=False):
        self.k = k
        self.name = name
        self.h = h
        self.sem = k.new_sem("e_" + name)
        self.cnt = 0
        self.last = None
        self.seen = {}
        self.strict = strict

    def wait(self, ev):
        sem, val, key = ev
        prod = self.k.engs.get(key)
        if prod is not None and val > prod.cnt:
            assert prod.last is not None and val == prod.cnt + 1, (key, val, prod.cnt)
            prod.last.then_inc(prod.sem, 1)
            prod.last = None
            prod.cnt += 1
        if self.seen.get(key, 0) < val:
            self.h.wait_ge(sem, val)
            self.seen[key] = val


class K:
    def __init__(self, nc, es):
        self.nc = nc
        self.es = es
        self.nsem = 0
        self.pe = Eng(self, "pe", nc.tensor)
        self.act = Eng(self, "act", nc.scalar)
        self.dve = Eng(self, "dve", nc.vector)
        self.pool = Eng(self, "pool", nc.gpsimd, strict=True)
        self.sp = Eng(self, "sp", nc.sync)
        self.engs = {e.name: e for e in (self.pe, self.act, self.dve, self.pool, self.sp)}
        self.out_events = []
        self.dsems = {}
        self.uid = 0
        self.dma_last = {}

    def new_sem(self, name):
        self.nsem += 1
        return self.es.enter_context(self.nc.semaphore(name))

    def sb(self, name, shape, dt=F32, es=None):
        self.uid += 1
        h = (es or self.es).enter_context(self.nc.sbuf_tensor("%s_%d" % (name, self.uid), list(shape), dt))
        return T(h, name)

    def ps(self, name, shape, dt=F32):
        h = self.es.enter_context(self.nc.psum_tensor(name, list(shape), dt))
        return T(h, name)

    def dram(self, name, shape, dt):
        h = self.nc.dram_tensor(name, list(shape), dt, kind="Internal")
        return T(h, name)

    def _pre(self, eng, rd, wr):
        for v in rd:
            t = v.t
            if t.w is not None:
                eng.wait(t.w)
        for v in wr:
            t = v.t
            if t.w is not None and (eng.strict or t.w[2] != eng.name):
                eng.wait(t.w)
            for key, ev in t.r.items():
                if eng.strict or key != eng.name:
                    eng.wait(ev)

    def _post(self, ev, rd, wr):
        for v in wr:
            v.t.w = ev
            v.t.r = {}
        for v in rd:
            if v.t.w is ev:
                continue
            v.t.r[ev[2]] = ev

    def op(self, eng, fn, rd, wr):
        rd = [v for v in rd if isinstance(v, V)]
        wr = [v for v in wr if isinstance(v, V)]
        self._pre(eng, rd, wr)
        ins = fn()
        eng.last = ins
        ev = (eng.sem, eng.cnt + 1, eng.name)
        self._post(ev, rd, wr)
        return ins

    def dma(self, q, out, in_, semt=None, **kw):
        rd = [in_] if isinstance(in_, V) else []
        wr = [out] if isinstance(out, V) else []
        self._pre(q, rd, wr)
        o = out.ap if isinstance(out, V) else out
        i = in_.ap if isinstance(in_, V) else in_
        ins = q.h.dma_start(out=o, in_=i, **kw)
        st = semt or (wr[0].t if wr else rd[0].t)
        key = "d_" + st.name
        if key not in self.dsems:
            self.dsems[key] = [self.new_sem(key), 0]
        ent = self.dsems[key]
        ent[1] += 16
        ins.then_inc(ent[0], 16)
        ev = (ent[0], ent[1], key)
        self.dma_last[key] = ev
        self._post(ev, rd, wr)
        if not wr:
            self.out_events.append(ev)
        return ev

    def barrier(self):
        engs = [self.pe, self.act, self.dve, self.pool, self.sp]
        evs = [(e.sem, e.cnt + (1 if e.last is not None else 0), e.name) for e in engs
               if e.cnt > 0 or e.last is not None]
        for e in engs:
            for ev in evs:
                if ev[2] != e.name:
                    e.wait(ev)
            for ev in self.dma_last.values():
                e.wait(ev)

    def mm(self, out, lhsT, rhs, start=True, stop=True, **kw):
        return self.op(self.pe, lambda: self.nc.tensor.matmul(out.ap, lhsT=lhsT.ap, rhs=rhs.ap, start=start, stop=stop, **kw),
                       [lhsT, rhs], [out])

    def tr(self, out, in_, ident):
        return self.op(self.pe, lambda: self.nc.tensor.transpose(out.ap, in_.ap, ident.ap), [in_, ident], [out])

    def actf(self, out, in_, func, bias=None, scale=None, accum=None):
        kw = {}
        if bias is not None:
            kw["bias"] = bias.ap if isinstance(bias, V) else bias
        if scale is not None:
            kw["scale"] = scale.ap if isinstance(scale, V) else scale
        if accum is not None:
            kw["accum_out"] = accum.ap
        return self.op(self.act, lambda: self.nc.scalar.activation(out=out.ap, in_=in_.ap, func=func, **kw),
                       [in_, bias, scale], [out, accum])

    def _e(self, eng):
        return {"dve": self.dve, "pool": self.pool, "act": self.act}[eng] if isinstance(eng, str) else eng

    def tt(self, eng, out, a, b, op):
        e = self._e(eng)
        return self.op(e, lambda: e.h.tensor_tensor(out=out.ap, in0=a.ap, in1=b.ap, op=op), [a, b], [out])

    def ts(self, eng, out, a, s1, op0, s2=None, op1=None, accum=None):
        e = self._e(eng)
        a1 = s1.ap if isinstance(s1, V) else s1
        a2 = s2.ap if isinstance(s2, V) else s2
        kw = {}
        if op1 is not None:
            kw["op1"] = op1
        if accum is not None:
            kw["accum_out"] = accum.ap
        return self.op(e, lambda: e.h.tensor_scalar(out=out.ap, in0=a.ap, scalar1=a1, scalar2=a2, op0=op0, **kw),
                       [a, s1, s2], [out, accum])

    def stt(self, out, a, s, b, op0, op1):
        e = self.dve
        sa = s.ap if isinstance(s, V) else s
        return self.op(e, lambda: e.h.scalar_tensor_tensor(out=out.ap, in0=a.ap, scalar=sa, in1=b.ap, op0=op0, op1=op1),
                       [a, s, b], [out])

    def cp(self, eng, out, in_):
        e = self._e(eng)
        if e is self.act:
            return self.op(e, lambda: self.nc.scalar.copy(out=out.ap, in_=in_.ap), [in_], [out])
        return self.op(e, lambda: e.h.tensor_copy(out=out.ap, in_=in_.ap), [in_], [out])

    def memset(self, eng, out, val):
        e = self._e(eng)
        return self.op(e, lambda: e.h.memset(out.ap, val), [], [out])


def _wplan():
    plan = []
    for h in range(MHL):
        plan += [("U%d" % h, "in", 0 + h * 512), ("Z%d" % h, "in", 1024 + h * 512), ("V%d" % h, "in", 2048 + h * 512),
                 ("QK%d" % h, "qk", h)]
    for h in range(MHL):
        plan += [("BRM%d" % h, "brm", h)]
    plan += [("QH0", "in", 3076), ("FH0", "in", 3588), ("IH0", "in", 4100), ("GH0", "in", 4612)]
    plan += [("BRH", "brh", 0)]
    plan += [("GA0", "in", 5124), ("GA1", "in", 5124 + 512), ("GB0", "in", 6148), ("GB1", "in", 6148 + 512)]
    plan += [("OUT0", "sq", ("w_out", 0)), ("OUT1", "sq", ("w_out", 1)), ("PG0", "sq", ("w_pg", 0)), ("PG1", "sq", ("w_pg", 1)),
             ("PLE", "ple", 0)]
    return plan


def build(TP=4096, NS=8, dbg=None):
    nc = bass.Bass("TRN2", target_bir_lowering=False)
    dbg = dbg or {}

    def din(name, shape):
        return nc.dram_tensor(name, list(shape), F32, kind="ExternalInput").ap()

    def dout(name, shape):
        return nc.dram_tensor(name, list(shape), F32, kind="ExternalOutput").ap()

    I = dict(
        xp=din("xp", [TP, D]), pp=din("pp", [TP, PLE]),
        xs=din("xs", [NS * 32, D]), ps=din("ps", [NS * 32, PLE]),
        sconv=din("sconv", [NS * 3, MIL]), sC=din("sC", [NS, MHL, HD, HD]), sn=din("sn", [NS, MHL * 4, 128]),
        sm=din("sm", [NS, MHL]), sS=din("sS", [NS, HGL, HE, HE]),
        norm_g=din("norm_g", [D]), w_in=din("w_in", [D, NINL]), b_ig=din("b_ig", [MHL]), b_fg=din("b_fg", [MHL]),
        conv_w=din("conv_w", [4, MIL]), conv_b=din("conv_b", [MIL]), w_qm=din("w_qm", [MHL, HD, HD]),
        w_km=din("w_km", [MHL, HD, HD]), mnorm_g=din("mnorm_g", [MIL]), m_skip=din("m_skip", [MIL]),
        w_brm=din("w_brm", [MIL, D]), hgrn_lb=din("hgrn_lb", [2, HWL]), hnorm_g=din("hnorm_g", [HWL]),
        w_brh=din("w_brh", [HWL, D]), w_out=din("w_out", [D, D]), w_ple=din("w_ple", [PLE, D]), w_pg=din("w_pg", [D, D]),
        final_g=din("final_g", [D]),
    )
    O = dict(
        yp=dout("yp", [TP, D]), ys=dout("ys", [NS * 32, D]),
        conv_p=dout("conv_p", [3, MIL]), C_p=dout("C_p", [MHL, HD, HD]), n_p=dout("n_p", [MHL * 4, 128]),
        m_p=dout("m_p", [MHL, 1]), S_p=dout("S_p", [HGL, HE, HE]),
        conv_s=dout("conv_s", [NS * 3, MIL]), C_s=dout("C_s", [NS, MHL, HD, HD]), n_s=dout("n_s", [NS, MHL * 4, 128]),
        m_s=dout("m_s", [NS, MHL, 1]), S_s=dout("S_s", [NS, HGL, HE, HE]),
    )
    with ExitStack() as es:
        k = K(nc, es)
        _program(nc, k, I, O, TP, NS, dbg)
    return nc


def _program(nc, k, I, O, TP, NS, dbg):
    sp, pe, act, dve, pool = k.sp, k.pe, k.act, k.dve, k.pool
    plan = _wplan()
    NW = len(plan)
    do_prompt = dbg.get("prompt", True)
    do_sample = dbg.get("sample", True)

    NRING = 3
    ring = [k.sb("wring%d" % i, [128, 4096], BF16) for i in range(NRING)]
    wstate = dict(next_load=0, total=0)
    seq = []

    def w_issue(upto):
        while wstate["next_load"] < min(upto + 1, len(seq)):
            j = wstate["next_load"]
            pi = seq[j]
            sp.wait(grp_ev[pi // GRP])
            k.dma(sp, ring[j % NRING].v, wscr.h[pi])
            wstate["next_load"] += 1

    def w_get(key, hold=0):
        j = wstate["total"]
        assert plan[seq[j]][0] == key, (plan[seq[j]][0], key)
        w_issue(j + NRING - 1 - hold)
        wstate["total"] += 1
        return ring[j % NRING]

    def w3(w, kk, lo=0, hi=4096):
        return V(w, w.h[:, lo:hi].rearrange("p (k c) -> p k c", k=kk))

    identF = k.sb("identF", [128, 128])
    identB = k.sb("identB", [128, 128], BF16)
    utri = k.sb("utri", [128, 128])
    mnegT = k.sb("mnegT", [128, 128])
    maskbd = k.sb("maskbd", [128, 128])
    sel = k.sb("sel", [2, 2, 128])
    ones4 = k.sb("ones4", [2, 128])
    onesb = k.sb("onesb", [128, 1], BF16)
    m01p = k.sb("m01p", [128, 512])
    m01s = k.sb("m01s", [128, 256])

    def asel(t, pattern, cmp, fill, cm):
        k.op(pool, lambda: nc.gpsimd.affine_select(out=t.h[:], in_=t.h[:], pattern=pattern, compare_op=cmp, fill=fill,
                                                   base=0, channel_multiplier=cm), [t.v], [t.v])
    k.memset(pool, identF.v, 1.0)
    asel(identF, [[-1, 128]], ALU.is_equal, 0.0, 1)
    k.cp(pool, identB.v, identF.v)
    k.memset(pool, utri.v, 1.0)
    asel(utri, [[1, 128]], ALU.is_ge, 0.0, -1)
    k.memset(pool, mnegT.v, 0.0)
    asel(mnegT, [[1, 128]], ALU.is_ge, NEG, -1)
    k.cp(pool, maskbd.v, utri.v)
    k.memset(pool, maskbd[0:64, 64:128], 0.0)
    k.memset(pool, sel.v, 1.0)
    asel(sel, [[-1, 2], [0, 128]], ALU.is_equal, 0.0, 1)
    k.memset(pool, ones4.v, 1.0)
    k.memset(pool, onesb.v, 1.0)
    k.memset(pool, m01p.v, 1.0)
    k.memset(pool, m01p.v.rr("p (c l) -> p c l", l=64)[:, :, 0:1], 0.0)
    k.memset(pool, m01s.v, 1.0)
    k.memset(pool, m01s.v.rr("p (c l) -> p c l", l=32)[:, :, 0:1], 0.0)

    cst = T(None, "cst")

    def cload(name, shape, src, **kw):
        t = k.sb(name, shape)
        k.dma(act, t.v, src, semt=cst, **kw)
        return t
    slow = dict(allow_slow_non_contiguous=True)
    rows_sb = k.sb("rows_sb", [76, 128])
    cols = k.sb("cols", [128, 76])
    k.dma(sp, rows_sb[0:32, :], I["conv_w"].rearrange("j (c p) -> (j c) p", p=128), semt=cst)
    k.dma(sp, rows_sb[32:40, :], I["conv_b"].rearrange("(c p) -> c p", p=128), semt=cst)
    k.dma(sp, rows_sb[40:48, :], I["mnorm_g"].rearrange("(c p) -> c p", p=128), semt=cst)
    k.dma(sp, rows_sb[48:56, :], I["m_skip"].rearrange("(c p) -> c p", p=128), semt=cst)
    k.dma(sp, rows_sb[56:64, :], I["norm_g"].rearrange("(c p) -> c p", p=128), semt=cst)
    k.dma(sp, rows_sb[64:68, :], I["hnorm_g"].rearrange("(c p) -> c p", p=128), semt=cst)
    k.dma(sp, rows_sb[68:76, :], I["hgrn_lb"].rearrange("r (c p) -> (r c) p", p=128), semt=cst)
    cw_col = TA(cols, cols.h[:, 0:32].rearrange("p (j c) -> p c j", j=4), "cw_col")
    cb_col = TA(cols, cols.h[:, 32:40], "cb_col")
    mg_col = TA(cols, cols.h[:, 40:48], "mg_col")
    sk_col = TA(cols, cols.h[:, 48:56], "sk_col")
    ng_col = TA(cols, cols.h[:, 56:64], "ng_col")
    hg_col = TA(cols, cols.h[:, 64:68], "hg_col")
    lb_raw = TA(cols, cols.h[:, 68:76].rearrange("p (r c) -> p c r", r=2), "lb_raw")
    big_bc = cload("big_bc", [128, 2], I["b_ig"].partition_broadcast(128))
    bfg_bc = cload("bfg_bc", [128, 2], I["b_fg"].partition_broadcast(128))
    fg_bc = cload("fg_bc", [128, D], I["final_g"].partition_broadcast(128))
    for t_ in (rows_sb, big_bc, bfg_bc, fg_bc):
        t_.w = k.dma_last["d_cst"]
    lb_col = k.sb("lb_col", [128, 4])
    oml_col = k.sb("oml_col", [128, 4])
    noml_col = k.sb("noml_col", [128, 4])

    wscr = k.dram("wscr", [NW, 128, 4096], BF16)
    GRP = 1
    wgrp = [T(None, "wg%d" % g) for g in range((NW + GRP - 1) // GRP)]
    wg_sb = k.sb("wg_sb", [128, 8, 4], BF16)
    grp_ev = {}
    for i, (key, kind, arg) in enumerate(plan):
        dst = wscr.h[i]
        g = wgrp[i // GRP]
        if kind == "in":
            src = I["w_in"][:, arg:arg + 512].rearrange("(k p) c -> p k c", p=128)
            ev = k.dma(pool, dst.rearrange("p (k c) -> p k c", k=8), src, semt=g)
        elif kind == "qk":
            for j, wn in enumerate(("w_qm", "w_km")):
                src = I[wn][arg].rearrange("(k p) c -> p k c", p=128)
                ev = k.dma(pool, dst[:, j * 2048:(j + 1) * 2048].rearrange("p (k c) -> p k c", k=4), src, semt=g)
        elif kind == "brm":
            src = I["w_brm"][arg * 512:(arg + 1) * 512, :].rearrange("(k p) c -> p k c", p=128)
            ev = k.dma(pool, dst.rearrange("p (k c) -> p k c", k=4), src, semt=g)
        elif kind == "sq":
            src = I[arg[0]][:, arg[1] * 512:(arg[1] + 1) * 512].rearrange("(k p) c -> p k c", p=128)
            ev = k.dma(pool, dst.rearrange("p (k c) -> p k c", k=8), src, semt=g)
        elif kind == "brh":
            src = I["w_brh"].rearrange("(k p) c -> p k c", p=128)
            ev = k.dma(pool, dst.rearrange("p (k c) -> p k c", k=4), src, semt=g)
        elif kind == "ple":
            src = I["w_ple"].rearrange("(k p) c -> p k c", p=128)
            ev = k.dma(pool, dst[:, 0:2048].rearrange("p (k c) -> p k c", k=2), src, semt=g)
        grp_ev[i // GRP] = ev
        if i == 1:
            k.dma(pool, wg_sb.v, I["w_in"][:, 3072:3076].rearrange("(k p) c -> p k c", p=128), allow_slow_non_contiguous=True)
    k.out_events = []

    pa = [k.ps("pa%d" % i, [128, 512]) for i in range(2)]
    psm = k.ps("psm", [128, 512])
    pnum = k.ps("pnum", [128, 512])
    pint = k.ps("pint", [128, 512])
    pdc = k.ps("pdc", [128, 512])
    ptr = k.ps("ptr", [128, 512])
    ptrb = k.ps("ptrb", [128, 1024], BF16)
    stt_ = dict(pa=0, dc=0)
    k.tr(ptr[:, 0:76], rows_sb.v, identF[0:76, 0:76])
    k.cp(dve, cols.v, ptr[:, 0:76])
    k.tt(dve, lb_col.v, lb_raw[:, :, 0], lb_raw[:, :, 1], ALU.subtract)
    k.actf(lb_col.v, lb_col.v, AF.Sigmoid)
    k.ts(dve, oml_col.v, lb_col.v, -1.0, ALU.mult, 1.0, ALU.add)
    k.ts(dve, noml_col.v, oml_col.v, -1.0, ALU.mult)

    pa_list = [pa[0], pa[1]]

    def next_pa():
        stt_["pa"] = (stt_["pa"] + 1) % len(pa_list)
        return pa_list[stt_["pa"]]

    def set_pa(lst):
        pa_list[:] = lst
        stt_["pa"] = 0

    def next_dc():
        stt_["dc"] = (stt_["dc"] + 1) % 3
        return [pdc, pa[0], pa[1]][stt_["dc"]]

    C_nat = k.sb("C_nat", [128, MHL, 4, 512])
    CT_pp = [k.sb("CT_bf%d" % i, [128, MHL, 4, 512], BF16) for i in range(2)]
    n_col = k.sb("n_col", [128, 8, 8])
    S_st = k.sb("S_st", [128, HGL, HE])
    S_bf = [k.sb("S_bf%d" % i, [128, HGL, HE], BF16) for i in range(2)]
    hist = k.sb("hist", [128, 8, 8, 3])
    mcar = k.sb("mcar", [2, 8])
    cvo = k.sb("cvo", [24, MIL])
    nout = k.sb("nout", [8, 8, 128])

    xsrc = {(n_, w_): k.dram("xsrc_%d%s" % (n_, w_), [n_, D], F32) for n_ in (512, 256) for w_ in "ab"}
    xdst = {(n_, w_): k.dram("xdst_%d%s" % (n_, w_), [n_, D], F32) for n_ in (512, 256) for w_ in "ab"}

    def exchange(NT, which, src_t, NSB):
        xs_, xd_ = xsrc[(NT, which)].h, xdst[(NT, which)].h
        for b_ in range(NSB):
            pool.wait(k.dma(pool, xs_[b_ * 128:(b_ + 1) * 128, :], src_t[:, b_, :]))
        cci = nc.gpsimd.collective_compute("AllReduce", ALU.add, ins=[xs_[:, :]], outs=[xd_[:, :]], replica_groups=RG)
        ccst["n"] += 1
        cci.then_inc(ccsem)
        return (ccsem, ccst["n"], "cc"), xd_
    ccsem = k.new_sem("ccsem")
    ccst = dict(n=0)
    RG = [[0, 1], [2, 3], [4, 5], [6, 7]]

    def tile(x_dram, p_dram, y_dram, NT, TS, NU, NSEG, first, last, sample, tix=0):
        NSB = NT // 128
        SEG = NT // NSEG
        L = 32 if sample else 64
        NCH = TS // L
        m01 = m01s if sample else m01p
        hmask = utri if sample else maskbd

        def utok(u):
            return slice(u * TS, (u + 1) * TS)

        def blk(b):
            return slice(b * 128, (b + 1) * 128)

        with ExitStack() as tes:
            xnT = k.sb("xnT", [128, 8, NT], BF16, es=tes)
            y_acc = k.sb("y_acc", [128, NSB, D], es=tes)
            CT_cur, CT_nxt = CT_pp[tix % 2], CT_pp[(tix + 1) % 2]
            u_tm = k.sb("u_tm", [128, NU, 2], es=tes)
            negM = k.sb("negM", [2, NU, TS], es=tes)
            wi_tm = k.sb("wi_tm", [128, NU, 2], es=tes)
            cl_tm = k.sb("cl_tm", [128, NU, 2], es=tes)
            dec_bc = k.sb("dec_bc", [128, NU, 2], es=tes)
            wiT_all = k.sb("wiT_all", [2, NU, TS], es=tes)
            wl_all = k.sb("wl_all", [128, NU, 2], es=tes)
            wlb_all = k.sb("wlb_all", [128, NU, 2], BF16, es=tes)

            with ExitStack() as pes:
                x_tm = k.sb("x_tm", [128, NSB, D], es=pes)
                xs_b = k.sb("xs_b", [128, D], BF16, es=pes)
                junk = k.sb("junk", [128, D], es=pes)
                ssq = k.sb("ssq", [128, 4], es=pes)
                for b in range(NSB):
                    k.dma(sp, x_tm[:, b, :], x_dram[b * 128:(b + 1) * 128, :])
                if sample:
                    k.dma(sp, mcar[0:2, 0:NU], I["sm"].rearrange("s h -> h s"), allow_slow_non_contiguous=True)
                    sc_tm = k.sb("sc_tm", [24, MIL], es=pes)
                    NS3 = NSEG * 3
                    k.dma(sp, sc_tm[0:NSEG * 3, :], I["sconv"])
                    for grp in range(2):
                        for cc in range(4):
                            c16 = grp * 4 + cc
                            k.tr(ptr[:, cc * NS3:(cc + 1) * NS3], sc_tm[0:NS3, c16 * 128:(c16 + 1) * 128], identF[0:NS3, 0:NS3])
                        k.cp(act, hist[:, grp * 4:(grp + 1) * 4, 0:NSEG, :],
                             ptr[:, 0:4 * NS3].rr("p (c s j) -> p c s j", c=4, j=3))
                    nrow = k.sb("nrow", [8, 128], es=pes)
                    for u in range(NU):
                        k.dma(sp, nrow.v, I["sn"][u])
                        k.tr(psm[:, 256:264], nrow.v, identF[0:8, 0:8])
                        k.cp(act, n_col[:, u, :], psm[:, 256:264])
                elif first:
                    k.memset(pool, mcar.v, 0.0)
                    k.memset(pool, hist.v, 0.0)
                    k.memset(pool, n_col.v, 0.0)
                    k.memset(pool, S_st.v, 0.0)
                    k.memset(pool, S_bf[0].v, 0.0)
                for b in range(NSB):
                    k.actf(junk.v, x_tm[:, b, :], AF.Square, accum=ssq[:, 0:1])
                    k.actf(ssq[:, 1:2], ssq[:, 0:1], AF.Ln, scale=1.0 / D, bias=EPS)
                    k.actf(ssq[:, 2:3], ssq[:, 1:2], AF.Exp, scale=-0.5)
                    k.ts(dve, xs_b.v, x_tm[:, b, :], ssq[:, 2:3], ALU.mult)
                    for c in range(8):
                        k.tr(ptrb[:, c * 128:(c + 1) * 128], xs_b[:, c * 128:(c + 1) * 128], identB.v)
                    k.tt(dve, xnT[:, :, blk(b)], ptrb.v.rr("p (c t) -> p c t", c=8),
                         V(cols, ng_col.h.unsqueeze(2).to_broadcast([128, 8, 128])), ALU.mult)
            k.barrier()

            with ExitStack() as pes:
                u_exts = [k.sb("u_ext%d" % i, [128, NSEG, SEG + 3], es=pes) for i in range(2)]
                cv = k.sb("cv", [128, NSEG, SEG], es=pes)
                gs = [k.sb("gs%d" % i, [128, 2], es=pes) for i in range(8)]
                g_ig, g_fg, g_e1, g_sp, g_csp, g_d4, g_d4b, g_wl = gs
                gr = [k.sb("gr%d" % i, [2, TS], es=pes) for i in range(5)]
                g_uT, g_M, g_mT, g_wiT, g_clT = gr
                BS = [dict(cT=k.sb("cT%d" % i, [128, 4, NT], BF16, es=pes), szT=k.sb("szT%d" % i, [128, 4, NT], BF16, es=pes),
                           qT=k.sb("qT%d" % i, [128, 4, NT], BF16, es=pes), kT=k.sb("kT%d" % i, [128, 4, NT], BF16, es=pes),
                           k_tm=k.sb("k_tm%d" % i, [128, NU, 512], BF16, es=pes), v_tm=k.sb("v_tm%d" % i, [128, NU, 512], BF16, es=pes),
                           hmT=k.sb("hmT%d" % i, [128, 4, NT], BF16, es=pes)) for i in range(2)]
                pdc2 = V(ptrb, ptrb.h[:].bitcast(F32))
                dcst = dict(i=0)

                def next_dc2():
                    dcst["i"] ^= 1
                    return pdc.v if dcst["i"] else pdc2
                NCT = 2 if sample else NU - 1
                CTtmp = [k.sb("CTtmp%d" % i, [128, 4, 512], BF16, es=pes) for i in range(NCT)]
                n_bfs = k.sb("n_bfs", [128, NU + 1, 4], BF16, es=pes)
                dT_sb = [k.sb("dT_sb%d" % i, [128, 128], es=pes) for i in range(2)]
                sT_sb = [k.sb("sT_sb%d" % i, [128, 128], BF16, es=pes) for i in range(2)]
                qwT = [k.sb("qwT%d" % i, [128, 4, 128], BF16, es=pes) for i in range(2)]
                hraw = [k.sb("hraw%d" % i, [128, 512], es=pes) for i in range(2)]
                hn_sb = [k.sb("hn_sb%d" % i, [128, 512], es=pes) for i in range(2)]
                tmp_sb = [k.sb("tmp_sb%d" % i, [128, 4, 128], es=pes) for i in range(2)]
                wv_all = k.sb("wv_all", [128, NU, 512], BF16, es=pes)
                dd = [[k.sb("dd%d_%d" % (i, j), [128, 8], es=pes) for i in range(6)] for j in range(2)]
                SC = float(HD) ** -0.5
                cnt = dict(u=0)

                tb = dict(i=0)

                def make_CT3(h, dst):
                    for dc in range(4):
                        tb["i"] = (tb["i"] + 1) % 3
                        bank = [ptr, pnum, pint][tb["i"]]
                        for ec in range(4):
                            k.tr(bank[:, ec * 128:(ec + 1) * 128], C_nat[:, h, ec, dc * 128:(dc + 1) * 128], identF.v)
                        k.cp(act, dst[:, dc, :], bank.v)
                        yield

                def gate_gen():
                    for u in range(NU):
                        slot = u if sample else 0
                        G = psm[0:TS, 384:388]
                        for kk in range(8):
                            k.mm(G, xnT[:, kk, utok(u)], wg_sb[:, kk, :], start=(kk == 0), stop=(kk == 7))
                        k.tt(dve, g_ig[0:TS, :], G[:, 0:2], big_bc[0:TS, :], ALU.add)
                        k.tt(dve, g_fg[0:TS, :], G[:, 2:4], bfg_bc[0:TS, :], ALU.add)
                        k.actf(g_e1[0:TS, :], g_fg[0:TS, :], AF.Exp, scale=-1.0)
                        k.actf(g_sp[0:TS, :], g_e1[0:TS, :], AF.Ln, bias=1.0)
                        yield
                        k.mm(psm[0:TS, 392:394], utri[0:TS, 0:TS], g_sp[0:TS, :])
                        k.cp(act, g_csp[0:TS, :], psm[0:TS, 392:394])
                        yield
                        k.tt(dve, u_tm[0:TS, u, :], g_ig[0:TS, :], g_csp[0:TS, :], ALU.add)
                        k.tr(psm[0:2, 256:256 + TS], u_tm[0:TS, u, :], identF[0:TS, 0:TS])
                        k.cp(dve, g_uT.v, psm[0:2, 256:256 + TS])
                        yield
                        k.tr(psm[0:2, 128:128 + TS], g_csp[0:TS, :], identF[0:TS, 0:TS])
                        k.op(dve, lambda: nc.vector.tensor_tensor_scan(out=g_M.h[:], data0=g_uT.h[:], data1=g_uT.h[:],
                                                                        initial=mcar.h[0:2, slot:slot + 1], op0=ALU.max, op1=ALU.max),
                             [g_uT.v, mcar.v], [g_M.v])
                        k.ts(dve, negM[0:2, u, :], g_M.v, -1.0, ALU.mult)
                        yield
                        k.tt(dve, g_mT.v, g_M.v, psm[0:2, 128:128 + TS], ALU.subtract)
                        k.actf(g_wiT.v, g_M.v, AF.Exp, scale=-1.0, bias=mcar[0:2, slot:slot + 1])
                        k.actf(g_clT.v, g_mT.v, AF.Exp, scale=-1.0)
                        yield
                        k.cp(dve, mcar[0:2, slot:slot + 1], g_mT[0:2, TS - 1:TS])
                        k.tr(psm[0:TS, 396:398], g_wiT.v, identF[0:2, 0:2])
                        k.cp(act, wi_tm[0:TS, u, :], psm[0:TS, 396:398])
                        k.tr(psm[0:TS, 400:402], g_clT.v, identF[0:2, 0:2])
                        k.cp(act, cl_tm[0:TS, u, :], psm[0:TS, 400:402])
                        yield
                        k.ts(dve, g_d4[0:2, 0:2], identF[0:2, 0:2], g_wiT[0:2, TS - 1:TS], ALU.mult)
                        k.mm(psm[:, 404:406], ones4.v, g_d4[0:2, 0:2])
                        k.cp(act, dec_bc[:, u, :], psm[:, 404:406])
                        yield
                        k.cp(dve, wiT_all[0:2, u, :], g_wiT.v)
                        k.ts(dve, g_d4b[0:2, 0:2], identF[0:2, 0:2], negM[0:2, u, TS - 1:TS], ALU.mult)
                        k.mm(psm[:, 408:410], ones4.v, g_d4b[0:2, 0:2])
                        k.tt(dve, g_wl[0:TS, :], u_tm[0:TS, u, :], psm[0:TS, 408:410], ALU.add)
                        k.actf(wl_all[0:TS, u, :], g_wl[0:TS, :], AF.Exp)
                        k.cp(dve, wlb_all[0:TS, u, :], wl_all[0:TS, u, :])
                        yield
                        if sample or (last and u == NU - 1):
                            k.dma(pool, O["m_s"][u] if sample else O["m_p"], mcar[0:2, slot:slot + 1])
                def proj_gen(h, B_):
                    cT, szT, qT, kT, k_tm, v_tm, hmT = (B_[n_] for n_ in ('cT', 'szT', 'qT', 'kT', 'k_tm', 'v_tm', 'hmT'))
                    W = w_get("U%d" % h)
                    W3 = w3(W, 8)
                    for cc in range(4):
                        c16 = 4 * h + cc
                        acc = next_pa()
                        for kk in range(8):
                            k.mm(acc[:, 0:NT], W3[:, kk, cc * 128:(cc + 1) * 128], xnT[:, kk, :], start=(kk == 0), stop=(kk == 7))
                        u_ext = u_exts[cc % 2]
                        k.cp(pool, u_ext[:, :, 0:3], hist[:, c16, 0:NSEG, :])
                        k.cp(act, u_ext[:, :, 3:3 + SEG], acc[:, 0:NT].rr("p (s t) -> p s t", s=NSEG))
                        k.cp(pool, hist[:, c16, 0:NSEG, :], u_ext[:, :, SEG:SEG + 3])
                        k.actf(cv.v, u_ext[:, :, 0:SEG], AF.Identity, scale=cw_col[:, c16, 0:1], bias=cb_col[:, c16:c16 + 1])
                        for j in range(1, 4):
                            k.stt(cv.v, u_ext[:, :, j:j + SEG], cw_col[:, c16, j:j + 1], cv.v, ALU.mult, ALU.add)
                        k.actf(cT[:, cc, :].rr("p (s t) -> p s t", s=NSEG), cv.v, AF.Silu)
                        yield
                    W = w_get("Z%d" % h)
                    W3 = w3(W, 8)
                    for cc in range(4):
                        acc = next_pa()
                        for kk in range(8):
                            k.mm(acc[:, 0:NT], W3[:, kk, cc * 128:(cc + 1) * 128], xnT[:, kk, :], start=(kk == 0), stop=(kk == 7))
                        k.actf(szT[:, cc, :], acc[:, 0:NT], AF.Silu)
                        yield
                    W = w_get("V%d" % h)
                    W3 = w3(W, 8)
                    for u in range(NU):
                        acc = next_pa()
                        for kk in range(8):
                            k.mm(acc[0:TS, :], xnT[:, kk, utok(u)], W3[:, kk, :], start=(kk == 0), stop=(kk == 7))
                        k.cp(act if u % 2 else dve, v_tm[0:TS, u, :], acc[0:TS, :])
                        yield
                    W = w_get("QK%d" % h)
                    Wq = w3(W, 4, 0, 2048)
                    Wk = w3(W, 4, 2048, 4096)
                    for ec in range(4):
                        acc = next_pa()
                        for dc in range(4):
                            k.mm(acc[:, 0:NT], Wq[:, dc, ec * 128:(ec + 1) * 128], cT[:, dc, :], start=(dc == 0), stop=(dc == 3))
                        k.cp(act, qT[:, ec, :], acc[:, 0:NT])
                        acc = next_pa()
                        for dc in range(4):
                            k.mm(acc[:, 0:NT], Wk[:, dc, ec * 128:(ec + 1) * 128], cT[:, dc, :], start=(dc == 0), stop=(dc == 3))
                        k.ts(dve, kT[:, ec, :], acc[:, 0:NT], SC, ALU.mult)
                        yield
                    for u in range(NU):
                        acc = next_pa()
                        for dc in range(4):
                            k.mm(acc[0:TS, :], cT[:, dc, utok(u)], Wk[:, dc, :], start=(dc == 0), stop=(dc == 3))
                        k.ts(dve, k_tm[0:TS, u, :], acc[0:TS, :], SC, ALU.mult)
                        yield
                    for ec in range(4):
                        c16 = 4 * h + ec
                        k.stt(cT[:, ec, :], cT[:, ec, :], sk_col[:, c16:c16 + 1], szT[:, ec, :], ALU.mult, ALU.mult)
                        k.actf(szT[:, ec, :], szT[:, ec, :], AF.Copy, scale=mg_col[:, c16:c16 + 1])
                        yield


                def passes_gen(h, B_):
                    cT, szT, qT, kT, k_tm, v_tm, hmT = (B_[n_] for n_ in ('cT', 'szT', 'qT', 'kT', 'k_tm', 'v_tm', 'hmT'))
                    for u in range(NU):
                        k.actf(wv_all[0:TS, u, :], v_tm[0:TS, u, :], AF.Copy, scale=wl_all[0:TS, u, h:h + 1])

                    def state_pass(u):
                        slot = u if sample else 0
                        if sample:
                            k.dma(sp, C_nat[:, h], I["sC"][u, h].rearrange("(ec p) d -> p ec d", p=128))
                            yield from make_CT3(h, CTtmp[u % 2])
                        elif first and u == 0:
                            k.memset(pool, C_nat[:, h], 0.0)
                            k.memset(pool, CT_cur[:, h], 0.0)
                        k.cp(dve, n_bfs[:, u, :], n_col[:, slot, 4 * h:4 * h + 4])
                        for ec in range(4):
                            dcp = next_dc2()
                            k.mm(dcp, wv_all[0:TS, u, ec * 128:(ec + 1) * 128], k_tm[0:TS, u, :])
                            k.stt(C_nat[:, h, ec, :], C_nat[:, h, ec, :], dec_bc[:, u, h:h + 1], dcp, ALU.mult, ALU.add)
                            yield
                        for dc in range(4):
                            k.mm(psm[:, 420 + dc:421 + dc], k_tm[0:TS, u, dc * 128:(dc + 1) * 128], wlb_all[0:TS, u, h:h + 1])
                        k.stt(n_col[:, slot, 4 * h:4 * h + 4], n_col[:, slot, 4 * h:4 * h + 4], dec_bc[:, u, h:h + 1],
                              psm[:, 420:424], ALU.mult, ALU.add)
                        if sample:
                            k.dma(pool, O["C_s"][u, h].rearrange("(ec p) d -> p ec d", p=128), C_nat[:, h])
                        elif last and u == NU - 1:
                            k.dma(pool, O["C_p"][h].rearrange("(ec p) d -> p ec d", p=128), C_nat[:, h])
                        else:
                            yield from make_CT3(h, CTtmp[u] if u < NU - 1 else CT_nxt[:, h])

                    def h_group(us):
                        R_ = [(u_ % 2, u_) for u_ in us]
                        CTs = {u: (CTtmp[u % 2] if sample else (CT_cur[:, h] if u == 0 else CTtmp[u - 1])) for u in us}
                        hb = [ptr, ptr]
                        yield
                        for i2, u in R_:
                            nm = psm[0:TS, 0:TS]
                            k.mm(nm, sel[0:2, h, 0:TS], negM[0:2, u, :], start=True, stop=False)
                            k.mm(nm, identF[0:TS, 0:TS], mnegT[0:TS, 0:TS], start=False, stop=True)
                            kq = psm[0:TS, 128:128 + TS]
                            for ec in range(4):
                                k.mm(kq, kT[:, ec, utok(u)], qT[:, ec, utok(u)], start=(ec == 0), stop=(ec == 3))
                            k.mm(psm[:, 256:256 + TS], sel[0:2, h, :], wiT_all[0:2, u, :])
                        yield
                        for i2, u in R_:
                            k.actf(dT_sb[i2][0:TS, 0:TS], psm[0:TS, 0:TS], AF.Exp, bias=u_tm[0:TS, u, h:h + 1])
                        yield
                        for i2, u in R_:
                            k.tt(dve, sT_sb[i2][0:TS, 0:TS], psm[0:TS, 128:128 + TS], dT_sb[i2][0:TS, 0:TS], ALU.mult)
                            k.tt(dve, qwT[i2][:, :, 0:TS], qT[:, :, utok(u)],
                                 V(psm, psm.h[:, 256:256 + TS].unsqueeze(1).to_broadcast([128, 4, TS])), ALU.mult)
                        yield
                        for i2, u in R_:
                            pn = pnum if i2 == 0 else pint
                            k.mm(pn[0:TS, :], sT_sb[i2][0:TS, 0:TS], v_tm[0:TS, u, :], start=True, stop=False)
                            for dc in range(4):
                                k.mm(pn[0:TS, :], qwT[i2][:, dc, 0:TS], CTs[u][:, dc, :], start=False, stop=(dc == 3))
                            dn_ = psm[0:TS, 416 + i2:417 + i2]
                            k.mm(dn_, sT_sb[i2][0:TS, 0:TS], onesb[0:TS, :], start=True, stop=False)
                            for dc in range(4):
                                k.mm(dn_, qwT[i2][:, dc, 0:TS], n_bfs[:, u, dc:dc + 1], start=False, stop=(dc == 3))
                        yield
                        for i2, u in R_:
                            d_rs, d_den, d_aden, d_rden, d_st, d_mv = dd[i2]
                            dn_ = psm[0:TS, 416 + i2:417 + i2]
                            k.ts(dve, d_den[0:TS, 0:1], dn_, -1.0, ALU.mult)
                            k.tt(dve, d_aden[0:TS, 0:1], d_den[0:TS, 0:1], dn_, ALU.max)
                            k.tt(dve, d_aden[0:TS, 0:1], d_aden[0:TS, 0:1], cl_tm[0:TS, u, h:h + 1], ALU.max)
                            k.op(dve, lambda d_rden=d_rden, d_aden=d_aden: nc.vector.reciprocal(out=d_rden.h[0:TS, 0:1], in_=d_aden.h[0:TS, 0:1]),
                                 [d_aden.v], [d_rden.v])
                        yield
                        for i2, u in R_:
                            pn = pnum if i2 == 0 else pint
                            k.actf(hraw[i2][0:TS, :], pn[0:TS, :], AF.Copy, scale=dd[i2][3][0:TS, 0:1])
                        yield
                        for i2, u in R_:
                            d_rs, d_den, d_aden, d_rden, d_st, d_mv = dd[i2]
                            k.op(dve, lambda d_st=d_st, i2=i2: nc.vector.bn_stats(out=d_st.h[0:TS, 0:6], in_=hraw[i2].h[0:TS, :]), [hraw[i2].v], [d_st.v])
                            k.op(dve, lambda d_st=d_st, d_mv=d_mv: nc.vector.bn_aggr(out=d_mv.h[0:TS, 0:2], in_=d_st.h[0:TS, 0:6]), [d_st.v], [d_mv.v])
                        yield
                        for i2, u in R_:
                            d_rs, d_den, d_aden, d_rden, d_st, d_mv = dd[i2]
                            k.actf(d_rs[0:TS, 0:1], d_mv[0:TS, 1:2], AF.Ln, bias=EPS)
                            k.actf(d_rs[0:TS, 1:2], d_rs[0:TS, 0:1], AF.Exp, scale=-0.5)
                        yield
                        for i2, u in R_:
                            d_rs, d_den, d_aden, d_rden, d_st, d_mv = dd[i2]
                            k.ts(dve, hn_sb[i2][0:TS, :], hraw[i2][0:TS, :], d_mv[0:TS, 0:1], ALU.subtract, d_rs[0:TS, 1:2], ALU.mult)
                        yield
                        for i2, u in R_:
                            for ec in range(4):
                                k.tr(hb[i2][:, ec * TS:(ec + 1) * TS], hn_sb[i2][0:TS, ec * 128:(ec + 1) * 128], identF[0:TS, 0:TS])
                        yield
                        for i2, u in R_:
                            k.tt(dve, tmp_sb[i2][:, :, 0:TS], hb[i2][:, 0:4 * TS].rr("p (e t) -> p e t", e=4), szT[:, :, utok(u)], ALU.mult)
                            k.tt(pool, hmT[:, :, utok(u)], tmp_sb[i2][:, :, 0:TS], cT[:, :, utok(u)], ALU.add)


                    if sample:
                        for u in range(NU):
                            yield from state_pass(u)
                            yield from h_group([u])
                    else:
                        for u in range(NU):
                            yield from state_pass(u)
                        for u in range(NU):
                            yield from h_group([u])
                def brm_gen(h, B_):
                    cT, szT, qT, kT, k_tm, v_tm, hmT = (B_[n_] for n_ in ('cT', 'szT', 'qT', 'kT', 'k_tm', 'v_tm', 'hmT'))
                    W = w_get("BRM%d" % h)
                    W3 = w3(W, 4)
                    for b in range(NSB):
                        for q in range(2):
                            acc = next_pa()
                            for ec in range(4):
                                k.mm(acc.v, hmT[:, ec, blk(b)], W3[:, ec, q * 512:(q + 1) * 512], start=(ec == 0), stop=(ec == 3))
                            dst = y_acc[:, b, q * 512:(q + 1) * 512]
                            if h == 0:
                                k.cp(act, dst, acc.v)
                                yield
                            else:
                                k.tt(dve, dst, dst, acc.v, ALU.add)
                                yield


                def run(ga, gb, ra=1, rb=1):
                    gens = [g_ for g_ in (ga, gb) if g_ is not None]
                    rate = {id(ga): ra, id(gb): rb}
                    while gens:
                        for g_ in list(gens):
                            for _ in range(rate[id(g_)]):
                                try:
                                    next(g_)
                                except StopIteration:
                                    gens.remove(g_)
                                    break
                run(gate_gen(), proj_gen(0, BS[0]), 1, 1)
                run(passes_gen(0, BS[0]), proj_gen(1, BS[1]), 3, 1)
                run(passes_gen(1, BS[1]), brm_gen(0, BS[0]), 3, 1)
                run(brm_gen(1, BS[1]), None)
                for slot in range(NU if sample else 1):
                    if sample or last:
                        k.tr(psm[0:8, 256:384], n_col[:, slot, :], identF.v)
                        k.cp(act, nout[0:8, slot, :], psm[0:8, 256:384])
                        k.dma(pool, O["n_s"][slot] if sample else O["n_p"], nout[0:8, slot, :])
                if sample or last:
                    for grp in range(2):
                        for cc in range(4):
                            c16 = grp * 4 + cc
                            k.tr(ptr[0:NSEG * 3, cc * 128:(cc + 1) * 128], hist[:, c16, 0:NSEG, :].rr("p s j -> p (s j)"), identF.v)
                        k.cp(act, cvo[0:NSEG * 3, grp * 512:(grp + 1) * 512], ptr[0:NSEG * 3, :])
                    k.dma(pool, O["conv_s"] if sample else O["conv_p"], cvo[0:NSEG * 3, :])
                evA, xdA = exchange(NT, "a", y_acc, NSB)
            k.barrier()

            yb_acc = k.sb("yb_acc", [128, NSB, D], es=tes)
            if True:
                with ExitStack() as pes:
                    ohT = k.sb("ohT", [128, 4, NT], BF16, es=pes)
                    qs_sb = k.sb("qs_sb", [128, 4, NT], es=pes)
                    sigf = k.sb("sigf", [128, NT], es=pes)
                    lfh = k.sb("lfh", [128, NT], es=pes)
                    kh_sb = k.sb("kh_sb", [128, NT], es=pes)
                    g_sb = k.sb("g_sb", [128, NT], es=pes)
                    eg_sb = k.sb("eg_sb", [128, NT], es=pes)
                    eng_sb = k.sb("eng_sb", [128, NT], es=pes)
                    qtT = k.sb("qtT", [128, 4, NT], BF16, es=pes)
                    ktT = k.sb("ktT", [128, 4, NT], BF16, es=pes)
                    sgT = k.sb("sgT", [128, 4, NT], BF16, es=pes)
                    egl = k.sb("egl", [128, 4, NT // L], es=pes)
                    vh_tm = k.sb("vh_tm", [128, NU, 512], BF16, es=pes)
                    kt_tm = k.sb("kt_tm", [128, 512], BF16, es=pes)
                    aT_sb = k.sb("aT_sb", [128, 4, 128], BF16, es=pes)
                    o_sb = k.sb("o_sb", [128, 512], es=pes)
                    sq_sb = k.sb("sq_sb", [128, 512], es=pes)
                    on_sb = k.sb("on_sb", [128, 4, 128], es=pes)
                    hs = k.sb("hs", [128, 12], es=pes)
                    for g in range(1):
                        hsl = slice(4 * g, 4 * g + 4)
                        set_pa([pa[0], pa[1], pdc, ptr])
                        W = w_get("QH%d" % g)
                        W3 = w3(W, 8)
                        for j in range(4):
                            acc = next_pa()
                            for kk in range(8):
                                k.mm(acc[:, 0:NT], W3[:, kk, j * 128:(j + 1) * 128], xnT[:, kk, :], start=(kk == 0), stop=(kk == 7))
                            k.actf(qs_sb[:, j, :], acc[:, 0:NT], AF.Silu)
                        W = w_get("FH%d" % g)
                        W3 = w3(W, 8)
                        for j in range(4):
                            hh = 4 * g + j
                            acc = next_pa()
                            for kk in range(8):
                                k.mm(acc[:, 0:NT], W3[:, kk, j * 128:(j + 1) * 128], xnT[:, kk, :], start=(kk == 0), stop=(kk == 7))
                            k.actf(sigf.v, acc[:, 0:NT], AF.Sigmoid)
                            k.actf(lfh.v, sigf.v, AF.Ln, scale=oml_col[:, hh:hh + 1], bias=lb_col[:, hh:hh + 1])
                            k.ts(dve, kh_sb.v, sigf.v, noml_col[:, hh:hh + 1], ALU.mult, oml_col[:, hh:hh + 1], ALU.add)
                            k.op(dve, lambda: nc.vector.tensor_tensor_scan(out=g_sb.h[:], data0=m01.h[:, 0:NT], data1=lfh.h[:],
                                                                            initial=0.0, op0=ALU.mult, op1=ALU.add),
                                 [m01.v, lfh.v], [g_sb.v])
                            k.actf(eg_sb.v, g_sb.v, AF.Exp)
                            k.actf(eng_sb.v, g_sb.v, AF.Exp, scale=-1.0)
                            k.tt(pool, qtT[:, j, :], qs_sb[:, j, :], eg_sb.v, ALU.mult)
                            k.tt(dve, ktT[:, j, :], kh_sb.v, eng_sb.v, ALU.mult)
                            k.cp(dve, egl[:, j, :], eg_sb.v.rr("p (c l) -> p c l", l=L)[:, :, L - 1])
                        W = w_get("IH%d" % g)
                        W3 = w3(W, 8)
                        for u in range(NU):
                            acc = next_pa()
                            for kk in range(8):
                                k.mm(acc[0:TS, :], xnT[:, kk, utok(u)], W3[:, kk, :], start=(kk == 0), stop=(kk == 7))
                            k.cp(act if u % 2 else dve, vh_tm[0:TS, u, :], acc[0:TS, :])
                        W = w_get("GH%d" % g)
                        W3 = w3(W, 8)
                        for j in range(4):
                            acc = next_pa()
                            for kk in range(8):
                                k.mm(acc[:, 0:NT], W3[:, kk, j * 128:(j + 1) * 128], xnT[:, kk, :], start=(kk == 0), stop=(kk == 7))
                            k.actf(sgT[:, j, :], acc[:, 0:NT], AF.Silu)
                        set_pa([pa[0], pa[1]])
                        for u in range(NU):
                            if sample:
                                k.dma(sp, S_st[:, hsl, :], I["sS"][u].rearrange("h c e -> c h e"))
                                k.cp(act, S_bf[0][:, hsl, :], S_st[:, hsl, :])
                            for j in range(4):
                                k.tr(ptrb[0:TS, j * 128:(j + 1) * 128], ktT[:, j, utok(u)], identB.v)
                            k.cp(act, kt_tm[0:TS, :], ptrb[0:TS, 0:512])
                            for j in range(4):
                                k.mm(pnum[0:TS, j * TS:(j + 1) * TS], ktT[:, j, utok(u)], qtT[:, j, utok(u)])
                            k.tt(dve, aT_sb[0:TS, :, 0:TS], pnum[0:TS, 0:4 * TS].rr("p (j t) -> p j t", j=4),
                                 V(hmask, hmask.h[0:TS, 0:TS].unsqueeze(1).to_broadcast([TS, 4, TS])), ALU.mult)
                            def s_update(c, dst_bf):
                                rows = slice(c * L, (c + 1) * L)
                                gci = u * NCH + c
                                for j in range(4):
                                    k.mm(pdc[:, j * 128:(j + 1) * 128], kt_tm[rows, j * 128:(j + 1) * 128],
                                         vh_tm[rows, u, j * 128:(j + 1) * 128])
                                k.tt(dve, S_st[:, hsl, :], S_st[:, hsl, :], pdc.v.rr("p (j e) -> p j e", j=4), ALU.add)
                                k.tt(dve, S_st[:, hsl, :], S_st[:, hsl, :],
                                     V(egl, egl.h[:, :, gci:gci + 1].to_broadcast([128, 4, 128])), ALU.mult)
                                if dst_bf is not None:
                                    k.cp(act, dst_bf[:, hsl, :], S_st[:, hsl, :])
                            for c in range(NCH - 1):
                                s_update(c, S_bf[c + 1])
                            for j in range(4):
                                hh = 4 * g + j
                                for c in range(NCH):
                                    rows = slice(c * L, (c + 1) * L)
                                    k.mm(pint[rows, j * 128:(j + 1) * 128], qtT[:, j, u * TS + c * L:u * TS + (c + 1) * L],
                                         S_bf[c][:, hh, :], start=True, stop=False, skip_group_check=True)
                                k.mm(pint[0:TS, j * 128:(j + 1) * 128], aT_sb[0:TS, j, 0:TS], vh_tm[0:TS, u, j * 128:(j + 1) * 128],
                                     start=False, stop=True, skip_group_check=True)
                            s_update(NCH - 1, None if sample else S_bf[0])
                            k.cp(act, o_sb[0:TS, :], pint[0:TS, :])
                            k.tt(pool, sq_sb[0:TS, :], o_sb[0:TS, :], o_sb[0:TS, :], ALU.mult)
                            k.op(dve, lambda: nc.vector.tensor_reduce(out=hs.h[0:TS, 0:4],
                                                                       in_=sq_sb.h[0:TS, :].rearrange("p (j e) -> p j e", j=4),
                                                                       axis=AX.X, op=ALU.add), [sq_sb.v], [hs.v])
                            k.actf(hs[0:TS, 4:8], hs[0:TS, 0:4], AF.Ln, scale=1.0 / HE, bias=EPS)
                            k.actf(hs[0:TS, 8:12], hs[0:TS, 4:8], AF.Exp, scale=-0.5)
                            k.tt(dve, on_sb[0:TS, :, :], o_sb[0:TS, :].rr("p (j e) -> p j e", j=4),
                                 V(hs, hs.h[0:TS, 8:12].unsqueeze(2).to_broadcast([TS, 4, 128])), ALU.mult)
                            for j in range(4):
                                k.tr(ptr[:, j * TS:(j + 1) * TS], on_sb[0:TS, j, :], identF[0:TS, 0:TS])
                            for j in range(4):
                                hh = 4 * g + j
                                k.stt(ohT[:, hh, utok(u)], ptr[:, j * TS:(j + 1) * TS], hg_col[:, hh:hh + 1], sgT[:, j, utok(u)],
                                      ALU.mult, ALU.mult)
                            if sample:
                                k.dma(pool, O["S_s"][u].rearrange("h c e -> c h e"), S_st[:, hsl, :])
                            elif last and u == NU - 1:
                                k.dma(pool, O["S_p"].rearrange("h c e -> c h e"), S_st[:, hsl, :])
                    W = w_get("BRH")
                    W3 = w3(W, 4)
                    for b in range(NSB):
                        for q in range(2):
                            acc = next_pa()
                            for hh in range(4):
                                k.mm(acc.v, ohT[:, hh, blk(b)], W3[:, hh, q * 512:(q + 1) * 512], start=(hh == 0), stop=(hh == 3))
                            k.cp(act, yb_acc[:, b, q * 512:(q + 1) * 512], acc.v)
                    evB, xdB = exchange(NT, "b", yb_acc, NSB)
            k.barrier()

            with ExitStack() as pes:
                x2 = k.sb("x2", [128, NSB, D], es=pes)
                tT = k.sb("tT", [128, 8, NT], BF16, es=pes)
                pT = k.sb("pT", [128, 2, NT], BF16, es=pes)
                p_tm = k.sb("p_tm", [128, NSB, PLE], es=pes)
                bf_sb = k.sb("bf_sb", [128, D], BF16, es=pes)
                sg_sb = k.sb("sg2_sb", [128, 512], es=pes)
                sga = k.sb("sga", [128, NSB, D], es=pes)
                sgb = k.sb("sgb", [128, NSB, D], es=pes)
                outb = [k.sb("outb%d" % i, [128, D], es=pes) for i in range(2)]
                ssq = k.sb("ssq2", [128, 4], es=pes)
                set_pa([pa[0], pa[1], pnum, pint])
                for b in range(NSB):
                    k.dma(sp, x2[:, b, :], x_dram[b * 128:(b + 1) * 128, :])
                    k.dma(sp, p_tm[:, b, :], p_dram[b * 128:(b + 1) * 128, :])

                def to_T(dst, src_fn, nchunk):
                    for b in range(NSB):
                        k.cp(act, bf_sb[:, 0:nchunk * 128], src_fn(b))
                        for c in range(nchunk):
                            k.tr(ptrb[:, c * 128:(c + 1) * 128], bf_sb[:, c * 128:(c + 1) * 128], identB.v)
                        k.cp(dve, dst[:, 0:nchunk, blk(b)], ptrb[:, 0:nchunk * 128].rr("p (c t) -> p c t", c=nchunk))
                to_T(pT, lambda b: p_tm[:, b, :], 2)
                for nm_, sgt in (("GA", sga), ("GB", sgb)):
                    for q in range(2):
                        W = w_get("%s%d" % (nm_, q))
                        W3 = w3(W, 8)
                        for b in range(NSB):
                            acc = next_pa()
                            for kk in range(8):
                                k.mm(acc.v, xnT[:, kk, blk(b)], W3[:, kk, :], start=(kk == 0), stop=(kk == 7))
                            k.actf(sgt[:, b, q * 512:(q + 1) * 512], acc.v, AF.Sigmoid)
                sp.wait(evA)
                sp.wait(evB)
                for b in range(NSB):
                    k.dma(sp, y_acc[:, b, :], xdA[b * 128:(b + 1) * 128, :])
                    k.dma(sp, yb_acc[:, b, :], xdB[b * 128:(b + 1) * 128, :])
                    k.tt(dve, y_acc[:, b, :], y_acc[:, b, :], sga[:, b, :], ALU.mult)
                    k.tt(dve, yb_acc[:, b, :], yb_acc[:, b, :], sgb[:, b, :], ALU.mult)
                    k.tt(dve, y_acc[:, b, :], y_acc[:, b, :], yb_acc[:, b, :], ALU.add)
                to_T(tT, lambda b: y_acc[:, b, :], 8)
                for q in range(2):
                    W = w_get("OUT%d" % q)
                    W3 = w3(W, 8)
                    for b in range(NSB):
                        acc = next_pa()
                        for kk in range(8):
                            k.mm(acc.v, tT[:, kk, blk(b)], W3[:, kk, :], start=(kk == 0), stop=(kk == 7))
                        dst = x2[:, b, q * 512:(q + 1) * 512]
                        k.tt(dve, dst, dst, acc.v, ALU.add)
                to_T(tT, lambda b: x2[:, b, :], 8)
                Wp = [w_get("PG0"), w_get("PG1", hold=1), w_get("PLE", hold=2)]
                Wple = w3(Wp[2], 2, 0, 2048)
                for b in range(NSB):
                    for q in range(2):
                        W3 = w3(Wp[q], 8)
                        acc = next_pa()
                        for kk in range(8):
                            k.mm(acc.v, tT[:, kk, blk(b)], W3[:, kk, :], start=(kk == 0), stop=(kk == 7))
                        k.actf(sg_sb.v, acc.v, AF.Sigmoid)
                        acc2 = next_pa()
                        for kk in range(2):
                            k.mm(acc2.v, pT[:, kk, blk(b)], Wple[:, kk, q * 512:(q + 1) * 512], start=(kk == 0), stop=(kk == 1))
                        k.tt(dve, sg_sb.v, sg_sb.v, acc2.v, ALU.mult)
                        dst = x2[:, b, q * 512:(q + 1) * 512]
                        k.tt(pool, dst, dst, sg_sb.v, ALU.add)
                    ob = outb[b % 2]
                    k.actf(ob.v, x2[:, b, :], AF.Square, accum=ssq[:, 0:1])
                    k.actf(ssq[:, 1:2], ssq[:, 0:1], AF.Ln, scale=1.0 / D, bias=EPS)
                    k.actf(ssq[:, 2:3], ssq[:, 1:2], AF.Exp, scale=-0.5)
                    k.stt(ob.v, x2[:, b, :], ssq[:, 2:3], fg_bc.v, ALU.mult, ALU.mult)
                    k.dma(pool, y_dram[b * 128:(b + 1) * 128, :], ob.v)
                set_pa([pa[0], pa[1]])
            k.barrier()

    NTP = min(512, TP)
    ntile = TP // NTP if do_prompt else 0
    for ti in range(ntile):
        seq.extend(range(NW))
    if do_sample:
        seq.extend(range(NW))
    for ti in range(ntile):
        tile(I["xp"][ti * NTP:(ti + 1) * NTP, :], I["pp"][ti * NTP:(ti + 1) * NTP, :], O["yp"][ti * NTP:(ti + 1) * NTP, :],
             NTP, 128, NTP // 128, 1, ti == 0, ti == ntile - 1, False, tix=ti)
    if do_sample:
        tile(I["xs"], I["ps"], O["ys"], NS * 32, 32, NS, NS, True, True, True)
    for ev in k.out_events:
        sp.wait(ev)


_NC_CACHE = {}


def _in_map(core, inputs, NS):
    b = core // 2
    hf = core % 2
    s0 = (core // 2) * NS
    f = lambda a: np.ascontiguousarray(a, dtype=np.float32)
    hs = slice(2 * hf, 2 * hf + 2)
    cs = slice(hf * MIL, (hf + 1) * MIL)
    gs = slice(hf * HWL, (hf + 1) * HWL)
    w = inputs["w_in"][0]
    w_in = np.concatenate([
        w[:, 0 + hf * MIL:0 + (hf + 1) * MIL], w[:, 4096 + hf * MIL:4096 + (hf + 1) * MIL], w[:, 2048 + hf * MIL:2048 + (hf + 1) * MIL],
        w[:, 6144 + 2 * hf:6144 + 2 * hf + 2], w[:, 6148 + 2 * hf:6148 + 2 * hf + 2],
        w[:, 6152 + hf * HWL:6152 + (hf + 1) * HWL], w[:, 7176 + hf * HWL:7176 + (hf + 1) * HWL],
        w[:, 8200 + hf * HWL:8200 + (hf + 1) * HWL], w[:, 9224 + hf * HWL:9224 + (hf + 1) * HWL],
        w[:, 10248:12296]], axis=1)
    assert w_in.shape[1] == NINL
    m = dict(
        xp=f(inputs["x_prompt"][b]), pp=f(inputs["p_prompt"][0, b]),
        xs=f(inputs["x_sample"][s0:s0 + NS].reshape(NS * 32, D)), ps=f(inputs["p_sample"][0, s0:s0 + NS].reshape(NS * 32, PLE)),
        sconv=f(inputs["state_conv"][0, s0:s0 + NS][:, :, cs].reshape(NS * 3, MIL)), sC=f(inputs["state_mlstm_C"][0, s0:s0 + NS, hs]),
        sn=f(inputs["state_mlstm_n"][0, s0:s0 + NS, hs].reshape(NS, MHL * 4, 128)), sm=f(inputs["state_mlstm_m"][0, s0:s0 + NS, hs]),
        sS=f(inputs["state_hgrn"][0, s0:s0 + NS, 4 * hf:4 * hf + 4]),
        norm_g=f(inputs["norm_g"][0]), w_in=f(w_in), b_ig=f(inputs["b_ig"][0, hs]), b_fg=f(inputs["b_fg"][0, hs]),
        conv_w=f(inputs["conv_w"][0][:, cs]), conv_b=f(inputs["conv_b"][0, cs]), w_qm=f(inputs["w_qm"][0, hs]), w_km=f(inputs["w_km"][0, hs]),
        mnorm_g=f(inputs["mnorm_g"][0, cs]), m_skip=f(inputs["m_skip"][0, cs]), w_brm=f(inputs["w_brm"][0, cs]),
        hgrn_lb=f(inputs["hgrn_lb"][:, gs]), hnorm_g=f(inputs["hnorm_g"][0, gs]), w_brh=f(inputs["w_brh"][0, gs]), w_out=f(inputs["w_out"][0]),
        w_ple=f(inputs["w_ple"][0]), w_pg=f(inputs["w_pg"][0]), final_g=f(inputs["final_g"]),
    )
    return m


def kernel(**inputs):
    inputs = {k_: np.asarray(v) for k_, v in inputs.items()}
    B, TP = inputs["x_prompt"].shape[:2]
    NSEQ = inputs["x_sample"].shape[0]
    NCORE = 8
    NS = NSEQ // (NCORE // 2)
    key = (TP, NS)
    if key not in _NC_CACHE:
        _NC_CACHE[key] = build(TP, NS)
    nc = _NC_CACHE[key]
    in_maps = [_in_map(c, inputs, NS) for c in range(NCORE)]
    res = run_bass_kernel_spmd(nc, in_maps, core_ids=list(range(NCORE)))
    R = res.results
    f32 = np.float32
    NP = NCORE // 2
    cat2 = lambda name, fn, ax: [np.concatenate([fn(R[2 * p][name]), fn(R[2 * p + 1][name])], axis=ax) for p in range(NP)]
    y_prompt = np.stack([R[2 * b]["yp"] for b in range(B)]).astype(f32)
    y_sample = np.concatenate([R[2 * p]["ys"].reshape(NS, 32, D) for p in range(NP)]).astype(f32)
    conv_p = np.stack(cat2("conv_p", lambda a: a, 1))[None].astype(f32)
    C_p = np.stack(cat2("C_p", lambda a: a, 0))[None].astype(f32)
    n_p = np.stack(cat2("n_p", lambda a: a.reshape(MHL, HD), 0))[None].astype(f32)
    m_p = np.stack(cat2("m_p", lambda a: a.reshape(MHL), 0))[None].astype(f32)
    S_p = np.stack(cat2("S_p", lambda a: a, 0))[None].astype(f32)
    conv_s = np.concatenate(cat2("conv_s", lambda a: a.reshape(NS, 3, MIL), 2))[None].astype(f32)
    C_s = np.concatenate(cat2("C_s", lambda a: a, 1))[None].astype(f32)
    n_s = np.concatenate(cat2("n_s", lambda a: a.reshape(NS, MHL, HD), 1))[None].astype(f32)
    m_s = np.concatenate(cat2("m_s", lambda a: a.reshape(NS, MHL), 1))[None].astype(f32)
    S_s = np.concatenate(cat2("S_s", lambda a: a, 1))[None].astype(f32)
    return (y_prompt, y_sample, conv_p, C_p, n_p, m_p, S_p, conv_s, C_s, n_s, m_s, S_s)
```

```python
import numpy as np
from contextlib import ExitStack
import concourse.bass as bass
import concourse.mybir as mybir
from concourse.bass_utils import run_bass_kernel_spmd

F32 = mybir.dt.float32
BF16 = mybir.dt.bfloat16
AF = mybir.ActivationFunctionType
ALU = mybir.AluOpType
AX = mybir.AxisListType

D = 1024
MI = 2048
MH = 4
HD = 512
HH = 8
HE = 128
PLE = 256
NIN = 12296
MHL = 2
MIL = 1024
HGL = 4
HWL = 512
NINL = 7172
EPS = 1e-6
NEG = -30000.0


class V:
    __slots__ = ("t", "ap")

    def __init__(self, t, ap):
        self.t = t
        self.ap = ap

    def __getitem__(self, key):
        return V(self.t, self.ap[key])

    def bitcast(self, dt):
        return V(self.t, self.ap.bitcast(dt))

    def rr(self, pat, **kw):
        return V(self.t, self.ap.rearrange(pat, **kw))

    def bc(self, shape):
        return V(self.t, self.ap.to_broadcast(list(shape)))


class T:
    def __init__(self, h, name):
        self.h = h
        self.name = name
        self.w = None
        self.r = {}
        self.dsem = None
        self.dcnt = 0

    def __getitem__(self, key):
        return V(self, self.h[key])

    @property
    def v(self):
        return V(self, self.h[:])


class TA:
    def __init__(self, t, ap, name):
        self.t = t
        self.h = ap
        self.name = name

    def __getitem__(self, key):
        return V(self.t, self.h[key])

    @property
    def v(self):
        return V(self.t, self.h)


class Eng:
    def __init__(self, k, name, h, strict=False):
        self.k = k
        self.name = name
        self.h = h
        self.sem = k.new_sem("e_" + name)
        self.cnt = 0
        self.last = None
        self.seen = {}
        self.strict = strict

    def wait(self, ev):
        sem, val, key = ev
        prod = self.k.engs.get(key)
        if prod is not None and val > prod.cnt:
            assert prod.last is not None and val == prod.cnt + 1, (key, val, prod.cnt)
            prod.last.then_inc(prod.sem, 1)
            prod.last = None
            prod.cnt += 1
        if self.seen.get(key, 0) < val:
            self.h.wait_ge(sem, val)
            self.seen[key] = val


class K:
    def __init__(self, nc, es):
        self.nc = nc
        self.es = es
        self.nsem = 0
        self.pe = Eng(self, "pe", nc.tensor)
        self.act = Eng(self, "act", nc.scalar)
        self.dve = Eng(self, "dve", nc.vector)
        self.pool = Eng(self, "pool", nc.gpsimd, strict=True)
        self.sp = Eng(self, "sp", nc.sync)
        self.engs = {e.name: e for e in (self.pe, self.act, self.dve, self.pool, self.sp)}
        self.out_events = []
        self.dsems = {}
        self.uid = 0
        self.dma_last = {}

    def new_sem(self, name):
        self.nsem += 1
        return self.es.enter_context(self.nc.semaphore(name))

    def sb(self, name, shape, dt=F32, es=None):
        self.uid += 1
        h = (es or self.es).enter_context(self.nc.sbuf_tensor("%s_%d" % (name, self.uid), list(shape), dt))
        return T(h, name)

    def ps(self, name, shape, dt=F32):
        h = self.es.enter_context(self.nc.psum_tensor(name, list(shape), dt))
        return T(h, name)

    def dram(self, name, shape, dt):
        h = self.nc.dram_tensor(name, list(shape), dt, kind="Internal")
        return T(h, name)

    def _pre(self, eng, rd, wr):
        for v in rd:
            t = v.t
            if t.w is not None:
                eng.wait(t.w)
        for v in wr:
            t = v.t
            if t.w is not None and (eng.strict or t.w[2] != eng.name):
                eng.wait(t.w)
            for key, ev in t.r.items():
                if eng.strict or key != eng.name:
                    eng.wait(ev)

    def _post(self, ev, rd, wr):
        for v in wr:
            v.t.w = ev
            v.t.r = {}
        for v in rd:
            if v.t.w is ev:
                continue
            v.t.r[ev[2]] = ev

    def op(self, eng, fn, rd, wr):
        rd = [v for v in rd if isinstance(v, V)]
        wr = [v for v in wr if isinstance(v, V)]
        self._pre(eng, rd, wr)
        ins = fn()
        eng.last = ins
        ev = (eng.sem, eng.cnt + 1, eng.name)
        self._post(ev, rd, wr)
        return ins

    def dma(self, q, out, in_, semt=None, **kw):
        rd = [in_] if isinstance(in_, V) else []
        wr = [out] if isinstance(out, V) else []
        self._pre(q, rd, wr)
        o = out.ap if isinstance(out, V) else out
        i = in_.ap if isinstance(in_, V) else in_
        ins = q.h.dma_start(out=o, in_=i, **kw)
        st = semt or (wr[0].t if wr else rd[0].t)
        key = "d_" + st.name
        if key not in self.dsems:
            self.dsems[key] = [self.new_sem(key), 0]
        ent = self.dsems[key]
        ent[1] += 16
        ins.then_inc(ent[0], 16)
        ev = (ent[0], ent[1], key)
        self.dma_last[key] = ev
        self._post(ev, rd, wr)
        if not wr:
            self.out_events.append(ev)
        return ev

    def barrier(self):
        engs = [self.pe, self.act, self.dve, self.pool, self.sp]
        evs = [(e.sem, e.cnt + (1 if e.last is not None else 0), e.name) for e in engs
               if e.cnt > 0 or e.last is not None]
        for e in engs:
            for ev in evs:
                if ev[2] != e.name:
                    e.wait(ev)
            for ev in self.dma_last.values():
                e.wait(ev)

    def mm(self, out, lhsT, rhs, start=True, stop=True, **kw):
        return self.op(self.pe, lambda: self.nc.tensor.matmul(out.ap, lhsT=lhsT.ap, rhs=rhs.ap, start=start, stop=stop, **kw),
                       [lhsT, rhs], [out])

    def tr(self, out, in_, ident):
        return self.op(self.pe, lambda: self.nc.tensor.transpose(out.ap, in_.ap, ident.ap), [in_, ident], [out])

    def actf(self, out, in_, func, bias=None, scale=None, accum=None):
        kw = {}
        if bias is not None:
            kw["bias"] = bias.ap if isinstance(bias, V) else bias
        if scale is not None:
            kw["scale"] = scale.ap if isinstance(scale, V) else scale
        if accum is not None:
            kw["accum_out"] = accum.ap
        return self.op(self.act, lambda: self.nc.scalar.activation(out=out.ap, in_=in_.ap, func=func, **kw),
                       [in_, bias, scale], [out, accum])

    def _e(self, eng):
        return {"dve": self.dve, "pool": self.pool, "act": self.act}[eng] if isinstance(eng, str) else eng

    def tt(self, eng, out, a, b, op):
        e = self._e(eng)
        return self.op(e, lambda: e.h.tensor_tensor(out=out.ap, in0=a.ap, in1=b.ap, op=op), [a, b], [out])

    def ts(self, eng, out, a, s1, op0, s2=None, op1=None, accum=None):
        e = self._e(eng)
        a1 = s1.ap if isinstance(s1, V) else s1
        a2 = s2.ap if isinstance(s2, V) else s2
        kw = {}
        if op1 is not None:
            kw["op1"] = op1
        if accum is not None:
            kw["accum_out"] = accum.ap
        return self.op(e, lambda: e.h.tensor_scalar(out=out.ap, in0=a.ap, scalar1=a1, scalar2=a2, op0=op0, **kw),
                       [a, s1, s2], [out, accum])

    def stt(self, out, a, s, b, op0, op1):
        e = self.dve
        sa = s.ap if isinstance(s, V) else s
        return self.op(e, lambda: e.h.scalar_tensor_tensor(out=out.ap, in0=a.ap, scalar=sa, in1=b.ap, op0=op0, op1=op1),
                       [a, s, b], [out])

    def cp(self, eng, out, in_):
        e = self._e(eng)
        if e is self.act:
            return self.op(e, lambda: self.nc.scalar.copy(out=out.ap, in_=in_.ap), [in_], [out])
        return self.op(e, lambda: e.h.tensor_copy(out=out.ap, in_=in_.ap), [in_], [out])

    def memset(self, eng, out, val):
        e = self._e(eng)
        return self.op(e, lambda: e.h.memset(out.ap, val), [], [out])


def _wplan():
    plan = []
    for h in range(MHL):
        plan += [("U%d" % h, "in", 0 + h * 512), ("Z%d" % h, "in", 1024 + h * 512), ("V%d" % h, "in", 2048 + h * 512),
                 ("QK%d" % h, "qk", h)]
    for h in range(MHL):
        plan += [("BRM%d" % h, "brm", h)]
    plan += [("QH0", "in", 3076), ("FH0", "in", 3588), ("IH0", "in", 4100), ("GH0", "in", 4612)]
    plan += [("BRH", "brh", 0)]
    plan += [("GA0", "in", 5124), ("GA1", "in", 5124 + 512), ("GB0", "in", 6148), ("GB1", "in", 6148 + 512)]
    plan += [("OUT0", "sq", ("w_out", 0)), ("OUT1", "sq", ("w_out", 1)), ("PG0", "sq", ("w_pg", 0)), ("PG1", "sq", ("w_pg", 1)),
             ("PLE", "ple", 0)]
    return plan


def build(TP=4096, NS=8, dbg=None):
    nc = bass.Bass("TRN2", target_bir_lowering=False)
    dbg = dbg or {}

    def din(name, shape):
        return nc.dram_tensor(name, list(shape), F32, kind="ExternalInput").ap()

    def dout(name, shape):
        return nc.dram_tensor(name, list(shape), F32, kind="ExternalOutput").ap()

    I = dict(
        xp=din("xp", [TP, D]), pp=din("pp", [TP, PLE]),
        xs=din("xs", [NS * 32, D]), ps=din("ps", [NS * 32, PLE]),
        sconv=din("sconv", [NS * 3, MIL]), sC=din("sC", [NS, MHL, HD, HD]), sn=din("sn", [NS, MHL * 4, 128]),
        sm=din("sm", [NS, MHL]), sS=din("sS", [NS, HGL, HE, HE]),
        norm_g=din("norm_g", [D]), w_in=din("w_in", [D, NINL]), b_ig=din("b_ig", [MHL]), b_fg=din("b_fg", [MHL]),
        conv_w=din("conv_w", [4, MIL]), conv_b=din("conv_b", [MIL]), w_qm=din("w_qm", [MHL, HD, HD]),
        w_km=din("w_km", [MHL, HD, HD]), mnorm_g=din("mnorm_g", [MIL]), m_skip=din("m_skip", [MIL]),
        w_brm=din("w_brm", [MIL, D]), hgrn_lb=din("hgrn_lb", [2, HWL]), hnorm_g=din("hnorm_g", [HWL]),
        w_brh=din("w_brh", [HWL, D]), w_out=din("w_out", [D, D]), w_ple=din("w_ple", [PLE, D]), w_pg=din("w_pg", [D, D]),
        final_g=din("final_g", [D]),
    )
    O = dict(
        yp=dout("yp", [TP, D]), ys=dout("ys", [NS * 32, D]),
        conv_p=dout("conv_p", [3, MIL]), C_p=dout("C_p", [MHL, HD, HD]), n_p=dout("n_p", [MHL * 4, 128]),
        m_p=dout("m_p", [MHL, 1]), S_p=dout("S_p", [HGL, HE, HE]),
        conv_s=dout("conv_s", [NS * 3, MIL]), C_s=dout("C_s", [NS, MHL, HD, HD]), n_s=dout("n_s", [NS, MHL * 4, 128]),
        m_s=dout("m_s", [NS, MHL, 1]), S_s=dout("S_s", [NS, HGL, HE, HE]),
    )
    with ExitStack() as es:
        k = K(nc, es)
        _program(nc, k, I, O, TP, NS, dbg)
    return nc


def _program(nc, k, I, O, TP, NS, dbg):
    sp, pe, act, dve, pool = k.sp, k.pe, k.act, k.dve, k.pool
    plan = _wplan()
    NW = len(plan)
    do_prompt = dbg.get("prompt", True)
    do_sample = dbg.get("sample", True)

    NRING = 3
    ring = [k.sb("wring%d" % i, [128, 4096], BF16) for i in range(NRING)]
    wstate = dict(next_load=0, total=0)
    seq = []

    def w_issue(upto):
        while wstate["next_load"] < min(upto + 1, len(seq)):
            j = wstate["next_load"]
            pi = seq[j]
            sp.wait(grp_ev[pi // GRP])
            k.dma(sp, ring[j % NRING].v, wscr.h[pi])
            wstate["next_load"] += 1

    def w_get(key, hold=0):
        j = wstate["total"]
        assert plan[seq[j]][0] == key, (plan[seq[j]][0], key)
        w_issue(j + NRING - 1 - hold)
        wstate["total"] += 1
        return ring[j % NRING]

    def w3(w, kk, lo=0, hi=4096):
        return V(w, w.h[:, lo:hi].rearrange("p (k c) -> p k c", k=kk))

    identF = k.sb("identF", [128, 128])
    identB = k.sb("identB", [128, 128], BF16)
    utri = k.sb("utri", [128, 128])
    mnegT = k.sb("mnegT", [128, 128])
    maskbd = k.sb("maskbd", [128, 128])
    sel = k.sb("sel", [2, 2, 128])
    ones4 = k.sb("ones4", [2, 128])
    onesb = k.sb("onesb", [128, 1], BF16)
    m01p = k.sb("m01p", [128, 512])
    m01s = k.sb("m01s", [128, 256])

    def asel(t, pattern, cmp, fill, cm):
        k.op(pool, lambda: nc.gpsimd.affine_select(out=t.h[:], in_=t.h[:], pattern=pattern, compare_op=cmp, fill=fill,
                                                   base=0, channel_multiplier=cm), [t.v], [t.v])
    k.memset(pool, identF.v, 1.0)
    asel(identF, [[-1, 128]], ALU.is_equal, 0.0, 1)
    k.cp(pool, identB.v, identF.v)
    k.memset(pool, utri.v, 1.0)
    asel(utri, [[1, 128]], ALU.is_ge, 0.0, -1)
    k.memset(pool, mnegT.v, 0.0)
    asel(mnegT, [[1, 128]], ALU.is_ge, NEG, -1)
    k.cp(pool, maskbd.v, utri.v)
    k.memset(pool, maskbd[0:64, 64:128], 0.0)
    k.memset(pool, sel.v, 1.0)
    asel(sel, [[-1, 2], [0, 128]], ALU.is_equal, 0.0, 1)
    k.memset(pool, ones4.v, 1.0)
    k.memset(pool, onesb.v, 1.0)
    k.memset(pool, m01p.v, 1.0)
    k.memset(pool, m01p.v.rr("p (c l) -> p c l", l=64)[:, :, 0:1], 0.0)
    k.memset(pool, m01s.v, 1.0)
    k.memset(pool, m01s.v.rr("p (c l) -> p c l", l=32)[:, :, 0:1], 0.0)

    cst = T(None, "cst")

    def cload(name, shape, src, **kw):
        t = k.sb(name, shape)
        k.dma(act, t.v, src, semt=cst, **kw)
        return t
    slow = dict(allow_slow_non_contiguous=True)
    rows_sb = k.sb("rows_sb", [76, 128])
    cols = k.sb("cols", [128, 76])
    k.dma(sp, rows_sb[0:32, :], I["conv_w"].rearrange("j (c p) -> (j c) p", p=128), semt=cst)
    k.dma(sp, rows_sb[32:40, :], I["conv_b"].rearrange("(c p) -> c p", p=128), semt=cst)
    k.dma(sp, rows_sb[40:48, :], I["mnorm_g"].rearrange("(c p) -> c p", p=128), semt=cst)
    k.dma(sp, rows_sb[48:56, :], I["m_skip"].rearrange("(c p) -> c p", p=128), semt=cst)
    k.dma(sp, rows_sb[56:64, :], I["norm_g"].rearrange("(c p) -> c p", p=128), semt=cst)
    k.dma(sp, rows_sb[64:68, :], I["hnorm_g"].rearrange("(c p) -> c p", p=128), semt=cst)
    k.dma(sp, rows_sb[68:76, :], I["hgrn_lb"].rearrange("r (c p) -> (r c) p", p=128), semt=cst)
    cw_col = TA(cols, cols.h[:, 0:32].rearrange("p (j c) -> p c j", j=4), "cw_col")
    cb_col = TA(cols, cols.h[:, 32:40], "cb_col")
    mg_col = TA(cols, cols.h[:, 40:48], "mg_col")
    sk_col = TA(cols, cols.h[:, 48:56], "sk_col")
    ng_col = TA(cols, cols.h[:, 56:64], "ng_col")
    hg_col = TA(cols, cols.h[:, 64:68], "hg_col")
    lb_raw = TA(cols, cols.h[:, 68:76].rearrange("p (r c) -> p c r", r=2), "lb_raw")
    big_bc = cload("big_bc", [128, 2], I["b_ig"].partition_broadcast(128))
    bfg_bc = cload("bfg_bc", [128, 2], I["b_fg"].partition_broadcast(128))
    fg_bc = cload("fg_bc", [128, D], I["final_g"].partition_broadcast(128))
    for t_ in (rows_sb, big_bc, bfg_bc, fg_bc):
        t_.w = k.dma_last["d_cst"]
    lb_col = k.sb("lb_col", [128, 4])
    oml_col = k.sb("oml_col", [128, 4])
    noml_col = k.sb("noml_col", [128, 4])

    wscr = k.dram("wscr", [NW, 128, 4096], BF16)
    GRP = 1
    wgrp = [T(None, "wg%d" % g) for g in range((NW + GRP - 1) // GRP)]
    wg_sb = k.sb("wg_sb", [128, 8, 4], BF16)
    grp_ev = {}
    for i, (key, kind, arg) in enumerate(plan):
        dst = wscr.h[i]
        g = wgrp[i // GRP]
        if kind == "in":
            src = I["w_in"][:, arg:arg + 512].rearrange("(k p) c -> p k c", p=128)
            ev = k.dma(pool, dst.rearrange("p (k c) -> p k c", k=8), src, semt=g)
        elif kind == "qk":
            for j, wn in enumerate(("w_qm", "w_km")):
                src = I[wn][arg].rearrange("(k p) c -> p k c", p=128)
                ev = k.dma(pool, dst[:, j * 2048:(j + 1) * 2048].rearrange("p (k c) -> p k c", k=4), src, semt=g)
        elif kind == "brm":
            src = I["w_brm"][arg * 512:(arg + 1) * 512, :].rearrange("(k p) c -> p k c", p=128)
            ev = k.dma(pool, dst.rearrange("p (k c) -> p k c", k=4), src, semt=g)
        elif kind == "sq":
            src = I[arg[0]][:, arg[1] * 512:(arg[1] + 1) * 512].rearrange("(k p) c -> p k c", p=128)
            ev = k.dma(pool, dst.rearrange("p (k c) -> p k c", k=8), src, semt=g)
        elif kind == "brh":
            src = I["w_brh"].rearrange("(k p) c -> p k c", p=128)
            ev = k.dma(pool, dst.rearrange("p (k c) -> p k c", k=4), src, semt=g)
        elif kind == "ple":
            src = I["w_ple"].rearrange("(k p) c -> p k c", p=128)
            ev = k.dma(pool, dst[:, 0:2048].rearrange("p (k c) -> p k c", k=2), src, semt=g)
        grp_ev[i // GRP] = ev
        if i == 1:
            k.dma(pool, wg_sb.v, I["w_in"][:, 3072:3076].rearrange("(k p) c -> p k c", p=128), allow_slow_non_contiguous=True)
    k.out_events = []

    pa = [k.ps("pa%d" % i, [128, 512]) for i in range(2)]
    psm = k.ps("psm", [128, 512])
    pnum = k.ps("pnum", [128, 512])
    pint = k.ps("pint", [128, 512])
    pdc = k.ps("pdc", [128, 512])
    ptr = k.ps("ptr", [128, 512])
    ptrb = k.ps("ptrb", [128, 1024], BF16)
    stt_ = dict(pa=0, dc=0)
    k.tr(ptr[:, 0:76], rows_sb.v, identF[0:76, 0:76])
    k.cp(dve, cols.v, ptr[:, 0:76])
    k.tt(dve, lb_col.v, lb_raw[:, :, 0], lb_raw[:, :, 1], ALU.subtract)
    k.actf(lb_col.v, lb_col.v, AF.Sigmoid)
    k.ts(dve, oml_col.v, lb_col.v, -1.0, ALU.mult, 1.0, ALU.add)
    k.ts(dve, noml_col.v, oml_col.v, -1.0, ALU.mult)

    pa_list = [pa[0], pa[1]]

    def next_pa():
        stt_["pa"] = (stt_["pa"] + 1) % len(pa_list)
        return pa_list[stt_["pa"]]

    def set_pa(lst):
        pa_list[:] = lst
        stt_["pa"] = 0

    def next_dc():
        stt_["dc"] = (stt_["dc"] + 1) % 3
        return [pdc, pa[0], pa[1]][stt_["dc"]]

    C_nat = k.sb("C_nat", [128, MHL, 4, 512])
    CT_pp = [k.sb("CT_bf%d" % i, [128, MHL, 4, 512], BF16) for i in range(2)]
    n_col = k.sb("n_col", [128, 8, 8])
    S_st = k.sb("S_st", [128, HGL, HE])
    S_bf = [k.sb("S_bf%d" % i, [128, HGL, HE], BF16) for i in range(2)]
    hist = k.sb("hist", [128, 8, 8, 3])
    mcar = k.sb("mcar", [2, 8])
    cvo = k.sb("cvo", [24, MIL])
    nout = k.sb("nout", [8, 8, 128])

    xsrc = {(n_, w_): k.dram("xsrc_%d%s" % (n_, w_), [n_, D], F32) for n_ in (512, 256) for w_ in "ab"}
    xdst = {(n_, w_): k.dram("xdst_%d%s" % (n_, w_), [n_, D], F32) for n_ in (512, 256) for w_ in "ab"}

    def exchange(NT, which, src_t, NSB):
        xs_, xd_ = xsrc[(NT, which)].h, xdst[(NT, which)].h
        for b_ in range(NSB):
            pool.wait(k.dma(pool, xs_[b_ * 128:(b_ + 1) * 128, :], src_t[:, b_, :]))
        cci = nc.gpsimd.collective_compute("AllReduce", ALU.add, ins=[xs_[:, :]], outs=[xd_[:, :]], replica_groups=RG)
        ccst["n"] += 1
        cci.then_inc(ccsem)
        return (ccsem, ccst["n"], "cc"), xd_
    ccsem = k.new_sem("ccsem")
    ccst = dict(n=0)
    RG = [[0, 1], [2, 3], [4, 5], [6, 7]]

    def tile(x_dram, p_dram, y_dram, NT, TS, NU, NSEG, first, last, sample, tix=0):
        NSB = NT // 128
        SEG = NT // NSEG
        L = 32 if sample else 64
        NCH = TS // L
        m01 = m01s if sample else m01p
        hmask = utri if sample else maskbd

        def utok(u):
            return slice(u * TS, (u + 1) * TS)

        def blk(b):
            return slice(b * 128, (b + 1) * 128)

        with ExitStack() as tes:
            xnT = k.sb("xnT", [128, 8, NT], BF16, es=tes)
            y_acc = k.sb("y_acc", [128, NSB, D], es=tes)
            CT_cur, CT_nxt = CT_pp[tix % 2], CT_pp[(tix + 1) % 2]
            u_tm = k.sb("u_tm", [128, NU, 2], es=tes)
            negM = k.sb("negM", [2, NU, TS], es=tes)
            wi_tm = k.sb("wi_tm", [128, NU, 2], es=tes)
            cl_tm = k.sb("cl_tm", [128, NU, 2], es=tes)
            dec_bc = k.sb("dec_bc", [128, NU, 2], es=tes)
            wiT_all = k.sb("wiT_all", [2, NU, TS], es=tes)
            wl_all = k.sb("wl_all", [128, NU, 2], es=tes)
            wlb_all = k.sb("wlb_all", [128, NU, 2], BF16, es=tes)

            with ExitStack() as pes:
                x_tm = k.sb("x_tm", [128, NSB, D], es=pes)
                xs_b = k.sb("xs_b", [128, D], BF16, es=pes)
                junk = k.sb("junk", [128, D], es=pes)
                ssq = k.sb("ssq", [128, 4], es=pes)
                for b in range(NSB):
                    k.dma(sp, x_tm[:, b, :], x_dram[b * 128:(b + 1) * 128, :])
                if sample:
                    k.dma(sp, mcar[0:2, 0:NU], I["sm"].rearrange("s h -> h s"), allow_slow_non_contiguous=True)
                    sc_tm = k.sb("sc_tm", [24, MIL], es=pes)
                    NS3 = NSEG * 3
                    k.dma(sp, sc_tm[0:NSEG * 3, :], I["sconv"])
                    for grp in range(2):
                        for cc in range(4):
                            c16 = grp * 4 + cc
                            k.tr(ptr[:, cc * NS3:(cc + 1) * NS3], sc_tm[0:NS3, c16 * 128:(c16 + 1) * 128], identF[0:NS3, 0:NS3])
                        k.cp(act, hist[:, grp * 4:(grp + 1) * 4, 0:NSEG, :],
                             ptr[:, 0:4 * NS3].rr("p (c s j) -> p c s j", c=4, j=3))
                    nrow = k.sb("nrow", [8, 128], es=pes)
                    for u in range(NU):
                        k.dma(sp, nrow.v, I["sn"][u])
                        k.tr(psm[:, 256:264], nrow.v, identF[0:8, 0:8])
                        k.cp(act, n_col[:, u, :], psm[:, 256:264])
                elif first:
                    k.memset(pool, mcar.v, 0.0)
                    k.memset(pool, hist.v, 0.0)
                    k.memset(pool, n_col.v, 0.0)
                    k.memset(pool, S_st.v, 0.0)
                    k.memset(pool, S_bf[0].v, 0.0)
                for b in range(NSB):
                    k.actf(junk.v, x_tm[:, b, :], AF.Square, accum=ssq[:, 0:1])
                    k.actf(ssq[:, 1:2], ssq[:, 0:1], AF.Ln, scale=1.0 / D, bias=EPS)
                    k.actf(ssq[:, 2:3], ssq[:, 1:2], AF.Exp, scale=-0.5)
                    k.ts(dve, xs_b.v, x_tm[:, b, :], ssq[:, 2:3], ALU.mult)
                    for c in range(8):
                        k.tr(ptrb[:, c * 128:(c + 1) * 128], xs_b[:, c * 128:(c + 1) * 128], identB.v)
                    k.tt(dve, xnT[:, :, blk(b)], ptrb.v.rr("p (c t) -> p c t", c=8),
                         V(cols, ng_col.h.unsqueeze(2).to_broadcast([128, 8, 128])), ALU.mult)
            k.barrier()

            with ExitStack() as pes:
                u_exts = [k.sb("u_ext%d" % i, [128, NSEG, SEG + 3], es=pes) for i in range(2)]
                cv = k.sb("cv", [128, NSEG, SEG], es=pes)
                gs = [k.sb("gs%d" % i, [128, 2], es=pes) for i in range(8)]
                g_ig, g_fg, g_e1, g_sp, g_csp, g_d4, g_d4b, g_wl = gs
                gr = [k.sb("gr%d" % i, [2, TS], es=pes) for i in range(5)]
                g_uT, g_M, g_mT, g_wiT, g_clT = gr
                BS = [dict(cT=k.sb("cT%d" % i, [128, 4, NT], BF16, es=pes), szT=k.sb("szT%d" % i, [128, 4, NT], BF16, es=pes),
                           qT=k.sb("qT%d" % i, [128, 4, NT], BF16, es=pes), kT=k.sb("kT%d" % i, [128, 4, NT], BF16, es=pes),
                           k_tm=k.sb("k_tm%d" % i, [128, NU, 512], BF16, es=pes), v_tm=k.sb("v_tm%d" % i, [128, NU, 512], BF16, es=pes),
                           hmT=k.sb("hmT%d" % i, [128, 4, NT], BF16, es=pes)) for i in range(2)]
                pdc2 = V(ptrb, ptrb.h[:].bitcast(F32))
                dcst = dict(i=0)

                def next_dc2():
                    dcst["i"] ^= 1
                    return pdc.v if dcst["i"] else pdc2
                NCT = 2 if sample else NU - 1
                CTtmp = [k.sb("CTtmp%d" % i, [128, 4, 512], BF16, es=pes) for i in range(NCT)]
                n_bfs = k.sb("n_bfs", [128, NU + 1, 4], BF16, es=pes)
                dT_sb = [k.sb("dT_sb%d" % i, [128, 128], es=pes) for i in range(2)]
                sT_sb = [k.sb("sT_sb%d" % i, [128, 128], BF16, es=pes) for i in range(2)]
                qwT = [k.sb("qwT%d" % i, [128, 4, 128], BF16, es=pes) for i in range(2)]
                hraw = [k.sb("hraw%d" % i, [128, 512], es=pes) for i in range(2)]
                hn_sb = [k.sb("hn_sb%d" % i, [128, 512], es=pes) for i in range(2)]
                tmp_sb = [k.sb("tmp_sb%d" % i, [128, 4, 128], es=pes) for i in range(2)]
                wv_all = k.sb("wv_all", [128, NU, 512], BF16, es=pes)
                dd = [[k.sb("dd%d_%d" % (i, j), [128, 8], es=pes) for i in range(6)] for j in range(2)]
                SC = float(HD) ** -0.5
                cnt = dict(u=0)

                tb = dict(i=0)

                def make_CT3(h, dst):
                    for dc in range(4):
                        tb["i"] = (tb["i"] + 1) % 3
                        bank = [ptr, pnum, pint][tb["i"]]
                        for ec in range(4):
                            k.tr(bank[:, ec * 128:(ec + 1) * 128], C_nat[:, h, ec, dc * 128:(dc + 1) * 128], identF.v)
                        k.cp(act, dst[:, dc, :], bank.v)
                        yield

                def gate_gen():
                    for u in range(NU):
                        slot = u if sample else 0
                        G = psm[0:TS, 384:388]
                        for kk in range(8):
                            k.mm(G, xnT[:, kk, utok(u)], wg_sb[:, kk, :], start=(kk == 0), stop=(kk == 7))
                        k.tt(dve, g_ig[0:TS, :], G[:, 0:2], big_bc[0:TS, :], ALU.add)
                        k.tt(dve, g_fg[0:TS, :], G[:, 2:4], bfg_bc[0:TS, :], ALU.add)
                        k.actf(g_e1[0:TS, :], g_fg[0:TS, :], AF.Exp, scale=-1.0)
                        k.actf(g_sp[0:TS, :], g_e1[0:TS, :], AF.Ln, bias=1.0)
                        yield
                        k.mm(psm[0:TS, 392:394], utri[0:TS, 0:TS], g_sp[0:TS, :])
                        k.cp(act, g_csp[0:TS, :], psm[0:TS, 392:394])
                        yield
                        k.tt(dve, u_tm[0:TS, u, :], g_ig[0:TS, :], g_csp[0:TS, :], ALU.add)
                        k.tr(psm[0:2, 256:256 + TS], u_tm[0:TS, u, :], identF[0:TS, 0:TS])
                        k.cp(dve, g_uT.v, psm[0:2, 256:256 + TS])
                        yield
                        k.tr(psm[0:2, 128:128 + TS], g_csp[0:TS, :], identF[0:TS, 0:TS])
                        k.op(dve, lambda: nc.vector.tensor_tensor_scan(out=g_M.h[:], data0=g_uT.h[:], data1=g_uT.h[:],
                                                                        initial=mcar.h[0:2, slot:slot + 1], op0=ALU.max, op1=ALU.max),
                             [g_uT.v, mcar.v], [g_M.v])
                        k.ts(dve, negM[0:2, u, :], g_M.v, -1.0, ALU.mult)
                        yield
                        k.tt(dve, g_mT.v, g_M.v, psm[0:2, 128:128 + TS], ALU.subtract)
                        k.actf(g_wiT.v, g_M.v, AF.Exp, scale=-1.0, bias=mcar[0:2, slot:slot + 1])
                        k.actf(g_clT.v, g_mT.v, AF.Exp, scale=-1.0)
                        yield
                        k.cp(dve, mcar[0:2, slot:slot + 1], g_mT[0:2, TS - 1:TS])
                        k.tr(psm[0:TS, 396:398], g_wiT.v, identF[0:2, 0:2])
                        k.cp(act, wi_tm[0:TS, u, :], psm[0:TS, 396:398])
                        k.tr(psm[0:TS, 400:402], g_clT.v, identF[0:2, 0:2])
                        k.cp(act, cl_tm[0:TS, u, :], psm[0:TS, 400:402])
                        yield
                        k.ts(dve, g_d4[0:2, 0:2], identF[0:2, 0:2], g_wiT[0:2, TS - 1:TS], ALU.mult)
                        k.mm(psm[:, 404:406], ones4.v, g_d4[0:2, 0:2])
                        k.cp(act, dec_bc[:, u, :], psm[:, 404:406])
                        yield
                        k.cp(dve, wiT_all[0:2, u, :], g_wiT.v)
                        k.ts(dve, g_d4b[0:2, 0:2], identF[0:2, 0:2], negM[0:2, u, TS - 1:TS], ALU.mult)
                        k.mm(psm[:, 408:410], ones4.v, g_d4b[0:2, 0:2])
                        k.tt(dve, g_wl[0:TS, :], u_tm[0:TS, u, :], psm[0:TS, 408:410], ALU.add)
                        k.actf(wl_all[0:TS, u, :], g_wl[0:TS, :], AF.Exp)
                        k.cp(dve, wlb_all[0:TS, u, :], wl_all[0:TS, u, :])
                        yield
                        if sample or (last and u == NU - 1):
                            k.dma(pool, O["m_s"][u] if sample else O["m_p"], mcar[0:2, slot:slot + 1])
                def proj_gen(h, B_):
                    cT, szT, qT, kT, k_tm, v_tm, hmT = (B_[n_] for n_ in ('cT', 'szT', 'qT', 'kT', 'k_tm', 'v_tm', 'hmT'))
                    W = w_get("U%d" % h)
                    W3 = w3(W, 8)
                    for cc in range(4):
                        c16 = 4 * h + cc
                        acc = next_pa()
                        for kk in range(8):
                            k.mm(acc[:, 0:NT], W3[:, kk, cc * 128:(cc + 1) * 128], xnT[:, kk, :], start=(kk == 0), stop=(kk == 7))
                        u_ext = u_exts[cc % 2]
                        k.cp(pool, u_ext[:, :, 0:3], hist[:, c16, 0:NSEG, :])
                        k.cp(act, u_ext[:, :, 3:3 + SEG], acc[:, 0:NT].rr("p (s t) -> p s t", s=NSEG))
                        k.cp(pool, hist[:, c16, 0:NSEG, :], u_ext[:, :, SEG:SEG + 3])
                        k.actf(cv.v, u_ext[:, :, 0:SEG], AF.Identity, scale=cw_col[:, c16, 0:1], bias=cb_col[:, c16:c16 + 1])
                        for j in range(1, 4):
                            k.stt(cv.v, u_ext[:, :, j:j + SEG], cw_col[:, c16, j:j + 1], cv.v, ALU.mult, ALU.add)
                        k.actf(cT[:, cc, :].rr("p (s t) -> p s t", s=NSEG), cv.v, AF.Silu)
                        yield
                    W = w_get("Z%d" % h)
                    W3 = w3(W, 8)
                    for cc in range(4):
                        acc = next_pa()
                        for kk in range(8):
                            k.mm(acc[:, 0:NT], W3[:, kk, cc * 128:(cc + 1) * 128], xnT[:, kk, :], start=(kk == 0), stop=(kk == 7))
                        k.actf(szT[:, cc, :], acc[:, 0:NT], AF.Silu)
                        yield
                    W = w_get("V%d" % h)
                    W3 = w3(W, 8)
                    for u in range(NU):
                        acc = next_pa()
                        for kk in range(8):
                            k.mm(acc[0:TS, :], xnT[:, kk, utok(u)], W3[:, kk, :], start=(kk == 0), stop=(kk == 7))
                        k.cp(act if u % 2 else dve, v_tm[0:TS, u, :], acc[0:TS, :])
                        yield
                    W = w_get("QK%d" % h)
                    Wq = w3(W, 4, 0, 2048)
                    Wk = w3(W, 4, 2048, 4096)
                    for ec in range(4):
                        acc = next_pa()
                        for dc in range(4):
                            k.mm(acc[:, 0:NT], Wq[:, dc, ec * 128:(ec + 1) * 128], cT[:, dc, :], start=(dc == 0), stop=(dc == 3))
                        k.cp(act, qT[:, ec, :], acc[:, 0:NT])
                        acc = next_pa()
                        for dc in range(4):
                            k.mm(acc[:, 0:NT], Wk[:, dc, ec * 128:(ec + 1) * 128], cT[:, dc, :], start=(dc == 0), stop=(dc == 3))
                        k.ts(dve, kT[:, ec, :], acc[:, 0:NT], SC, ALU.mult)
                        yield
                    for u in range(NU):
                        acc = next_pa()
                        for dc in range(4):
                            k.mm(acc[0:TS, :], cT[:, dc, utok(u)], Wk[:, dc, :], start=(dc == 0), stop=(dc == 3))
                        k.ts(dve, k_tm[0:TS, u, :], acc[0:TS, :], SC, ALU.mult)
                        yield
                    for ec in range(4):
                        c16 = 4 * h + ec
                        k.stt(cT[:, ec, :], cT[:, ec, :], sk_col[:, c16:c16 + 1], szT[:, ec, :], ALU.mult, ALU.mult)
                        k.actf(szT[:, ec, :], szT[:, ec, :], AF.Copy, scale=mg_col[:, c16:c16 + 1])
                        yield


                def passes_gen(h, B_):
                    cT, szT, qT, kT, k_tm, v_tm, hmT = (B_[n_] for n_ in ('cT', 'szT', 'qT', 'kT', 'k_tm', 'v_tm', 'hmT'))
                    for u in range(NU):
                        k.actf(wv_all[0:TS, u, :], v_tm[0:TS, u, :], AF.Copy, scale=wl_all[0:TS, u, h:h + 1])

                    def state_pass(u):
                        slot = u if sample else 0
                        if sample:
                            k.dma(sp, C_nat[:, h], I["sC"][u, h].rearrange("(ec p) d -> p ec d", p=128))
                            yield from make_CT3(h, CTtmp[u % 2])
                        elif first and u == 0:
                            k.memset(pool, C_nat[:, h], 0.0)
                            k.memset(pool, CT_cur[:, h], 0.0)
                        k.cp(dve, n_bfs[:, u, :], n_col[:, slot, 4 * h:4 * h + 4])
                        for ec in range(4):
                            dcp = next_dc2()
                            k.mm(dcp, wv_all[0:TS, u, ec * 128:(ec + 1) * 128], k_tm[0:TS, u, :])
                            k.stt(C_nat[:, h, ec, :], C_nat[:, h, ec, :], dec_bc[:, u, h:h + 1], dcp, ALU.mult, ALU.add)
                            yield
                        for dc in range(4):
                            k.mm(psm[:, 420 + dc:421 + dc], k_tm[0:TS, u, dc * 128:(dc + 1) * 128], wlb_all[0:TS, u, h:h + 1])
                        k.stt(n_col[:, slot, 4 * h:4 * h + 4], n_col[:, slot, 4 * h:4 * h + 4], dec_bc[:, u, h:h + 1],
                              psm[:, 420:424], ALU.mult, ALU.add)
                        if sample:
                            k.dma(pool, O["C_s"][u, h].rearrange("(ec p) d -> p ec d", p=128), C_nat[:, h])
                        elif last and u == NU - 1:
                            k.dma(pool, O["C_p"][h].rearrange("(ec p) d -> p ec d", p=128), C_nat[:, h])
                        else:
                            yield from make_CT3(h, CTtmp[u] if u < NU - 1 else CT_nxt[:, h])

                    def h_group(us):
                        R_ = [(u_ % 2, u_) for u_ in us]
                        CTs = {u: (CTtmp[u % 2] if sample else (CT_cur[:, h] if u == 0 else CTtmp[u - 1])) for u in us}
                        hb = [ptr, ptr]
                        yield
                        for i2, u in R_:
                            nm = psm[0:TS, 0:TS]
                            k.mm(nm, sel[0:2, h, 0:TS], negM[0:2, u, :], start=True, stop=False)
                            k.mm(nm, identF[0:TS, 0:TS], mnegT[0:TS, 0:TS], start=False, stop=True)
                            kq = psm[0:TS, 128:128 + TS]
                            for ec in range(4):
                                k.mm(kq, kT[:, ec, utok(u)], qT[:, ec, utok(u)], start=(ec == 0), stop=(ec == 3))
                            k.mm(psm[:, 256:256 + TS], sel[0:2, h, :], wiT_all[0:2, u, :])
                        yield
                        for i2, u in R_:
                            k.actf(dT_sb[i2][0:TS, 0:TS], psm[0:TS, 0:TS], AF.Exp, bias=u_tm[0:TS, u, h:h + 1])
                        yield
                        for i2, u in R_:
                            k.tt(dve, sT_sb[i2][0:TS, 0:TS], psm[0:TS, 128:128 + TS], dT_sb[i2][0:TS, 0:TS], ALU.mult)
                            k.tt(dve, qwT[i2][:, :, 0:TS], qT[:, :, utok(u)],
                                 V(psm, psm.h[:, 256:256 + TS].unsqueeze(1).to_broadcast([128, 4, TS])), ALU.mult)
                        yield
                        for i2, u in R_:
                            pn = pnum if i2 == 0 else pint
                            k.mm(pn[0:TS, :], sT_sb[i2][0:TS, 0:TS], v_tm[0:TS, u, :], start=True, stop=False)
                            for dc in range(4):
                                k.mm(pn[0:TS, :], qwT[i2][:, dc, 0:TS], CTs[u][:, dc, :], start=False, stop=(dc == 3))
                            dn_ = psm[0:TS, 416 + i2:417 + i2]
                            k.mm(dn_, sT_sb[i2][0:TS, 0:TS], onesb[0:TS, :], start=True, stop=False)
                            for dc in range(4):
                                k.mm(dn_, qwT[i2][:, dc, 0:TS], n_bfs[:, u, dc:dc + 1], start=False, stop=(dc == 3))
                        yield
                        for i2, u in R_:
                            d_rs, d_den, d_aden, d_rden, d_st, d_mv = dd[i2]
                            dn_ = psm[0:TS, 416 + i2:417 + i2]
                            k.ts(dve, d_den[0:TS, 0:1], dn_, -1.0, ALU.mult)
                            k.tt(dve, d_aden[0:TS, 0:1], d_den[0:TS, 0:1], dn_, ALU.max)
                            k.tt(dve, d_aden[0:TS, 0:1], d_aden[0:TS, 0:1], cl_tm[0:TS, u, h:h + 1], ALU.max)
                            k.op(dve, lambda d_rden=d_rden, d_aden=d_aden: nc.vector.reciprocal(out=d_rden.h[0:TS, 0:1], in_=d_aden.h[0:TS, 0:1]),
                                 [d_aden.v], [d_rden.v])
                        yield
                        for i2, u in R_:
                            pn = pnum if i2 == 0 else pint
                            k.actf(hraw[i2][0:TS, :], pn[0:TS, :], AF.Copy, scale=dd[i2][3][0:TS, 0:1])
                        yield
                        for i2, u in R_:
                            d_rs, d_den, d_aden, d_rden, d_st, d_mv = dd[i2]
                            k.op(dve, lambda d_st=d_st, i2=i2: nc.vector.bn_stats(out=d_st.h[0:TS, 0:6], in_=hraw[i2].h[0:TS, :]), [hraw[i2].v], [d_st.v])
                            k.op(dve, lambda d_st=d_st, d_mv=d_mv: nc.vector.bn_aggr(out=d_mv.h[0:TS, 0:2], in_=d_st.h[0:TS, 0:6]), [d_st.v], [d_mv.v])
                        yield
                        for i2, u in R_:
                            d_rs, d_den, d_aden, d_rden, d_st, d_mv = dd[i2]
                            k.actf(d_rs[0:TS, 0:1], d_mv[0:TS, 1:2], AF.Ln, bias=EPS)
                            k.actf(d_rs[0:TS, 1:2], d_rs[0:TS, 0:1], AF.Exp, scale=-0.5)
                        yield
                        for i2, u in R_:
                            d_rs, d_den, d_aden, d_rden, d_st, d_mv = dd[i2]
                            k.ts(dve, hn_sb[i2][0:TS, :], hraw[i2][0:TS, :], d_mv[0:TS, 0:1], ALU.subtract, d_rs[0:TS, 1:2], ALU.mult)
                        yield
                        for i2, u in R_:
                            for ec in range(4):
                                k.tr(hb[i2][:, ec * TS:(ec + 1) * TS], hn_sb[i2][0:TS, ec * 128:(ec + 1) * 128], identF[0:TS, 0:TS])
                        yield
                        for i2, u in R_:
                            k.tt(dve, tmp_sb[i2][:, :, 0:TS], hb[i2][:, 0:4 * TS].rr("p (e t) -> p e t", e=4), szT[:, :, utok(u)], ALU.mult)
                            k.tt(pool, hmT[:, :, utok(u)], tmp_sb[i2][:, :, 0:TS], cT[:, :, utok(u)], ALU.add)


                    if sample:
                        for u in range(NU):
                            yield from state_pass(u)
                            yield from h_group([u])
                    else:
                        for u in range(NU):
                            yield from state_pass(u)
                        for u in range(NU):
                            yield from h_group([u])
                def brm_gen(h, B_):
                    cT, szT, qT, kT, k_tm, v_tm, hmT = (B_[n_] for n_ in ('cT', 'szT', 'qT', 'kT', 'k_tm', 'v_tm', 'hmT'))
                    W = w_get("BRM%d" % h)
                    W3 = w3(W, 4)
                    for b in range(NSB):
                        for q in range(2):
                            acc = next_pa()
                            for ec in range(4):
                                k.mm(acc.v, hmT[:, ec, blk(b)], W3[:, ec, q * 512:(q + 1) * 512], start=(ec == 0), stop=(ec == 3))
                            dst = y_acc[:, b, q * 512:(q + 1) * 512]
                            if h == 0:
                                k.cp(act, dst, acc.v)
                                yield
                            else:
                                k.tt(dve, dst, dst, acc.v, ALU.add)
                                yield


                def run(ga, gb, ra=1, rb=1):
                    gens = [g_ for g_ in (ga, gb) if g_ is not None]
                    rate = {id(ga): ra, id(gb): rb}
                    while gens:
                        for g_ in list(gens):
                            for _ in range(rate[id(g_)]):
                                try:
                                    next(g_)
                                except StopIteration:
                                    gens.remove(g_)
                                    break
                run(gate_gen(), proj_gen(0, BS[0]), 1, 1)
                run(passes_gen(0, BS[0]), proj_gen(1, BS[1]), 2, 1)
                run(passes_gen(1, BS[1]), brm_gen(0, BS[0]), 3, 1)
                run(brm_gen(1, BS[1]), None)
                for slot in range(NU if sample else 1):
                    if sample or last:
                        k.tr(psm[0:8, 256:384], n_col[:, slot, :], identF.v)
                        k.cp(act, nout[0:8, slot, :], psm[0:8, 256:384])
                        k.dma(pool, O["n_s"][slot] if sample else O["n_p"], nout[0:8, slot, :])
                if sample or last:
                    for grp in range(2):
                        for cc in range(4):
                            c16 = grp * 4 + cc
                            k.tr(ptr[0:NSEG * 3, cc * 128:(cc + 1) * 128], hist[:, c16, 0:NSEG, :].rr("p s j -> p (s j)"), identF.v)
                        k.cp(act, cvo[0:NSEG * 3, grp * 512:(grp + 1) * 512], ptr[0:NSEG * 3, :])
                    k.dma(pool, O["conv_s"] if sample else O["conv_p"], cvo[0:NSEG * 3, :])
                evA, xdA = exchange(NT, "a", y_acc, NSB)
            k.barrier()

            yb_acc = k.sb("yb_acc", [128, NSB, D], es=tes)
            if True:
                with ExitStack() as pes:
                    ohT = k.sb("ohT", [128, 4, NT], BF16, es=pes)
                    qs_sb = k.sb("qs_sb", [128, 4, NT], es=pes)
                    sigf = k.sb("sigf", [128, NT], es=pes)
                    lfh = k.sb("lfh", [128, NT], es=pes)
                    kh_sb = k.sb("kh_sb", [128, NT], es=pes)
                    g_sb = k.sb("g_sb", [128, NT], es=pes)
                    eg_sb = k.sb("eg_sb", [128, NT], es=pes)
                    eng_sb = k.sb("eng_sb", [128, NT], es=pes)
                    qtT = k.sb("qtT", [128, 4, NT], BF16, es=pes)
                    ktT = k.sb("ktT", [128, 4, NT], BF16, es=pes)
                    sgT = k.sb("sgT", [128, 4, NT], BF16, es=pes)
                    egl = k.sb("egl", [128, 4, NT // L], es=pes)
                    vh_tm = k.sb("vh_tm", [128, NU, 512], BF16, es=pes)
                    kt_tm = k.sb("kt_tm", [128, 512], BF16, es=pes)
                    aT_sb = k.sb("aT_sb", [128, 4, 128], BF16, es=pes)
                    o_sb = k.sb("o_sb", [128, 512], es=pes)
                    sq_sb = k.sb("sq_sb", [128, 512], es=pes)
                    on_sb = k.sb("on_sb", [128, 4, 128], es=pes)
                    hs = k.sb("hs", [128, 12], es=pes)
                    for g in range(1):
                        hsl = slice(4 * g, 4 * g + 4)
                        set_pa([pa[0], pa[1], pdc, ptr])
                        W = w_get("QH%d" % g)
                        W3 = w3(W, 8)
                        for j in range(4):
                            acc = next_pa()
                            for kk in range(8):
                                k.mm(acc[:, 0:NT], W3[:, kk, j * 128:(j + 1) * 128], xnT[:, kk, :], start=(kk == 0), stop=(kk == 7))
                            k.actf(qs_sb[:, j, :], acc[:, 0:NT], AF.Silu)
                        W = w_get("FH%d" % g)
                        W3 = w3(W, 8)
                        for j in range(4):
                            hh = 4 * g + j
                            acc = next_pa()
                            for kk in range(8):
                                k.mm(acc[:, 0:NT], W3[:, kk, j * 128:(j + 1) * 128], xnT[:, kk, :], start=(kk == 0), stop=(kk == 7))
                            k.actf(sigf.v, acc[:, 0:NT], AF.Sigmoid)
                            k.actf(lfh.v, sigf.v, AF.Ln, scale=oml_col[:, hh:hh + 1], bias=lb_col[:, hh:hh + 1])
                            k.ts(dve, kh_sb.v, sigf.v, noml_col[:, hh:hh + 1], ALU.mult, oml_col[:, hh:hh + 1], ALU.add)
                            k.op(dve, lambda: nc.vector.tensor_tensor_scan(out=g_sb.h[:], data0=m01.h[:, 0:NT], data1=lfh.h[:],
                                                                            initial=0.0, op0=ALU.mult, op1=ALU.add),
                                 [m01.v, lfh.v], [g_sb.v])
                            k.actf(eg_sb.v, g_sb.v, AF.Exp)
                            k.actf(eng_sb.v, g_sb.v, AF.Exp, scale=-1.0)
                            k.tt(pool, qtT[:, j, :], qs_sb[:, j, :], eg_sb.v, ALU.mult)
                            k.tt(dve, ktT[:, j, :], kh_sb.v, eng_sb.v, ALU.mult)
                            k.cp(dve, egl[:, j, :], eg_sb.v.rr("p (c l) -> p c l", l=L)[:, :, L - 1])
                        W = w_get("IH%d" % g)
                        W3 = w3(W, 8)
                        for u in range(NU):
                            acc = next_pa()
                            for kk in range(8):
                                k.mm(acc[0:TS, :], xnT[:, kk, utok(u)], W3[:, kk, :], start=(kk == 0), stop=(kk == 7))
                            k.cp(act if u % 2 else dve, vh_tm[0:TS, u, :], acc[0:TS, :])
                        W = w_get("GH%d" % g)
                        W3 = w3(W, 8)
                        for j in range(4):
                            acc = next_pa()
                            for kk in range(8):
                                k.mm(acc[:, 0:NT], W3[:, kk, j * 128:(j + 1) * 128], xnT[:, kk, :], start=(kk == 0), stop=(kk == 7))
                            k.actf(sgT[:, j, :], acc[:, 0:NT], AF.Silu)
                        set_pa([pa[0], pa[1]])
                        for u in range(NU):
                            if sample:
                                k.dma(sp, S_st[:, hsl, :], I["sS"][u].rearrange("h c e -> c h e"))
                                k.cp(act, S_bf[0][:, hsl, :], S_st[:, hsl, :])
                            for j in range(4):
                                k.tr(ptrb[0:TS, j * 128:(j + 1) * 128], ktT[:, j, utok(u)], identB.v)
                            k.cp(act, kt_tm[0:TS, :], ptrb[0:TS, 0:512])
                            for j in range(4):
                                k.mm(pnum[0:TS, j * TS:(j + 1) * TS], ktT[:, j, utok(u)], qtT[:, j, utok(u)])
                            k.tt(dve, aT_sb[0:TS, :, 0:TS], pnum[0:TS, 0:4 * TS].rr("p (j t) -> p j t", j=4),
                                 V(hmask, hmask.h[0:TS, 0:TS].unsqueeze(1).to_broadcast([TS, 4, TS])), ALU.mult)
                            def s_update(c, dst_bf):
                                rows = slice(c * L, (c + 1) * L)
                                gci = u * NCH + c
                                for j in range(4):
                                    k.mm(pdc[:, j * 128:(j + 1) * 128], kt_tm[rows, j * 128:(j + 1) * 128],
                                         vh_tm[rows, u, j * 128:(j + 1) * 128])
                                k.tt(dve, S_st[:, hsl, :], S_st[:, hsl, :], pdc.v.rr("p (j e) -> p j e", j=4), ALU.add)
                                k.tt(dve, S_st[:, hsl, :], S_st[:, hsl, :],
                                     V(egl, egl.h[:, :, gci:gci + 1].to_broadcast([128, 4, 128])), ALU.mult)
                                if dst_bf is not None:
                                    k.cp(act, dst_bf[:, hsl, :], S_st[:, hsl, :])
                            for c in range(NCH - 1):
                                s_update(c, S_bf[c + 1])
                            for j in range(4):
                                hh = 4 * g + j
                                for c in range(NCH):
                                    rows = slice(c * L, (c + 1) * L)
                                    k.mm(pint[rows, j * 128:(j + 1) * 128], qtT[:, j, u * TS + c * L:u * TS + (c + 1) * L],
                                         S_bf[c][:, hh, :], start=True, stop=False, skip_group_check=True)
                                k.mm(pint[0:TS, j * 128:(j + 1) * 128], aT_sb[0:TS, j, 0:TS], vh_tm[0:TS, u, j * 128:(j + 1) * 128],
                                     start=False, stop=True, skip_group_check=True)
                            s_update(NCH - 1, None if sample else S_bf[0])
                            k.cp(act, o_sb[0:TS, :], pint[0:TS, :])
                            k.tt(pool, sq_sb[0:TS, :], o_sb[0:TS, :], o_sb[0:TS, :], ALU.mult)
                            k.op(dve, lambda: nc.vector.tensor_reduce(out=hs.h[0:TS, 0:4],
                                                                       in_=sq_sb.h[0:TS, :].rearrange("p (j e) -> p j e", j=4),
                                                                       axis=AX.X, op=ALU.add), [sq_sb.v], [hs.v])
                            k.actf(hs[0:TS, 4:8], hs[0:TS, 0:4], AF.Ln, scale=1.0 / HE, bias=EPS)
                            k.actf(hs[0:TS, 8:12], hs[0:TS, 4:8], AF.Exp, scale=-0.5)
                            k.tt(dve, on_sb[0:TS, :, :], o_sb[0:TS, :].rr("p (j e) -> p j e", j=4),
                                 V(hs, hs.h[0:TS, 8:12].unsqueeze(2).to_broadcast([TS, 4, 128])), ALU.mult)
                            for j in range(4):
                                k.tr(ptr[:, j * TS:(j + 1) * TS], on_sb[0:TS, j, :], identF[0:TS, 0:TS])
                            for j in range(4):
                                hh = 4 * g + j
                                k.stt(ohT[:, hh, utok(u)], ptr[:, j * TS:(j + 1) * TS], hg_col[:, hh:hh + 1], sgT[:, j, utok(u)],
                                      ALU.mult, ALU.mult)
                            if sample:
                                k.dma(pool, O["S_s"][u].rearrange("h c e -> c h e"), S_st[:, hsl, :])
                            elif last and u == NU - 1:
                                k.dma(pool, O["S_p"].rearrange("h c e -> c h e"), S_st[:, hsl, :])
                    W = w_get("BRH")
                    W3 = w3(W, 4)
                    for b in range(NSB):
                        for q in range(2):
                            acc = next_pa()
                            for hh in range(4):
                                k.mm(acc.v, ohT[:, hh, blk(b)], W3[:, hh, q * 512:(q + 1) * 512], start=(hh == 0), stop=(hh == 3))
                            k.cp(act, yb_acc[:, b, q * 512:(q + 1) * 512], acc.v)
                    evB, xdB = exchange(NT, "b", yb_acc, NSB)
            k.barrier()

            with ExitStack() as pes:
                x2 = k.sb("x2", [128, NSB, D], es=pes)
                tT = k.sb("tT", [128, 8, NT], BF16, es=pes)
                pT = k.sb("pT", [128, 2, NT], BF16, es=pes)
                p_tm = k.sb("p_tm", [128, NSB, PLE], es=pes)
                bf_sb = k.sb("bf_sb", [128, D], BF16, es=pes)
                sg_sb = k.sb("sg2_sb", [128, 512], es=pes)
                sga = k.sb("sga", [128, NSB, D], es=pes)
                sgb = k.sb("sgb", [128, NSB, D], es=pes)
                outb = [k.sb("outb%d" % i, [128, D], es=pes) for i in range(2)]
                ssq = k.sb("ssq2", [128, 4], es=pes)
                set_pa([pa[0], pa[1], pnum, pint])
                for b in range(NSB):
                    k.dma(sp, x2[:, b, :], x_dram[b * 128:(b + 1) * 128, :])
                    k.dma(sp, p_tm[:, b, :], p_dram[b * 128:(b + 1) * 128, :])

                def to_T(dst, src_fn, nchunk):
                    for b in range(NSB):
                        k.cp(act, bf_sb[:, 0:nchunk * 128], src_fn(b))
                        for c in range(nchunk):
                            k.tr(ptrb[:, c * 128:(c + 1) * 128], bf_sb[:, c * 128:(c + 1) * 128], identB.v)
                        k.cp(dve, dst[:, 0:nchunk, blk(b)], ptrb[:, 0:nchunk * 128].rr("p (c t) -> p c t", c=nchunk))
                to_T(pT, lambda b: p_tm[:, b, :], 2)
                for nm_, sgt in (("GA", sga), ("GB", sgb)):
                    for q in range(2):
                        W = w_get("%s%d" % (nm_, q))
                        W3 = w3(W, 8)
                        for b in range(NSB):
                            acc = next_pa()
                            for kk in range(8):
                                k.mm(acc.v, xnT[:, kk, blk(b)], W3[:, kk, :], start=(kk == 0), stop=(kk == 7))
                            k.actf(sgt[:, b, q * 512:(q + 1) * 512], acc.v, AF.Sigmoid)
                sp.wait(evA)
                sp.wait(evB)
                for b in range(NSB):
                    k.dma(sp, y_acc[:, b, :], xdA[b * 128:(b + 1) * 128, :])
                    k.dma(sp, yb_acc[:, b, :], xdB[b * 128:(b + 1) * 128, :])
                    k.tt(dve, y_acc[:, b, :], y_acc[:, b, :], sga[:, b, :], ALU.mult)
                    k.tt(dve, yb_acc[:, b, :], yb_acc[:, b, :], sgb[:, b, :], ALU.mult)
                    k.tt(dve, y_acc[:, b, :], y_acc[:, b, :], yb_acc[:, b, :], ALU.add)
                to_T(tT, lambda b: y_acc[:, b, :], 8)
                for q in range(2):
                    W = w_get("OUT%d" % q)
                    W3 = w3(W, 8)
                    for b in range(NSB):
                        acc = next_pa()
                        for kk in range(8):
                            k.mm(acc.v, tT[:, kk, blk(b)], W3[:, kk, :], start=(kk == 0), stop=(kk == 7))
                        dst = x2[:, b, q * 512:(q + 1) * 512]
                        k.tt(dve, dst, dst, acc.v, ALU.add)
                to_T(tT, lambda b: x2[:, b, :], 8)
                Wp = [w_get("PG0"), w_get("PG1", hold=1), w_get("PLE", hold=2)]
                Wple = w3(Wp[2], 2, 0, 2048)
                for b in range(NSB):
                    for q in range(2):
                        W3 = w3(Wp[q], 8)
                        acc = next_pa()
                        for kk in range(8):
                            k.mm(acc.v, tT[:, kk, blk(b)], W3[:, kk, :], start=(kk == 0), stop=(kk == 7))
                        k.actf(sg_sb.v, acc.v, AF.Sigmoid)
                        acc2 = next_pa()
                        for kk in range(2):
                            k.mm(acc2.v, pT[:, kk, blk(b)], Wple[:, kk, q * 512:(q + 1) * 512], start=(kk == 0), stop=(kk == 1))
                        k.tt(dve, sg_sb.v, sg_sb.v, acc2.v, ALU.mult)
                        dst = x2[:, b, q * 512:(q + 1) * 512]
                        k.tt(pool, dst, dst, sg_sb.v, ALU.add)
                    ob = outb[b % 2]
                    k.actf(ob.v, x2[:, b, :], AF.Square, accum=ssq[:, 0:1])
                    k.actf(ssq[:, 1:2], ssq[:, 0:1], AF.Ln, scale=1.0 / D, bias=EPS)
                    k.actf(ssq[:, 2:3], ssq[:, 1:2], AF.Exp, scale=-0.5)
                    k.stt(ob.v, x2[:, b, :], ssq[:, 2:3], fg_bc.v, ALU.mult, ALU.mult)
                    k.dma(pool, y_dram[b * 128:(b + 1) * 128, :], ob.v)
                set_pa([pa[0], pa[1]])
            k.barrier()

    NTP = min(512, TP)
    ntile = TP // NTP if do_prompt else 0
    for ti in range(ntile):
        seq.extend(range(NW))
    if do_sample:
        seq.extend(range(NW))
    for ti in range(ntile):
        tile(I["xp"][ti * NTP:(ti + 1) * NTP, :], I["pp"][ti * NTP:(ti + 1) * NTP, :], O["yp"][ti * NTP:(ti + 1) * NTP, :],
             NTP, 128, NTP // 128, 1, ti == 0, ti == ntile - 1, False, tix=ti)
    if do_sample:
        tile(I["xs"], I["ps"], O["ys"], NS * 32, 32, NS, NS, True, True, True)
    for ev in k.out_events:
        sp.wait(ev)


_NC_CACHE = {}


def _in_map(core, inputs, NS):
    b = core // 2
    hf = core % 2
    s0 = (core // 2) * NS
    f = lambda a: np.ascontiguousarray(a, dtype=np.float32)
    hs = slice(2 * hf, 2 * hf + 2)
    cs = slice(hf * MIL, (hf + 1) * MIL)
    gs = slice(hf * HWL, (hf + 1) * HWL)
    w = inputs["w_in"][0]
    w_in = np.concatenate([
        w[:, 0 + hf * MIL:0 + (hf + 1) * MIL], w[:, 4096 + hf * MIL:4096 + (hf + 1) * MIL], w[:, 2048 + hf * MIL:2048 + (hf + 1) * MIL],
        w[:, 6144 + 2 * hf:6144 + 2 * hf + 2], w[:, 6148 + 2 * hf:6148 + 2 * hf + 2],
        w[:, 6152 + hf * HWL:6152 + (hf + 1) * HWL], w[:, 7176 + hf * HWL:7176 + (hf + 1) * HWL],
        w[:, 8200 + hf * HWL:8200 + (hf + 1) * HWL], w[:, 9224 + hf * HWL:9224 + (hf + 1) * HWL],
        w[:, 10248:12296]], axis=1)
    assert w_in.shape[1] == NINL
    m = dict(
        xp=f(inputs["x_prompt"][b]), pp=f(inputs["p_prompt"][0, b]),
        xs=f(inputs["x_sample"][s0:s0 + NS].reshape(NS * 32, D)), ps=f(inputs["p_sample"][0, s0:s0 + NS].reshape(NS * 32, PLE)),
        sconv=f(inputs["state_conv"][0, s0:s0 + NS][:, :, cs].reshape(NS * 3, MIL)), sC=f(inputs["state_mlstm_C"][0, s0:s0 + NS, hs]),
        sn=f(inputs["state_mlstm_n"][0, s0:s0 + NS, hs].reshape(NS, MHL * 4, 128)), sm=f(inputs["state_mlstm_m"][0, s0:s0 + NS, hs]),
        sS=f(inputs["state_hgrn"][0, s0:s0 + NS, 4 * hf:4 * hf + 4]),
        norm_g=f(inputs["norm_g"][0]), w_in=f(w_in), b_ig=f(inputs["b_ig"][0, hs]), b_fg=f(inputs["b_fg"][0, hs]),
        conv_w=f(inputs["conv_w"][0][:, cs]), conv_b=f(inputs["conv_b"][0, cs]), w_qm=f(inputs["w_qm"][0, hs]), w_km=f(inputs["w_km"][0, hs]),
        mnorm_g=f(inputs["mnorm_g"][0, cs]), m_skip=f(inputs["m_skip"][0, cs]), w_brm=f(inputs["w_brm"][0, cs]),
        hgrn_lb=f(inputs["hgrn_lb"][:, gs]), hnorm_g=f(inputs["hnorm_g"][0, gs]), w_brh=f(inputs["w_brh"][0, gs]), w_out=f(inputs["w_out"][0]),
        w_ple=f(inputs["w_ple"][0]), w_pg=f(inputs["w_pg"][0]), final_g=f(inputs["final_g"]),
    )
    return m


def kernel(**inputs):
    inputs = {k_: np.asarray(v) for k_, v in inputs.items()}
    B, TP = inputs["x_prompt"].shape[:2]
    NSEQ = inputs["x_sample"].shape[0]
    NCORE = 8
    NS = NSEQ // (NCORE // 2)
    key = (TP, NS)
    if key not in _NC_CACHE:
        _NC_CACHE[key] = build(TP, NS)
    nc = _NC_CACHE[key]
    in_maps = [_in_map(c, inputs, NS) for c in range(NCORE)]
    res = run_bass_kernel_spmd(nc, in_maps, core_ids=list(range(NCORE)))
    R = res.results
    f32 = np.float32
    NP = NCORE // 2
    cat2 = lambda name, fn, ax: [np.concatenate([fn(R[2 * p][name]), fn(R[2 * p + 1][name])], axis=ax) for p in range(NP)]
    y_prompt = np.stack([R[2 * b]["yp"] for b in range(B)]).astype(f32)
    y_sample = np.concatenate([R[2 * p]["ys"].reshape(NS, 32, D) for p in range(NP)]).astype(f32)
    conv_p = np.stack(cat2("conv_p", lambda a: a, 1))[None].astype(f32)
    C_p = np.stack(cat2("C_p", lambda a: a, 0))[None].astype(f32)
    n_p = np.stack(cat2("n_p", lambda a: a.reshape(MHL, HD), 0))[None].astype(f32)
    m_p = np.stack(cat2("m_p", lambda a: a.reshape(MHL), 0))[None].astype(f32)
    S_p = np.stack(cat2("S_p", lambda a: a, 0))[None].astype(f32)
    conv_s = np.concatenate(cat2("conv_s", lambda a: a.reshape(NS, 3, MIL), 2))[None].astype(f32)
    C_s = np.concatenate(cat2("C_s", lambda a: a, 1))[None].astype(f32)
    n_s = np.concatenate(cat2("n_s", lambda a: a.reshape(NS, MHL, HD), 1))[None].astype(f32)
    m_s = np.concatenate(cat2("m_s", lambda a: a.reshape(NS, MHL), 1))[None].astype(f32)
    S_s = np.concatenate(cat2("S_s", lambda a: a, 1))[None].astype(f32)
    return (y_prompt, y_sample, conv_p, C_p, n_p, m_p, S_p, conv_s, C_s, n_s, m_s, S_s)
```

```python
import numpy as np
from contextlib import ExitStack
import concourse.bass as bass
import concourse.mybir as mybir
from concourse.bass_utils import run_bass_kernel_spmd

F32 = mybir.dt.float32
BF16 = mybir.dt.bfloat16
AF = mybir.ActivationFunctionType
ALU = mybir.AluOpType
AX = mybir.AxisListType

D = 1024
MI = 2048
MH = 4
HD = 512
HH = 8
HE = 128
PLE = 256
NIN = 12296
MHL = 2
MIL = 1024
HGL = 4
HWL = 512
NINL = 7172
EPS = 1e-6
NEG = -30000.0


class V:
    __slots__ = ("t", "ap")

    def __init__(self, t, ap):
        self.t = t
        self.ap = ap

    def __getitem__(self, key):
        return V(self.t, self.ap[key])

    def bitcast(self, dt):
        return V(self.t, self.ap.bitcast(dt))

    def rr(self, pat, **kw):
        return V(self.t, self.ap.rearrange(pat, **kw))

    def bc(self, shape):
        return V(self.t, self.ap.to_broadcast(list(shape)))


class T:
    def __init__(self, h, name):
        self.h = h
        self.name = name
        self.w = None
        self.r = {}
        self.dsem = None
        self.dcnt = 0

    def __getitem__(self, key):
        return V(self, self.h[key])

    @property
    def v(self):
        return V(self, self.h[:])


class TA:
    def __init__(self, t, ap, name):
        self.t = t
        self.h = ap
        self.name = name

    def __getitem__(self, key):
        return V(self.t, self.h[key])

    @property
    def v(self):
        return V(self.t, self.h)


class Eng:
    def __init__(self, k, name, h, strict=False):
        self.k = k
        self.name = name
        self.h = h
        self.sem = k.new_sem("e_" + name)
        self.cnt = 0
        self.last = None
        self.seen = {}
        self.strict = strict

    def wait(self, ev):
        sem, val, key = ev
        prod = self.k.engs.get(key)
        if prod is not None and val > prod.cnt:
            assert prod.last is not None and val == prod.cnt + 1, (key, val, prod.cnt)
            prod.last.then_inc(prod.sem, 1)
            prod.last = None
            prod.cnt += 1
        if self.seen.get(key, 0) < val:
            self.h.wait_ge(sem, val)
            self.seen[key] = val


class K:
    def __init__(self, nc, es):
        self.nc = nc
        self.es = es
        self.nsem = 0
        self.pe = Eng(self, "pe", nc.tensor)
        self.act = Eng(self, "act", nc.scalar)
        self.dve = Eng(self, "dve", nc.vector)
        self.pool = Eng(self, "pool", nc.gpsimd, strict=True)
        self.sp = Eng(self, "sp", nc.sync)
        self.engs = {e.name: e for e in (self.pe, self.act, self.dve, self.pool, self.sp)}
        self.out_events = []
        self.dsems = {}
        self.uid = 0
        self.dma_last = {}

    def new_sem(self, name):
        self.nsem += 1
        return self.es.enter_context(self.nc.semaphore(name))

    def sb(self, name, shape, dt=F32, es=None):
        self.uid += 1
        h = (es or self.es).enter_context(self.nc.sbuf_tensor("%s_%d" % (name, self.uid), list(shape), dt))
        return T(h, name)

    def ps(self, name, shape, dt=F32):
        h = self.es.enter_context(self.nc.psum_tensor(name, list(shape), dt))
        return T(h, name)

    def dram(self, name, shape, dt):
        h = self.nc.dram_tensor(name, list(shape), dt, kind="Internal")
        return T(h, name)

    def _pre(self, eng, rd, wr):
        for v in rd:
            t = v.t
            if t.w is not None:
                eng.wait(t.w)
        for v in wr:
            t = v.t
            if t.w is not None and (eng.strict or t.w[2] != eng.name):
                eng.wait(t.w)
            for key, ev in t.r.items():
                if eng.strict or key != eng.name:
                    eng.wait(ev)

    def _post(self, ev, rd, wr):
        for v in wr:
            v.t.w = ev
            v.t.r = {}
        for v in rd:
            if v.t.w is ev:
                continue
            v.t.r[ev[2]] = ev

    def op(self, eng, fn, rd, wr):
        rd = [v for v in rd if isinstance(v, V)]
        wr = [v for v in wr if isinstance(v, V)]
        self._pre(eng, rd, wr)
        ins = fn()
        eng.last = ins
        ev = (eng.sem, eng.cnt + 1, eng.name)
        self._post(ev, rd, wr)
        return ins

    def dma(self, q, out, in_, semt=None, **kw):
        rd = [in_] if isinstance(in_, V) else []
        wr = [out] if isinstance(out, V) else []
        self._pre(q, rd, wr)
        o = out.ap if isinstance(out, V) else out
        i = in_.ap if isinstance(in_, V) else in_
        ins = q.h.dma_start(out=o, in_=i, **kw)
        st = semt or (wr[0].t if wr else rd[0].t)
        key = "d_" + st.name
        if key not in self.dsems:
            self.dsems[key] = [self.new_sem(key), 0]
        ent = self.dsems[key]
        ent[1] += 16
        ins.then_inc(ent[0], 16)
        ev = (ent[0], ent[1], key)
        self.dma_last[key] = ev
        self._post(ev, rd, wr)
        if not wr:
            self.out_events.append(ev)
        return ev

    def barrier(self):
        engs = [self.pe, self.act, self.dve, self.pool, self.sp]
        evs = [(e.sem, e.cnt + (1 if e.last is not None else 0), e.name) for e in engs
               if e.cnt > 0 or e.last is not None]
        for e in engs:
            for ev in evs:
                if ev[2] != e.name:
                    e.wait(ev)
            for ev in self.dma_last.values():
                e.wait(ev)

    def mm(self, out, lhsT, rhs, start=True, stop=True, **kw):
        return self.op(self.pe, lambda: self.nc.tensor.matmul(out.ap, lhsT=lhsT.ap, rhs=rhs.ap, start=start, stop=stop, **kw),
                       [lhsT, rhs], [out])

    def tr(self, out, in_, ident):
        return self.op(self.pe, lambda: self.nc.tensor.transpose(out.ap, in_.ap, ident.ap), [in_, ident], [out])

    def actf(self, out, in_, func, bias=None, scale=None, accum=None):
        kw = {}
        if bias is not None:
            kw["bias"] = bias.ap if isinstance(bias, V) else bias
        if scale is not None:
            kw["scale"] = scale.ap if isinstance(scale, V) else scale
        if accum is not None:
            kw["accum_out"] = accum.ap
        return self.op(self.act, lambda: self.nc.scalar.activation(out=out.ap, in_=in_.ap, func=func, **kw),
                       [in_, bias, scale], [out, accum])

    def _e(self, eng):
        return {"dve": self.dve, "pool": self.pool, "act": self.act}[eng] if isinstance(eng, str) else eng

    def tt(self, eng, out, a, b, op):
        e = self._e(eng)
        return self.op(e, lambda: e.h.tensor_tensor(out=out.ap, in0=a.ap, in1=b.ap, op=op), [a, b], [out])

    def ts(self, eng, out, a, s1, op0, s2=None, op1=None, accum=None):
        e = self._e(eng)
        a1 = s1.ap if isinstance(s1, V) else s1
        a2 = s2.ap if isinstance(s2, V) else s2
        kw = {}
        if op1 is not None:
            kw["op1"] = op1
        if accum is not None:
            kw["accum_out"] = accum.ap
        return self.op(e, lambda: e.h.tensor_scalar(out=out.ap, in0=a.ap, scalar1=a1, scalar2=a2, op0=op0, **kw),
                       [a, s1, s2], [out, accum])

    def stt(self, out, a, s, b, op0, op1):
        e = self.dve
        sa = s.ap if isinstance(s, V) else s
        return self.op(e, lambda: e.h.scalar_tensor_tensor(out=out.ap, in0=a.ap, scalar=sa, in1=b.ap, op0=op0, op1=op1),
                       [a, s, b], [out])

    def cp(self, eng, out, in_):
        e = self._e(eng)
        if e is self.act:
            return self.op(e, lambda: self.nc.scalar.copy(out=out.ap, in_=in_.ap), [in_], [out])
        return self.op(e, lambda: e.h.tensor_copy(out=out.ap, in_=in_.ap), [in_], [out])

    def memset(self, eng, out, val):
        e = self._e(eng)
        return self.op(e, lambda: e.h.memset(out.ap, val), [], [out])


def _wplan():
    plan = []
    for h in range(MHL):
        plan += [("U%d" % h, "in", 0 + h * 512), ("Z%d" % h, "in", 1024 + h * 512), ("V%d" % h, "in", 2048 + h * 512),
                 ("QK%d" % h, "qk", h)]
    for h in range(MHL):
        plan += [("BRM%d" % h, "brm", h)]
    plan += [("QH0", "in", 3076), ("FH0", "in", 3588), ("IH0", "in", 4100), ("GH0", "in", 4612)]
    plan += [("BRH", "brh", 0)]
    plan += [("GA0", "in", 5124), ("GA1", "in", 5124 + 512), ("GB0", "in", 6148), ("GB1", "in", 6148 + 512)]
    plan += [("OUT0", "sq", ("w_out", 0)), ("OUT1", "sq", ("w_out", 1)), ("PG0", "sq", ("w_pg", 0)), ("PG1", "sq", ("w_pg", 1)),
             ("PLE", "ple", 0)]
    return plan


def build(TP=4096, NS=8, dbg=None):
    nc = bass.Bass("TRN2", target_bir_lowering=False)
    dbg = dbg or {}

    def din(name, shape):
        return nc.dram_tensor(name, list(shape), F32, kind="ExternalInput").ap()

    def dout(name, shape):
        return nc.dram_tensor(name, list(shape), F32, kind="ExternalOutput").ap()

    I = dict(
        xp=din("xp", [TP, D]), pp=din("pp", [TP, PLE]),
        xs=din("xs", [NS * 32, D]), ps=din("ps", [NS * 32, PLE]),
        sconv=din("sconv", [NS * 3, MIL]), sC=din("sC", [NS, MHL, HD, HD]), sn=din("sn", [NS, MHL * 4, 128]),
        sm=din("sm", [NS, MHL]), sS=din("sS", [NS, HGL, HE, HE]),
        norm_g=din("norm_g", [D]), w_in=din("w_in", [D, NINL]), b_ig=din("b_ig", [MHL]), b_fg=din("b_fg", [MHL]),
        conv_w=din("conv_w", [4, MIL]), conv_b=din("conv_b", [MIL]), w_qm=din("w_qm", [MHL, HD, HD]),
        w_km=din("w_km", [MHL, HD, HD]), mnorm_g=din("mnorm_g", [MIL]), m_skip=din("m_skip", [MIL]),
        w_brm=din("w_brm", [MIL, D]), hgrn_lb=din("hgrn_lb", [2, HWL]), hnorm_g=din("hnorm_g", [HWL]),
        w_brh=din("w_brh", [HWL, D]), w_out=din("w_out", [D, D]), w_ple=din("w_ple", [PLE, D]), w_pg=din("w_pg", [D, D]),
        final_g=din("final_g", [D]),
    )
    O = dict(
        yp=dout("yp", [TP, D]), ys=dout("ys", [NS * 32, D]),
        conv_p=dout("conv_p", [3, MIL]), C_p=dout("C_p", [MHL, HD, HD]), n_p=dout("n_p", [MHL * 4, 128]),
        m_p=dout("m_p", [MHL, 1]), S_p=dout("S_p", [HGL, HE, HE]),
        conv_s=dout("conv_s", [NS * 3, MIL]), C_s=dout("C_s", [NS, MHL, HD, HD]), n_s=dout("n_s", [NS, MHL * 4, 128]),
        m_s=dout("m_s", [NS, MHL, 1]), S_s=dout("S_s", [NS, HGL, HE, HE]),
    )
    with ExitStack() as es:
        k = K(nc, es)
        _program(nc, k, I, O, TP, NS, dbg)
    return nc


def _program(nc, k, I, O, TP, NS, dbg):
    sp, pe, act, dve, pool = k.sp, k.pe, k.act, k.dve, k.pool
    plan = _wplan()
    NW = len(plan)
    do_prompt = dbg.get("prompt", True)
    do_sample = dbg.get("sample", True)

    NRING = 3
    ring = [k.sb("wring%d" % i, [128, 4096], BF16) for i in range(NRING)]
    wstate = dict(next_load=0, total=0)
    seq = []

    def w_issue(upto):
        while wstate["next_load"] < min(upto + 1, len(seq)):
            j = wstate["next_load"]
            pi = seq[j]
            sp.wait(grp_ev[pi // GRP])
            k.dma(sp, ring[j % NRING].v, wscr.h[pi])
            wstate["next_load"] += 1

    def w_get(key, hold=0):
        j = wstate["total"]
        assert plan[seq[j]][0] == key, (plan[seq[j]][0], key)
        w_issue(j + NRING - 1 - hold)
        wstate["total"] += 1
        return ring[j % NRING]

    def w3(w, kk, lo=0, hi=4096):
        return V(w, w.h[:, lo:hi].rearrange("p (k c) -> p k c", k=kk))

    identF = k.sb("identF", [128, 128])
    identB = k.sb("identB", [128, 128], BF16)
    utri = k.sb("utri", [128, 128])
    mnegT = k.sb("mnegT", [128, 128])
    maskbd = k.sb("maskbd", [128, 128])
    sel = k.sb("sel", [2, 2, 128])
    ones4 = k.sb("ones4", [2, 128])
    onesb = k.sb("onesb", [128, 1], BF16)
    m01p = k.sb("m01p", [128, 512])
    m01s = k.sb("m01s", [128, 256])

    def asel(t, pattern, cmp, fill, cm):
        k.op(pool, lambda: nc.gpsimd.affine_select(out=t.h[:], in_=t.h[:], pattern=pattern, compare_op=cmp, fill=fill,
                                                   base=0, channel_multiplier=cm), [t.v], [t.v])
    k.memset(pool, identF.v, 1.0)
    asel(identF, [[-1, 128]], ALU.is_equal, 0.0, 1)
    k.cp(pool, identB.v, identF.v)
    k.memset(pool, utri.v, 1.0)
    asel(utri, [[1, 128]], ALU.is_ge, 0.0, -1)
    k.memset(pool, mnegT.v, 0.0)
    asel(mnegT, [[1, 128]], ALU.is_ge, NEG, -1)
    k.cp(pool, maskbd.v, utri.v)
    k.memset(pool, maskbd[0:64, 64:128], 0.0)
    k.memset(pool, sel.v, 1.0)
    asel(sel, [[-1, 2], [0, 128]], ALU.is_equal, 0.0, 1)
    k.memset(pool, ones4.v, 1.0)
    k.memset(pool, onesb.v, 1.0)
    k.memset(pool, m01p.v, 1.0)
    k.memset(pool, m01p.v.rr("p (c l) -> p c l", l=64)[:, :, 0:1], 0.0)
    k.memset(pool, m01s.v, 1.0)
    k.memset(pool, m01s.v.rr("p (c l) -> p c l", l=32)[:, :, 0:1], 0.0)

    cst = T(None, "cst")

    def cload(name, shape, src, **kw):
        t = k.sb(name, shape)
        k.dma(act, t.v, src, semt=cst, **kw)
        return t
    slow = dict(allow_slow_non_contiguous=True)
    rows_sb = k.sb("rows_sb", [76, 128])
    cols = k.sb("cols", [128, 76])
    k.dma(sp, rows_sb[0:32, :], I["conv_w"].rearrange("j (c p) -> (j c) p", p=128), semt=cst)
    k.dma(sp, rows_sb[32:40, :], I["conv_b"].rearrange("(c p) -> c p", p=128), semt=cst)
    k.dma(sp, rows_sb[40:48, :], I["mnorm_g"].rearrange("(c p) -> c p", p=128), semt=cst)
    k.dma(sp, rows_sb[48:56, :], I["m_skip"].rearrange("(c p) -> c p", p=128), semt=cst)
    k.dma(sp, rows_sb[56:64, :], I["norm_g"].rearrange("(c p) -> c p", p=128), semt=cst)
    k.dma(sp, rows_sb[64:68, :], I["hnorm_g"].rearrange("(c p) -> c p", p=128), semt=cst)
    k.dma(sp, rows_sb[68:76, :], I["hgrn_lb"].rearrange("r (c p) -> (r c) p", p=128), semt=cst)
    cw_col = TA(cols, cols.h[:, 0:32].rearrange("p (j c) -> p c j", j=4), "cw_col")
    cb_col = TA(cols, cols.h[:, 32:40], "cb_col")
    mg_col = TA(cols, cols.h[:, 40:48], "mg_col")
    sk_col = TA(cols, cols.h[:, 48:56], "sk_col")
    ng_col = TA(cols, cols.h[:, 56:64], "ng_col")
    hg_col = TA(cols, cols.h[:, 64:68], "hg_col")
    lb_raw = TA(cols, cols.h[:, 68:76].rearrange("p (r c) -> p c r", r=2), "lb_raw")
    big_bc = cload("big_bc", [128, 2], I["b_ig"].partition_broadcast(128))
    bfg_bc = cload("bfg_bc", [128, 2], I["b_fg"].partition_broadcast(128))
    fg_bc = cload("fg_bc", [128, D], I["final_g"].partition_broadcast(128))
    for t_ in (rows_sb, big_bc, bfg_bc, fg_bc):
        t_.w = k.dma_last["d_cst"]
    lb_col = k.sb("lb_col", [128, 4])
    oml_col = k.sb("oml_col", [128, 4])
    noml_col = k.sb("noml_col", [128, 4])

    wscr = k.dram("wscr", [NW, 128, 4096], BF16)
    GRP = 1
    wgrp = [T(None, "wg%d" % g) for g in range((NW + GRP - 1) // GRP)]
    wg_sb = k.sb("wg_sb", [128, 8, 4], BF16)
    grp_ev = {}
    for i, (key, kind, arg) in enumerate(plan):
        dst = wscr.h[i]
        g = wgrp[i // GRP]
        if kind == "in":
            src = I["w_in"][:, arg:arg + 512].rearrange("(k p) c -> p k c", p=128)
            ev = k.dma(pool, dst.rearrange("p (k c) -> p k c", k=8), src, semt=g)
        elif kind == "qk":
            for j, wn in enumerate(("w_qm", "w_km")):
                src = I[wn][arg].rearrange("(k p) c -> p k c", p=128)
                ev = k.dma(pool, dst[:, j * 2048:(j + 1) * 2048].rearrange("p (k c) -> p k c", k=4), src, semt=g)
        elif kind == "brm":
            src = I["w_brm"][arg * 512:(arg + 1) * 512, :].rearrange("(k p) c -> p k c", p=128)
            ev = k.dma(pool, dst.rearrange("p (k c) -> p k c", k=4), src, semt=g)
        elif kind == "sq":
            src = I[arg[0]][:, arg[1] * 512:(arg[1] + 1) * 512].rearrange("(k p) c -> p k c", p=128)
            ev = k.dma(pool, dst.rearrange("p (k c) -> p k c", k=8), src, semt=g)
        elif kind == "brh":
            src = I["w_brh"].rearrange("(k p) c -> p k c", p=128)
            ev = k.dma(pool, dst.rearrange("p (k c) -> p k c", k=4), src, semt=g)
        elif kind == "ple":
            src = I["w_ple"].rearrange("(k p) c -> p k c", p=128)
            ev = k.dma(pool, dst[:, 0:2048].rearrange("p (k c) -> p k c", k=2), src, semt=g)
        grp_ev[i // GRP] = ev
        if i == 1:
            k.dma(pool, wg_sb.v, I["w_in"][:, 3072:3076].rearrange("(k p) c -> p k c", p=128), allow_slow_non_contiguous=True)
    k.out_events = []

    pa = [k.ps("pa%d" % i, [128, 512]) for i in range(2)]
    psm = k.ps("psm", [128, 512])
    pnum = k.ps("pnum", [128, 512])
    pint = k.ps("pint", [128, 512])
    pdc = k.ps("pdc", [128, 512])
    ptr = k.ps("ptr", [128, 512])
    ptrb = k.ps("ptrb", [128, 1024], BF16)
    stt_ = dict(pa=0, dc=0)
    k.tr(ptr[:, 0:76], rows_sb.v, identF[0:76, 0:76])
    k.cp(dve, cols.v, ptr[:, 0:76])
    k.tt(dve, lb_col.v, lb_raw[:, :, 0], lb_raw[:, :, 1], ALU.subtract)
    k.actf(lb_col.v, lb_col.v, AF.Sigmoid)
    k.ts(dve, oml_col.v, lb_col.v, -1.0, ALU.mult, 1.0, ALU.add)
    k.ts(dve, noml_col.v, oml_col.v, -1.0, ALU.mult)

    pa_list = [pa[0], pa[1]]

    def next_pa():
        stt_["pa"] = (stt_["pa"] + 1) % len(pa_list)
        return pa_list[stt_["pa"]]

    def set_pa(lst):
        pa_list[:] = lst
        stt_["pa"] = 0

    def next_dc():
        stt_["dc"] = (stt_["dc"] + 1) % 3
        return [pdc, pa[0], pa[1]][stt_["dc"]]

    C_nat = k.sb("C_nat", [128, MHL, 4, 512])
    CT_pp = [k.sb("CT_bf%d" % i, [128, MHL, 4, 512], BF16) for i in range(2)]
    n_col = k.sb("n_col", [128, 8, 8])
    S_st = k.sb("S_st", [128, HGL, HE])
    S_bf = [k.sb("S_bf%d" % i, [128, HGL, HE], BF16) for i in range(2)]
    hist = k.sb("hist", [128, 8, 8, 3])
    mcar = k.sb("mcar", [2, 8])
    cvo = k.sb("cvo", [24, MIL])
    nout = k.sb("nout", [8, 8, 128])

    xsrc = {(n_, w_): k.dram("xsrc_%d%s" % (n_, w_), [n_, D], F32) for n_ in (512, 256) for w_ in "ab"}
    xdst = {(n_, w_): k.dram("xdst_%d%s" % (n_, w_), [n_, D], F32) for n_ in (512, 256) for w_ in "ab"}

    def exchange(NT, which, src_t, NSB):
        xs_, xd_ = xsrc[(NT, which)].h, xdst[(NT, which)].h
        for b_ in range(NSB):
            pool.wait(k.dma(pool, xs_[b_ * 128:(b_ + 1) * 128, :], src_t[:, b_, :]))
        cci = nc.gpsimd.collective_compute("AllReduce", ALU.add, ins=[xs_[:, :]], outs=[xd_[:, :]], replica_groups=RG)
        ccst["n"] += 1
        cci.then_inc(ccsem)
        return (ccsem, ccst["n"], "cc"), xd_
    ccsem = k.new_sem("ccsem")
    ccst = dict(n=0)
    RG = [[0, 1], [2, 3], [4, 5], [6, 7]]

    def tile(x_dram, p_dram, y_dram, NT, TS, NU, NSEG, first, last, sample, tix=0):
        NSB = NT // 128
        SEG = NT // NSEG
        L = 32 if sample else 64
        NCH = TS // L
        m01 = m01s if sample else m01p
        hmask = utri if sample else maskbd

        def utok(u):
            return slice(u * TS, (u + 1) * TS)

        def blk(b):
            return slice(b * 128, (b + 1) * 128)

        with ExitStack() as tes:
            xnT = k.sb("xnT", [128, 8, NT], BF16, es=tes)
            y_acc = k.sb("y_acc", [128, NSB, D], es=tes)
            CT_cur, CT_nxt = CT_pp[tix % 2], CT_pp[(tix + 1) % 2]
            u_tm = k.sb("u_tm", [128, NU, 2], es=tes)
            negM = k.sb("negM", [2, NU, TS], es=tes)
            wi_tm = k.sb("wi_tm", [128, NU, 2], es=tes)
            cl_tm = k.sb("cl_tm", [128, NU, 2], es=tes)
            dec_bc = k.sb("dec_bc", [128, NU, 2], es=tes)
            wiT_all = k.sb("wiT_all", [2, NU, TS], es=tes)
            wl_all = k.sb("wl_all", [128, NU, 2], es=tes)
            wlb_all = k.sb("wlb_all", [128, NU, 2], BF16, es=tes)

            with ExitStack() as pes:
                x_tm = k.sb("x_tm", [128, NSB, D], es=pes)
                xs_b = k.sb("xs_b", [128, D], BF16, es=pes)
                junk = k.sb("junk", [128, D], es=pes)
                ssq = k.sb("ssq", [128, 4], es=pes)
                for b in range(NSB):
                    k.dma(sp, x_tm[:, b, :], x_dram[b * 128:(b + 1) * 128, :])
                if sample:
                    k.dma(sp, mcar[0:2, 0:NU], I["sm"].rearrange("s h -> h s"), allow_slow_non_contiguous=True)
                    sc_tm = k.sb("sc_tm", [24, MIL], es=pes)
                    NS3 = NSEG * 3
                    k.dma(sp, sc_tm[0:NSEG * 3, :], I["sconv"])
                    for grp in range(2):
                        for cc in range(4):
                            c16 = grp * 4 + cc
                            k.tr(ptr[:, cc * NS3:(cc + 1) * NS3], sc_tm[0:NS3, c16 * 128:(c16 + 1) * 128], identF[0:NS3, 0:NS3])
                        k.cp(act, hist[:, grp * 4:(grp + 1) * 4, 0:NSEG, :],
                             ptr[:, 0:4 * NS3].rr("p (c s j) -> p c s j", c=4, j=3))
                    nrow = k.sb("nrow", [8, 128], es=pes)
                    for u in range(NU):
                        k.dma(sp, nrow.v, I["sn"][u])
                        k.tr(psm[:, 256:264], nrow.v, identF[0:8, 0:8])
                        k.cp(act, n_col[:, u, :], psm[:, 256:264])
                elif first:
                    k.memset(pool, mcar.v, 0.0)
                    k.memset(pool, hist.v, 0.0)
                    k.memset(pool, n_col.v, 0.0)
                    k.memset(pool, S_st.v, 0.0)
                    k.memset(pool, S_bf[0].v, 0.0)
                for b in range(NSB):
                    k.actf(junk.v, x_tm[:, b, :], AF.Square, accum=ssq[:, 0:1])
                    k.actf(ssq[:, 1:2], ssq[:, 0:1], AF.Ln, scale=1.0 / D, bias=EPS)
                    k.actf(ssq[:, 2:3], ssq[:, 1:2], AF.Exp, scale=-0.5)
                    k.ts(dve, xs_b.v, x_tm[:, b, :], ssq[:, 2:3], ALU.mult)
                    for c in range(8):
                        k.tr(ptrb[:, c * 128:(c + 1) * 128], xs_b[:, c * 128:(c + 1) * 128], identB.v)
                    k.tt(dve, xnT[:, :, blk(b)], ptrb.v.rr("p (c t) -> p c t", c=8),
                         V(cols, ng_col.h.unsqueeze(2).to_broadcast([128, 8, 128])), ALU.mult)
            k.barrier()

            with ExitStack() as pes:
                u_exts = [k.sb("u_ext%d" % i, [128, NSEG, SEG + 3], es=pes) for i in range(2)]
                cv = k.sb("cv", [128, NSEG, SEG], es=pes)
                gs = [k.sb("gs%d" % i, [128, 2], es=pes) for i in range(8)]
                g_ig, g_fg, g_e1, g_sp, g_csp, g_d4, g_d4b, g_wl = gs
                gr = [k.sb("gr%d" % i, [2, TS], es=pes) for i in range(5)]
                g_uT, g_M, g_mT, g_wiT, g_clT = gr
                BS = [dict(cT=k.sb("cT%d" % i, [128, 4, NT], BF16, es=pes), szT=k.sb("szT%d" % i, [128, 4, NT], BF16, es=pes),
                           qT=k.sb("qT%d" % i, [128, 4, NT], BF16, es=pes), kT=k.sb("kT%d" % i, [128, 4, NT], BF16, es=pes),
                           k_tm=k.sb("k_tm%d" % i, [128, NU, 512], BF16, es=pes), v_tm=k.sb("v_tm%d" % i, [128, NU, 512], BF16, es=pes),
                           hmT=k.sb("hmT%d" % i, [128, 4, NT], BF16, es=pes)) for i in range(2)]
                pdc2 = V(ptrb, ptrb.h[:].bitcast(F32))
                dcst = dict(i=0)

                def next_dc2():
                    dcst["i"] ^= 1
                    return pdc.v if dcst["i"] else pdc2
                NCT = 2 if sample else NU - 1
                CTtmp = [k.sb("CTtmp%d" % i, [128, 4, 512], BF16, es=pes) for i in range(NCT)]
                n_bfs = k.sb("n_bfs", [128, NU + 1, 4], BF16, es=pes)
                dT_sb = [k.sb("dT_sb%d" % i, [128, 128], es=pes) for i in range(2)]
                sT_sb = [k.sb("sT_sb%d" % i, [128, 128], BF16, es=pes) for i in range(2)]
                qwT = [k.sb("qwT%d" % i, [128, 4, 128], BF16, es=pes) for i in range(2)]
                hraw = [k.sb("hraw%d" % i, [128, 512], es=pes) for i in range(2)]
                hn_sb = [k.sb("hn_sb%d" % i, [128, 512], es=pes) for i in range(2)]
                tmp_sb = [k.sb("tmp_sb%d" % i, [128, 4, 128], es=pes) for i in range(2)]
                wv_all = k.sb("wv_all", [128, NU, 512], BF16, es=pes)
                dd = [[k.sb("dd%d_%d" % (i, j), [128, 8], es=pes) for i in range(6)] for j in range(2)]
                SC = float(HD) ** -0.5
                cnt = dict(u=0)

                tb = dict(i=0)

                def make_CT3(h, dst):
                    for dc in range(4):
                        tb["i"] = (tb["i"] + 1) % 3
                        bank = [ptr, pnum, pint][tb["i"]]
                        for ec in range(4):
                            k.tr(bank[:, ec * 128:(ec + 1) * 128], C_nat[:, h, ec, dc * 128:(dc + 1) * 128], identF.v)
                        k.cp(act, dst[:, dc, :], bank.v)
                        yield

                def gate_gen():
                    for u in range(NU):
                        slot = u if sample else 0
                        G = psm[0:TS, 384:388]
                        for kk in range(8):
                            k.mm(G, xnT[:, kk, utok(u)], wg_sb[:, kk, :], start=(kk == 0), stop=(kk == 7))
                        k.tt(dve, g_ig[0:TS, :], G[:, 0:2], big_bc[0:TS, :], ALU.add)
                        k.tt(dve, g_fg[0:TS, :], G[:, 2:4], bfg_bc[0:TS, :], ALU.add)
                        k.actf(g_e1[0:TS, :], g_fg[0:TS, :], AF.Exp, scale=-1.0)
                        k.actf(g_sp[0:TS, :], g_e1[0:TS, :], AF.Ln, bias=1.0)
                        yield
                        k.mm(psm[0:TS, 392:394], utri[0:TS, 0:TS], g_sp[0:TS, :])
                        k.cp(act, g_csp[0:TS, :], psm[0:TS, 392:394])
                        yield
                        k.tt(dve, u_tm[0:TS, u, :], g_ig[0:TS, :], g_csp[0:TS, :], ALU.add)
                        k.tr(psm[0:2, 256:256 + TS], u_tm[0:TS, u, :], identF[0:TS, 0:TS])
                        k.cp(dve, g_uT.v, psm[0:2, 256:256 + TS])
                        yield
                        k.tr(psm[0:2, 128:128 + TS], g_csp[0:TS, :], identF[0:TS, 0:TS])
                        k.op(dve, lambda: nc.vector.tensor_tensor_scan(out=g_M.h[:], data0=g_uT.h[:], data1=g_uT.h[:],
                                                                        initial=mcar.h[0:2, slot:slot + 1], op0=ALU.max, op1=ALU.max),
                             [g_uT.v, mcar.v], [g_M.v])
                        k.ts(dve, negM[0:2, u, :], g_M.v, -1.0, ALU.mult)
                        yield
                        k.tt(dve, g_mT.v, g_M.v, psm[0:2, 128:128 + TS], ALU.subtract)
                        k.actf(g_wiT.v, g_M.v, AF.Exp, scale=-1.0, bias=mcar[0:2, slot:slot + 1])
                        k.actf(g_clT.v, g_mT.v, AF.Exp, scale=-1.0)
                        yield
                        k.cp(dve, mcar[0:2, slot:slot + 1], g_mT[0:2, TS - 1:TS])
                        k.tr(psm[0:TS, 396:398], g_wiT.v, identF[0:2, 0:2])
                        k.cp(act, wi_tm[0:TS, u, :], psm[0:TS, 396:398])
                        k.tr(psm[0:TS, 400:402], g_clT.v, identF[0:2, 0:2])
                        k.cp(act, cl_tm[0:TS, u, :], psm[0:TS, 400:402])
                        yield
                        k.ts(dve, g_d4[0:2, 0:2], identF[0:2, 0:2], g_wiT[0:2, TS - 1:TS], ALU.mult)
                        k.mm(psm[:, 404:406], ones4.v, g_d4[0:2, 0:2])
                        k.cp(act, dec_bc[:, u, :], psm[:, 404:406])
                        yield
                        k.cp(dve, wiT_all[0:2, u, :], g_wiT.v)
                        k.ts(dve, g_d4b[0:2, 0:2], identF[0:2, 0:2], negM[0:2, u, TS - 1:TS], ALU.mult)
                        k.mm(psm[:, 408:410], ones4.v, g_d4b[0:2, 0:2])
                        k.tt(dve, g_wl[0:TS, :], u_tm[0:TS, u, :], psm[0:TS, 408:410], ALU.add)
                        k.actf(wl_all[0:TS, u, :], g_wl[0:TS, :], AF.Exp)
                        k.cp(dve, wlb_all[0:TS, u, :], wl_all[0:TS, u, :])
                        yield
                        if sample or (last and u == NU - 1):
                            k.dma(pool, O["m_s"][u] if sample else O["m_p"], mcar[0:2, slot:slot + 1])
                def proj_gen(h, B_):
                    cT, szT, qT, kT, k_tm, v_tm, hmT = (B_[n_] for n_ in ('cT', 'szT', 'qT', 'kT', 'k_tm', 'v_tm', 'hmT'))
                    W = w_get("U%d" % h)
                    W3 = w3(W, 8)
                    for cc in range(4):
                        c16 = 4 * h + cc
                        acc = next_pa()
                        for kk in range(8):
                            k.mm(acc[:, 0:NT], W3[:, kk, cc * 128:(cc + 1) * 128], xnT[:, kk, :], start=(kk == 0), stop=(kk == 7))
                        u_ext = u_exts[cc % 2]
                        k.cp(pool, u_ext[:, :, 0:3], hist[:, c16, 0:NSEG, :])
                        k.cp(act, u_ext[:, :, 3:3 + SEG], acc[:, 0:NT].rr("p (s t) -> p s t", s=NSEG))
                        k.cp(pool, hist[:, c16, 0:NSEG, :], u_ext[:, :, SEG:SEG + 3])
                        k.actf(cv.v, u_ext[:, :, 0:SEG], AF.Identity, scale=cw_col[:, c16, 0:1], bias=cb_col[:, c16:c16 + 1])
                        for j in range(1, 4):
                            k.stt(cv.v, u_ext[:, :, j:j + SEG], cw_col[:, c16, j:j + 1], cv.v, ALU.mult, ALU.add)
                        k.actf(cT[:, cc, :].rr("p (s t) -> p s t", s=NSEG), cv.v, AF.Silu)
                        yield
                    W = w_get("Z%d" % h)
                    W3 = w3(W, 8)
                    for cc in range(4):
                        acc = next_pa()
                        for kk in range(8):
                            k.mm(acc[:, 0:NT], W3[:, kk, cc * 128:(cc + 1) * 128], xnT[:, kk, :], start=(kk == 0), stop=(kk == 7))
                        k.actf(szT[:, cc, :], acc[:, 0:NT], AF.Silu)
                        yield
                    W = w_get("V%d" % h)
                    W3 = w3(W, 8)
                    for u in range(NU):
                        acc = next_pa()
                        for kk in range(8):
                            k.mm(acc[0:TS, :], xnT[:, kk, utok(u)], W3[:, kk, :], start=(kk == 0), stop=(kk == 7))
                        k.cp(act if u % 2 else dve, v_tm[0:TS, u, :], acc[0:TS, :])
                        yield
                    W = w_get("QK%d" % h)
                    Wq = w3(W, 4, 0, 2048)
                    Wk = w3(W, 4, 2048, 4096)
                    for ec in range(4):
                        acc = next_pa()
                        for dc in range(4):
                            k.mm(acc[:, 0:NT], Wq[:, dc, ec * 128:(ec + 1) * 128], cT[:, dc, :], start=(dc == 0), stop=(dc == 3))
                        k.cp(act, qT[:, ec, :], acc[:, 0:NT])
                        acc = next_pa()
                        for dc in range(4):
                            k.mm(acc[:, 0:NT], Wk[:, dc, ec * 128:(ec + 1) * 128], cT[:, dc, :], start=(dc == 0), stop=(dc == 3))
                        k.ts(dve, kT[:, ec, :], acc[:, 0:NT], SC, ALU.mult)
                        yield
                    for u in range(NU):
                        acc = next_pa()
                        for dc in range(4):
                            k.mm(acc[0:TS, :], cT[:, dc, utok(u)], Wk[:, dc, :], start=(dc == 0), stop=(dc == 3))
                        k.ts(dve, k_tm[0:TS, u, :], acc[0:TS, :], SC, ALU.mult)
                        yield
                    for ec in range(4):
                        c16 = 4 * h + ec
                        k.stt(cT[:, ec, :], cT[:, ec, :], sk_col[:, c16:c16 + 1], szT[:, ec, :], ALU.mult, ALU.mult)
                        k.actf(szT[:, ec, :], szT[:, ec, :], AF.Copy, scale=mg_col[:, c16:c16 + 1])
                        yield


                def passes_gen(h, B_):
                    cT, szT, qT, kT, k_tm, v_tm, hmT = (B_[n_] for n_ in ('cT', 'szT', 'qT', 'kT', 'k_tm', 'v_tm', 'hmT'))
                    for u in range(NU):
                        k.actf(wv_all[0:TS, u, :], v_tm[0:TS, u, :], AF.Copy, scale=wl_all[0:TS, u, h:h + 1])

                    def state_pass(u):
                        slot = u if sample else 0
                        if sample:
                            k.dma(sp, C_nat[:, h], I["sC"][u, h].rearrange("(ec p) d -> p ec d", p=128))
                            yield from make_CT3(h, CTtmp[u % 2])
                        elif first and u == 0:
                            k.memset(pool, C_nat[:, h], 0.0)
                            k.memset(pool, CT_cur[:, h], 0.0)
                        k.cp(dve, n_bfs[:, u, :], n_col[:, slot, 4 * h:4 * h + 4])
                        for ec in range(4):
                            dcp = next_dc2()
                            k.mm(dcp, wv_all[0:TS, u, ec * 128:(ec + 1) * 128], k_tm[0:TS, u, :])
                            k.stt(C_nat[:, h, ec, :], C_nat[:, h, ec, :], dec_bc[:, u, h:h + 1], dcp, ALU.mult, ALU.add)
                            yield
                        for dc in range(4):
                            k.mm(psm[:, 420 + dc:421 + dc], k_tm[0:TS, u, dc * 128:(dc + 1) * 128], wlb_all[0:TS, u, h:h + 1])
                        k.stt(n_col[:, slot, 4 * h:4 * h + 4], n_col[:, slot, 4 * h:4 * h + 4], dec_bc[:, u, h:h + 1],
                              psm[:, 420:424], ALU.mult, ALU.add)
                        if sample:
                            k.dma(pool, O["C_s"][u, h].rearrange("(ec p) d -> p ec d", p=128), C_nat[:, h])
                        elif last and u == NU - 1:
                            k.dma(pool, O["C_p"][h].rearrange("(ec p) d -> p ec d", p=128), C_nat[:, h])
                        else:
                            yield from make_CT3(h, CTtmp[u] if u < NU - 1 else CT_nxt[:, h])

                    def h_group(us):
                        R_ = [(u_ % 2, u_) for u_ in us]
                        CTs = {u: (CTtmp[u % 2] if sample else (CT_cur[:, h] if u == 0 else CTtmp[u - 1])) for u in us}
                        hb = [ptr, ptr]
                        yield
                        for i2, u in R_:
                            nm = psm[0:TS, 0:TS]
                            k.mm(nm, sel[0:2, h, 0:TS], negM[0:2, u, :], start=True, stop=False)
                            k.mm(nm, identF[0:TS, 0:TS], mnegT[0:TS, 0:TS], start=False, stop=True)
                            kq = psm[0:TS, 128:128 + TS]
                            for ec in range(4):
                                k.mm(kq, kT[:, ec, utok(u)], qT[:, ec, utok(u)], start=(ec == 0), stop=(ec == 3))
                            k.mm(psm[:, 256:256 + TS], sel[0:2, h, :], wiT_all[0:2, u, :])
                        yield
                        for i2, u in R_:
                            k.actf(dT_sb[i2][0:TS, 0:TS], psm[0:TS, 0:TS], AF.Exp, bias=u_tm[0:TS, u, h:h + 1])
                        yield
                        for i2, u in R_:
                            k.tt(dve, sT_sb[i2][0:TS, 0:TS], psm[0:TS, 128:128 + TS], dT_sb[i2][0:TS, 0:TS], ALU.mult)
                            k.tt(dve, qwT[i2][:, :, 0:TS], qT[:, :, utok(u)],
                                 V(psm, psm.h[:, 256:256 + TS].unsqueeze(1).to_broadcast([128, 4, TS])), ALU.mult)
                        yield
                        for i2, u in R_:
                            pn = pnum if i2 == 0 else pint
                            k.mm(pn[0:TS, :], sT_sb[i2][0:TS, 0:TS], v_tm[0:TS, u, :], start=True, stop=False)
                            for dc in range(4):
                                k.mm(pn[0:TS, :], qwT[i2][:, dc, 0:TS], CTs[u][:, dc, :], start=False, stop=(dc == 3))
                            dn_ = psm[0:TS, 416 + i2:417 + i2]
                            k.mm(dn_, sT_sb[i2][0:TS, 0:TS], onesb[0:TS, :], start=True, stop=False)
                            for dc in range(4):
                                k.mm(dn_, qwT[i2][:, dc, 0:TS], n_bfs[:, u, dc:dc + 1], start=False, stop=(dc == 3))
                        yield
                        for i2, u in R_:
                            d_rs, d_den, d_aden, d_rden, d_st, d_mv = dd[i2]
                            dn_ = psm[0:TS, 416 + i2:417 + i2]
                            k.ts(dve, d_den[0:TS, 0:1], dn_, -1.0, ALU.mult)
                            k.tt(dve, d_aden[0:TS, 0:1], d_den[0:TS, 0:1], dn_, ALU.max)
                            k.tt(dve, d_aden[0:TS, 0:1], d_aden[0:TS, 0:1], cl_tm[0:TS, u, h:h + 1], ALU.max)
                            k.op(dve, lambda d_rden=d_rden, d_aden=d_aden: nc.vector.reciprocal(out=d_rden.h[0:TS, 0:1], in_=d_aden.h[0:TS, 0:1]),
                                 [d_aden.v], [d_rden.v])
                        yield
                        for i2, u in R_:
                            pn = pnum if i2 == 0 else pint
                            k.actf(hraw[i2][0:TS, :], pn[0:TS, :], AF.Copy, scale=dd[i2][3][0:TS, 0:1])
                        yield
                        for i2, u in R_:
                            d_rs, d_den, d_aden, d_rden, d_st, d_mv = dd[i2]
                            k.op(dve, lambda d_st=d_st, i2=i2: nc.vector.bn_stats(out=d_st.h[0:TS, 0:6], in_=hraw[i2].h[0:TS, :]), [hraw[i2].v], [d_st.v])
                            k.op(dve, lambda d_st=d_st, d_mv=d_mv: nc.vector.bn_aggr(out=d_mv.h[0:TS, 0:2], in_=d_st.h[0:TS, 0:6]), [d_st.v], [d_mv.v])
                        yield
                        for i2, u in R_:
                            d_rs, d_den, d_aden, d_rden, d_st, d_mv = dd[i2]
                            k.actf(d_rs[0:TS, 0:1], d_mv[0:TS, 1:2], AF.Ln, bias=EPS)
                            k.actf(d_rs[0:TS, 1:2], d_rs[0:TS, 0:1], AF.Exp, scale=-0.5)
                        yield
                        for i2, u in R_:
                            d_rs, d_den, d_aden, d_rden, d_st, d_mv = dd[i2]
                            k.ts(dve, hn_sb[i2][0:TS, :], hraw[i2][0:TS, :], d_mv[0:TS, 0:1], ALU.subtract, d_rs[0:TS, 1:2], ALU.mult)
                        yield
                        for i2, u in R_:
                            for ec in range(4):
                                k.tr(hb[i2][:, ec * TS:(ec + 1) * TS], hn_sb[i2][0:TS, ec * 128:(ec + 1) * 128], identF[0:TS, 0:TS])
                        yield
                        for i2, u in R_:
                            k.tt(dve, tmp_sb[i2][:, :, 0:TS], hb[i2][:, 0:4 * TS].rr("p (e t) -> p e t", e=4), szT[:, :, utok(u)], ALU.mult)
                            k.tt(pool, hmT[:, :, utok(u)], tmp_sb[i2][:, :, 0:TS], cT[:, :, utok(u)], ALU.add)


                    if sample:
                        for u in range(NU):
                            yield from state_pass(u)
                            yield from h_group([u])
                    else:
                        for u in range(NU):
                            yield from state_pass(u)
                        for u in range(NU):
                            yield from h_group([u])
                def brm_gen(h, B_):
                    cT, szT, qT, kT, k_tm, v_tm, hmT = (B_[n_] for n_ in ('cT', 'szT', 'qT', 'kT', 'k_tm', 'v_tm', 'hmT'))
                    W = w_get("BRM%d" % h)
                    W3 = w3(W, 4)
                    for b in range(NSB):
                        for q in range(2):
                            acc = next_pa()
                            for ec in range(4):
                                k.mm(acc.v, hmT[:, ec, blk(b)], W3[:, ec, q * 512:(q + 1) * 512], start=(ec == 0), stop=(ec == 3))
                            dst = y_acc[:, b, q * 512:(q + 1) * 512]
                            if h == 0:
                                k.cp(act, dst, acc.v)
                                yield
                            else:
                                k.tt(dve, dst, dst, acc.v, ALU.add)
                                yield


                def run(ga, gb, ra=1, rb=1):
                    gens = [g_ for g_ in (ga, gb) if g_ is not None]
                    rate = {id(ga): ra, id(gb): rb}
                    while gens:
                        for g_ in list(gens):
                            for _ in range(rate[id(g_)]):
                                try:
                                    next(g_)
                                except StopIteration:
                                    gens.remove(g_)
                                    break
                run(gate_gen(), proj_gen(0, BS[0]), 1, 1)
                run(passes_gen(0, BS[0]), proj_gen(1, BS[1]), 3, 1)
                run(passes_gen(1, BS[1]), brm_gen(0, BS[0]), 3, 1)
                run(brm_gen(1, BS[1]), None)
                for slot in range(NU if sample else 1):
                    if sample or last:
                        k.tr(psm[0:8, 256:384], n_col[:, slot, :], identF.v)
                        k.cp(act, nout[0:8, slot, :], psm[0:8, 256:384])
                        k.dma(pool, O["n_s"][slot] if sample else O["n_p"], nout[0:8, slot, :])
                if sample or last:
                    for grp in range(2):
                        for cc in range(4):
                            c16 = grp * 4 + cc
                            k.tr(ptr[0:NSEG * 3, cc * 128:(cc + 1) * 128], hist[:, c16, 0:NSEG, :].rr("p s j -> p (s j)"), identF.v)
                        k.cp(act, cvo[0:NSEG * 3, grp * 512:(grp + 1) * 512], ptr[0:NSEG * 3, :])
                    k.dma(pool, O["conv_s"] if sample else O["conv_p"], cvo[0:NSEG * 3, :])
                evA, xdA = exchange(NT, "a", y_acc, NSB)
            k.barrier()

            yb_acc = k.sb("yb_acc", [128, NSB, D], es=tes)
            if True:
                with ExitStack() as pes:
                    ohT = k.sb("ohT", [128, 4, NT], BF16, es=pes)
                    qs_sb = k.sb("qs_sb", [128, 4, NT], es=pes)
                    sigf = k.sb("sigf", [128, NT], es=pes)
                    lfh = k.sb("lfh", [128, NT], es=pes)
                    kh_sb = k.sb("kh_sb", [128, NT], es=pes)
                    g_sb = k.sb("g_sb", [128, NT], es=pes)
                    eg_sb = k.sb("eg_sb", [128, NT], es=pes)
                    eng_sb = k.sb("eng_sb", [128, NT], es=pes)
                    qtT = k.sb("qtT", [128, 4, NT], BF16, es=pes)
                    ktT = k.sb("ktT", [128, 4, NT], BF16, es=pes)
                    sgT = k.sb("sgT", [128, 4, NT], BF16, es=pes)
                    egl = k.sb("egl", [128, 4, NT // L], es=pes)
                    vh_tm = k.sb("vh_tm", [128, NU, 512], BF16, es=pes)
                    kt_tm = k.sb("kt_tm", [128, 512], BF16, es=pes)
                    aT_sb = k.sb("aT_sb", [128, 4, 128], BF16, es=pes)
                    o_sbs = [k.sb("o_sb%d" % i, [128, 512], es=pes) for i in range(2)]
                    sq_sb = k.sb("sq_sb", [128, 512], es=pes)
                    on_sb = k.sb("on_sb", [128, 4, 128], es=pes)
                    hs = k.sb("hs", [128, 12], es=pes)
                    for g in range(1):
                        hsl = slice(4 * g, 4 * g + 4)
                        set_pa([pa[0], pa[1], pdc, ptr])
                        W = w_get("QH%d" % g)
                        W3 = w3(W, 8)
                        for j in range(4):
                            acc = next_pa()
                            for kk in range(8):
                                k.mm(acc[:, 0:NT], W3[:, kk, j * 128:(j + 1) * 128], xnT[:, kk, :], start=(kk == 0), stop=(kk == 7))
                            k.actf(qs_sb[:, j, :], acc[:, 0:NT], AF.Silu)
                        W = w_get("FH%d" % g)
                        W3 = w3(W, 8)
                        for j in range(4):
                            hh = 4 * g + j
                            acc = next_pa()
                            for kk in range(8):
                                k.mm(acc[:, 0:NT], W3[:, kk, j * 128:(j + 1) * 128], xnT[:, kk, :], start=(kk == 0), stop=(kk == 7))
                            k.actf(sigf.v, acc[:, 0:NT], AF.Sigmoid)
                            k.actf(lfh.v, sigf.v, AF.Ln, scale=oml_col[:, hh:hh + 1], bias=lb_col[:, hh:hh + 1])
                            k.ts(dve, kh_sb.v, sigf.v, noml_col[:, hh:hh + 1], ALU.mult, oml_col[:, hh:hh + 1], ALU.add)
                            k.op(dve, lambda: nc.vector.tensor_tensor_scan(out=g_sb.h[:], data0=m01.h[:, 0:NT], data1=lfh.h[:],
                                                                            initial=0.0, op0=ALU.mult, op1=ALU.add),
                                 [m01.v, lfh.v], [g_sb.v])
                            k.actf(eg_sb.v, g_sb.v, AF.Exp)
                            k.actf(eng_sb.v, g_sb.v, AF.Exp, scale=-1.0)
                            k.tt(pool, qtT[:, j, :], qs_sb[:, j, :], eg_sb.v, ALU.mult)
                            k.tt(dve, ktT[:, j, :], kh_sb.v, eng_sb.v, ALU.mult)
                            k.cp(dve, egl[:, j, :], eg_sb.v.rr("p (c l) -> p c l", l=L)[:, :, L - 1])
                        W = w_get("IH%d" % g)
                        W3 = w3(W, 8)
                        for u in range(NU):
                            acc = next_pa()
                            for kk in range(8):
                                k.mm(acc[0:TS, :], xnT[:, kk, utok(u)], W3[:, kk, :], start=(kk == 0), stop=(kk == 7))
                            k.cp(act if u % 2 else dve, vh_tm[0:TS, u, :], acc[0:TS, :])
                        W = w_get("GH%d" % g)
                        W3 = w3(W, 8)
                        for j in range(4):
                            acc = next_pa()
                            for kk in range(8):
                                k.mm(acc[:, 0:NT], W3[:, kk, j * 128:(j + 1) * 128], xnT[:, kk, :], start=(kk == 0), stop=(kk == 7))
                            k.actf(sgT[:, j, :], acc[:, 0:NT], AF.Silu)
                        set_pa([pa[0], pa[1]])
                        def hg_front(u):
                            o_sb = o_sbs[u % 2]
                            if sample:
                                k.dma(sp, S_st[:, hsl, :], I["sS"][u].rearrange("h c e -> c h e"))
                                k.cp(act, S_bf[0][:, hsl, :], S_st[:, hsl, :])
                            for j in range(4):
                                k.tr(ptrb[0:TS, j * 128:(j + 1) * 128], ktT[:, j, utok(u)], identB.v)
                            k.cp(act, kt_tm[0:TS, :], ptrb[0:TS, 0:512])
                            for j in range(4):
                                k.mm(pnum[0:TS, j * TS:(j + 1) * TS], ktT[:, j, utok(u)], qtT[:, j, utok(u)])
                            k.tt(dve, aT_sb[0:TS, :, 0:TS], pnum[0:TS, 0:4 * TS].rr("p (j t) -> p j t", j=4),
                                 V(hmask, hmask.h[0:TS, 0:TS].unsqueeze(1).to_broadcast([TS, 4, TS])), ALU.mult)
                            def s_update(c, dst_bf):
                                rows = slice(c * L, (c + 1) * L)
                                gci = u * NCH + c
                                for j in range(4):
                                    k.mm(pdc[:, j * 128:(j + 1) * 128], kt_tm[rows, j * 128:(j + 1) * 128],
                                         vh_tm[rows, u, j * 128:(j + 1) * 128])
                                k.tt(dve, S_st[:, hsl, :], S_st[:, hsl, :], pdc.v.rr("p (j e) -> p j e", j=4), ALU.add)
                                k.tt(dve, S_st[:, hsl, :], S_st[:, hsl, :],
                                     V(egl, egl.h[:, :, gci:gci + 1].to_broadcast([128, 4, 128])), ALU.mult)
                                if dst_bf is not None:
                                    k.cp(act, dst_bf[:, hsl, :], S_st[:, hsl, :])
                            for c in range(NCH - 1):
                                s_update(c, S_bf[c + 1])
                            for j in range(4):
                                hh = 4 * g + j
                                for c in range(NCH):
                                    rows = slice(c * L, (c + 1) * L)
                                    k.mm(pint[rows, j * 128:(j + 1) * 128], qtT[:, j, u * TS + c * L:u * TS + (c + 1) * L],
                                         S_bf[c][:, hh, :], start=True, stop=False, skip_group_check=True)
                                k.mm(pint[0:TS, j * 128:(j + 1) * 128], aT_sb[0:TS, j, 0:TS], vh_tm[0:TS, u, j * 128:(j + 1) * 128],
                                     start=False, stop=True, skip_group_check=True)
                            s_update(NCH - 1, None if sample else S_bf[0])
                            k.cp(act, o_sb[0:TS, :], pint[0:TS, :])
                            if sample:
                                k.dma(pool, O["S_s"][u].rearrange("h c e -> c h e"), S_st[:, hsl, :])
                            elif last and u == NU - 1:
                                k.dma(pool, O["S_p"].rearrange("h c e -> c h e"), S_st[:, hsl, :])

                        def hg_back(u):
                            o_sb = o_sbs[u % 2]
                            k.actf(sq_sb[0:TS, :], o_sb[0:TS, :], AF.Square)
                            k.op(dve, lambda: nc.vector.tensor_reduce(out=hs.h[0:TS, 0:4],
                                                                       in_=sq_sb.h[0:TS, :].rearrange("p (j e) -> p j e", j=4),
                                                                       axis=AX.X, op=ALU.add), [sq_sb.v], [hs.v])
                            k.actf(hs[0:TS, 4:8], hs[0:TS, 0:4], AF.Ln, scale=1.0 / HE, bias=EPS)
                            k.actf(hs[0:TS, 8:12], hs[0:TS, 4:8], AF.Exp, scale=-0.5)
                            k.tt(dve, on_sb[0:TS, :, :], o_sb[0:TS, :].rr("p (j e) -> p j e", j=4),
                                 V(hs, hs.h[0:TS, 8:12].unsqueeze(2).to_broadcast([TS, 4, 128])), ALU.mult)
                            for j in range(4):
                                k.tr(ptr[:, j * TS:(j + 1) * TS], on_sb[0:TS, j, :], identF[0:TS, 0:TS])
                            for j in range(4):
                                hh = 4 * g + j
                                k.stt(ohT[:, hh, utok(u)], ptr[:, j * TS:(j + 1) * TS], hg_col[:, hh:hh + 1], sgT[:, j, utok(u)],
                                      ALU.mult, ALU.mult)

                        hg_front(0)
                        for u in range(NU):
                            if u + 1 < NU:
                                hg_front(u + 1)
                            hg_back(u)
                    W = w_get("BRH")
                    W3 = w3(W, 4)
                    for b in range(NSB):
                        for q in range(2):
                            acc = next_pa()
                            for hh in range(4):
                                k.mm(acc.v, ohT[:, hh, blk(b)], W3[:, hh, q * 512:(q + 1) * 512], start=(hh == 0), stop=(hh == 3))
                            k.cp(act, yb_acc[:, b, q * 512:(q + 1) * 512], acc.v)
                    evB, xdB = exchange(NT, "b", yb_acc, NSB)
            k.barrier()

            with ExitStack() as pes:
                x2 = k.sb("x2", [128, NSB, D], es=pes)
                tT = k.sb("tT", [128, 8, NT], BF16, es=pes)
                pT = k.sb("pT", [128, 2, NT], BF16, es=pes)
                p_tm = k.sb("p_tm", [128, NSB, PLE], es=pes)
                bf_sb = k.sb("bf_sb", [128, D], BF16, es=pes)
                sg_sb = k.sb("sg2_sb", [128, 512], es=pes)
                sga = k.sb("sga", [128, NSB, D], es=pes)
                sgb = k.sb("sgb", [128, NSB, D], es=pes)
                outb = [k.sb("outb%d" % i, [128, D], es=pes) for i in range(2)]
                ssq = k.sb("ssq2", [128, 4], es=pes)
                set_pa([pa[0], pa[1], pnum, pint])
                for b in range(NSB):
                    k.dma(sp, x2[:, b, :], x_dram[b * 128:(b + 1) * 128, :])
                    k.dma(sp, p_tm[:, b, :], p_dram[b * 128:(b + 1) * 128, :])

                def to_T(dst, src_fn, nchunk):
                    for b in range(NSB):
                        k.cp(act, bf_sb[:, 0:nchunk * 128], src_fn(b))
                        for c in range(nchunk):
                            k.tr(ptrb[:, c * 128:(c + 1) * 128], bf_sb[:, c * 128:(c + 1) * 128], identB.v)
                        k.cp(dve, dst[:, 0:nchunk, blk(b)], ptrb[:, 0:nchunk * 128].rr("p (c t) -> p c t", c=nchunk))
                to_T(pT, lambda b: p_tm[:, b, :], 2)
                for nm_, sgt in (("GA", sga), ("GB", sgb)):
                    for q in range(2):
                        W = w_get("%s%d" % (nm_, q))
                        W3 = w3(W, 8)
                        for b in range(NSB):
                            acc = next_pa()
                            for kk in range(8):
                                k.mm(acc.v, xnT[:, kk, blk(b)], W3[:, kk, :], start=(kk == 0), stop=(kk == 7))
                            k.actf(sgt[:, b, q * 512:(q + 1) * 512], acc.v, AF.Sigmoid)
                sp.wait(evA)
                sp.wait(evB)
                for b in range(NSB):
                    k.dma(sp, y_acc[:, b, :], xdA[b * 128:(b + 1) * 128, :])
                    k.dma(sp, yb_acc[:, b, :], xdB[b * 128:(b + 1) * 128, :])
                    k.tt(dve, y_acc[:, b, :], y_acc[:, b, :], sga[:, b, :], ALU.mult)
                    k.tt(dve, yb_acc[:, b, :], yb_acc[:, b, :], sgb[:, b, :], ALU.mult)
                    k.tt(dve, y_acc[:, b, :], y_acc[:, b, :], yb_acc[:, b, :], ALU.add)
                to_T(tT, lambda b: y_acc[:, b, :], 8)
                for q in range(2):
                    W = w_get("OUT%d" % q)
                    W3 = w3(W, 8)
                    for b in range(NSB):
                        acc = next_pa()
                        for kk in range(8):
                            k.mm(acc.v, tT[:, kk, blk(b)], W3[:, kk, :], start=(kk == 0), stop=(kk == 7))
                        dst = x2[:, b, q * 512:(q + 1) * 512]
                        k.tt(dve, dst, dst, acc.v, ALU.add)
                to_T(tT, lambda b: x2[:, b, :], 8)
                Wp = [w_get("PG0"), w_get("PG1", hold=1), w_get("PLE", hold=2)]
                Wple = w3(Wp[2], 2, 0, 2048)
                for b in range(NSB):
                    for q in range(2):
                        W3 = w3(Wp[q], 8)
                        acc = next_pa()
                        for kk in range(8):
                            k.mm(acc.v, tT[:, kk, blk(b)], W3[:, kk, :], start=(kk == 0), stop=(kk == 7))
                        k.actf(sg_sb.v, acc.v, AF.Sigmoid)
                        acc2 = next_pa()
                        for kk in range(2):
                            k.mm(acc2.v, pT[:, kk, blk(b)], Wple[:, kk, q * 512:(q + 1) * 512], start=(kk == 0), stop=(kk == 1))
                        k.tt(dve, sg_sb.v, sg_sb.v, acc2.v, ALU.mult)
                        dst = x2[:, b, q * 512:(q + 1) * 512]
                        k.tt(pool, dst, dst, sg_sb.v, ALU.add)
                    ob = outb[b % 2]
                    k.actf(ob.v, x2[:, b, :], AF.Square, accum=ssq[:, 0:1])
                    k.actf(ssq[:, 1:2], ssq[:, 0:1], AF.Ln, scale=1.0 / D, bias=EPS)
                    k.actf(ssq[:, 2:3], ssq[:, 1:2], AF.Exp, scale=-0.5)
                    k.stt(ob.v, x2[:, b, :], ssq[:, 2:3], fg_bc.v, ALU.mult, ALU.mult)
                    k.dma(pool, y_dram[b * 128:(b + 1) * 128, :], ob.v)
                set_pa([pa[0], pa[1]])
            k.barrier()

    NTP = min(512, TP)
    ntile = TP // NTP if do_prompt else 0
    for ti in range(ntile):
        seq.extend(range(NW))
    if do_sample:
        seq.extend(range(NW))
    for ti in range(ntile):
        tile(I["xp"][ti * NTP:(ti + 1) * NTP, :], I["pp"][ti * NTP:(ti + 1) * NTP, :], O["yp"][ti * NTP:(ti + 1) * NTP, :],
             NTP, 128, NTP // 128, 1, ti == 0, ti == ntile - 1, False, tix=ti)
    if do_sample:
        tile(I["xs"], I["ps"], O["ys"], NS * 32, 32, NS, NS, True, True, True)
    for ev in k.out_events:
        sp.wait(ev)


_NC_CACHE = {}


def _in_map(core, inputs, NS):
    b = core // 2
    hf = core % 2
    s0 = (core // 2) * NS
    f = lambda a: np.ascontiguousarray(a, dtype=np.float32)
    hs = slice(2 * hf, 2 * hf + 2)
    cs = slice(hf * MIL, (hf + 1) * MIL)
    gs = slice(hf * HWL, (hf + 1) * HWL)
    w = inputs["w_in"][0]
    w_in = np.concatenate([
        w[:, 0 + hf * MIL:0 + (hf + 1) * MIL], w[:, 4096 + hf * MIL:4096 + (hf + 1) * MIL], w[:, 2048 + hf * MIL:2048 + (hf + 1) * MIL],
        w[:, 6144 + 2 * hf:6144 + 2 * hf + 2], w[:, 6148 + 2 * hf:6148 + 2 * hf + 2],
        w[:, 6152 + hf * HWL:6152 + (hf + 1) * HWL], w[:, 7176 + hf * HWL:7176 + (hf + 1) * HWL],
        w[:, 8200 + hf * HWL:8200 + (hf + 1) * HWL], w[:, 9224 + hf * HWL:9224 + (hf + 1) * HWL],
        w[:, 10248:12296]], axis=1)
    assert w_in.shape[1] == NINL
    m = dict(
        xp=f(inputs["x_prompt"][b]), pp=f(inputs["p_prompt"][0, b]),
        xs=f(inputs["x_sample"][s0:s0 + NS].reshape(NS * 32, D)), ps=f(inputs["p_sample"][0, s0:s0 + NS].reshape(NS * 32, PLE)),
        sconv=f(inputs["state_conv"][0, s0:s0 + NS][:, :, cs].reshape(NS * 3, MIL)), sC=f(inputs["state_mlstm_C"][0, s0:s0 + NS, hs]),
        sn=f(inputs["state_mlstm_n"][0, s0:s0 + NS, hs].reshape(NS, MHL * 4, 128)), sm=f(inputs["state_mlstm_m"][0, s0:s0 + NS, hs]),
        sS=f(inputs["state_hgrn"][0, s0:s0 + NS, 4 * hf:4 * hf + 4]),
        norm_g=f(inputs["norm_g"][0]), w_in=f(w_in), b_ig=f(inputs["b_ig"][0, hs]), b_fg=f(inputs["b_fg"][0, hs]),
        conv_w=f(inputs["conv_w"][0][:, cs]), conv_b=f(inputs["conv_b"][0, cs]), w_qm=f(inputs["w_qm"][0, hs]), w_km=f(inputs["w_km"][0, hs]),
        mnorm_g=f(inputs["mnorm_g"][0, cs]), m_skip=f(inputs["m_skip"][0, cs]), w_brm=f(inputs["w_brm"][0, cs]),
        hgrn_lb=f(inputs["hgrn_lb"][:, gs]), hnorm_g=f(inputs["hnorm_g"][0, gs]), w_brh=f(inputs["w_brh"][0, gs]), w_out=f(inputs["w_out"][0]),
        w_ple=f(inputs["w_ple"][0]), w_pg=f(inputs["w_pg"][0]), final_g=f(inputs["final_g"]),
    )
    return m


def kernel(**inputs):
    inputs = {k_: np.asarray(v) for k_, v in inputs.items()}
    B, TP = inputs["x_prompt"].shape[:2]
    NSEQ = inputs["x_sample"].shape[0]
    NCORE = 8
    NS = NSEQ // (NCORE // 2)
    key = (TP, NS)
    if key not in _NC_CACHE:
        _NC_CACHE[key] = build(TP, NS)
    nc = _NC_CACHE[key]
    in_maps = [_in_map(c, inputs, NS) for c in range(NCORE)]
    res = run_bass_kernel_spmd(nc, in_maps, core_ids=list(range(NCORE)))
    R = res.results
    f32 = np.float32
    NP = NCORE // 2
    cat2 = lambda name, fn, ax: [np.concatenate([fn(R[2 * p][name]), fn(R[2 * p + 1][name])], axis=ax) for p in range(NP)]
    y_prompt = np.stack([R[2 * b]["yp"] for b in range(B)]).astype(f32)
    y_sample = np.concatenate([R[2 * p]["ys"].reshape(NS, 32, D) for p in range(NP)]).astype(f32)
    conv_p = np.stack(cat2("conv_p", lambda a: a, 1))[None].astype(f32)
    C_p = np.stack(cat2("C_p", lambda a: a, 0))[None].astype(f32)
    n_p = np.stack(cat2("n_p", lambda a: a.reshape(MHL, HD), 0))[None].astype(f32)
    m_p = np.stack(cat2("m_p", lambda a: a.reshape(MHL), 0))[None].astype(f32)
    S_p = np.stack(cat2("S_p", lambda a: a, 0))[None].astype(f32)
    conv_s = np.concatenate(cat2("conv_s", lambda a: a.reshape(NS, 3, MIL), 2))[None].astype(f32)
    C_s = np.concatenate(cat2("C_s", lambda a: a, 1))[None].astype(f32)
    n_s = np.concatenate(cat2("n_s", lambda a: a.reshape(NS, MHL, HD), 1))[None].astype(f32)
    m_s = np.concatenate(cat2("m_s", lambda a: a.reshape(NS, MHL), 1))[None].astype(f32)
    S_s = np.concatenate(cat2("S_s", lambda a: a, 1))[None].astype(f32)
    return (y_prompt, y_sample, conv_p, C_p, n_p, m_p, S_p, conv_s, C_s, n_s, m_s, S_s)
```

```python
import numpy as np
from contextlib import ExitStack
import concourse.bass as bass
import concourse.mybir as mybir
from concourse.bass_utils import run_bass_kernel_spmd

F32 = mybir.dt.float32
BF16 = mybir.dt.bfloat16
AF = mybir.ActivationFunctionType
ALU = mybir.AluOpType
AX = mybir.AxisListType

D = 1024
MI = 2048
MH = 4
HD = 512
HH = 8
HE = 128
PLE = 256
NIN = 12296
MHL = 2
MIL = 1024
HGL = 4
HWL = 512
NINL = 7172
EPS = 1e-6
NEG = -30000.0


class V:
    __slots__ = ("t", "ap")

    def __init__(self, t, ap):
        self.t = t
        self.ap = ap

    def __getitem__(self, key):
        return V(self.t, self.ap[key])

    def bitcast(self, dt):
        return V(self.t, self.ap.bitcast(dt))

    def rr(self, pat, **kw):
        return V(self.t, self.ap.rearrange(pat, **kw))

    def bc(self, shape):
        return V(self.t, self.ap.to_broadcast(list(shape)))


class T:
    def __init__(self, h, name):
        self.h = h
        self.name = name
        self.w = None
        self.r = {}
        self.dsem = None
        self.dcnt = 0

    def __getitem__(self, key):
        return V(self, self.h[key])

    @property
    def v(self):
        return V(self, self.h[:])


class TA:
    def __init__(self, t, ap, name):
        self.t = t
        self.h = ap
        self.name = name

    def __getitem__(self, key):
        return V(self.t, self.h[key])

    @property
    def v(self):
        return V(self.t, self.h)


class Eng:
    def __init__(self, k, name, h, strict=False):
        self.k = k
        self.name = name
        self.h = h
        self.sem = k.new_sem("e_" + name)
        self.cnt = 0
        self.last = None
        self.seen = {}
        self.strict = strict

    def wait(self, ev):
        sem, val, key = ev
        prod = self.k.engs.get(key)
        if prod is not None and val > prod.cnt:
            assert prod.last is not None and val == prod.cnt + 1, (key, val, prod.cnt)
            prod.last.then_inc(prod.sem, 1)
            prod.last = None
            prod.cnt += 1
        if self.seen.get(key, 0) < val:
            self.h.wait_ge(sem, val)
            self.seen[key] = val


class K:
    def __init__(self, nc, es):
        self.nc = nc
        self.es = es
        self.nsem = 0
        self.pe = Eng(self, "pe", nc.tensor)
        self.act = Eng(self, "act", nc.scalar)
        self.dve = Eng(self, "dve", nc.vector)
        self.pool = Eng(self, "pool", nc.gpsimd, strict=True)
        self.sp = Eng(self, "sp", nc.sync)
        self.engs = {e.name: e for e in (self.pe, self.act, self.dve, self.pool, self.sp)}
        self.out_events = []
        self.dsems = {}
        self.uid = 0
        self.dma_last = {}

    def new_sem(self, name):
        self.nsem += 1
        return self.es.enter_context(self.nc.semaphore(name))

    def sb(self, name, shape, dt=F32, es=None):
        self.uid += 1
        h = (es or self.es).enter_context(self.nc.sbuf_tensor("%s_%d" % (name, self.uid), list(shape), dt))
        return T(h, name)

    def ps(self, name, shape, dt=F32):
        h = self.es.enter_context(self.nc.psum_tensor(name, list(shape), dt))
        return T(h, name)

    def dram(self, name, shape, dt):
        h = self.nc.dram_tensor(name, list(shape), dt, kind="Internal")
        return T(h, name)

    def _pre(self, eng, rd, wr, skip_key=None):
        for v in rd:
            t = v.t
            if t.w is not None:
                eng.wait(t.w)
        for v in wr:
            t = v.t
            if t.w is not None and t.w[2] != skip_key and (eng.strict or t.w[2] != eng.name):
                eng.wait(t.w)
            for key, ev in t.r.items():
                if eng.strict or key != eng.name:
                    eng.wait(ev)

    def _post(self, ev, rd, wr):
        for v in wr:
            v.t.w = ev
            v.t.r = {}
        for v in rd:
            if v.t.w is ev:
                continue
            v.t.r[ev[2]] = ev

    def op(self, eng, fn, rd, wr):
        rd = [v for v in rd if isinstance(v, V)]
        wr = [v for v in wr if isinstance(v, V)]
        self._pre(eng, rd, wr)
        ins = fn()
        eng.last = ins
        ev = (eng.sem, eng.cnt + 1, eng.name)
        self._post(ev, rd, wr)
        return ins

    def dma(self, q, out, in_, semt=None, **kw):
        rd = [in_] if isinstance(in_, V) else []
        wr = [out] if isinstance(out, V) else []
        st = semt or (wr[0].t if wr else rd[0].t)
        key = "d_" + st.name
        self._pre(q, rd, wr, skip_key=key)
        o = out.ap if isinstance(out, V) else out
        i = in_.ap if isinstance(in_, V) else in_
        ins = q.h.dma_start(out=o, in_=i, **kw)
        if key not in self.dsems:
            self.dsems[key] = [self.new_sem(key), 0]
        ent = self.dsems[key]
        ent[1] += 16
        ins.then_inc(ent[0], 16)
        ev = (ent[0], ent[1], key)
        self.dma_last[key] = ev
        self._post(ev, rd, wr)
        if not wr:
            self.out_events.append(ev)
        return ev

    def barrier(self):
        engs = [self.pe, self.act, self.dve, self.pool, self.sp]
        evs = [(e.sem, e.cnt + (1 if e.last is not None else 0), e.name) for e in engs
               if e.cnt > 0 or e.last is not None]
        for e in engs:
            for ev in evs:
                if ev[2] != e.name:
                    e.wait(ev)
            for ev in self.dma_last.values():
                e.wait(ev)

    def mm(self, out, lhsT, rhs, start=True, stop=True, **kw):
        return self.op(self.pe, lambda: self.nc.tensor.matmul(out.ap, lhsT=lhsT.ap, rhs=rhs.ap, start=start, stop=stop, **kw),
                       [lhsT, rhs], [out])

    def tr(self, out, in_, ident):
        return self.op(self.pe, lambda: self.nc.tensor.transpose(out.ap, in_.ap, ident.ap), [in_, ident], [out])

    def actf(self, out, in_, func, bias=None, scale=None, accum=None):
        kw = {}
        if bias is not None:
            kw["bias"] = bias.ap if isinstance(bias, V) else bias
        if scale is not None:
            kw["scale"] = scale.ap if isinstance(scale, V) else scale
        if accum is not None:
            kw["accum_out"] = accum.ap
        return self.op(self.act, lambda: self.nc.scalar.activation(out=out.ap, in_=in_.ap, func=func, **kw),
                       [in_, bias, scale], [out, accum])

    def _e(self, eng):
        return {"dve": self.dve, "pool": self.pool, "act": self.act}[eng] if isinstance(eng, str) else eng

    def tt(self, eng, out, a, b, op):
        e = self._e(eng)
        return self.op(e, lambda: e.h.tensor_tensor(out=out.ap, in0=a.ap, in1=b.ap, op=op), [a, b], [out])

    def ts(self, eng, out, a, s1, op0, s2=None, op1=None, accum=None):
        e = self._e(eng)
        a1 = s1.ap if isinstance(s1, V) else s1
        a2 = s2.ap if isinstance(s2, V) else s2
        kw = {}
        if op1 is not None:
            kw["op1"] = op1
        if accum is not None:
            kw["accum_out"] = accum.ap
        return self.op(e, lambda: e.h.tensor_scalar(out=out.ap, in0=a.ap, scalar1=a1, scalar2=a2, op0=op0, **kw),
                       [a, s1, s2], [out, accum])

    def stt(self, out, a, s, b, op0, op1):
        e = self.dve
        sa = s.ap if isinstance(s, V) else s
        return self.op(e, lambda: e.h.scalar_tensor_tensor(out=out.ap, in0=a.ap, scalar=sa, in1=b.ap, op0=op0, op1=op1),
                       [a, s, b], [out])

    def cp(self, eng, out, in_):
        e = self._e(eng)
        if e is self.act:
            return self.op(e, lambda: self.nc.scalar.copy(out=out.ap, in_=in_.ap), [in_], [out])
        return self.op(e, lambda: e.h.tensor_copy(out=out.ap, in_=in_.ap), [in_], [out])

    def memset(self, eng, out, val):
        e = self._e(eng)
        return self.op(e, lambda: e.h.memset(out.ap, val), [], [out])


def _wplan():
    plan = []
    for h in range(MHL):
        plan += [("U%d" % h, "in", 0 + h * 512), ("Z%d" % h, "in", 1024 + h * 512), ("V%d" % h, "in", 2048 + h * 512),
                 ("QK%d" % h, "qk", h)]
    for h in range(MHL):
        plan += [("BRM%d" % h, "brm", h)]
    plan += [("QH0", "in", 3076), ("FH0", "in", 3588), ("IH0", "in", 4100), ("GH0", "in", 4612)]
    plan += [("BRH", "brh", 0)]
    plan += [("GA0", "in", 5124), ("GA1", "in", 5124 + 512), ("GB0", "in", 6148), ("GB1", "in", 6148 + 512)]
    plan += [("OUT0", "sq", ("w_out", 0)), ("OUT1", "sq", ("w_out", 1)), ("PG0", "sq", ("w_pg", 0)), ("PG1", "sq", ("w_pg", 1)),
             ("PLE", "ple", 0)]
    return plan


def build(TP=4096, NS=8, dbg=None):
    nc = bass.Bass("TRN2", target_bir_lowering=False)
    dbg = dbg or {}

    def din(name, shape):
        return nc.dram_tensor(name, list(shape), F32, kind="ExternalInput").ap()

    def dout(name, shape):
        return nc.dram_tensor(name, list(shape), F32, kind="ExternalOutput").ap()

    I = dict(
        xp=din("xp", [TP, D]), pp=din("pp", [TP, PLE]),
        xs=din("xs", [NS * 32, D]), ps=din("ps", [NS * 32, PLE]),
        sconv=din("sconv", [NS * 3, MIL]), sC=din("sC", [NS, MHL, HD, HD]), sn=din("sn", [NS, MHL * 4, 128]),
        sm=din("sm", [NS, MHL]), sS=din("sS", [NS, HGL, HE, HE]),
        norm_g=din("norm_g", [D]), w_in=din("w_in", [D, NINL]), b_ig=din("b_ig", [MHL]), b_fg=din("b_fg", [MHL]),
        conv_w=din("conv_w", [4, MIL]), conv_b=din("conv_b", [MIL]), w_qm=din("w_qm", [MHL, HD, HD]),
        w_km=din("w_km", [MHL, HD, HD]), mnorm_g=din("mnorm_g", [MIL]), m_skip=din("m_skip", [MIL]),
        w_brm=din("w_brm", [MIL, D]), hgrn_lb=din("hgrn_lb", [2, HWL]), hnorm_g=din("hnorm_g", [HWL]),
        w_brh=din("w_brh", [HWL, D]), w_out=din("w_out", [D, D]), w_ple=din("w_ple", [PLE, D]), w_pg=din("w_pg", [D, D]),
        final_g=din("final_g", [D]),
    )
    O = dict(
        yp=dout("yp", [TP, D]), ys=dout("ys", [NS * 32, D]),
        conv_p=dout("conv_p", [3, MIL]), C_p=dout("C_p", [MHL, HD, HD]), n_p=dout("n_p", [MHL * 4, 128]),
        m_p=dout("m_p", [MHL, 1]), S_p=dout("S_p", [HGL, HE, HE]),
        conv_s=dout("conv_s", [NS * 3, MIL]), C_s=dout("C_s", [NS, MHL, HD, HD]), n_s=dout("n_s", [NS, MHL * 4, 128]),
        m_s=dout("m_s", [NS, MHL, 1]), S_s=dout("S_s", [NS, HGL, HE, HE]),
    )
    with ExitStack() as es:
        k = K(nc, es)
        _program(nc, k, I, O, TP, NS, dbg)
    return nc


def _program(nc, k, I, O, TP, NS, dbg):
    sp, pe, act, dve, pool = k.sp, k.pe, k.act, k.dve, k.pool
    plan = _wplan()
    NW = len(plan)
    do_prompt = dbg.get("prompt", True)
    do_sample = dbg.get("sample", True)

    NRING = 3
    ring = [k.sb("wring%d" % i, [128, 4096], BF16) for i in range(NRING)]
    wstate = dict(next_load=0, total=0)
    seq = []

    def w_issue(upto):
        while wstate["next_load"] < min(upto + 1, len(seq)):
            j = wstate["next_load"]
            pi = seq[j]
            sp.wait(grp_ev[pi // GRP])
            k.dma(sp, ring[j % NRING].v, wscr.h[pi])
            wstate["next_load"] += 1

    def w_get(key, hold=0):
        j = wstate["total"]
        assert plan[seq[j]][0] == key, (plan[seq[j]][0], key)
        w_issue(j + NRING - 1 - hold)
        wstate["total"] += 1
        return ring[j % NRING]

    def w3(w, kk, lo=0, hi=4096):
        return V(w, w.h[:, lo:hi].rearrange("p (k c) -> p k c", k=kk))

    identF = k.sb("identF", [128, 128])
    identB = k.sb("identB", [128, 128], BF16)
    utri = k.sb("utri", [128, 128])
    mnegT = k.sb("mnegT", [128, 128])
    maskbd = k.sb("maskbd", [128, 128])
    sel = k.sb("sel", [2, 2, 128])
    ones4 = k.sb("ones4", [2, 128])
    onesb = k.sb("onesb", [128, 1], BF16)
    m01p = k.sb("m01p", [128, 512])
    m01s = k.sb("m01s", [128, 256])

    def asel(t, pattern, cmp, fill, cm):
        k.op(pool, lambda: nc.gpsimd.affine_select(out=t.h[:], in_=t.h[:], pattern=pattern, compare_op=cmp, fill=fill,
                                                   base=0, channel_multiplier=cm), [t.v], [t.v])
    k.memset(pool, identF.v, 1.0)
    asel(identF, [[-1, 128]], ALU.is_equal, 0.0, 1)
    k.cp(pool, identB.v, identF.v)
    k.memset(pool, utri.v, 1.0)
    asel(utri, [[1, 128]], ALU.is_ge, 0.0, -1)
    k.memset(pool, mnegT.v, 0.0)
    asel(mnegT, [[1, 128]], ALU.is_ge, NEG, -1)
    k.cp(pool, maskbd.v, utri.v)
    k.memset(pool, maskbd[0:64, 64:128], 0.0)
    k.memset(pool, sel.v, 1.0)
    asel(sel, [[-1, 2], [0, 128]], ALU.is_equal, 0.0, 1)
    k.memset(pool, ones4.v, 1.0)
    k.memset(pool, onesb.v, 1.0)
    k.memset(pool, m01p.v, 1.0)
    k.memset(pool, m01p.v.rr("p (c l) -> p c l", l=64)[:, :, 0:1], 0.0)
    k.memset(pool, m01s.v, 1.0)
    k.memset(pool, m01s.v.rr("p (c l) -> p c l", l=32)[:, :, 0:1], 0.0)

    cst = T(None, "cst")

    def cload(name, shape, src, **kw):
        t = k.sb(name, shape)
        k.dma(act, t.v, src, semt=cst, **kw)
        return t
    slow = dict(allow_slow_non_contiguous=True)
    rows_sb = k.sb("rows_sb", [76, 128])
    cols = k.sb("cols", [128, 76])
    k.dma(sp, rows_sb[0:32, :], I["conv_w"].rearrange("j (c p) -> (j c) p", p=128), semt=cst)
    k.dma(sp, rows_sb[32:40, :], I["conv_b"].rearrange("(c p) -> c p", p=128), semt=cst)
    k.dma(sp, rows_sb[40:48, :], I["mnorm_g"].rearrange("(c p) -> c p", p=128), semt=cst)
    k.dma(sp, rows_sb[48:56, :], I["m_skip"].rearrange("(c p) -> c p", p=128), semt=cst)
    k.dma(sp, rows_sb[56:64, :], I["norm_g"].rearrange("(c p) -> c p", p=128), semt=cst)
    k.dma(sp, rows_sb[64:68, :], I["hnorm_g"].rearrange("(c p) -> c p", p=128), semt=cst)
    k.dma(sp, rows_sb[68:76, :], I["hgrn_lb"].rearrange("r (c p) -> (r c) p", p=128), semt=cst)
    cw_col = TA(cols, cols.h[:, 0:32].rearrange("p (j c) -> p c j", j=4), "cw_col")
    cb_col = TA(cols, cols.h[:, 32:40], "cb_col")
    mg_col = TA(cols, cols.h[:, 40:48], "mg_col")
    sk_col = TA(cols, cols.h[:, 48:56], "sk_col")
    ng_col = TA(cols, cols.h[:, 56:64], "ng_col")
    hg_col = TA(cols, cols.h[:, 64:68], "hg_col")
    lb_raw = TA(cols, cols.h[:, 68:76].rearrange("p (r c) -> p c r", r=2), "lb_raw")
    big_bc = cload("big_bc", [128, 2], I["b_ig"].partition_broadcast(128))
    bfg_bc = cload("bfg_bc", [128, 2], I["b_fg"].partition_broadcast(128))
    fg_bc = cload("fg_bc", [128, D], I["final_g"].partition_broadcast(128))
    for t_ in (rows_sb, big_bc, bfg_bc, fg_bc):
        t_.w = k.dma_last["d_cst"]
    lb_col = k.sb("lb_col", [128, 4])
    oml_col = k.sb("oml_col", [128, 4])
    noml_col = k.sb("noml_col", [128, 4])

    wscr = k.dram("wscr", [NW, 128, 4096], BF16)
    GRP = 1
    wgrp = [T(None, "wg%d" % g) for g in range((NW + GRP - 1) // GRP)]
    wg_sb = k.sb("wg_sb", [128, 8, 4], BF16)
    grp_ev = {}
    for i, (key, kind, arg) in enumerate(plan):
        dst = wscr.h[i]
        g = wgrp[i // GRP]
        if kind == "in":
            src = I["w_in"][:, arg:arg + 512].rearrange("(k p) c -> p k c", p=128)
            ev = k.dma(pool, dst.rearrange("p (k c) -> p k c", k=8), src, semt=g)
        elif kind == "qk":
            for j, wn in enumerate(("w_qm", "w_km")):
                src = I[wn][arg].rearrange("(k p) c -> p k c", p=128)
                ev = k.dma(pool, dst[:, j * 2048:(j + 1) * 2048].rearrange("p (k c) -> p k c", k=4), src, semt=g)
        elif kind == "brm":
            src = I["w_brm"][arg * 512:(arg + 1) * 512, :].rearrange("(k p) c -> p k c", p=128)
            ev = k.dma(pool, dst.rearrange("p (k c) -> p k c", k=4), src, semt=g)
        elif kind == "sq":
            src = I[arg[0]][:, arg[1] * 512:(arg[1] + 1) * 512].rearrange("(k p) c -> p k c", p=128)
            ev = k.dma(pool, dst.rearrange("p (k c) -> p k c", k=8), src, semt=g)
        elif kind == "brh":
            src = I["w_brh"].rearrange("(k p) c -> p k c", p=128)
            ev = k.dma(pool, dst.rearrange("p (k c) -> p k c", k=4), src, semt=g)
        elif kind == "ple":
            src = I["w_ple"].rearrange("(k p) c -> p k c", p=128)
            ev = k.dma(pool, dst[:, 0:2048].rearrange("p (k c) -> p k c", k=2), src, semt=g)
        grp_ev[i // GRP] = ev
        if i == 1:
            k.dma(pool, wg_sb.v, I["w_in"][:, 3072:3076].rearrange("(k p) c -> p k c", p=128), allow_slow_non_contiguous=True)
    k.out_events = []

    pa = [k.ps("pa%d" % i, [128, 512]) for i in range(2)]
    psm = k.ps("psm", [128, 512])
    pnum = k.ps("pnum", [128, 512])
    pint = k.ps("pint", [128, 512])
    pdc = k.ps("pdc", [128, 512])
    ptr = k.ps("ptr", [128, 512])
    ptrb = k.ps("ptrb", [128, 1024], BF16)
    stt_ = dict(pa=0, dc=0)
    k.tr(ptr[:, 0:76], rows_sb.v, identF[0:76, 0:76])
    k.cp(dve, cols.v, ptr[:, 0:76])
    k.tt(dve, lb_col.v, lb_raw[:, :, 0], lb_raw[:, :, 1], ALU.subtract)
    k.actf(lb_col.v, lb_col.v, AF.Sigmoid)
    k.ts(dve, oml_col.v, lb_col.v, -1.0, ALU.mult, 1.0, ALU.add)
    k.ts(dve, noml_col.v, oml_col.v, -1.0, ALU.mult)

    pa_list = [pa[0], pa[1]]

    def next_pa():
        stt_["pa"] = (stt_["pa"] + 1) % len(pa_list)
        return pa_list[stt_["pa"]]

    def set_pa(lst):
        pa_list[:] = lst
        stt_["pa"] = 0

    def next_dc():
        stt_["dc"] = (stt_["dc"] + 1) % 3
        return [pdc, pa[0], pa[1]][stt_["dc"]]

    C_nat = k.sb("C_nat", [128, MHL, 4, 512])
    CT_pp = [k.sb("CT_bf%d" % i, [128, MHL, 4, 512], BF16) for i in range(2)]
    n_col = k.sb("n_col", [128, 8, 8])
    S_st = k.sb("S_st", [128, HGL, HE])
    S_bf = [k.sb("S_bf%d" % i, [128, HGL, HE], BF16) for i in range(2)]
    hist = k.sb("hist", [128, 8, 8, 3])
    mcar = k.sb("mcar", [2, 8])
    cvo = k.sb("cvo", [24, MIL])
    nout = k.sb("nout", [8, 8, 128])

    xsrc = {(n_, w_): k.dram("xsrc_%d%s" % (n_, w_), [n_, D], F32) for n_ in (512, 256) for w_ in "ab"}
    xdst = {(n_, w_): k.dram("xdst_%d%s" % (n_, w_), [n_, D], F32) for n_ in (512, 256) for w_ in "ab"}

    def exchange(NT, which, src_t, NSB):
        xs_, xd_ = xsrc[(NT, which)].h, xdst[(NT, which)].h
        for b_ in range(NSB):
            pool.wait(k.dma(pool, xs_[b_ * 128:(b_ + 1) * 128, :], src_t[:, b_, :]))
        cci = nc.gpsimd.collective_compute("AllReduce", ALU.add, ins=[xs_[:, :]], outs=[xd_[:, :]], replica_groups=RG)
        ccst["n"] += 1
        cci.then_inc(ccsem)
        return (ccsem, ccst["n"], "cc"), xd_
    ccsem = k.new_sem("ccsem")
    ccst = dict(n=0)
    RG = [[0, 1], [2, 3], [4, 5], [6, 7]]

    def tile(x_dram, p_dram, y_dram, NT, TS, NU, NSEG, first, last, sample, tix=0):
        NSB = NT // 128
        SEG = NT // NSEG
        L = 32 if sample else 64
        NCH = TS // L
        m01 = m01s if sample else m01p
        hmask = utri if sample else maskbd

        def utok(u):
            return slice(u * TS, (u + 1) * TS)

        def blk(b):
            return slice(b * 128, (b + 1) * 128)

        with ExitStack() as tes:
            xnT = k.sb("xnT", [128, 8, NT], BF16, es=tes)
            y_acc = k.sb("y_acc", [128, NSB, D], es=tes)
            CT_cur, CT_nxt = CT_pp[tix % 2], CT_pp[(tix + 1) % 2]
            u_tm = k.sb("u_tm", [128, NU, 2], es=tes)
            negM = k.sb("negM", [2, NU, TS], es=tes)
            wi_tm = k.sb("wi_tm", [128, NU, 2], es=tes)
            cl_tm = k.sb("cl_tm", [128, NU, 2], es=tes)
            dec_bc = k.sb("dec_bc", [128, NU, 2], es=tes)
            wiT_all = k.sb("wiT_all", [2, NU, TS], es=tes)
            wl_all = k.sb("wl_all", [128, NU, 2], es=tes)
            wlb_all = k.sb("wlb_all", [128, NU, 2], BF16, es=tes)

            with ExitStack() as pes:
                x_tm = k.sb("x_tm", [128, NSB, D], es=pes)
                xs_b = k.sb("xs_b", [128, D], BF16, es=pes)
                junk = k.sb("junk", [128, D], es=pes)
                ssq = k.sb("ssq", [128, 4], es=pes)
                for b in range(NSB):
                    k.dma(sp, x_tm[:, b, :], x_dram[b * 128:(b + 1) * 128, :])
                if sample:
                    k.dma(sp, mcar[0:2, 0:NU], I["sm"].rearrange("s h -> h s"), allow_slow_non_contiguous=True)
                    sc_tm = k.sb("sc_tm", [24, MIL], es=pes)
                    NS3 = NSEG * 3
                    k.dma(sp, sc_tm[0:NSEG * 3, :], I["sconv"])
                    for grp in range(2):
                        for cc in range(4):
                            c16 = grp * 4 + cc
                            k.tr(ptr[:, cc * NS3:(cc + 1) * NS3], sc_tm[0:NS3, c16 * 128:(c16 + 1) * 128], identF[0:NS3, 0:NS3])
                        k.cp(act, hist[:, grp * 4:(grp + 1) * 4, 0:NSEG, :],
                             ptr[:, 0:4 * NS3].rr("p (c s j) -> p c s j", c=4, j=3))
                    nrow = k.sb("nrow", [8, 128], es=pes)
                    for u in range(NU):
                        k.dma(sp, nrow.v, I["sn"][u])
                        k.tr(psm[:, 256:264], nrow.v, identF[0:8, 0:8])
                        k.cp(act, n_col[:, u, :], psm[:, 256:264])
                elif first:
                    k.memset(pool, mcar.v, 0.0)
                    k.memset(pool, hist.v, 0.0)
                    k.memset(pool, n_col.v, 0.0)
                    k.memset(pool, S_st.v, 0.0)
                    k.memset(pool, S_bf[0].v, 0.0)
                for b in range(NSB):
                    k.actf(junk.v, x_tm[:, b, :], AF.Square, accum=ssq[:, 0:1])
                    k.actf(ssq[:, 1:2], ssq[:, 0:1], AF.Ln, scale=1.0 / D, bias=EPS)
                    k.actf(ssq[:, 2:3], ssq[:, 1:2], AF.Exp, scale=-0.5)
                    k.ts(dve, xs_b.v, x_tm[:, b, :], ssq[:, 2:3], ALU.mult)
                    for c in range(8):
                        k.tr(ptrb[:, c * 128:(c + 1) * 128], xs_b[:, c * 128:(c + 1) * 128], identB.v)
                    k.tt(dve, xnT[:, :, blk(b)], ptrb.v.rr("p (c t) -> p c t", c=8),
                         V(cols, ng_col.h.unsqueeze(2).to_broadcast([128, 8, 128])), ALU.mult)
            k.barrier()

            with ExitStack() as pes:
                u_exts = [k.sb("u_ext%d" % i, [128, NSEG, SEG + 3], es=pes) for i in range(2)]
                cv = k.sb("cv", [128, NSEG, SEG], es=pes)
                gs = [k.sb("gs%d" % i, [128, 2], es=pes) for i in range(8)]
                g_ig, g_fg, g_e1, g_sp, g_csp, g_d4, g_d4b, g_wl = gs
                gr = [k.sb("gr%d" % i, [2, TS], es=pes) for i in range(5)]
                g_uT, g_M, g_mT, g_wiT, g_clT = gr
                BS = [dict(cT=k.sb("cT%d" % i, [128, 4, NT], BF16, es=pes), szT=k.sb("szT%d" % i, [128, 4, NT], BF16, es=pes),
                           qT=k.sb("qT%d" % i, [128, 4, NT], BF16, es=pes), kT=k.sb("kT%d" % i, [128, 4, NT], BF16, es=pes),
                           k_tm=k.sb("k_tm%d" % i, [128, NU, 512], BF16, es=pes), v_tm=k.sb("v_tm%d" % i, [128, NU, 512], BF16, es=pes),
                           hmT=k.sb("hmT%d" % i, [128, 4, NT], BF16, es=pes)) for i in range(2)]
                pdc2 = V(ptrb, ptrb.h[:].bitcast(F32))
                dcst = dict(i=0)

                def next_dc2():
                    dcst["i"] ^= 1
                    return pdc.v if dcst["i"] else pdc2
                NCT = 2 if sample else NU - 1
                CTtmp = [k.sb("CTtmp%d" % i, [128, 4, 512], BF16, es=pes) for i in range(NCT)]
                n_bfs = k.sb("n_bfs", [128, NU + 1, 4], BF16, es=pes)
                dT_sb = [k.sb("dT_sb%d" % i, [128, 128], es=pes) for i in range(2)]
                sT_sb = [k.sb("sT_sb%d" % i, [128, 128], BF16, es=pes) for i in range(2)]
                qwT = [k.sb("qwT%d" % i, [128, 4, 128], BF16, es=pes) for i in range(2)]
                hraw = [k.sb("hraw%d" % i, [128, 512], es=pes) for i in range(2)]
                hn_sb = [k.sb("hn_sb%d" % i, [128, 512], es=pes) for i in range(2)]
                tmp_sb = [k.sb("tmp_sb%d" % i, [128, 4, 128], es=pes) for i in range(2)]
                wv_all = k.sb("wv_all", [128, NU, 512], BF16, es=pes)
                dd = [[k.sb("dd%d_%d" % (i, j), [128, 8], es=pes) for i in range(6)] for j in range(2)]
                SC = float(HD) ** -0.5
                cnt = dict(u=0)

                tb = dict(i=0)

                def make_CT3(h, dst):
                    for dc in range(4):
                        tb["i"] = (tb["i"] + 1) % 3
                        bank = [ptr, pnum, pint][tb["i"]]
                        for ec in range(4):
                            k.tr(bank[:, ec * 128:(ec + 1) * 128], C_nat[:, h, ec, dc * 128:(dc + 1) * 128], identF.v)
                        k.cp(act, dst[:, dc, :], bank.v)
                        yield

                def gate_gen():
                    for u in range(NU):
                        slot = u if sample else 0
                        G = psm[0:TS, 384:388]
                        for kk in range(8):
                            k.mm(G, xnT[:, kk, utok(u)], wg_sb[:, kk, :], start=(kk == 0), stop=(kk == 7))
                        k.tt(dve, g_ig[0:TS, :], G[:, 0:2], big_bc[0:TS, :], ALU.add)
                        k.tt(dve, g_fg[0:TS, :], G[:, 2:4], bfg_bc[0:TS, :], ALU.add)
                        k.actf(g_e1[0:TS, :], g_fg[0:TS, :], AF.Exp, scale=-1.0)
                        k.actf(g_sp[0:TS, :], g_e1[0:TS, :], AF.Ln, bias=1.0)
                        yield
                        k.mm(psm[0:TS, 392:394], utri[0:TS, 0:TS], g_sp[0:TS, :])
                        k.cp(act, g_csp[0:TS, :], psm[0:TS, 392:394])
                        yield
                        k.tt(dve, u_tm[0:TS, u, :], g_ig[0:TS, :], g_csp[0:TS, :], ALU.add)
                        k.tr(psm[0:2, 256:256 + TS], u_tm[0:TS, u, :], identF[0:TS, 0:TS])
                        k.cp(dve, g_uT.v, psm[0:2, 256:256 + TS])
                        yield
                        k.tr(psm[0:2, 128:128 + TS], g_csp[0:TS, :], identF[0:TS, 0:TS])
                        k.op(dve, lambda: nc.vector.tensor_tensor_scan(out=g_M.h[:], data0=g_uT.h[:], data1=g_uT.h[:],
                                                                        initial=mcar.h[0:2, slot:slot + 1], op0=ALU.max, op1=ALU.max),
                             [g_uT.v, mcar.v], [g_M.v])
                        k.ts(dve, negM[0:2, u, :], g_M.v, -1.0, ALU.mult)
                        yield
                        k.tt(dve, g_mT.v, g_M.v, psm[0:2, 128:128 + TS], ALU.subtract)
                        k.actf(g_wiT.v, g_M.v, AF.Exp, scale=-1.0, bias=mcar[0:2, slot:slot + 1])
                        k.actf(g_clT.v, g_mT.v, AF.Exp, scale=-1.0)
                        yield
                        k.cp(dve, mcar[0:2, slot:slot + 1], g_mT[0:2, TS - 1:TS])
                        k.tr(psm[0:TS, 396:398], g_wiT.v, identF[0:2, 0:2])
                        k.cp(act, wi_tm[0:TS, u, :], psm[0:TS, 396:398])
                        k.tr(psm[0:TS, 400:402], g_clT.v, identF[0:2, 0:2])
                        k.cp(act, cl_tm[0:TS, u, :], psm[0:TS, 400:402])
                        yield
                        k.ts(dve, g_d4[0:2, 0:2], identF[0:2, 0:2], g_wiT[0:2, TS - 1:TS], ALU.mult)
                        k.mm(psm[:, 404:406], ones4.v, g_d4[0:2, 0:2])
                        k.cp(act, dec_bc[:, u, :], psm[:, 404:406])
                        yield
                        k.cp(dve, wiT_all[0:2, u, :], g_wiT.v)
                        k.ts(dve, g_d4b[0:2, 0:2], identF[0:2, 0:2], negM[0:2, u, TS - 1:TS], ALU.mult)
                        k.mm(psm[:, 408:410], ones4.v, g_d4b[0:2, 0:2])
                        k.tt(dve, g_wl[0:TS, :], u_tm[0:TS, u, :], psm[0:TS, 408:410], ALU.add)
                        k.actf(wl_all[0:TS, u, :], g_wl[0:TS, :], AF.Exp)
                        k.cp(dve, wlb_all[0:TS, u, :], wl_all[0:TS, u, :])
                        yield
                        if sample or (last and u == NU - 1):
                            k.dma(pool, O["m_s"][u] if sample else O["m_p"], mcar[0:2, slot:slot + 1])
                def proj_gen(h, B_):
                    cT, szT, qT, kT, k_tm, v_tm, hmT = (B_[n_] for n_ in ('cT', 'szT', 'qT', 'kT', 'k_tm', 'v_tm', 'hmT'))
                    W = w_get("U%d" % h)
                    W3 = w3(W, 8)
                    for cc in range(4):
                        c16 = 4 * h + cc
                        acc = next_pa()
                        for kk in range(8):
                            k.mm(acc[:, 0:NT], W3[:, kk, cc * 128:(cc + 1) * 128], xnT[:, kk, :], start=(kk == 0), stop=(kk == 7))
                        u_ext = u_exts[cc % 2]
                        k.cp(pool, u_ext[:, :, 0:3], hist[:, c16, 0:NSEG, :])
                        k.cp(act, u_ext[:, :, 3:3 + SEG], acc[:, 0:NT].rr("p (s t) -> p s t", s=NSEG))
                        k.cp(pool, hist[:, c16, 0:NSEG, :], u_ext[:, :, SEG:SEG + 3])
                        k.actf(cv.v, u_ext[:, :, 0:SEG], AF.Identity, scale=cw_col[:, c16, 0:1], bias=cb_col[:, c16:c16 + 1])
                        for j in range(1, 4):
                            k.stt(cv.v, u_ext[:, :, j:j + SEG], cw_col[:, c16, j:j + 1], cv.v, ALU.mult, ALU.add)
                        k.actf(cT[:, cc, :].rr("p (s t) -> p s t", s=NSEG), cv.v, AF.Silu)
                        yield
                    W = w_get("Z%d" % h)
                    W3 = w3(W, 8)
                    for cc in range(4):
                        acc = next_pa()
                        for kk in range(8):
                            k.mm(acc[:, 0:NT], W3[:, kk, cc * 128:(cc + 1) * 128], xnT[:, kk, :], start=(kk == 0), stop=(kk == 7))
                        k.actf(szT[:, cc, :], acc[:, 0:NT], AF.Silu)
                        yield
                    W = w_get("V%d" % h)
                    W3 = w3(W, 8)
                    for u in range(NU):
                        acc = next_pa()
                        for kk in range(8):
                            k.mm(acc[0:TS, :], xnT[:, kk, utok(u)], W3[:, kk, :], start=(kk == 0), stop=(kk == 7))
                        k.cp(act if u % 2 else dve, v_tm[0:TS, u, :], acc[0:TS, :])
                        yield
                    W = w_get("QK%d" % h)
                    Wq = w3(W, 4, 0, 2048)
                    Wk = w3(W, 4, 2048, 4096)
                    for ec in range(4):
                        acc = next_pa()
                        for dc in range(4):
                            k.mm(acc[:, 0:NT], Wq[:, dc, ec * 128:(ec + 1) * 128], cT[:, dc, :], start=(dc == 0), stop=(dc == 3))
                        k.cp(act, qT[:, ec, :], acc[:, 0:NT])
                        acc = next_pa()
                        for dc in range(4):
                            k.mm(acc[:, 0:NT], Wk[:, dc, ec * 128:(ec + 1) * 128], cT[:, dc, :], start=(dc == 0), stop=(dc == 3))
                        k.ts(dve, kT[:, ec, :], acc[:, 0:NT], SC, ALU.mult)
                        yield
                    for u in range(NU):
                        acc = next_pa()
                        for dc in range(4):
                            k.mm(acc[0:TS, :], cT[:, dc, utok(u)], Wk[:, dc, :], start=(dc == 0), stop=(dc == 3))
                        k.ts(dve, k_tm[0:TS, u, :], acc[0:TS, :], SC, ALU.mult)
                        yield
                    for ec in range(4):
                        c16 = 4 * h + ec
                        k.stt(cT[:, ec, :], cT[:, ec, :], sk_col[:, c16:c16 + 1], szT[:, ec, :], ALU.mult, ALU.mult)
                        k.actf(szT[:, ec, :], szT[:, ec, :], AF.Copy, scale=mg_col[:, c16:c16 + 1])
                        yield


                def passes_gen(h, B_):
                    cT, szT, qT, kT, k_tm, v_tm, hmT = (B_[n_] for n_ in ('cT', 'szT', 'qT', 'kT', 'k_tm', 'v_tm', 'hmT'))
                    for u in range(NU):
                        k.actf(wv_all[0:TS, u, :], v_tm[0:TS, u, :], AF.Copy, scale=wl_all[0:TS, u, h:h + 1])

                    def state_pass(u):
                        slot = u if sample else 0
                        if sample:
                            k.dma(sp, C_nat[:, h], I["sC"][u, h].rearrange("(ec p) d -> p ec d", p=128))
                            yield from make_CT3(h, CTtmp[u % 2])
                        elif first and u == 0:
                            k.memset(pool, C_nat[:, h], 0.0)
                            k.memset(pool, CT_cur[:, h], 0.0)
                        k.cp(dve, n_bfs[:, u, :], n_col[:, slot, 4 * h:4 * h + 4])
                        for ec in range(4):
                            dcp = next_dc2()
                            k.mm(dcp, wv_all[0:TS, u, ec * 128:(ec + 1) * 128], k_tm[0:TS, u, :])
                            k.stt(C_nat[:, h, ec, :], C_nat[:, h, ec, :], dec_bc[:, u, h:h + 1], dcp, ALU.mult, ALU.add)
                            yield
                        for dc in range(4):
                            k.mm(psm[:, 420 + dc:421 + dc], k_tm[0:TS, u, dc * 128:(dc + 1) * 128], wlb_all[0:TS, u, h:h + 1])
                        k.stt(n_col[:, slot, 4 * h:4 * h + 4], n_col[:, slot, 4 * h:4 * h + 4], dec_bc[:, u, h:h + 1],
                              psm[:, 420:424], ALU.mult, ALU.add)
                        if sample:
                            k.dma(pool, O["C_s"][u, h].rearrange("(ec p) d -> p ec d", p=128), C_nat[:, h])
                        elif last and u == NU - 1:
                            k.dma(pool, O["C_p"][h].rearrange("(ec p) d -> p ec d", p=128), C_nat[:, h])
                        else:
                            yield from make_CT3(h, CTtmp[u] if u < NU - 1 else CT_nxt[:, h])

                    def h_group(us):
                        R_ = [(u_ % 2, u_) for u_ in us]
                        CTs = {u: (CTtmp[u % 2] if sample else (CT_cur[:, h] if u == 0 else CTtmp[u - 1])) for u in us}
                        hb = [ptr, ptr]
                        yield
                        for i2, u in R_:
                            nm = psm[0:TS, 0:TS]
                            k.mm(nm, sel[0:2, h, 0:TS], negM[0:2, u, :], start=True, stop=False)
                            k.mm(nm, identF[0:TS, 0:TS], mnegT[0:TS, 0:TS], start=False, stop=True)
                            kq = psm[0:TS, 128:128 + TS]
                            for ec in range(4):
                                k.mm(kq, kT[:, ec, utok(u)], qT[:, ec, utok(u)], start=(ec == 0), stop=(ec == 3))
                            k.mm(psm[:, 256:256 + TS], sel[0:2, h, :], wiT_all[0:2, u, :])
                        yield
                        for i2, u in R_:
                            k.actf(dT_sb[i2][0:TS, 0:TS], psm[0:TS, 0:TS], AF.Exp, bias=u_tm[0:TS, u, h:h + 1])
                        yield
                        for i2, u in R_:
                            k.tt(dve, sT_sb[i2][0:TS, 0:TS], psm[0:TS, 128:128 + TS], dT_sb[i2][0:TS, 0:TS], ALU.mult)
                            k.tt(dve, qwT[i2][:, :, 0:TS], qT[:, :, utok(u)],
                                 V(psm, psm.h[:, 256:256 + TS].unsqueeze(1).to_broadcast([128, 4, TS])), ALU.mult)
                        yield
                        for i2, u in R_:
                            pn = pnum if i2 == 0 else pint
                            k.mm(pn[0:TS, :], sT_sb[i2][0:TS, 0:TS], v_tm[0:TS, u, :], start=True, stop=False)
                            for dc in range(4):
                                k.mm(pn[0:TS, :], qwT[i2][:, dc, 0:TS], CTs[u][:, dc, :], start=False, stop=(dc == 3))
                            dn_ = psm[0:TS, 416 + i2:417 + i2]
                            k.mm(dn_, sT_sb[i2][0:TS, 0:TS], onesb[0:TS, :], start=True, stop=False)
                            for dc in range(4):
                                k.mm(dn_, qwT[i2][:, dc, 0:TS], n_bfs[:, u, dc:dc + 1], start=False, stop=(dc == 3))
                        yield
                        for i2, u in R_:
                            d_rs, d_den, d_aden, d_rden, d_st, d_mv = dd[i2]
                            dn_ = psm[0:TS, 416 + i2:417 + i2]
                            k.ts(dve, d_den[0:TS, 0:1], dn_, -1.0, ALU.mult)
                            k.tt(dve, d_aden[0:TS, 0:1], d_den[0:TS, 0:1], dn_, ALU.max)
                            k.tt(dve, d_aden[0:TS, 0:1], d_aden[0:TS, 0:1], cl_tm[0:TS, u, h:h + 1], ALU.max)
                            k.op(dve, lambda d_rden=d_rden, d_aden=d_aden: nc.vector.reciprocal(out=d_rden.h[0:TS, 0:1], in_=d_aden.h[0:TS, 0:1]),
                                 [d_aden.v], [d_rden.v])
                        yield
                        for i2, u in R_:
                            pn = pnum if i2 == 0 else pint
                            k.actf(hraw[i2][0:TS, :], pn[0:TS, :], AF.Copy, scale=dd[i2][3][0:TS, 0:1])
                        yield
                        for i2, u in R_:
                            d_rs, d_den, d_aden, d_rden, d_st, d_mv = dd[i2]
                            k.op(dve, lambda d_st=d_st, i2=i2: nc.vector.bn_stats(out=d_st.h[0:TS, 0:6], in_=hraw[i2].h[0:TS, :]), [hraw[i2].v], [d_st.v])
                            k.op(dve, lambda d_st=d_st, d_mv=d_mv: nc.vector.bn_aggr(out=d_mv.h[0:TS, 0:2], in_=d_st.h[0:TS, 0:6]), [d_st.v], [d_mv.v])
                        yield
                        for i2, u in R_:
                            d_rs, d_den, d_aden, d_rden, d_st, d_mv = dd[i2]
                            k.actf(d_rs[0:TS, 0:1], d_mv[0:TS, 1:2], AF.Ln, bias=EPS)
                            k.actf(d_rs[0:TS, 1:2], d_rs[0:TS, 0:1], AF.Exp, scale=-0.5)
                        yield
                        for i2, u in R_:
                            d_rs, d_den, d_aden, d_rden, d_st, d_mv = dd[i2]
                            k.ts(dve, hn_sb[i2][0:TS, :], hraw[i2][0:TS, :], d_mv[0:TS, 0:1], ALU.subtract, d_rs[0:TS, 1:2], ALU.mult)
                        yield
                        for i2, u in R_:
                            for ec in range(4):
                                k.tr(hb[i2][:, ec * TS:(ec + 1) * TS], hn_sb[i2][0:TS, ec * 128:(ec + 1) * 128], identF[0:TS, 0:TS])
                        yield
                        for i2, u in R_:
                            k.tt(dve, tmp_sb[i2][:, :, 0:TS], hb[i2][:, 0:4 * TS].rr("p (e t) -> p e t", e=4), szT[:, :, utok(u)], ALU.mult)
                            k.tt(pool, hmT[:, :, utok(u)], tmp_sb[i2][:, :, 0:TS], cT[:, :, utok(u)], ALU.add)


                    if sample:
                        for u in range(NU):
                            yield from state_pass(u)
                            yield from h_group([u])
                    else:
                        for u in range(NU):
                            yield from state_pass(u)
                        for u in range(NU):
                            yield from h_group([u])
                def brm_gen(h, B_):
                    cT, szT, qT, kT, k_tm, v_tm, hmT = (B_[n_] for n_ in ('cT', 'szT', 'qT', 'kT', 'k_tm', 'v_tm', 'hmT'))
                    W = w_get("BRM%d" % h)
                    W3 = w3(W, 4)
                    for b in range(NSB):
                        for q in range(2):
                            acc = next_pa()
                            for ec in range(4):
                                k.mm(acc.v, hmT[:, ec, blk(b)], W3[:, ec, q * 512:(q + 1) * 512], start=(ec == 0), stop=(ec == 3))
                            dst = y_acc[:, b, q * 512:(q + 1) * 512]
                            if h == 0:
                                k.cp(act, dst, acc.v)
                                yield
                            else:
                                k.tt(dve, dst, dst, acc.v, ALU.add)
                                yield


                def run(ga, gb, ra=1, rb=1):
                    gens = [g_ for g_ in (ga, gb) if g_ is not None]
                    rate = {id(ga): ra, id(gb): rb}
                    while gens:
                        for g_ in list(gens):
                            for _ in range(rate[id(g_)]):
                                try:
                                    next(g_)
                                except StopIteration:
                                    gens.remove(g_)
                                    break
                run(gate_gen(), proj_gen(0, BS[0]), 1, 1)
                run(passes_gen(0, BS[0]), proj_gen(1, BS[1]), 3, 1)
                run(passes_gen(1, BS[1]), brm_gen(0, BS[0]), 3, 1)
                run(brm_gen(1, BS[1]), None)
                for slot in range(NU if sample else 1):
                    if sample or last:
                        k.tr(psm[0:8, 256:384], n_col[:, slot, :], identF.v)
                        k.cp(act, nout[0:8, slot, :], psm[0:8, 256:384])
                        k.dma(pool, O["n_s"][slot] if sample else O["n_p"], nout[0:8, slot, :])
                if sample or last:
                    for grp in range(2):
                        for cc in range(4):
                            c16 = grp * 4 + cc
                            k.tr(ptr[0:NSEG * 3, cc * 128:(cc + 1) * 128], hist[:, c16, 0:NSEG, :].rr("p s j -> p (s j)"), identF.v)
                        k.cp(act, cvo[0:NSEG * 3, grp * 512:(grp + 1) * 512], ptr[0:NSEG * 3, :])
                    k.dma(pool, O["conv_s"] if sample else O["conv_p"], cvo[0:NSEG * 3, :])
                evA, xdA = exchange(NT, "a", y_acc, NSB)
            k.barrier()

            yb_acc = k.sb("yb_acc", [128, NSB, D], es=tes)
            if True:
                with ExitStack() as pes:
                    ohT = k.sb("ohT", [128, 4, NT], BF16, es=pes)
                    qs_sb = k.sb("qs_sb", [128, 4, NT], es=pes)
                    sigf = k.sb("sigf", [128, NT], es=pes)
                    lfh = k.sb("lfh", [128, NT], es=pes)
                    kh_sb = k.sb("kh_sb", [128, NT], es=pes)
                    g_sb = k.sb("g_sb", [128, NT], es=pes)
                    eg_sb = k.sb("eg_sb", [128, NT], es=pes)
                    eng_sb = k.sb("eng_sb", [128, NT], es=pes)
                    qtT = k.sb("qtT", [128, 4, NT], BF16, es=pes)
                    ktT = k.sb("ktT", [128, 4, NT], BF16, es=pes)
                    sgT = k.sb("sgT", [128, 4, NT], BF16, es=pes)
                    egl = k.sb("egl", [128, 4, NT // L], es=pes)
                    vh_tm = k.sb("vh_tm", [128, NU, 512], BF16, es=pes)
                    kt_tm = k.sb("kt_tm", [128, 512], BF16, es=pes)
                    aT_sb = k.sb("aT_sb", [128, 4, 128], BF16, es=pes)
                    o_sb = k.sb("o_sb", [128, 512], es=pes)
                    sq_sb = k.sb("sq_sb", [128, 512], es=pes)
                    on_sb = k.sb("on_sb", [128, 4, 128], es=pes)
                    hs = k.sb("hs", [128, 12], es=pes)
                    for g in range(1):
                        hsl = slice(4 * g, 4 * g + 4)
                        set_pa([pa[0], pa[1], pdc, ptr])
                        W = w_get("QH%d" % g)
                        W3 = w3(W, 8)
                        for j in range(4):
                            acc = next_pa()
                            for kk in range(8):
                                k.mm(acc[:, 0:NT], W3[:, kk, j * 128:(j + 1) * 128], xnT[:, kk, :], start=(kk == 0), stop=(kk == 7))
                            k.actf(qs_sb[:, j, :], acc[:, 0:NT], AF.Silu)
                        W = w_get("FH%d" % g)
                        W3 = w3(W, 8)
                        for j in range(4):
                            hh = 4 * g + j
                            acc = next_pa()
                            for kk in range(8):
                                k.mm(acc[:, 0:NT], W3[:, kk, j * 128:(j + 1) * 128], xnT[:, kk, :], start=(kk == 0), stop=(kk == 7))
                            k.actf(sigf.v, acc[:, 0:NT], AF.Sigmoid)
                            k.actf(lfh.v, sigf.v, AF.Ln, scale=oml_col[:, hh:hh + 1], bias=lb_col[:, hh:hh + 1])
                            k.ts(dve, kh_sb.v, sigf.v, noml_col[:, hh:hh + 1], ALU.mult, oml_col[:, hh:hh + 1], ALU.add)
                            k.op(dve, lambda: nc.vector.tensor_tensor_scan(out=g_sb.h[:], data0=m01.h[:, 0:NT], data1=lfh.h[:],
                                                                            initial=0.0, op0=ALU.mult, op1=ALU.add),
                                 [m01.v, lfh.v], [g_sb.v])
                            k.actf(eg_sb.v, g_sb.v, AF.Exp)
                            k.actf(eng_sb.v, g_sb.v, AF.Exp, scale=-1.0)
                            k.tt(pool, qtT[:, j, :], qs_sb[:, j, :], eg_sb.v, ALU.mult)
                            k.tt(dve, ktT[:, j, :], kh_sb.v, eng_sb.v, ALU.mult)
                            k.cp(dve, egl[:, j, :], eg_sb.v.rr("p (c l) -> p c l", l=L)[:, :, L - 1])
                        W = w_get("IH%d" % g)
                        W3 = w3(W, 8)
                        for u in range(NU):
                            acc = next_pa()
                            for kk in range(8):
                                k.mm(acc[0:TS, :], xnT[:, kk, utok(u)], W3[:, kk, :], start=(kk == 0), stop=(kk == 7))
                            k.cp(act if u % 2 else dve, vh_tm[0:TS, u, :], acc[0:TS, :])
                        W = w_get("GH%d" % g)
                        W3 = w3(W, 8)
                        for j in range(4):
                            acc = next_pa()
                            for kk in range(8):
                                k.mm(acc[:, 0:NT], W3[:, kk, j * 128:(j + 1) * 128], xnT[:, kk, :], start=(kk == 0), stop=(kk == 7))
                            k.actf(sgT[:, j, :], acc[:, 0:NT], AF.Silu)
                        set_pa([pa[0], pa[1]])
                        for u in range(NU):
                            if sample:
                                k.dma(sp, S_st[:, hsl, :], I["sS"][u].rearrange("h c e -> c h e"))
                                k.cp(act, S_bf[0][:, hsl, :], S_st[:, hsl, :])
                            for j in range(4):
                                k.tr(ptrb[0:TS, j * 128:(j + 1) * 128], ktT[:, j, utok(u)], identB.v)
                            k.cp(act, kt_tm[0:TS, :], ptrb[0:TS, 0:512])
                            for j in range(4):
                                k.mm(pnum[0:TS, j * TS:(j + 1) * TS], ktT[:, j, utok(u)], qtT[:, j, utok(u)])
                            k.tt(dve, aT_sb[0:TS, :, 0:TS], pnum[0:TS, 0:4 * TS].rr("p (j t) -> p j t", j=4),
                                 V(hmask, hmask.h[0:TS, 0:TS].unsqueeze(1).to_broadcast([TS, 4, TS])), ALU.mult)
                            def s_update(c, dst_bf):
                                rows = slice(c * L, (c + 1) * L)
                                gci = u * NCH + c
                                for j in range(4):
                                    k.mm(pdc[:, j * 128:(j + 1) * 128], kt_tm[rows, j * 128:(j + 1) * 128],
                                         vh_tm[rows, u, j * 128:(j + 1) * 128])
                                k.tt(dve, S_st[:, hsl, :], S_st[:, hsl, :], pdc.v.rr("p (j e) -> p j e", j=4), ALU.add)
                                k.tt(dve, S_st[:, hsl, :], S_st[:, hsl, :],
                                     V(egl, egl.h[:, :, gci:gci + 1].to_broadcast([128, 4, 128])), ALU.mult)
                                if dst_bf is not None:
                                    k.cp(act, dst_bf[:, hsl, :], S_st[:, hsl, :])
                            for c in range(NCH - 1):
                                s_update(c, S_bf[c + 1])
                            for j in range(4):
                                hh = 4 * g + j
                                for c in range(NCH):
                                    rows = slice(c * L, (c + 1) * L)
                                    k.mm(pint[rows, j * 128:(j + 1) * 128], qtT[:, j, u * TS + c * L:u * TS + (c + 1) * L],
                                         S_bf[c][:, hh, :], start=True, stop=False, skip_group_check=True)
                                k.mm(pint[0:TS, j * 128:(j + 1) * 128], aT_sb[0:TS, j, 0:TS], vh_tm[0:TS, u, j * 128:(j + 1) * 128],
                                     start=False, stop=True, skip_group_check=True)
                            s_update(NCH - 1, None if sample else S_bf[0])
                            k.cp(act, o_sb[0:TS, :], pint[0:TS, :])
                            k.tt(pool, sq_sb[0:TS, :], o_sb[0:TS, :], o_sb[0:TS, :], ALU.mult)
                            k.op(dve, lambda: nc.vector.tensor_reduce(out=hs.h[0:TS, 0:4],
                                                                       in_=sq_sb.h[0:TS, :].rearrange("p (j e) -> p j e", j=4),
                                                                       axis=AX.X, op=ALU.add), [sq_sb.v], [hs.v])
                            k.actf(hs[0:TS, 4:8], hs[0:TS, 0:4], AF.Ln, scale=1.0 / HE, bias=EPS)
                            k.actf(hs[0:TS, 8:12], hs[0:TS, 4:8], AF.Exp, scale=-0.5)
                            k.tt(dve, on_sb[0:TS, :, :], o_sb[0:TS, :].rr("p (j e) -> p j e", j=4),
                                 V(hs, hs.h[0:TS, 8:12].unsqueeze(2).to_broadcast([TS, 4, 128])), ALU.mult)
                            for j in range(4):
                                k.tr(ptr[:, j * TS:(j + 1) * TS], on_sb[0:TS, j, :], identF[0:TS, 0:TS])
                            for j in range(4):
                                hh = 4 * g + j
                                k.stt(ohT[:, hh, utok(u)], ptr[:, j * TS:(j + 1) * TS], hg_col[:, hh:hh + 1], sgT[:, j, utok(u)],
                                      ALU.mult, ALU.mult)
                            if sample:
                                k.dma(pool, O["S_s"][u].rearrange("h c e -> c h e"), S_st[:, hsl, :])
                            elif last and u == NU - 1:
                                k.dma(pool, O["S_p"].rearrange("h c e -> c h e"), S_st[:, hsl, :])
                    W = w_get("BRH")
                    W3 = w3(W, 4)
                    for b in range(NSB):
                        for q in range(2):
                            acc = next_pa()
                            for hh in range(4):
                                k.mm(acc.v, ohT[:, hh, blk(b)], W3[:, hh, q * 512:(q + 1) * 512], start=(hh == 0), stop=(hh == 3))
                            k.cp(act, yb_acc[:, b, q * 512:(q + 1) * 512], acc.v)
                    evB, xdB = exchange(NT, "b", yb_acc, NSB)
            k.barrier()

            with ExitStack() as pes:
                x2 = k.sb("x2", [128, NSB, D], es=pes)
                tT = k.sb("tT", [128, 8, NT], BF16, es=pes)
                pT = k.sb("pT", [128, 2, NT], BF16, es=pes)
                p_tm = k.sb("p_tm", [128, NSB, PLE], es=pes)
                bf_sb = k.sb("bf_sb", [128, D], BF16, es=pes)
                sg_sb = k.sb("sg2_sb", [128, 512], es=pes)
                sga = k.sb("sga", [128, NSB, D], es=pes)
                sgb = k.sb("sgb", [128, NSB, D], es=pes)
                outb = [k.sb("outb%d" % i, [128, D], es=pes) for i in range(2)]
                ssq = k.sb("ssq2", [128, 4], es=pes)
                set_pa([pa[0], pa[1], pnum, pint])
                for b in range(NSB):
                    k.dma(sp, x2[:, b, :], x_dram[b * 128:(b + 1) * 128, :])
                    k.dma(sp, p_tm[:, b, :], p_dram[b * 128:(b + 1) * 128, :])

                def to_T(dst, src_fn, nchunk):
                    for b in range(NSB):
                        k.cp(act, bf_sb[:, 0:nchunk * 128], src_fn(b))
                        for c in range(nchunk):
                            k.tr(ptrb[:, c * 128:(c + 1) * 128], bf_sb[:, c * 128:(c + 1) * 128], identB.v)
                        k.cp(dve, dst[:, 0:nchunk, blk(b)], ptrb[:, 0:nchunk * 128].rr("p (c t) -> p c t", c=nchunk))
                to_T(pT, lambda b: p_tm[:, b, :], 2)
                for nm_, sgt in (("GA", sga), ("GB", sgb)):
                    for q in range(2):
                        W = w_get("%s%d" % (nm_, q))
                        W3 = w3(W, 8)
                        for b in range(NSB):
                            acc = next_pa()
                            for kk in range(8):
                                k.mm(acc.v, xnT[:, kk, blk(b)], W3[:, kk, :], start=(kk == 0), stop=(kk == 7))
                            k.actf(sgt[:, b, q * 512:(q + 1) * 512], acc.v, AF.Sigmoid)
                sp.wait(evA)
                sp.wait(evB)
                for b in range(NSB):
                    k.dma(sp, y_acc[:, b, :], xdA[b * 128:(b + 1) * 128, :])
                    k.dma(sp, yb_acc[:, b, :], xdB[b * 128:(b + 1) * 128, :])
                    k.tt(dve, y_acc[:, b, :], y_acc[:, b, :], sga[:, b, :], ALU.mult)
                    k.tt(dve, yb_acc[:, b, :], yb_acc[:, b, :], sgb[:, b, :], ALU.mult)
                    k.tt(dve, y_acc[:, b, :], y_acc[:, b, :], yb_acc[:, b, :], ALU.add)
                to_T(tT, lambda b: y_acc[:, b, :], 8)
                for q in range(2):
                    W = w_get("OUT%d" % q)
                    W3 = w3(W, 8)
                    for b in range(NSB):
                        acc = next_pa()
                        for kk in range(8):
                            k.mm(acc.v, tT[:, kk, blk(b)], W3[:, kk, :], start=(kk == 0), stop=(kk == 7))
                        dst = x2[:, b, q * 512:(q + 1) * 512]
                        k.tt(dve, dst, dst, acc.v, ALU.add)
                to_T(tT, lambda b: x2[:, b, :], 8)
                Wp = [w_get("PG0"), w_get("PG1", hold=1), w_get("PLE", hold=2)]
                Wple = w3(Wp[2], 2, 0, 2048)
                for b in range(NSB):
                    for q in range(2):
                        W3 = w3(Wp[q], 8)
                        acc = next_pa()
                        for kk in range(8):
                            k.mm(acc.v, tT[:, kk, blk(b)], W3[:, kk, :], start=(kk == 0), stop=(kk == 7))
                        k.actf(sg_sb.v, acc.v, AF.Sigmoid)
                        acc2 = next_pa()
                        for kk in range(2):
                            k.mm(acc2.v, pT[:, kk, blk(b)], Wple[:, kk, q * 512:(q + 1) * 512], start=(kk == 0), stop=(kk == 1))
                        k.tt(dve, sg_sb.v, sg_sb.v, acc2.v, ALU.mult)
                        dst = x2[:, b, q * 512:(q + 1) * 512]
                        k.tt(pool, dst, dst, sg_sb.v, ALU.add)
                    ob = outb[b % 2]
                    k.actf(ob.v, x2[:, b, :], AF.Square, accum=ssq[:, 0:1])
                    k.actf(ssq[:, 1:2], ssq[:, 0:1], AF.Ln, scale=1.0 / D, bias=EPS)
                    k.actf(ssq[:, 2:3], ssq[:, 1:2], AF.Exp, scale=-0.5)
                    k.stt(ob.v, x2[:, b, :], ssq[:, 2:3], fg_bc.v, ALU.mult, ALU.mult)
                    k.dma(pool, y_dram[b * 128:(b + 1) * 128, :], ob.v)
                set_pa([pa[0], pa[1]])
            k.barrier()

    NTP = min(512, TP)
    ntile = TP // NTP if do_prompt else 0
    for ti in range(ntile):
        seq.extend(range(NW))
    if do_sample:
        seq.extend(range(NW))
    for ti in range(ntile):
        tile(I["xp"][ti * NTP:(ti + 1) * NTP, :], I["pp"][ti * NTP:(ti + 1) * NTP, :], O["yp"][ti * NTP:(ti + 1) * NTP, :],
             NTP, 128, NTP // 128, 1, ti == 0, ti == ntile - 1, False, tix=ti)
    if do_sample:
        tile(I["xs"], I["ps"], O["ys"], NS * 32, 32, NS, NS, True, True, True)
    for ev in k.out_events:
        sp.wait(ev)


_NC_CACHE = {}


def _in_map(core, inputs, NS):
    b = core // 2
    hf = core % 2
    s0 = (core // 2) * NS
    f = lambda a: np.ascontiguousarray(a, dtype=np.float32)
    hs = slice(2 * hf, 2 * hf + 2)
    cs = slice(hf * MIL, (hf + 1) * MIL)
    gs = slice(hf * HWL, (hf + 1) * HWL)
    w = inputs["w_in"][0]
    w_in = np.concatenate([
        w[:, 0 + hf * MIL:0 + (hf + 1) * MIL], w[:, 4096 + hf * MIL:4096 + (hf + 1) * MIL], w[:, 2048 + hf * MIL:2048 + (hf + 1) * MIL],
        w[:, 6144 + 2 * hf:6144 + 2 * hf + 2], w[:, 6148 + 2 * hf:6148 + 2 * hf + 2],
        w[:, 6152 + hf * HWL:6152 + (hf + 1) * HWL], w[:, 7176 + hf * HWL:7176 + (hf + 1) * HWL],
        w[:, 8200 + hf * HWL:8200 + (hf + 1) * HWL], w[:, 9224 + hf * HWL:9224 + (hf + 1) * HWL],
        w[:, 10248:12296]], axis=1)
    assert w_in.shape[1] == NINL
    m = dict(
        xp=f(inputs["x_prompt"][b]), pp=f(inputs["p_prompt"][0, b]),
        xs=f(inputs["x_sample"][s0:s0 + NS].reshape(NS * 32, D)), ps=f(inputs["p_sample"][0, s0:s0 + NS].reshape(NS * 32, PLE)),
        sconv=f(inputs["state_conv"][0, s0:s0 + NS][:, :, cs].reshape(NS * 3, MIL)), sC=f(inputs["state_mlstm_C"][0, s0:s0 + NS, hs]),
        sn=f(inputs["state_mlstm_n"][0, s0:s0 + NS, hs].reshape(NS, MHL * 4, 128)), sm=f(inputs["state_mlstm_m"][0, s0:s0 + NS, hs]),
        sS=f(inputs["state_hgrn"][0, s0:s0 + NS, 4 * hf:4 * hf + 4]),
        norm_g=f(inputs["norm_g"][0]), w_in=f(w_in), b_ig=f(inputs["b_ig"][0, hs]), b_fg=f(inputs["b_fg"][0, hs]),
        conv_w=f(inputs["conv_w"][0][:, cs]), conv_b=f(inputs["conv_b"][0, cs]), w_qm=f(inputs["w_qm"][0, hs]), w_km=f(inputs["w_km"][0, hs]),
        mnorm_g=f(inputs["mnorm_g"][0, cs]), m_skip=f(inputs["m_skip"][0, cs]), w_brm=f(inputs["w_brm"][0, cs]),
        hgrn_lb=f(inputs["hgrn_lb"][:, gs]), hnorm_g=f(inputs["hnorm_g"][0, gs]), w_brh=f(inputs["w_brh"][0, gs]), w_out=f(inputs["w_out"][0]),
        w_ple=f(inputs["w_ple"][0]), w_pg=f(inputs["w_pg"][0]), final_g=f(inputs["final_g"]),
    )
    return m


def kernel(**inputs):
    inputs = {k_: np.asarray(v) for k_, v in inputs.items()}
    B, TP = inputs["x_prompt"].shape[:2]
    NSEQ = inputs["x_sample"].shape[0]
    NCORE = 8
    NS = NSEQ // (NCORE // 2)
    key = (TP, NS)
    if key not in _NC_CACHE:
        _NC_CACHE[key] = build(TP, NS)
    nc = _NC_CACHE[key]
    in_maps = [_in_map(c, inputs, NS) for c in range(NCORE)]
    res = run_bass_kernel_spmd(nc, in_maps, core_ids=list(range(NCORE)))
    R = res.results
    f32 = np.float32
    NP = NCORE // 2
    cat2 = lambda name, fn, ax: [np.concatenate([fn(R[2 * p][name]), fn(R[2 * p + 1][name])], axis=ax) for p in range(NP)]
    y_prompt = np.stack([R[2 * b]["yp"] for b in range(B)]).astype(f32)
    y_sample = np.concatenate([R[2 * p]["ys"].reshape(NS, 32, D) for p in range(NP)]).astype(f32)
    conv_p = np.stack(cat2("conv_p", lambda a: a, 1))[None].astype(f32)
    C_p = np.stack(cat2("C_p", lambda a: a, 0))[None].astype(f32)
    n_p = np.stack(cat2("n_p", lambda a: a.reshape(MHL, HD), 0))[None].astype(f32)
    m_p = np.stack(cat2("m_p", lambda a: a.reshape(MHL), 0))[None].astype(f32)
    S_p = np.stack(cat2("S_p", lambda a: a, 0))[None].astype(f32)
    conv_s = np.concatenate(cat2("conv_s", lambda a: a.reshape(NS, 3, MIL), 2))[None].astype(f32)
    C_s = np.concatenate(cat2("C_s", lambda a: a, 1))[None].astype(f32)
    n_s = np.concatenate(cat2("n_s", lambda a: a.reshape(NS, MHL, HD), 1))[None].astype(f32)
    m_s = np.concatenate(cat2("m_s", lambda a: a.reshape(NS, MHL), 1))[None].astype(f32)
    S_s = np.concatenate(cat2("S_s", lambda a: a, 1))[None].astype(f32)
    return (y_prompt, y_sample, conv_p, C_p, n_p, m_p, S_p, conv_s, C_s, n_s, m_s, S_s)
```

```python
import numpy as np
from contextlib import ExitStack
import concourse.bass as bass
import concourse.mybir as mybir
from concourse.bass_utils import run_bass_kernel_spmd

F32 = mybir.dt.float32
BF16 = mybir.dt.bfloat16
AF = mybir.ActivationFunctionType
ALU = mybir.AluOpType
AX = mybir.AxisListType

D = 1024
MI = 2048
MH = 4
HD = 512
HH = 8
HE = 128
PLE = 256
NIN = 12296
MHL = 2
MIL = 1024
HGL = 4
HWL = 512
NINL = 7172
EPS = 1e-6
NEG = -30000.0


class V:
    __slots__ = ("t", "ap")

    def __init__(self, t, ap):
        self.t = t
        self.ap = ap

    def __getitem__(self, key):
        return V(self.t, self.ap[key])

    def bitcast(self, dt):
        return V(self.t, self.ap.bitcast(dt))

    def rr(self, pat, **kw):
        return V(self.t, self.ap.rearrange(pat, **kw))

    def bc(self, shape):
        return V(self.t, self.ap.to_broadcast(list(shape)))


class T:
    def __init__(self, h, name):
        self.h = h
        self.name = name
        self.w = None
        self.r = {}
        self.dsem = None
        self.dcnt = 0

    def __getitem__(self, key):
        return V(self, self.h[key])

    @property
    def v(self):
        return V(self, self.h[:])


class TA:
    def __init__(self, t, ap, name):
        self.t = t
        self.h = ap
        self.name = name

    def __getitem__(self, key):
        return V(self.t, self.h[key])

    @property
    def v(self):
        return V(self.t, self.h)


class Eng:
    def __init__(self, k, name, h, strict=False):
        self.k = k
        self.name = name
        self.h = h
        self.sem = k.new_sem("e_" + name)
        self.cnt = 0
        self.last = None
        self.seen = {}
        self.strict = strict

    def wait(self, ev):
        sem, val, key = ev
        prod = self.k.engs.get(key)
        if prod is not None and val > prod.cnt:
            assert prod.last is not None and val == prod.cnt + 1, (key, val, prod.cnt)
            prod.last.then_inc(prod.sem, 1)
            prod.last = None
            prod.cnt += 1
        if self.seen.get(key, 0) < val:
            self.h.wait_ge(sem, val)
            self.seen[key] = val


class K:
    def __init__(self, nc, es):
        self.nc = nc
        self.es = es
        self.nsem = 0
        self.pe = Eng(self, "pe", nc.tensor)
        self.act = Eng(self, "act", nc.scalar)
        self.dve = Eng(self, "dve", nc.vector)
        self.pool = Eng(self, "pool", nc.gpsimd, strict=True)
        self.sp = Eng(self, "sp", nc.sync)
        self.engs = {e.name: e for e in (self.pe, self.act, self.dve, self.pool, self.sp)}
        self.out_events = []
        self.dsems = {}
        self.uid = 0
        self.dma_last = {}

    def new_sem(self, name):
        self.nsem += 1
        return self.es.enter_context(self.nc.semaphore(name))

    def sb(self, name, shape, dt=F32, es=None):
        self.uid += 1
        h = (es or self.es).enter_context(self.nc.sbuf_tensor("%s_%d" % (name, self.uid), list(shape), dt))
        return T(h, name)

    def ps(self, name, shape, dt=F32):
        h = self.es.enter_context(self.nc.psum_tensor(name, list(shape), dt))
        return T(h, name)

    def dram(self, name, shape, dt):
        h = self.nc.dram_tensor(name, list(shape), dt, kind="Internal")
        return T(h, name)

    def _pre(self, eng, rd, wr, skip_key=None):
        for v in rd:
            t = v.t
            if t.w is not None:
                eng.wait(t.w)
        for v in wr:
            t = v.t
            if t.w is not None and t.w[2] != skip_key and (eng.strict or t.w[2] != eng.name):
                eng.wait(t.w)
            for key, ev in t.r.items():
                if eng.strict or key != eng.name:
                    eng.wait(ev)

    def _post(self, ev, rd, wr):
        for v in wr:
            v.t.w = ev
            v.t.r = {}
        for v in rd:
            if v.t.w is ev:
                continue
            v.t.r[ev[2]] = ev

    def op(self, eng, fn, rd, wr):
        rd = [v for v in rd if isinstance(v, V)]
        wr = [v for v in wr if isinstance(v, V)]
        self._pre(eng, rd, wr)
        ins = fn()
        eng.last = ins
        ev = (eng.sem, eng.cnt + 1, eng.name)
        self._post(ev, rd, wr)
        return ins

    def dma(self, q, out, in_, semt=None, **kw):
        rd = [in_] if isinstance(in_, V) else []
        wr = [out] if isinstance(out, V) else []
        st = semt or (wr[0].t if wr else rd[0].t)
        key = "d_" + st.name
        self._pre(q, rd, wr, skip_key=key)
        o = out.ap if isinstance(out, V) else out
        i = in_.ap if isinstance(in_, V) else in_
        ins = q.h.dma_start(out=o, in_=i, **kw)
        if key not in self.dsems:
            self.dsems[key] = [self.new_sem(key), 0]
        ent = self.dsems[key]
        ent[1] += 16
        ins.then_inc(ent[0], 16)
        ev = (ent[0], ent[1], key)
        self.dma_last[key] = ev
        self._post(ev, rd, wr)
        if not wr:
            self.out_events.append(ev)
        return ev

    def barrier(self):
        engs = [self.pe, self.act, self.dve, self.pool, self.sp]
        evs = [(e.sem, e.cnt + (1 if e.last is not None else 0), e.name) for e in engs
               if e.cnt > 0 or e.last is not None]
        for e in engs:
            for ev in evs:
                if ev[2] != e.name:
                    e.wait(ev)
            for ev in self.dma_last.values():
                e.wait(ev)

    def mm(self, out, lhsT, rhs, start=True, stop=True, **kw):
        return self.op(self.pe, lambda: self.nc.tensor.matmul(out.ap, lhsT=lhsT.ap, rhs=rhs.ap, start=start, stop=stop, **kw),
                       [lhsT, rhs], [out])

    def tr(self, out, in_, ident):
        return self.op(self.pe, lambda: self.nc.tensor.transpose(out.ap, in_.ap, ident.ap), [in_, ident], [out])

    def actf(self, out, in_, func, bias=None, scale=None, accum=None):
        kw = {}
        if bias is not None:
            kw["bias"] = bias.ap if isinstance(bias, V) else bias
        if scale is not None:
            kw["scale"] = scale.ap if isinstance(scale, V) else scale
        if accum is not None:
            kw["accum_out"] = accum.ap
        return self.op(self.act, lambda: self.nc.scalar.activation(out=out.ap, in_=in_.ap, func=func, **kw),
                       [in_, bias, scale], [out, accum])

    def _e(self, eng):
        return {"dve": self.dve, "pool": self.pool, "act": self.act}[eng] if isinstance(eng, str) else eng

    def tt(self, eng, out, a, b, op):
        e = self._e(eng)
        return self.op(e, lambda: e.h.tensor_tensor(out=out.ap, in0=a.ap, in1=b.ap, op=op), [a, b], [out])

    def ts(self, eng, out, a, s1, op0, s2=None, op1=None, accum=None):
        e = self._e(eng)
        a1 = s1.ap if isinstance(s1, V) else s1
        a2 = s2.ap if isinstance(s2, V) else s2
        kw = {}
        if op1 is not None:
            kw["op1"] = op1
        if accum is not None:
            kw["accum_out"] = accum.ap
        return self.op(e, lambda: e.h.tensor_scalar(out=out.ap, in0=a.ap, scalar1=a1, scalar2=a2, op0=op0, **kw),
                       [a, s1, s2], [out, accum])

    def stt(self, out, a, s, b, op0, op1):
        e = self.dve
        sa = s.ap if isinstance(s, V) else s
        return self.op(e, lambda: e.h.scalar_tensor_tensor(out=out.ap, in0=a.ap, scalar=sa, in1=b.ap, op0=op0, op1=op1),
                       [a, s, b], [out])

    def cp(self, eng, out, in_):
        e = self._e(eng)
        if e is self.act:
            return self.op(e, lambda: self.nc.scalar.copy(out=out.ap, in_=in_.ap), [in_], [out])
        return self.op(e, lambda: e.h.tensor_copy(out=out.ap, in_=in_.ap), [in_], [out])

    def memset(self, eng, out, val):
        e = self._e(eng)
        return self.op(e, lambda: e.h.memset(out.ap, val), [], [out])


def _wplan():
    plan = []
    for h in range(MHL):
        plan += [("U%d" % h, "in", 0 + h * 512), ("Z%d" % h, "in", 1024 + h * 512), ("V%d" % h, "in", 2048 + h * 512),
                 ("QK%d" % h, "qk", h)]
    for h in range(MHL):
        plan += [("BRM%d" % h, "brm", h)]
    plan += [("QH0", "in", 3076), ("FH0", "in", 3588), ("IH0", "in", 4100), ("GH0", "in", 4612)]
    plan += [("BRH", "brh", 0)]
    plan += [("GA0", "in", 5124), ("GA1", "in", 5124 + 512), ("GB0", "in", 6148), ("GB1", "in", 6148 + 512)]
    plan += [("OUT0", "sq", ("w_out", 0)), ("OUT1", "sq", ("w_out", 1)), ("PG0", "sq", ("w_pg", 0)), ("PG1", "sq", ("w_pg", 1)),
             ("PLE", "ple", 0)]
    return plan


def build(TP=4096, NS=8, dbg=None):
    nc = bass.Bass("TRN2", target_bir_lowering=False)
    dbg = dbg or {}

    def din(name, shape):
        return nc.dram_tensor(name, list(shape), F32, kind="ExternalInput").ap()

    def dout(name, shape):
        return nc.dram_tensor(name, list(shape), F32, kind="ExternalOutput").ap()

    I = dict(
        xp=din("xp", [TP, D]), pp=din("pp", [TP, PLE]),
        xs=din("xs", [NS * 32, D]), ps=din("ps", [NS * 32, PLE]),
        sconv=din("sconv", [NS * 3, MIL]), sC=din("sC", [NS, MHL, HD, HD]), sn=din("sn", [NS, MHL * 4, 128]),
        sm=din("sm", [NS, MHL]), sS=din("sS", [NS, HGL, HE, HE]),
        norm_g=din("norm_g", [D]), w_in=din("w_in", [D, NINL]), b_ig=din("b_ig", [MHL]), b_fg=din("b_fg", [MHL]),
        conv_w=din("conv_w", [4, MIL]), conv_b=din("conv_b", [MIL]), w_qm=din("w_qm", [MHL, HD, HD]),
        w_km=din("w_km", [MHL, HD, HD]), mnorm_g=din("mnorm_g", [MIL]), m_skip=din("m_skip", [MIL]),
        w_brm=din("w_brm", [MIL, D]), hgrn_lb=din("hgrn_lb", [2, HWL]), hnorm_g=din("hnorm_g", [HWL]),
        w_brh=din("w_brh", [HWL, D]), w_out=din("w_out", [D, D]), w_ple=din("w_ple", [PLE, D]), w_pg=din("w_pg", [D, D]),
        final_g=din("final_g", [D]),
    )
    O = dict(
        yp=dout("yp", [TP, D]), ys=dout("ys", [NS * 32, D]),
        conv_p=dout("conv_p", [3, MIL]), C_p=dout("C_p", [MHL, HD, HD]), n_p=dout("n_p", [MHL * 4, 128]),
        m_p=dout("m_p", [MHL, 1]), S_p=dout("S_p", [HGL, HE, HE]),
        conv_s=dout("conv_s", [NS * 3, MIL]), C_s=dout("C_s", [NS, MHL, HD, HD]), n_s=dout("n_s", [NS, MHL * 4, 128]),
        m_s=dout("m_s", [NS, MHL, 1]), S_s=dout("S_s", [NS, HGL, HE, HE]),
    )
    with ExitStack() as es:
        k = K(nc, es)
        _program(nc, k, I, O, TP, NS, dbg)
    return nc


def _program(nc, k, I, O, TP, NS, dbg):
    sp, pe, act, dve, pool = k.sp, k.pe, k.act, k.dve, k.pool
    plan = _wplan()
    NW = len(plan)
    do_prompt = dbg.get("prompt", True)
    do_sample = dbg.get("sample", True)

    NRING = 3
    ring = [k.sb("wring%d" % i, [128, 4096], BF16) for i in range(NRING)]
    wstate = dict(next_load=0, total=0)
    seq = []

    def w_issue(upto):
        while wstate["next_load"] < min(upto + 1, len(seq)):
            j = wstate["next_load"]
            pi = seq[j]
            sp.wait(grp_ev[pi // GRP])
            k.dma(sp, ring[j % NRING].v, wscr.h[pi])
            wstate["next_load"] += 1

    def w_get(key, hold=0):
        j = wstate["total"]
        assert plan[seq[j]][0] == key, (plan[seq[j]][0], key)
        w_issue(j + NRING - 1 - hold)
        wstate["total"] += 1
        return ring[j % NRING]

    def w3(w, kk, lo=0, hi=4096):
        return V(w, w.h[:, lo:hi].rearrange("p (k c) -> p k c", k=kk))

    identF = k.sb("identF", [128, 128])
    identB = k.sb("identB", [128, 128], BF16)
    utri = k.sb("utri", [128, 128])
    mnegT = k.sb("mnegT", [128, 128])
    maskbd = k.sb("maskbd", [128, 128])
    sel = k.sb("sel", [2, 2, 128])
    ones4 = k.sb("ones4", [2, 128])
    onesb = k.sb("onesb", [128, 1], BF16)
    m01p = k.sb("m01p", [128, 512])
    m01s = k.sb("m01s", [128, 256])

    def asel(t, pattern, cmp, fill, cm):
        k.op(pool, lambda: nc.gpsimd.affine_select(out=t.h[:], in_=t.h[:], pattern=pattern, compare_op=cmp, fill=fill,
                                                   base=0, channel_multiplier=cm), [t.v], [t.v])
    k.memset(pool, identF.v, 1.0)
    asel(identF, [[-1, 128]], ALU.is_equal, 0.0, 1)
    k.cp(pool, identB.v, identF.v)
    k.memset(pool, utri.v, 1.0)
    asel(utri, [[1, 128]], ALU.is_ge, 0.0, -1)
    k.memset(pool, mnegT.v, 0.0)
    asel(mnegT, [[1, 128]], ALU.is_ge, NEG, -1)
    k.cp(pool, maskbd.v, utri.v)
    k.memset(pool, maskbd[0:64, 64:128], 0.0)
    k.memset(pool, sel.v, 1.0)
    asel(sel, [[-1, 2], [0, 128]], ALU.is_equal, 0.0, 1)
    k.memset(pool, ones4.v, 1.0)
    k.memset(pool, onesb.v, 1.0)
    k.memset(pool, m01p.v, 1.0)
    k.memset(pool, m01p.v.rr("p (c l) -> p c l", l=64)[:, :, 0:1], 0.0)
    k.memset(pool, m01s.v, 1.0)
    k.memset(pool, m01s.v.rr("p (c l) -> p c l", l=32)[:, :, 0:1], 0.0)

    cst = T(None, "cst")

    def cload(name, shape, src, **kw):
        t = k.sb(name, shape)
        k.dma(act, t.v, src, semt=cst, **kw)
        return t
    slow = dict(allow_slow_non_contiguous=True)
    rows_sb = k.sb("rows_sb", [76, 128])
    cols = k.sb("cols", [128, 76])
    k.dma(sp, rows_sb[0:32, :], I["conv_w"].rearrange("j (c p) -> (j c) p", p=128), semt=cst)
    k.dma(sp, rows_sb[32:40, :], I["conv_b"].rearrange("(c p) -> c p", p=128), semt=cst)
    k.dma(sp, rows_sb[40:48, :], I["mnorm_g"].rearrange("(c p) -> c p", p=128), semt=cst)
    k.dma(sp, rows_sb[48:56, :], I["m_skip"].rearrange("(c p) -> c p", p=128), semt=cst)
    k.dma(sp, rows_sb[56:64, :], I["norm_g"].rearrange("(c p) -> c p", p=128), semt=cst)
    k.dma(sp, rows_sb[64:68, :], I["hnorm_g"].rearrange("(c p) -> c p", p=128), semt=cst)
    k.dma(sp, rows_sb[68:76, :], I["hgrn_lb"].rearrange("r (c p) -> (r c) p", p=128), semt=cst)
    cw_col = TA(cols, cols.h[:, 0:32].rearrange("p (j c) -> p c j", j=4), "cw_col")
    cb_col = TA(cols, cols.h[:, 32:40], "cb_col")
    mg_col = TA(cols, cols.h[:, 40:48], "mg_col")
    sk_col = TA(cols, cols.h[:, 48:56], "sk_col")
    ng_col = TA(cols, cols.h[:, 56:64], "ng_col")
    hg_col = TA(cols, cols.h[:, 64:68], "hg_col")
    lb_raw = TA(cols, cols.h[:, 68:76].rearrange("p (r c) -> p c r", r=2), "lb_raw")
    big_bc = cload("big_bc", [128, 2], I["b_ig"].partition_broadcast(128))
    bfg_bc = cload("bfg_bc", [128, 2], I["b_fg"].partition_broadcast(128))
    fg_bc = cload("fg_bc", [128, D], I["final_g"].partition_broadcast(128))
    for t_ in (rows_sb, big_bc, bfg_bc, fg_bc):
        t_.w = k.dma_last["d_cst"]
    lb_col = k.sb("lb_col", [128, 4])
    oml_col = k.sb("oml_col", [128, 4])
    noml_col = k.sb("noml_col", [128, 4])

    wscr = k.dram("wscr", [NW, 128, 4096], BF16)
    GRP = 1
    wgrp = [T(None, "wg%d" % g) for g in range((NW + GRP - 1) // GRP)]
    wg_sb = k.sb("wg_sb", [128, 8, 4], BF16)
    grp_ev = {}
    for i, (key, kind, arg) in enumerate(plan):
        dst = wscr.h[i]
        g = wgrp[i // GRP]
        if kind == "in":
            src = I["w_in"][:, arg:arg + 512].rearrange("(k p) c -> p k c", p=128)
            ev = k.dma(pool, dst.rearrange("p (k c) -> p k c", k=8), src, semt=g)
        elif kind == "qk":
            for j, wn in enumerate(("w_qm", "w_km")):
                src = I[wn][arg].rearrange("(k p) c -> p k c", p=128)
                ev = k.dma(pool, dst[:, j * 2048:(j + 1) * 2048].rearrange("p (k c) -> p k c", k=4), src, semt=g)
        elif kind == "brm":
            src = I["w_brm"][arg * 512:(arg + 1) * 512, :].rearrange("(k p) c -> p k c", p=128)
            ev = k.dma(pool, dst.rearrange("p (k c) -> p k c", k=4), src, semt=g)
        elif kind == "sq":
            src = I[arg[0]][:, arg[1] * 512:(arg[1] + 1) * 512].rearrange("(k p) c -> p k c", p=128)
            ev = k.dma(pool, dst.rearrange("p (k c) -> p k c", k=8), src, semt=g)
        elif kind == "brh":
            src = I["w_brh"].rearrange("(k p) c -> p k c", p=128)
            ev = k.dma(pool, dst.rearrange("p (k c) -> p k c", k=4), src, semt=g)
        elif kind == "ple":
            src = I["w_ple"].rearrange("(k p) c -> p k c", p=128)
            ev = k.dma(pool, dst[:, 0:2048].rearrange("p (k c) -> p k c", k=2), src, semt=g)
        grp_ev[i // GRP] = ev
        if i == 1:
            k.dma(pool, wg_sb.v, I["w_in"][:, 3072:3076].rearrange("(k p) c -> p k c", p=128), allow_slow_non_contiguous=True)
    k.out_events = []

    pa = [k.ps("pa%d" % i, [128, 512]) for i in range(2)]
    psm = k.ps("psm", [128, 512])
    pnum = k.ps("pnum", [128, 512])
    pint = k.ps("pint", [128, 512])
    pdc = k.ps("pdc", [128, 512])
    ptr = k.ps("ptr", [128, 512])
    ptrb = k.ps("ptrb", [128, 1024], BF16)
    stt_ = dict(pa=0, dc=0)
    k.tr(ptr[:, 0:76], rows_sb.v, identF[0:76, 0:76])
    k.cp(dve, cols.v, ptr[:, 0:76])
    k.tt(dve, lb_col.v, lb_raw[:, :, 0], lb_raw[:, :, 1], ALU.subtract)
    k.actf(lb_col.v, lb_col.v, AF.Sigmoid)
    k.ts(dve, oml_col.v, lb_col.v, -1.0, ALU.mult, 1.0, ALU.add)
    k.ts(dve, noml_col.v, oml_col.v, -1.0, ALU.mult)

    pa_list = [pa[0], pa[1]]

    def next_pa():
        stt_["pa"] = (stt_["pa"] + 1) % len(pa_list)
        return pa_list[stt_["pa"]]

    def set_pa(lst):
        pa_list[:] = lst
        stt_["pa"] = 0

    def next_dc():
        stt_["dc"] = (stt_["dc"] + 1) % 3
        return [pdc, pa[0], pa[1]][stt_["dc"]]

    C_nat = k.sb("C_nat", [128, MHL, 4, 512])
    CT_pp = [k.sb("CT_bf%d" % i, [128, MHL, 4, 512], BF16) for i in range(2)]
    n_col = k.sb("n_col", [128, 8, 8])
    S_st = k.sb("S_st", [128, HGL, HE])
    S_bf = [k.sb("S_bf%d" % i, [128, HGL, HE], BF16) for i in range(2)]
    hist = k.sb("hist", [128, 8, 8, 3])
    mcar = k.sb("mcar", [2, 8])
    cvo = k.sb("cvo", [24, MIL])
    nout = k.sb("nout", [8, 8, 128])

    xsrc = {(n_, w_): k.dram("xsrc_%d%s" % (n_, w_), [n_, D], F32) for n_ in (512, 256) for w_ in "ab"}
    xdst = {(n_, w_): k.dram("xdst_%d%s" % (n_, w_), [n_, D], F32) for n_ in (512, 256) for w_ in "ab"}

    def exchange(NT, which, src_t, NSB):
        xs_, xd_ = xsrc[(NT, which)].h, xdst[(NT, which)].h
        ev_ = None
        for b_ in range(NSB):
            ev_ = k.dma(pool, xs_[b_ * 128:(b_ + 1) * 128, :], src_t[:, b_, :])
        pool.wait(ev_)
        cci = nc.gpsimd.collective_compute("AllReduce", ALU.add, ins=[xs_[:, :]], outs=[xd_[:, :]], replica_groups=RG)
        ccst["n"] += 1
        cci.then_inc(ccsem)
        return (ccsem, ccst["n"], "cc"), xd_
    ccsem = k.new_sem("ccsem")
    ccst = dict(n=0)
    RG = [[0, 1], [2, 3], [4, 5], [6, 7]]

    def tile(x_dram, p_dram, y_dram, NT, TS, NU, NSEG, first, last, sample, tix=0):
        NSB = NT // 128
        SEG = NT // NSEG
        L = 32 if sample else 64
        NCH = TS // L
        m01 = m01s if sample else m01p
        hmask = utri if sample else maskbd

        def utok(u):
            return slice(u * TS, (u + 1) * TS)

        def blk(b):
            return slice(b * 128, (b + 1) * 128)

        with ExitStack() as tes:
            xnT = k.sb("xnT", [128, 8, NT], BF16, es=tes)
            y_acc = k.sb("y_acc", [128, NSB, D], es=tes)
            CT_cur, CT_nxt = CT_pp[tix % 2], CT_pp[(tix + 1) % 2]
            u_tm = k.sb("u_tm", [128, NU, 2], es=tes)
            negM = k.sb("negM", [2, NU, TS], es=tes)
            wi_tm = k.sb("wi_tm", [128, NU, 2], es=tes)
            cl_tm = k.sb("cl_tm", [128, NU, 2], es=tes)
            dec_bc = k.sb("dec_bc", [128, NU, 2], es=tes)
            wiT_all = k.sb("wiT_all", [2, NU, TS], es=tes)
            wl_all = k.sb("wl_all", [128, NU, 2], es=tes)
            wlb_all = k.sb("wlb_all", [128, NU, 2], BF16, es=tes)

            with ExitStack() as pes:
                x_tm = k.sb("x_tm", [128, NSB, D], es=pes)
                xs_b = k.sb("xs_b", [128, D], BF16, es=pes)
                junk = k.sb("junk", [128, D], es=pes)
                ssq = k.sb("ssq", [128, 4], es=pes)
                for b in range(NSB):
                    k.dma(sp, x_tm[:, b, :], x_dram[b * 128:(b + 1) * 128, :])
                if sample:
                    k.dma(sp, mcar[0:2, 0:NU], I["sm"].rearrange("s h -> h s"), allow_slow_non_contiguous=True)
                    sc_tm = k.sb("sc_tm", [24, MIL], es=pes)
                    NS3 = NSEG * 3
                    k.dma(sp, sc_tm[0:NSEG * 3, :], I["sconv"])
                    for grp in range(2):
                        for cc in range(4):
                            c16 = grp * 4 + cc
                            k.tr(ptr[:, cc * NS3:(cc + 1) * NS3], sc_tm[0:NS3, c16 * 128:(c16 + 1) * 128], identF[0:NS3, 0:NS3])
                        k.cp(act, hist[:, grp * 4:(grp + 1) * 4, 0:NSEG, :],
                             ptr[:, 0:4 * NS3].rr("p (c s j) -> p c s j", c=4, j=3))
                    nrow = k.sb("nrow", [8, 128], es=pes)
                    for u in range(NU):
                        k.dma(sp, nrow.v, I["sn"][u])
                        k.tr(psm[:, 256:264], nrow.v, identF[0:8, 0:8])
                        k.cp(act, n_col[:, u, :], psm[:, 256:264])
                elif first:
                    k.memset(pool, mcar.v, 0.0)
                    k.memset(pool, hist.v, 0.0)
                    k.memset(pool, n_col.v, 0.0)
                    k.memset(pool, S_st.v, 0.0)
                    k.memset(pool, S_bf[0].v, 0.0)
                for b in range(NSB):
                    k.actf(junk.v, x_tm[:, b, :], AF.Square, accum=ssq[:, 0:1])
                    k.actf(ssq[:, 1:2], ssq[:, 0:1], AF.Ln, scale=1.0 / D, bias=EPS)
                    k.actf(ssq[:, 2:3], ssq[:, 1:2], AF.Exp, scale=-0.5)
                    k.ts(dve, xs_b.v, x_tm[:, b, :], ssq[:, 2:3], ALU.mult)
                    for c in range(8):
                        k.tr(ptrb[:, c * 128:(c + 1) * 128], xs_b[:, c * 128:(c + 1) * 128], identB.v)
                    k.tt(dve, xnT[:, :, blk(b)], ptrb.v.rr("p (c t) -> p c t", c=8),
                         V(cols, ng_col.h.unsqueeze(2).to_broadcast([128, 8, 128])), ALU.mult)
            k.barrier()

            with ExitStack() as pes:
                u_exts = [k.sb("u_ext%d" % i, [128, NSEG, SEG + 3], es=pes) for i in range(2)]
                cv = k.sb("cv", [128, NSEG, SEG], es=pes)
                gs = [k.sb("gs%d" % i, [128, 2], es=pes) for i in range(8)]
                g_ig, g_fg, g_e1, g_sp, g_csp, g_d4, g_d4b, g_wl = gs
                gr = [k.sb("gr%d" % i, [2, TS], es=pes) for i in range(5)]
                g_uT, g_M, g_mT, g_wiT, g_clT = gr
                BS = [dict(cT=k.sb("cT%d" % i, [128, 4, NT], BF16, es=pes), szT=k.sb("szT%d" % i, [128, 4, NT], BF16, es=pes),
                           qT=k.sb("qT%d" % i, [128, 4, NT], BF16, es=pes), kT=k.sb("kT%d" % i, [128, 4, NT], BF16, es=pes),
                           k_tm=k.sb("k_tm%d" % i, [128, NU, 512], BF16, es=pes), v_tm=k.sb("v_tm%d" % i, [128, NU, 512], BF16, es=pes),
                           hmT=k.sb("hmT%d" % i, [128, 4, NT], BF16, es=pes)) for i in range(2)]
                pdc2 = V(ptrb, ptrb.h[:].bitcast(F32))
                dcst = dict(i=0)

                def next_dc2():
                    dcst["i"] ^= 1
                    return pdc.v if dcst["i"] else pdc2
                NCT = 2 if sample else NU - 1
                CTtmp = [k.sb("CTtmp%d" % i, [128, 4, 512], BF16, es=pes) for i in range(NCT)]
                n_bfs = k.sb("n_bfs", [128, NU + 1, 4], BF16, es=pes)
                dT_sb = [k.sb("dT_sb%d" % i, [128, 128], es=pes) for i in range(2)]
                sT_sb = [k.sb("sT_sb%d" % i, [128, 128], BF16, es=pes) for i in range(2)]
                qwT = [k.sb("qwT%d" % i, [128, 4, 128], BF16, es=pes) for i in range(2)]
                hraw = [k.sb("hraw%d" % i, [128, 512], es=pes) for i in range(2)]
                hn_sb = [k.sb("hn_sb%d" % i, [128, 512], es=pes) for i in range(2)]
                tmp_sb = [k.sb("tmp_sb%d" % i, [128, 4, 128], es=pes) for i in range(2)]
                wv_all = k.sb("wv_all", [128, NU, 512], BF16, es=pes)
                dd = [[k.sb("dd%d_%d" % (i, j), [128, 8], es=pes) for i in range(6)] for j in range(2)]
                SC = float(HD) ** -0.5
                cnt = dict(u=0)

                tb = dict(i=0)

                def make_CT3(h, dst):
                    for dc in range(4):
                        tb["i"] = (tb["i"] + 1) % 3
                        bank = [ptr, pnum, pint][tb["i"]]
                        for ec in range(4):
                            k.tr(bank[:, ec * 128:(ec + 1) * 128], C_nat[:, h, ec, dc * 128:(dc + 1) * 128], identF.v)
                        k.cp(act, dst[:, dc, :], bank.v)
                        yield

                def gate_gen():
                    for u in range(NU):
                        slot = u if sample else 0
                        G = psm[0:TS, 384:388]
                        for kk in range(8):
                            k.mm(G, xnT[:, kk, utok(u)], wg_sb[:, kk, :], start=(kk == 0), stop=(kk == 7))
                        k.tt(dve, g_ig[0:TS, :], G[:, 0:2], big_bc[0:TS, :], ALU.add)
                        k.tt(dve, g_fg[0:TS, :], G[:, 2:4], bfg_bc[0:TS, :], ALU.add)
                        k.actf(g_e1[0:TS, :], g_fg[0:TS, :], AF.Exp, scale=-1.0)
                        k.actf(g_sp[0:TS, :], g_e1[0:TS, :], AF.Ln, bias=1.0)
                        yield
                        k.mm(psm[0:TS, 392:394], utri[0:TS, 0:TS], g_sp[0:TS, :])
                        k.cp(act, g_csp[0:TS, :], psm[0:TS, 392:394])
                        yield
                        k.tt(dve, u_tm[0:TS, u, :], g_ig[0:TS, :], g_csp[0:TS, :], ALU.add)
                        k.tr(psm[0:2, 256:256 + TS], u_tm[0:TS, u, :], identF[0:TS, 0:TS])
                        k.cp(dve, g_uT.v, psm[0:2, 256:256 + TS])
                        yield
                        k.tr(psm[0:2, 128:128 + TS], g_csp[0:TS, :], identF[0:TS, 0:TS])
                        k.op(dve, lambda: nc.vector.tensor_tensor_scan(out=g_M.h[:], data0=g_uT.h[:], data1=g_uT.h[:],
                                                                        initial=mcar.h[0:2, slot:slot + 1], op0=ALU.max, op1=ALU.max),
                             [g_uT.v, mcar.v], [g_M.v])
                        k.ts(dve, negM[0:2, u, :], g_M.v, -1.0, ALU.mult)
                        yield
                        k.tt(dve, g_mT.v, g_M.v, psm[0:2, 128:128 + TS], ALU.subtract)
                        k.actf(g_wiT.v, g_M.v, AF.Exp, scale=-1.0, bias=mcar[0:2, slot:slot + 1])
                        k.actf(g_clT.v, g_mT.v, AF.Exp, scale=-1.0)
                        yield
                        k.cp(dve, mcar[0:2, slot:slot + 1], g_mT[0:2, TS - 1:TS])
                        k.tr(psm[0:TS, 396:398], g_wiT.v, identF[0:2, 0:2])
                        k.cp(act, wi_tm[0:TS, u, :], psm[0:TS, 396:398])
                        k.tr(psm[0:TS, 400:402], g_clT.v, identF[0:2, 0:2])
                        k.cp(act, cl_tm[0:TS, u, :], psm[0:TS, 400:402])
                        yield
                        k.ts(dve, g_d4[0:2, 0:2], identF[0:2, 0:2], g_wiT[0:2, TS - 1:TS], ALU.mult)
                        k.mm(psm[:, 404:406], ones4.v, g_d4[0:2, 0:2])
                        k.cp(act, dec_bc[:, u, :], psm[:, 404:406])
                        yield
                        k.cp(dve, wiT_all[0:2, u, :], g_wiT.v)
                        k.ts(dve, g_d4b[0:2, 0:2], identF[0:2, 0:2], negM[0:2, u, TS - 1:TS], ALU.mult)
                        k.mm(psm[:, 408:410], ones4.v, g_d4b[0:2, 0:2])
                        k.tt(dve, g_wl[0:TS, :], u_tm[0:TS, u, :], psm[0:TS, 408:410], ALU.add)
                        k.actf(wl_all[0:TS, u, :], g_wl[0:TS, :], AF.Exp)
                        k.cp(dve, wlb_all[0:TS, u, :], wl_all[0:TS, u, :])
                        yield
                        if sample or (last and u == NU - 1):
                            k.dma(pool, O["m_s"][u] if sample else O["m_p"], mcar[0:2, slot:slot + 1])
                def proj_gen(h, B_):
                    cT, szT, qT, kT, k_tm, v_tm, hmT = (B_[n_] for n_ in ('cT', 'szT', 'qT', 'kT', 'k_tm', 'v_tm', 'hmT'))
                    W = w_get("U%d" % h)
                    W3 = w3(W, 8)
                    for cc in range(4):
                        c16 = 4 * h + cc
                        acc = next_pa()
                        for kk in range(8):
                            k.mm(acc[:, 0:NT], W3[:, kk, cc * 128:(cc + 1) * 128], xnT[:, kk, :], start=(kk == 0), stop=(kk == 7))
                        u_ext = u_exts[cc % 2]
                        k.cp(pool, u_ext[:, :, 0:3], hist[:, c16, 0:NSEG, :])
                        k.cp(act, u_ext[:, :, 3:3 + SEG], acc[:, 0:NT].rr("p (s t) -> p s t", s=NSEG))
                        k.cp(pool, hist[:, c16, 0:NSEG, :], u_ext[:, :, SEG:SEG + 3])
                        k.actf(cv.v, u_ext[:, :, 0:SEG], AF.Identity, scale=cw_col[:, c16, 0:1], bias=cb_col[:, c16:c16 + 1])
                        for j in range(1, 4):
                            k.stt(cv.v, u_ext[:, :, j:j + SEG], cw_col[:, c16, j:j + 1], cv.v, ALU.mult, ALU.add)
                        k.actf(cT[:, cc, :].rr("p (s t) -> p s t", s=NSEG), cv.v, AF.Silu)
                        yield
                    W = w_get("Z%d" % h)
                    W3 = w3(W, 8)
                    for cc in range(4):
                        acc = next_pa()
                        for kk in range(8):
                            k.mm(acc[:, 0:NT], W3[:, kk, cc * 128:(cc + 1) * 128], xnT[:, kk, :], start=(kk == 0), stop=(kk == 7))
                        k.actf(szT[:, cc, :], acc[:, 0:NT], AF.Silu)
                        yield
                    W = w_get("V%d" % h)
                    W3 = w3(W, 8)
                    for u in range(NU):
                        acc = next_pa()
                        for kk in range(8):
                            k.mm(acc[0:TS, :], xnT[:, kk, utok(u)], W3[:, kk, :], start=(kk == 0), stop=(kk == 7))
                        k.cp(act if u % 2 else dve, v_tm[0:TS, u, :], acc[0:TS, :])
                        yield
                    W = w_get("QK%d" % h)
                    Wq = w3(W, 4, 0, 2048)
                    Wk = w3(W, 4, 2048, 4096)
                    for ec in range(4):
                        acc = next_pa()
                        for dc in range(4):
                            k.mm(acc[:, 0:NT], Wq[:, dc, ec * 128:(ec + 1) * 128], cT[:, dc, :], start=(dc == 0), stop=(dc == 3))
                        k.cp(act, qT[:, ec, :], acc[:, 0:NT])
                        acc = next_pa()
                        for dc in range(4):
                            k.mm(acc[:, 0:NT], Wk[:, dc, ec * 128:(ec + 1) * 128], cT[:, dc, :], start=(dc == 0), stop=(dc == 3))
                        k.ts(dve, kT[:, ec, :], acc[:, 0:NT], SC, ALU.mult)
                        yield
                    for u in range(NU):
                        acc = next_pa()
                        for dc in range(4):
                            k.mm(acc[0:TS, :], cT[:, dc, utok(u)], Wk[:, dc, :], start=(dc == 0), stop=(dc == 3))
                        k.ts(dve, k_tm[0:TS, u, :], acc[0:TS, :], SC, ALU.mult)
                        yield
                    for ec in range(4):
                        c16 = 4 * h + ec
                        k.stt(cT[:, ec, :], cT[:, ec, :], sk_col[:, c16:c16 + 1], szT[:, ec, :], ALU.mult, ALU.mult)
                        k.actf(szT[:, ec, :], szT[:, ec, :], AF.Copy, scale=mg_col[:, c16:c16 + 1])
                        yield


                def passes_gen(h, B_):
                    cT, szT, qT, kT, k_tm, v_tm, hmT = (B_[n_] for n_ in ('cT', 'szT', 'qT', 'kT', 'k_tm', 'v_tm', 'hmT'))
                    for u in range(NU):
                        k.actf(wv_all[0:TS, u, :], v_tm[0:TS, u, :], AF.Copy, scale=wl_all[0:TS, u, h:h + 1])

                    def state_pass(u):
                        slot = u if sample else 0
                        if sample:
                            k.dma(sp, C_nat[:, h], I["sC"][u, h].rearrange("(ec p) d -> p ec d", p=128))
                            yield from make_CT3(h, CTtmp[u % 2])
                        elif first and u == 0:
                            k.memset(pool, C_nat[:, h], 0.0)
                            k.memset(pool, CT_cur[:, h], 0.0)
                        k.cp(dve, n_bfs[:, u, :], n_col[:, slot, 4 * h:4 * h + 4])
                        for ec in range(4):
                            dcp = next_dc2()
                            k.mm(dcp, wv_all[0:TS, u, ec * 128:(ec + 1) * 128], k_tm[0:TS, u, :])
                            k.stt(C_nat[:, h, ec, :], C_nat[:, h, ec, :], dec_bc[:, u, h:h + 1], dcp, ALU.mult, ALU.add)
                            yield
                        for dc in range(4):
                            k.mm(psm[:, 420 + dc:421 + dc], k_tm[0:TS, u, dc * 128:(dc + 1) * 128], wlb_all[0:TS, u, h:h + 1])
                        k.stt(n_col[:, slot, 4 * h:4 * h + 4], n_col[:, slot, 4 * h:4 * h + 4], dec_bc[:, u, h:h + 1],
                              psm[:, 420:424], ALU.mult, ALU.add)
                        if sample:
                            k.dma(pool, O["C_s"][u, h].rearrange("(ec p) d -> p ec d", p=128), C_nat[:, h])
                        elif last and u == NU - 1:
                            k.dma(pool, O["C_p"][h].rearrange("(ec p) d -> p ec d", p=128), C_nat[:, h])
                        else:
                            yield from make_CT3(h, CTtmp[u] if u < NU - 1 else CT_nxt[:, h])

                    def h_group(us):
                        R_ = [(u_ % 2, u_) for u_ in us]
                        CTs = {u: (CTtmp[u % 2] if sample else (CT_cur[:, h] if u == 0 else CTtmp[u - 1])) for u in us}
                        hb = [ptr, ptr]
                        yield
                        for i2, u in R_:
                            nm = psm[0:TS, 0:TS]
                            k.mm(nm, sel[0:2, h, 0:TS], negM[0:2, u, :], start=True, stop=False)
                            k.mm(nm, identF[0:TS, 0:TS], mnegT[0:TS, 0:TS], start=False, stop=True)
                            kq = psm[0:TS, 128:128 + TS]
                            for ec in range(4):
                                k.mm(kq, kT[:, ec, utok(u)], qT[:, ec, utok(u)], start=(ec == 0), stop=(ec == 3))
                            k.mm(psm[:, 256:256 + TS], sel[0:2, h, :], wiT_all[0:2, u, :])
                        yield
                        for i2, u in R_:
                            k.actf(dT_sb[i2][0:TS, 0:TS], psm[0:TS, 0:TS], AF.Exp, bias=u_tm[0:TS, u, h:h + 1])
                        yield
                        for i2, u in R_:
                            k.tt(dve, sT_sb[i2][0:TS, 0:TS], psm[0:TS, 128:128 + TS], dT_sb[i2][0:TS, 0:TS], ALU.mult)
                            k.tt(dve, qwT[i2][:, :, 0:TS], qT[:, :, utok(u)],
                                 V(psm, psm.h[:, 256:256 + TS].unsqueeze(1).to_broadcast([128, 4, TS])), ALU.mult)
                        yield
                        for i2, u in R_:
                            pn = pnum if i2 == 0 else pint
                            k.mm(pn[0:TS, :], sT_sb[i2][0:TS, 0:TS], v_tm[0:TS, u, :], start=True, stop=False)
                            for dc in range(4):
                                k.mm(pn[0:TS, :], qwT[i2][:, dc, 0:TS], CTs[u][:, dc, :], start=False, stop=(dc == 3))
                            dn_ = psm[0:TS, 416 + i2:417 + i2]
                            k.mm(dn_, sT_sb[i2][0:TS, 0:TS], onesb[0:TS, :], start=True, stop=False)
                            for dc in range(4):
                                k.mm(dn_, qwT[i2][:, dc, 0:TS], n_bfs[:, u, dc:dc + 1], start=False, stop=(dc == 3))
                        yield
                        for i2, u in R_:
                            d_rs, d_den, d_aden, d_rden, d_st, d_mv = dd[i2]
                            dn_ = psm[0:TS, 416 + i2:417 + i2]
                            k.ts(dve, d_den[0:TS, 0:1], dn_, -1.0, ALU.mult)
                            k.tt(dve, d_aden[0:TS, 0:1], d_den[0:TS, 0:1], dn_, ALU.max)
                            k.tt(dve, d_aden[0:TS, 0:1], d_aden[0:TS, 0:1], cl_tm[0:TS, u, h:h + 1], ALU.max)
                            k.op(dve, lambda d_rden=d_rden, d_aden=d_aden: nc.vector.reciprocal(out=d_rden.h[0:TS, 0:1], in_=d_aden.h[0:TS, 0:1]),
                                 [d_aden.v], [d_rden.v])
                        yield
                        for i2, u in R_:
                            pn = pnum if i2 == 0 else pint
                            k.actf(hraw[i2][0:TS, :], pn[0:TS, :], AF.Copy, scale=dd[i2][3][0:TS, 0:1])
                        yield
                        for i2, u in R_:
                            d_rs, d_den, d_aden, d_rden, d_st, d_mv = dd[i2]
                            k.op(dve, lambda d_st=d_st, i2=i2: nc.vector.bn_stats(out=d_st.h[0:TS, 0:6], in_=hraw[i2].h[0:TS, :]), [hraw[i2].v], [d_st.v])
                            k.op(dve, lambda d_st=d_st, d_mv=d_mv: nc.vector.bn_aggr(out=d_mv.h[0:TS, 0:2], in_=d_st.h[0:TS, 0:6]), [d_st.v], [d_mv.v])
                        yield
                        for i2, u in R_:
                            d_rs, d_den, d_aden, d_rden, d_st, d_mv = dd[i2]
                            k.actf(d_rs[0:TS, 0:1], d_mv[0:TS, 1:2], AF.Ln, bias=EPS)
                            k.actf(d_rs[0:TS, 1:2], d_rs[0:TS, 0:1], AF.Exp, scale=-0.5)
                        yield
                        for i2, u in R_:
                            d_rs, d_den, d_aden, d_rden, d_st, d_mv = dd[i2]
                            k.ts(dve, hn_sb[i2][0:TS, :], hraw[i2][0:TS, :], d_mv[0:TS, 0:1], ALU.subtract, d_rs[0:TS, 1:2], ALU.mult)
                        yield
                        for i2, u in R_:
                            for ec in range(4):
                                k.tr(hb[i2][:, ec * TS:(ec + 1) * TS], hn_sb[i2][0:TS, ec * 128:(ec + 1) * 128], identF[0:TS, 0:TS])
                        yield
                        for i2, u in R_:
                            k.tt(dve, tmp_sb[i2][:, :, 0:TS], hb[i2][:, 0:4 * TS].rr("p (e t) -> p e t", e=4), szT[:, :, utok(u)], ALU.mult)
                            k.tt(pool, hmT[:, :, utok(u)], tmp_sb[i2][:, :, 0:TS], cT[:, :, utok(u)], ALU.add)


                    if sample:
                        for u in range(NU):
                            yield from state_pass(u)
                            yield from h_group([u])
                    else:
                        for u in range(NU):
                            yield from state_pass(u)
                        for u in range(NU):
                            yield from h_group([u])
                def brm_gen(h, B_):
                    cT, szT, qT, kT, k_tm, v_tm, hmT = (B_[n_] for n_ in ('cT', 'szT', 'qT', 'kT', 'k_tm', 'v_tm', 'hmT'))
                    W = w_get("BRM%d" % h)
                    W3 = w3(W, 4)
                    for b in range(NSB):
                        for q in range(2):
                            acc = next_pa()
                            for ec in range(4):
                                k.mm(acc.v, hmT[:, ec, blk(b)], W3[:, ec, q * 512:(q + 1) * 512], start=(ec == 0), stop=(ec == 3))
                            dst = y_acc[:, b, q * 512:(q + 1) * 512]
                            if h == 0:
                                k.cp(act, dst, acc.v)
                                yield
                            else:
                                k.tt(dve, dst, dst, acc.v, ALU.add)
                                yield


                def run(ga, gb, ra=1, rb=1):
                    gens = [g_ for g_ in (ga, gb) if g_ is not None]
                    rate = {id(ga): ra, id(gb): rb}
                    while gens:
                        for g_ in list(gens):
                            for _ in range(rate[id(g_)]):
                                try:
                                    next(g_)
                                except StopIteration:
                                    gens.remove(g_)
                                    break
                run(gate_gen(), proj_gen(0, BS[0]), 1, 1)
                run(passes_gen(0, BS[0]), proj_gen(1, BS[1]), 3, 1)
                run(passes_gen(1, BS[1]), brm_gen(0, BS[0]), 3, 1)
                run(brm_gen(1, BS[1]), None)
                for slot in range(NU if sample else 1):
                    if sample or last:
                        k.tr(psm[0:8, 256:384], n_col[:, slot, :], identF.v)
                        k.cp(act, nout[0:8, slot, :], psm[0:8, 256:384])
                        k.dma(pool, O["n_s"][slot] if sample else O["n_p"], nout[0:8, slot, :])
                if sample or last:
                    for grp in range(2):
                        for cc in range(4):
                            c16 = grp * 4 + cc
                            k.tr(ptr[0:NSEG * 3, cc * 128:(cc + 1) * 128], hist[:, c16, 0:NSEG, :].rr("p s j -> p (s j)"), identF.v)
                        k.cp(act, cvo[0:NSEG * 3, grp * 512:(grp + 1) * 512], ptr[0:NSEG * 3, :])
                    k.dma(pool, O["conv_s"] if sample else O["conv_p"], cvo[0:NSEG * 3, :])
                evA, xdA = exchange(NT, "a", y_acc, NSB)
            k.barrier()

            yb_acc = k.sb("yb_acc", [128, NSB, D], es=tes)
            if True:
                with ExitStack() as pes:
                    ohT = k.sb("ohT", [128, 4, NT], BF16, es=pes)
                    qs_sb = k.sb("qs_sb", [128, 4, NT], es=pes)
                    sigf = k.sb("sigf", [128, NT], es=pes)
                    lfh = k.sb("lfh", [128, NT], es=pes)
                    kh_sb = k.sb("kh_sb", [128, NT], es=pes)
                    g_sb = k.sb("g_sb", [128, NT], es=pes)
                    eg_sb = k.sb("eg_sb", [128, NT], es=pes)
                    eng_sb = k.sb("eng_sb", [128, NT], es=pes)
                    qtT = k.sb("qtT", [128, 4, NT], BF16, es=pes)
                    ktT = k.sb("ktT", [128, 4, NT], BF16, es=pes)
                    sgT = k.sb("sgT", [128, 4, NT], BF16, es=pes)
                    egl = k.sb("egl", [128, 4, NT // L], es=pes)
                    vh_tm = k.sb("vh_tm", [128, NU, 512], BF16, es=pes)
                    kt_tm = k.sb("kt_tm", [128, 512], BF16, es=pes)
                    aT_sb = k.sb("aT_sb", [128, 4, 128], BF16, es=pes)
                    o_sb = k.sb("o_sb", [128, 512], es=pes)
                    sq_sb = k.sb("sq_sb", [128, 512], es=pes)
                    on_sb = k.sb("on_sb", [128, 4, 128], es=pes)
                    hs = k.sb("hs", [128, 12], es=pes)
                    for g in range(1):
                        hsl = slice(4 * g, 4 * g + 4)
                        set_pa([pa[0], pa[1], pdc, ptr])
                        W = w_get("QH%d" % g)
                        W3 = w3(W, 8)
                        for j in range(4):
                            acc = next_pa()
                            for kk in range(8):
                                k.mm(acc[:, 0:NT], W3[:, kk, j * 128:(j + 1) * 128], xnT[:, kk, :], start=(kk == 0), stop=(kk == 7))
                            k.actf(qs_sb[:, j, :], acc[:, 0:NT], AF.Silu)
                        W = w_get("FH%d" % g)
                        W3 = w3(W, 8)
                        for j in range(4):
                            hh = 4 * g + j
                            acc = next_pa()
                            for kk in range(8):
                                k.mm(acc[:, 0:NT], W3[:, kk, j * 128:(j + 1) * 128], xnT[:, kk, :], start=(kk == 0), stop=(kk == 7))
                            k.actf(sigf.v, acc[:, 0:NT], AF.Sigmoid)
                            k.actf(lfh.v, sigf.v, AF.Ln, scale=oml_col[:, hh:hh + 1], bias=lb_col[:, hh:hh + 1])
                            k.ts(dve, kh_sb.v, sigf.v, noml_col[:, hh:hh + 1], ALU.mult, oml_col[:, hh:hh + 1], ALU.add)
                            k.op(dve, lambda: nc.vector.tensor_tensor_scan(out=g_sb.h[:], data0=m01.h[:, 0:NT], data1=lfh.h[:],
                                                                            initial=0.0, op0=ALU.mult, op1=ALU.add),
                                 [m01.v, lfh.v], [g_sb.v])
                            k.actf(eg_sb.v, g_sb.v, AF.Exp)
                            k.actf(eng_sb.v, g_sb.v, AF.Exp, scale=-1.0)
                            k.tt(pool, qtT[:, j, :], qs_sb[:, j, :], eg_sb.v, ALU.mult)
                            k.tt(dve, ktT[:, j, :], kh_sb.v, eng_sb.v, ALU.mult)
                            k.cp(dve, egl[:, j, :], eg_sb.v.rr("p (c l) -> p c l", l=L)[:, :, L - 1])
                        W = w_get("IH%d" % g)
                        W3 = w3(W, 8)
                        for u in range(NU):
                            acc = next_pa()
                            for kk in range(8):
                                k.mm(acc[0:TS, :], xnT[:, kk, utok(u)], W3[:, kk, :], start=(kk == 0), stop=(kk == 7))
                            k.cp(act if u % 2 else dve, vh_tm[0:TS, u, :], acc[0:TS, :])
                        W = w_get("GH%d" % g)
                        W3 = w3(W, 8)
                        for j in range(4):
                            acc = next_pa()
                            for kk in range(8):
                                k.mm(acc[:, 0:NT], W3[:, kk, j * 128:(j + 1) * 128], xnT[:, kk, :], start=(kk == 0), stop=(kk == 7))
                            k.actf(sgT[:, j, :], acc[:, 0:NT], AF.Silu)
                        set_pa([pa[0], pa[1]])
                        for u in range(NU):
                            if sample:
                                k.dma(sp, S_st[:, hsl, :], I["sS"][u].rearrange("h c e -> c h e"))
                                k.cp(act, S_bf[0][:, hsl, :], S_st[:, hsl, :])
                            for j in range(4):
                                k.tr(ptrb[0:TS, j * 128:(j + 1) * 128], ktT[:, j, utok(u)], identB.v)
                            k.cp(act, kt_tm[0:TS, :], ptrb[0:TS, 0:512])
                            for j in range(4):
                                k.mm(pnum[0:TS, j * TS:(j + 1) * TS], ktT[:, j, utok(u)], qtT[:, j, utok(u)])
                            k.tt(dve, aT_sb[0:TS, :, 0:TS], pnum[0:TS, 0:4 * TS].rr("p (j t) -> p j t", j=4),
                                 V(hmask, hmask.h[0:TS, 0:TS].unsqueeze(1).to_broadcast([TS, 4, TS])), ALU.mult)
                            def s_update(c, dst_bf):
                                rows = slice(c * L, (c + 1) * L)
                                gci = u * NCH + c
                                for j in range(4):
                                    k.mm(pdc[:, j * 128:(j + 1) * 128], kt_tm[rows, j * 128:(j + 1) * 128],
                                         vh_tm[rows, u, j * 128:(j + 1) * 128])
                                k.tt(dve, S_st[:, hsl, :], S_st[:, hsl, :], pdc.v.rr("p (j e) -> p j e", j=4), ALU.add)
                                k.tt(dve, S_st[:, hsl, :], S_st[:, hsl, :],
                                     V(egl, egl.h[:, :, gci:gci + 1].to_broadcast([128, 4, 128])), ALU.mult)
                                if dst_bf is not None:
                                    k.cp(act, dst_bf[:, hsl, :], S_st[:, hsl, :])
                            for c in range(NCH - 1):
                                s_update(c, S_bf[c + 1])
                            for j in range(4):
                                hh = 4 * g + j
                                for c in range(NCH):
                                    rows = slice(c * L, (c + 1) * L)
                                    k.mm(pint[rows, j * 128:(j + 1) * 128], qtT[:, j, u * TS + c * L:u * TS + (c + 1) * L],
                                         S_bf[c][:, hh, :], start=True, stop=False, skip_group_check=True)
                                k.mm(pint[0:TS, j * 128:(j + 1) * 128], aT_sb[0:TS, j, 0:TS], vh_tm[0:TS, u, j * 128:(j + 1) * 128],
                                     start=False, stop=True, skip_group_check=True)
                            s_update(NCH - 1, None if sample else S_bf[0])
                            k.cp(act, o_sb[0:TS, :], pint[0:TS, :])
                            k.tt(pool, sq_sb[0:TS, :], o_sb[0:TS, :], o_sb[0:TS, :], ALU.mult)
                            k.op(dve, lambda: nc.vector.tensor_reduce(out=hs.h[0:TS, 0:4],
                                                                       in_=sq_sb.h[0:TS, :].rearrange("p (j e) -> p j e", j=4),
                                                                       axis=AX.X, op=ALU.add), [sq_sb.v], [hs.v])
                            k.actf(hs[0:TS, 4:8], hs[0:TS, 0:4], AF.Ln, scale=1.0 / HE, bias=EPS)
                            k.actf(hs[0:TS, 8:12], hs[0:TS, 4:8], AF.Exp, scale=-0.5)
                            k.tt(dve, on_sb[0:TS, :, :], o_sb[0:TS, :].rr("p (j e) -> p j e", j=4),
                                 V(hs, hs.h[0:TS, 8:12].unsqueeze(2).to_broadcast([TS, 4, 128])), ALU.mult)
                            for j in range(4):
                                k.tr(ptr[:, j * TS:(j + 1) * TS], on_sb[0:TS, j, :], identF[0:TS, 0:TS])
                            for j in range(4):
                                hh = 4 * g + j
                                k.stt(ohT[:, hh, utok(u)], ptr[:, j * TS:(j + 1) * TS], hg_col[:, hh:hh + 1], sgT[:, j, utok(u)],
                                      ALU.mult, ALU.mult)
                            if sample:
                                k.dma(pool, O["S_s"][u].rearrange("h c e -> c h e"), S_st[:, hsl, :])
                            elif last and u == NU - 1:
                                k.dma(pool, O["S_p"].rearrange("h c e -> c h e"), S_st[:, hsl, :])
                    W = w_get("BRH")
                    W3 = w3(W, 4)
                    for b in range(NSB):
                        for q in range(2):
                            acc = next_pa()
                            for hh in range(4):
                                k.mm(acc.v, ohT[:, hh, blk(b)], W3[:, hh, q * 512:(q + 1) * 512], start=(hh == 0), stop=(hh == 3))
                            k.cp(act, yb_acc[:, b, q * 512:(q + 1) * 512], acc.v)
                    evB, xdB = exchange(NT, "b", yb_acc, NSB)
            k.barrier()

            with ExitStack() as pes:
                x2 = k.sb("x2", [128, NSB, D], es=pes)
                tT = k.sb("tT", [128, 8, NT], BF16, es=pes)
                pT = k.sb("pT", [128, 2, NT], BF16, es=pes)
                p_tm = k.sb("p_tm", [128, NSB, PLE], es=pes)
                bf_sb = k.sb("bf_sb", [128, D], BF16, es=pes)
                sg_sb = k.sb("sg2_sb", [128, 512], es=pes)
                sga = k.sb("sga", [128, NSB, D], es=pes)
                sgb = k.sb("sgb", [128, NSB, D], es=pes)
                outb = [k.sb("outb%d" % i, [128, D], es=pes) for i in range(2)]
                ssq = k.sb("ssq2", [128, 4], es=pes)
                set_pa([pa[0], pa[1], pnum, pint])
                for b in range(NSB):
                    k.dma(sp, x2[:, b, :], x_dram[b * 128:(b + 1) * 128, :])
                    k.dma(sp, p_tm[:, b, :], p_dram[b * 128:(b + 1) * 128, :])

                def to_T(dst, src_fn, nchunk):
                    for b in range(NSB):
                        k.cp(act, bf_sb[:, 0:nchunk * 128], src_fn(b))
                        for c in range(nchunk):
                            k.tr(ptrb[:, c * 128:(c + 1) * 128], bf_sb[:, c * 128:(c + 1) * 128], identB.v)
                        k.cp(dve, dst[:, 0:nchunk, blk(b)], ptrb[:, 0:nchunk * 128].rr("p (c t) -> p c t", c=nchunk))
                to_T(pT, lambda b: p_tm[:, b, :], 2)
                for nm_, sgt in (("GA", sga), ("GB", sgb)):
                    for q in range(2):
                        W = w_get("%s%d" % (nm_, q))
                        W3 = w3(W, 8)
                        for b in range(NSB):
                            acc = next_pa()
                            for kk in range(8):
                                k.mm(acc.v, xnT[:, kk, blk(b)], W3[:, kk, :], start=(kk == 0), stop=(kk == 7))
                            k.actf(sgt[:, b, q * 512:(q + 1) * 512], acc.v, AF.Sigmoid)
                sp.wait(evA)
                sp.wait(evB)
                for b in range(NSB):
                    k.dma(sp, y_acc[:, b, :], xdA[b * 128:(b + 1) * 128, :])
                    k.dma(sp, yb_acc[:, b, :], xdB[b * 128:(b + 1) * 128, :])
                    k.tt(dve, y_acc[:, b, :], y_acc[:, b, :], sga[:, b, :], ALU.mult)
                    k.tt(dve, yb_acc[:, b, :], yb_acc[:, b, :], sgb[:, b, :], ALU.mult)
                    k.tt(dve, y_acc[:, b, :], y_acc[:, b, :], yb_acc[:, b, :], ALU.add)
                to_T(tT, lambda b: y_acc[:, b, :], 8)
                for q in range(2):
                    W = w_get("OUT%d" % q)
                    W3 = w3(W, 8)
                    for b in range(NSB):
                        acc = next_pa()
                        for kk in range(8):
                            k.mm(acc.v, tT[:, kk, blk(b)], W3[:, kk, :], start=(kk == 0), stop=(kk == 7))
                        dst = x2[:, b, q * 512:(q + 1) * 512]
                        k.tt(dve, dst, dst, acc.v, ALU.add)
                to_T(tT, lambda b: x2[:, b, :], 8)
                Wp = [w_get("PG0"), w_get("PG1", hold=1), w_get("PLE", hold=2)]
                Wple = w3(Wp[2], 2, 0, 2048)
                for b in range(NSB):
                    for q in range(2):
                        W3 = w3(Wp[q], 8)
                        acc = next_pa()
                        for kk in range(8):
                            k.mm(acc.v, tT[:, kk, blk(b)], W3[:, kk, :], start=(kk == 0), stop=(kk == 7))
                        k.actf(sg_sb.v, acc.v, AF.Sigmoid)
                        acc2 = next_pa()
                        for kk in range(2):
                            k.mm(acc2.v, pT[:, kk, blk(b)], Wple[:, kk, q * 512:(q + 1) * 512], start=(kk == 0), stop=(kk == 1))
                        k.tt(dve, sg_sb.v, sg_sb.v, acc2.v, ALU.mult)
                        dst = x2[:, b, q * 512:(q + 1) * 512]
                        k.tt(pool, dst, dst, sg_sb.v, ALU.add)
                    ob = outb[b % 2]
                    k.actf(ob.v, x2[:, b, :], AF.Square, accum=ssq[:, 0:1])
                    k.actf(ssq[:, 1:2], ssq[:, 0:1], AF.Ln, scale=1.0 / D, bias=EPS)
                    k.actf(ssq[:, 2:3], ssq[:, 1:2], AF.Exp, scale=-0.5)
                    k.stt(ob.v, x2[:, b, :], ssq[:, 2:3], fg_bc.v, ALU.mult, ALU.mult)
                    k.dma(pool, y_dram[b * 128:(b + 1) * 128, :], ob.v)
                set_pa([pa[0], pa[1]])
            k.barrier()

    NTP = min(512, TP)
    ntile = TP // NTP if do_prompt else 0
    for ti in range(ntile):
        seq.extend(range(NW))
    if do_sample:
        seq.extend(range(NW))
    for ti in range(ntile):
        tile(I["xp"][ti * NTP:(ti + 1) * NTP, :], I["pp"][ti * NTP:(ti + 1) * NTP, :], O["yp"][ti * NTP:(ti + 1) * NTP, :],
             NTP, 128, NTP // 128, 1, ti == 0, ti == ntile - 1, False, tix=ti)
    if do_sample:
        tile(I["xs"], I["ps"], O["ys"], NS * 32, 32, NS, NS, True, True, True)
    for ev in k.out_events:
        sp.wait(ev)


_NC_CACHE = {}


def _in_map(core, inputs, NS):
    b = core // 2
    hf = core % 2
    s0 = (core // 2) * NS
    f = lambda a: np.ascontiguousarray(a, dtype=np.float32)
    hs = slice(2 * hf, 2 * hf + 2)
    cs = slice(hf * MIL, (hf + 1) * MIL)
    gs = slice(hf * HWL, (hf + 1) * HWL)
    w = inputs["w_in"][0]
    w_in = np.concatenate([
        w[:, 0 + hf * MIL:0 + (hf + 1) * MIL], w[:, 4096 + hf * MIL:4096 + (hf + 1) * MIL], w[:, 2048 + hf * MIL:2048 + (hf + 1) * MIL],
        w[:, 6144 + 2 * hf:6144 + 2 * hf + 2], w[:, 6148 + 2 * hf:6148 + 2 * hf + 2],
        w[:, 6152 + hf * HWL:6152 + (hf + 1) * HWL], w[:, 7176 + hf * HWL:7176 + (hf + 1) * HWL],
        w[:, 8200 + hf * HWL:8200 + (hf + 1) * HWL], w[:, 9224 + hf * HWL:9224 + (hf + 1) * HWL],
        w[:, 10248:12296]], axis=1)
    assert w_in.shape[1] == NINL
    m = dict(
        xp=f(inputs["x_prompt"][b]), pp=f(inputs["p_prompt"][0, b]),
        xs=f(inputs["x_sample"][s0:s0 + NS].reshape(NS * 32, D)), ps=f(inputs["p_sample"][0, s0:s0 + NS].reshape(NS * 32, PLE)),
        sconv=f(inputs["state_conv"][0, s0:s0 + NS][:, :, cs].reshape(NS * 3, MIL)), sC=f(inputs["state_mlstm_C"][0, s0:s0 + NS, hs]),
        sn=f(inputs["state_mlstm_n"][0, s0:s0 + NS, hs].reshape(NS, MHL * 4, 128)), sm=f(inputs["state_mlstm_m"][0, s0:s0 + NS, hs]),
        sS=f(inputs["state_hgrn"][0, s0:s0 + NS, 4 * hf:4 * hf + 4]),
        norm_g=f(inputs["norm_g"][0]), w_in=f(w_in), b_ig=f(inputs["b_ig"][0, hs]), b_fg=f(inputs["b_fg"][0, hs]),
        conv_w=f(inputs["conv_w"][0][:, cs]), conv_b=f(inputs["conv_b"][0, cs]), w_qm=f(inputs["w_qm"][0, hs]), w_km=f(inputs["w_km"][0, hs]),
        mnorm_g=f(inputs["mnorm_g"][0, cs]), m_skip=f(inputs["m_skip"][0, cs]), w_brm=f(inputs["w_brm"][0, cs]),
        hgrn_lb=f(inputs["hgrn_lb"][:, gs]), hnorm_g=f(inputs["hnorm_g"][0, gs]), w_brh=f(inputs["w_brh"][0, gs]), w_out=f(inputs["w_out"][0]),
        w_ple=f(inputs["w_ple"][0]), w_pg=f(inputs["w_pg"][0]), final_g=f(inputs["final_g"]),
    )
    return m


def kernel(**inputs):
    inputs = {k_: np.asarray(v) for k_, v in inputs.items()}
    B, TP = inputs["x_prompt"].shape[:2]
    NSEQ = inputs["x_sample"].shape[0]
    NCORE = 8
    NS = NSEQ // (NCORE // 2)
    key = (TP, NS)
    if key not in _NC_CACHE:
        _NC_CACHE[key] = build(TP, NS)
    nc = _NC_CACHE[key]
    in_maps = [_in_map(c, inputs, NS) for c in range(NCORE)]
    res = run_bass_kernel_spmd(nc, in_maps, core_ids=list(range(NCORE)))
    R = res.results
    f32 = np.float32
    NP = NCORE // 2
    cat2 = lambda name, fn, ax: [np.concatenate([fn(R[2 * p][name]), fn(R[2 * p + 1][name])], axis=ax) for p in range(NP)]
    y_prompt = np.stack([R[2 * b]["yp"] for b in range(B)]).astype(f32)
    y_sample = np.concatenate([R[2 * p]["ys"].reshape(NS, 32, D) for p in range(NP)]).astype(f32)
    conv_p = np.stack(cat2("conv_p", lambda a: a, 1))[None].astype(f32)
    C_p = np.stack(cat2("C_p", lambda a: a, 0))[None].astype(f32)
    n_p = np.stack(cat2("n_p", lambda a: a.reshape(MHL, HD), 0))[None].astype(f32)
    m_p = np.stack(cat2("m_p", lambda a: a.reshape(MHL), 0))[None].astype(f32)
    S_p = np.stack(cat2("S_p", lambda a: a, 0))[None].astype(f32)
    conv_s = np.concatenate(cat2("conv_s", lambda a: a.reshape(NS, 3, MIL), 2))[None].astype(f32)
    C_s = np.concatenate(cat2("C_s", lambda a: a, 1))[None].astype(f32)
    n_s = np.concatenate(cat2("n_s", lambda a: a.reshape(NS, MHL, HD), 1))[None].astype(f32)
    m_s = np.concatenate(cat2("m_s", lambda a: a.reshape(NS, MHL), 1))[None].astype(f32)
    S_s = np.concatenate(cat2("S_s", lambda a: a, 1))[None].astype(f32)
    return (y_prompt, y_sample, conv_p, C_p, n_p, m_p, S_p, conv_s, C_s, n_s, m_s, S_s)
```

```python
import numpy as np
from contextlib import ExitStack
import concourse.bass as bass
import concourse.mybir as mybir
from concourse.bass_utils import run_bass_kernel_spmd

F32 = mybir.dt.float32
BF16 = mybir.dt.bfloat16
AF = mybir.ActivationFunctionType
ALU = mybir.AluOpType
AX = mybir.AxisListType

D = 1024
MI = 2048
MH = 4
HD = 512
HH = 8
HE = 128
PLE = 256
NIN = 12296
MHL = 2
MIL = 1024
HGL = 4
HWL = 512
NINL = 7172
EPS = 1e-6
NEG = -30000.0


class V:
    __slots__ = ("t", "ap")

    def __init__(self, t, ap):
        self.t = t
        self.ap = ap

    def __getitem__(self, key):
        return V(self.t, self.ap[key])

    def bitcast(self, dt):
        return V(self.t, self.ap.bitcast(dt))

    def rr(self, pat, **kw):
        return V(self.t, self.ap.rearrange(pat, **kw))

    def bc(self, shape):
        return V(self.t, self.ap.to_broadcast(list(shape)))


class T:
    def __init__(self, h, name):
        self.h = h
        self.name = name
        self.w = None
        self.r = {}
        self.dsem = None
        self.dcnt = 0

    def __getitem__(self, key):
        return V(self, self.h[key])

    @property
    def v(self):
        return V(self, self.h[:])


class TA:
    def __init__(self, t, ap, name):
        self.t = t
        self.h = ap
        self.name = name

    def __getitem__(self, key):
        return V(self.t, self.h[key])

    @property
    def v(self):
        return V(self.t, self.h)


class Eng:
    def __init__(self, k, name, h, strict=False):
        self.k = k
        self.name = name
        self.h = h
        self.sem = k.new_sem("e_" + name)
        self.cnt = 0
        self.last = None
        self.seen = {}
        self.strict = strict

    def wait(self, ev):
        sem, val, key = ev
        prod = self.k.engs.get(key)
        if prod is not None and val > prod.cnt:
            assert prod.last is not None and val == prod.cnt + 1, (key, val, prod.cnt)
            prod.last.then_inc(prod.sem, 1)
            prod.last = None
            prod.cnt += 1
        if self.seen.get(key, 0) < val:
            self.h.wait_ge(sem, val)
            self.seen[key] = val


class K:
    def __init__(self, nc, es):
        self.nc = nc
        self.es = es
        self.nsem = 0
        self.pe = Eng(self, "pe", nc.tensor)
        self.act = Eng(self, "act", nc.scalar)
        self.dve = Eng(self, "dve", nc.vector)
        self.pool = Eng(self, "pool", nc.gpsimd, strict=True)
        self.sp = Eng(self, "sp", nc.sync)
        self.engs = {e.name: e for e in (self.pe, self.act, self.dve, self.pool, self.sp)}
        self.out_events = []
        self.dsems = {}
        self.uid = 0
        self.dma_last = {}

    def new_sem(self, name):
        self.nsem += 1
        return self.es.enter_context(self.nc.semaphore(name))

    def sb(self, name, shape, dt=F32, es=None):
        self.uid += 1
        h = (es or self.es).enter_context(self.nc.sbuf_tensor("%s_%d" % (name, self.uid), list(shape), dt))
        return T(h, name)

    def ps(self, name, shape, dt=F32):
        h = self.es.enter_context(self.nc.psum_tensor(name, list(shape), dt))
        return T(h, name)

    def dram(self, name, shape, dt):
        h = self.nc.dram_tensor(name, list(shape), dt, kind="Internal")
        return T(h, name)

    def _pre(self, eng, rd, wr, skip_key=None):
        for v in rd:
            t = v.t
            if t.w is not None:
                eng.wait(t.w)
        for v in wr:
            t = v.t
            if t.w is not None and t.w[2] != skip_key and (eng.strict or t.w[2] != eng.name):
                eng.wait(t.w)
            for key, ev in t.r.items():
                if eng.strict or key != eng.name:
                    eng.wait(ev)

    def _post(self, ev, rd, wr):
        for v in wr:
            v.t.w = ev
            v.t.r = {}
        for v in rd:
            if v.t.w is ev:
                continue
            v.t.r[ev[2]] = ev

    def op(self, eng, fn, rd, wr):
        rd = [v for v in rd if isinstance(v, V)]
        wr = [v for v in wr if isinstance(v, V)]
        self._pre(eng, rd, wr)
        ins = fn()
        eng.last = ins
        ev = (eng.sem, eng.cnt + 1, eng.name)
        self._post(ev, rd, wr)
        return ins

    def dma(self, q, out, in_, semt=None, **kw):
        rd = [in_] if isinstance(in_, V) else []
        wr = [out] if isinstance(out, V) else []
        st = semt or (wr[0].t if wr else rd[0].t)
        key = "d_" + st.name
        self._pre(q, rd, wr, skip_key=key)
        o = out.ap if isinstance(out, V) else out
        i = in_.ap if isinstance(in_, V) else in_
        ins = q.h.dma_start(out=o, in_=i, **kw)
        if key not in self.dsems:
            self.dsems[key] = [self.new_sem(key), 0]
        ent = self.dsems[key]
        ent[1] += 16
        ins.then_inc(ent[0], 16)
        ev = (ent[0], ent[1], key)
        self.dma_last[key] = ev
        self._post(ev, rd, wr)
        if not wr:
            self.out_events.append(ev)
        return ev

    def barrier(self):
        engs = [self.pe, self.act, self.dve, self.pool, self.sp]
        evs = [(e.sem, e.cnt + (1 if e.last is not None else 0), e.name) for e in engs
               if e.cnt > 0 or e.last is not None]
        for e in engs:
            for ev in evs:
                if ev[2] != e.name:
                    e.wait(ev)
            for ev in self.dma_last.values():
                e.wait(ev)

    def mm(self, out, lhsT, rhs, start=True, stop=True, **kw):
        return self.op(self.pe, lambda: self.nc.tensor.matmul(out.ap, lhsT=lhsT.ap, rhs=rhs.ap, start=start, stop=stop, **kw),
                       [lhsT, rhs], [out])

    def tr(self, out, in_, ident):
        return self.op(self.pe, lambda: self.nc.tensor.transpose(out.ap, in_.ap, ident.ap), [in_, ident], [out])

    def actf(self, out, in_, func, bias=None, scale=None, accum=None):
        kw = {}
        if bias is not None:
            kw["bias"] = bias.ap if isinstance(bias, V) else bias
        if scale is not None:
            kw["scale"] = scale.ap if isinstance(scale, V) else scale
        if accum is not None:
            kw["accum_out"] = accum.ap
        return self.op(self.act, lambda: self.nc.scalar.activation(out=out.ap, in_=in_.ap, func=func, **kw),
                       [in_, bias, scale], [out, accum])

    def _e(self, eng):
        return {"dve": self.dve, "pool": self.pool, "act": self.act}[eng] if isinstance(eng, str) else eng

    def tt(self, eng, out, a, b, op):
        e = self._e(eng)
        return self.op(e, lambda: e.h.tensor_tensor(out=out.ap, in0=a.ap, in1=b.ap, op=op), [a, b], [out])

    def ts(self, eng, out, a, s1, op0, s2=None, op1=None, accum=None):
        e = self._e(eng)
        a1 = s1.ap if isinstance(s1, V) else s1
        a2 = s2.ap if isinstance(s2, V) else s2
        kw = {}
        if op1 is not None:
            kw["op1"] = op1
        if accum is not None:
            kw["accum_out"] = accum.ap
        return self.op(e, lambda: e.h.tensor_scalar(out=out.ap, in0=a.ap, scalar1=a1, scalar2=a2, op0=op0, **kw),
                       [a, s1, s2], [out, accum])

    def stt(self, out, a, s, b, op0, op1):
        e = self.dve
        sa = s.ap if isinstance(s, V) else s
        return self.op(e, lambda: e.h.scalar_tensor_tensor(out=out.ap, in0=a.ap, scalar=sa, in1=b.ap, op0=op0, op1=op1),
                       [a, s, b], [out])

    def cp(self, eng, out, in_):
        e = self._e(eng)
        if e is self.act:
            return self.op(e, lambda: self.nc.scalar.copy(out=out.ap, in_=in_.ap), [in_], [out])
        return self.op(e, lambda: e.h.tensor_copy(out=out.ap, in_=in_.ap), [in_], [out])

    def memset(self, eng, out, val):
        e = self._e(eng)
        return self.op(e, lambda: e.h.memset(out.ap, val), [], [out])


def _wplan():
    plan = []
    for h in range(MHL):
        plan += [("U%d" % h, "in", 0 + h * 512), ("Z%d" % h, "in", 1024 + h * 512), ("V%d" % h, "in", 2048 + h * 512),
                 ("QK%d" % h, "qk", h)]
    for h in range(MHL):
        plan += [("BRM%d" % h, "brm", h)]
    plan += [("QH0", "in", 3076), ("FH0", "in", 3588), ("IH0", "in", 4100), ("GH0", "in", 4612)]
    plan += [("BRH", "brh", 0)]
    plan += [("GA0", "in", 5124), ("GA1", "in", 5124 + 512), ("GB0", "in", 6148), ("GB1", "in", 6148 + 512)]
    plan += [("OUT0", "sq", ("w_out", 0)), ("OUT1", "sq", ("w_out", 1)), ("PG0", "sq", ("w_pg", 0)), ("PG1", "sq", ("w_pg", 1)),
             ("PLE", "ple", 0)]
    return plan


def build(TP=4096, NS=8, dbg=None):
    nc = bass.Bass("TRN2", target_bir_lowering=False)
    dbg = dbg or {}

    def din(name, shape):
        return nc.dram_tensor(name, list(shape), F32, kind="ExternalInput").ap()

    def dout(name, shape):
        return nc.dram_tensor(name, list(shape), F32, kind="ExternalOutput").ap()

    I = dict(
        xp=din("xp", [TP, D]), pp=din("pp", [TP, PLE]),
        xs=din("xs", [NS * 32, D]), ps=din("ps", [NS * 32, PLE]),
        sconv=din("sconv", [NS * 3, MIL]), sC=din("sC", [NS, MHL, HD, HD]), sn=din("sn", [NS, MHL * 4, 128]),
        sm=din("sm", [NS, MHL]), sS=din("sS", [NS, HGL, HE, HE]),
        norm_g=din("norm_g", [D]), w_in=din("w_in", [D, NINL]), b_ig=din("b_ig", [MHL]), b_fg=din("b_fg", [MHL]),
        conv_w=din("conv_w", [4, MIL]), conv_b=din("conv_b", [MIL]), w_qm=din("w_qm", [MHL, HD, HD]),
        w_km=din("w_km", [MHL, HD, HD]), mnorm_g=din("mnorm_g", [MIL]), m_skip=din("m_skip", [MIL]),
        w_brm=din("w_brm", [MIL, D]), hgrn_lb=din("hgrn_lb", [2, HWL]), hnorm_g=din("hnorm_g", [HWL]),
        w_brh=din("w_brh", [HWL, D]), w_out=din("w_out", [D, D]), w_ple=din("w_ple", [PLE, D]), w_pg=din("w_pg", [D, D]),
        final_g=din("final_g", [D]),
    )
    O = dict(
        yp=dout("yp", [TP, D]), ys=dout("ys", [NS * 32, D]),
        conv_p=dout("conv_p", [3, MIL]), C_p=dout("C_p", [MHL, HD, HD]), n_p=dout("n_p", [MHL * 4, 128]),
        m_p=dout("m_p", [MHL, 1]), S_p=dout("S_p", [HGL, HE, HE]),
        conv_s=dout("conv_s", [NS * 3, MIL]), C_s=dout("C_s", [NS, MHL, HD, HD]), n_s=dout("n_s", [NS, MHL * 4, 128]),
        m_s=dout("m_s", [NS, MHL, 1]), S_s=dout("S_s", [NS, HGL, HE, HE]),
    )
    with ExitStack() as es:
        k = K(nc, es)
        _program(nc, k, I, O, TP, NS, dbg)
    return nc


def _program(nc, k, I, O, TP, NS, dbg):
    sp, pe, act, dve, pool = k.sp, k.pe, k.act, k.dve, k.pool
    plan = _wplan()
    NW = len(plan)
    do_prompt = dbg.get("prompt", True)
    do_sample = dbg.get("sample", True)

    NRING = 3
    ring = [k.sb("wring%d" % i, [128, 4096], BF16) for i in range(NRING)]
    wstate = dict(next_load=0, total=0)
    seq = []

    def w_issue(upto):
        while wstate["next_load"] < min(upto + 1, len(seq)):
            j = wstate["next_load"]
            pi = seq[j]
            sp.wait(grp_ev[pi // GRP])
            k.dma(sp, ring[j % NRING].v, wscr.h[pi])
            wstate["next_load"] += 1

    def w_get(key, hold=0):
        j = wstate["total"]
        assert plan[seq[j]][0] == key, (plan[seq[j]][0], key)
        w_issue(j + NRING - 1 - hold)
        wstate["total"] += 1
        return ring[j % NRING]

    def w3(w, kk, lo=0, hi=4096):
        return V(w, w.h[:, lo:hi].rearrange("p (k c) -> p k c", k=kk))

    identF = k.sb("identF", [128, 128])
    identB = k.sb("identB", [128, 128], BF16)
    utri = k.sb("utri", [128, 128])
    mnegT = k.sb("mnegT", [128, 128])
    maskbd = k.sb("maskbd", [128, 128])
    sel = k.sb("sel", [2, 2, 128])
    ones4 = k.sb("ones4", [2, 128])
    onesb = k.sb("onesb", [128, 1], BF16)
    m01p = k.sb("m01p", [128, 512])
    m01s = k.sb("m01s", [128, 256])

    def asel(t, pattern, cmp, fill, cm):
        k.op(pool, lambda: nc.gpsimd.affine_select(out=t.h[:], in_=t.h[:], pattern=pattern, compare_op=cmp, fill=fill,
                                                   base=0, channel_multiplier=cm), [t.v], [t.v])
    k.memset(pool, identF.v, 1.0)
    asel(identF, [[-1, 128]], ALU.is_equal, 0.0, 1)
    k.cp(pool, identB.v, identF.v)
    k.memset(pool, utri.v, 1.0)
    asel(utri, [[1, 128]], ALU.is_ge, 0.0, -1)
    k.memset(pool, mnegT.v, 0.0)
    asel(mnegT, [[1, 128]], ALU.is_ge, NEG, -1)
    k.cp(pool, maskbd.v, utri.v)
    k.memset(pool, maskbd[0:64, 64:128], 0.0)
    k.memset(pool, sel.v, 1.0)
    asel(sel, [[-1, 2], [0, 128]], ALU.is_equal, 0.0, 1)
    k.memset(pool, ones4.v, 1.0)
    k.memset(pool, onesb.v, 1.0)
    k.memset(pool, m01p.v, 1.0)
    k.memset(pool, m01p.v.rr("p (c l) -> p c l", l=64)[:, :, 0:1], 0.0)
    k.memset(pool, m01s.v, 1.0)
    k.memset(pool, m01s.v.rr("p (c l) -> p c l", l=32)[:, :, 0:1], 0.0)

    cst = T(None, "cst")

    def cload(name, shape, src, **kw):
        t = k.sb(name, shape)
        k.dma(act, t.v, src, semt=cst, **kw)
        return t
    slow = dict(allow_slow_non_contiguous=True)
    rows_sb = k.sb("rows_sb", [76, 128])
    cols = k.sb("cols", [128, 76])
    k.dma(sp, rows_sb[0:32, :], I["conv_w"].rearrange("j (c p) -> (j c) p", p=128), semt=cst)
    k.dma(sp, rows_sb[32:40, :], I["conv_b"].rearrange("(c p) -> c p", p=128), semt=cst)
    k.dma(sp, rows_sb[40:48, :], I["mnorm_g"].rearrange("(c p) -> c p", p=128), semt=cst)
    k.dma(sp, rows_sb[48:56, :], I["m_skip"].rearrange("(c p) -> c p", p=128), semt=cst)
    k.dma(sp, rows_sb[56:64, :], I["norm_g"].rearrange("(c p) -> c p", p=128), semt=cst)
    k.dma(sp, rows_sb[64:68, :], I["hnorm_g"].rearrange("(c p) -> c p", p=128), semt=cst)
    k.dma(sp, rows_sb[68:76, :], I["hgrn_lb"].rearrange("r (c p) -> (r c) p", p=128), semt=cst)
    cw_col = TA(cols, cols.h[:, 0:32].rearrange("p (j c) -> p c j", j=4), "cw_col")
    cb_col = TA(cols, cols.h[:, 32:40], "cb_col")
    mg_col = TA(cols, cols.h[:, 40:48], "mg_col")
    sk_col = TA(cols, cols.h[:, 48:56], "sk_col")
    ng_col = TA(cols, cols.h[:, 56:64], "ng_col")
    hg_col = TA(cols, cols.h[:, 64:68], "hg_col")
    lb_raw = TA(cols, cols.h[:, 68:76].rearrange("p (r c) -> p c r", r=2), "lb_raw")
    big_bc = cload("big_bc", [128, 2], I["b_ig"].partition_broadcast(128))
    bfg_bc = cload("bfg_bc", [128, 2], I["b_fg"].partition_broadcast(128))
    fg_bc = cload("fg_bc", [128, D], I["final_g"].partition_broadcast(128))
    for t_ in (rows_sb, big_bc, bfg_bc, fg_bc):
        t_.w = k.dma_last["d_cst"]
    lb_col = k.sb("lb_col", [128, 4])
    oml_col = k.sb("oml_col", [128, 4])
    noml_col = k.sb("noml_col", [128, 4])

    wscr = k.dram("wscr", [NW, 128, 4096], BF16)
    GRP = 1
    wgrp = [T(None, "wg%d" % g) for g in range((NW + GRP - 1) // GRP)]
    wg_sb = k.sb("wg_sb", [128, 8, 4], BF16)
    grp_ev = {}
    for i, (key, kind, arg) in enumerate(plan):
        dst = wscr.h[i]
        g = wgrp[i // GRP]
        if kind == "in":
            src = I["w_in"][:, arg:arg + 512].rearrange("(k p) c -> p k c", p=128)
            ev = k.dma(pool, dst.rearrange("p (k c) -> p k c", k=8), src, semt=g)
        elif kind == "qk":
            for j, wn in enumerate(("w_qm", "w_km")):
                src = I[wn][arg].rearrange("(k p) c -> p k c", p=128)
                ev = k.dma(pool, dst[:, j * 2048:(j + 1) * 2048].rearrange("p (k c) -> p k c", k=4), src, semt=g)
        elif kind == "brm":
            src = I["w_brm"][arg * 512:(arg + 1) * 512, :].rearrange("(k p) c -> p k c", p=128)
            ev = k.dma(pool, dst.rearrange("p (k c) -> p k c", k=4), src, semt=g)
        elif kind == "sq":
            src = I[arg[0]][:, arg[1] * 512:(arg[1] + 1) * 512].rearrange("(k p) c -> p k c", p=128)
            ev = k.dma(pool, dst.rearrange("p (k c) -> p k c", k=8), src, semt=g)
        elif kind == "brh":
            src = I["w_brh"].rearrange("(k p) c -> p k c", p=128)
            ev = k.dma(pool, dst.rearrange("p (k c) -> p k c", k=4), src, semt=g)
        elif kind == "ple":
            src = I["w_ple"].rearrange("(k p) c -> p k c", p=128)
            ev = k.dma(pool, dst[:, 0:2048].rearrange("p (k c) -> p k c", k=2), src, semt=g)
        grp_ev[i // GRP] = ev
        if i == 1:
            k.dma(pool, wg_sb.v, I["w_in"][:, 3072:3076].rearrange("(k p) c -> p k c", p=128), allow_slow_non_contiguous=True)
    k.out_events = []

    pa = [k.ps("pa%d" % i, [128, 512]) for i in range(2)]
    psm = k.ps("psm", [128, 512])
    pnum = k.ps("pnum", [128, 512])
    pint = k.ps("pint", [128, 512])
    pdc = k.ps("pdc", [128, 512])
    ptr = k.ps("ptr", [128, 512])
    ptrb = k.ps("ptrb", [128, 1024], BF16)
    stt_ = dict(pa=0, dc=0)
    k.tr(ptr[:, 0:76], rows_sb.v, identF[0:76, 0:76])
    k.cp(dve, cols.v, ptr[:, 0:76])
    k.tt(dve, lb_col.v, lb_raw[:, :, 0], lb_raw[:, :, 1], ALU.subtract)
    k.actf(lb_col.v, lb_col.v, AF.Sigmoid)
    k.ts(dve, oml_col.v, lb_col.v, -1.0, ALU.mult, 1.0, ALU.add)
    k.ts(dve, noml_col.v, oml_col.v, -1.0, ALU.mult)

    pa_list = [pa[0], pa[1]]

    def next_pa():
        stt_["pa"] = (stt_["pa"] + 1) % len(pa_list)
        return pa_list[stt_["pa"]]

    def set_pa(lst):
        pa_list[:] = lst
        stt_["pa"] = 0

    def next_dc():
        stt_["dc"] = (stt_["dc"] + 1) % 3
        return [pdc, pa[0], pa[1]][stt_["dc"]]

    C_nat = k.sb("C_nat", [128, MHL, 4, 512])
    CT_pp = [k.sb("CT_bf%d" % i, [128, MHL, 4, 512], BF16) for i in range(2)]
    n_col = k.sb("n_col", [128, 8, 8])
    S_st = k.sb("S_st", [128, HGL, HE])
    S_bf = [k.sb("S_bf%d" % i, [128, HGL, HE], BF16) for i in range(2)]
    hist = k.sb("hist", [128, 8, 8, 3])
    mcar = k.sb("mcar", [2, 8])
    cvo = k.sb("cvo", [24, MIL])
    nout = k.sb("nout", [8, 8, 128])

    xsrc = {(n_, w_): k.dram("xsrc_%d%s" % (n_, w_), [n_, D], F32) for n_ in (512, 256) for w_ in "ab"}
    xdst = {(n_, w_): k.dram("xdst_%d%s" % (n_, w_), [n_, D], F32) for n_ in (512, 256) for w_ in "ab"}

    def exchange(NT, which, src_t, NSB):
        xs_, xd_ = xsrc[(NT, which)].h, xdst[(NT, which)].h
        ev_ = None
        for b_ in range(NSB):
            ev_ = k.dma(pool, xs_[b_ * 128:(b_ + 1) * 128, :], src_t[:, b_, :])
        pool.wait(ev_)
        cci = nc.gpsimd.collective_compute("AllReduce", ALU.add, ins=[xs_[:, :]], outs=[xd_[:, :]], replica_groups=RG)
        ccst["n"] += 1
        cci.then_inc(ccsem)
        return (ccsem, ccst["n"], "cc"), xd_
    ccsem = k.new_sem("ccsem")
    ccst = dict(n=0)
    RG = [[0, 1], [2, 3], [4, 5], [6, 7]]

    def tile(x_dram, p_dram, y_dram, NT, TS, NU, NSEG, first, last, sample, tix=0):
        NSB = NT // 128
        SEG = NT // NSEG
        L = 32 if sample else 64
        NCH = TS // L
        m01 = m01s if sample else m01p
        hmask = utri if sample else maskbd

        def utok(u):
            return slice(u * TS, (u + 1) * TS)

        def blk(b):
            return slice(b * 128, (b + 1) * 128)

        with ExitStack() as tes:
            xnT = k.sb("xnT", [128, 8, NT], BF16, es=tes)
            y_acc = k.sb("y_acc", [128, NSB, D], es=tes)
            CT_cur, CT_nxt = CT_pp[tix % 2], CT_pp[(tix + 1) % 2]
            u_tm = k.sb("u_tm", [128, NU, 2], es=tes)
            negM = k.sb("negM", [2, NU, TS], es=tes)
            wi_tm = k.sb("wi_tm", [128, NU, 2], es=tes)
            cl_tm = k.sb("cl_tm", [128, NU, 2], es=tes)
            dec_bc = k.sb("dec_bc", [128, NU, 2], es=tes)
            wiT_all = k.sb("wiT_all", [2, NU, TS], es=tes)
            wl_all = k.sb("wl_all", [128, NU, 2], es=tes)
            wlb_all = k.sb("wlb_all", [128, NU, 2], BF16, es=tes)

            with ExitStack() as pes:
                x_tm = k.sb("x_tm", [128, NSB, D], es=pes)
                xs_b = k.sb("xs_b", [128, D], BF16, es=pes)
                junk = k.sb("junk", [128, D], es=pes)
                ssq = k.sb("ssq", [128, 4], es=pes)
                for b in range(NSB):
                    k.dma(sp, x_tm[:, b, :], x_dram[b * 128:(b + 1) * 128, :])
                if sample:
                    k.dma(sp, mcar[0:2, 0:NU], I["sm"].rearrange("s h -> h s"), allow_slow_non_contiguous=True)
                    sc_tm = k.sb("sc_tm", [24, MIL], es=pes)
                    NS3 = NSEG * 3
                    k.dma(sp, sc_tm[0:NSEG * 3, :], I["sconv"])
                    for grp in range(2):
                        for cc in range(4):
                            c16 = grp * 4 + cc
                            k.tr(ptr[:, cc * NS3:(cc + 1) * NS3], sc_tm[0:NS3, c16 * 128:(c16 + 1) * 128], identF[0:NS3, 0:NS3])
                        k.cp(act, hist[:, grp * 4:(grp + 1) * 4, 0:NSEG, :],
                             ptr[:, 0:4 * NS3].rr("p (c s j) -> p c s j", c=4, j=3))
                    nrow = k.sb("nrow", [8, 128], es=pes)
                    for u in range(NU):
                        k.dma(sp, nrow.v, I["sn"][u])
                        k.tr(psm[:, 256:264], nrow.v, identF[0:8, 0:8])
                        k.cp(act, n_col[:, u, :], psm[:, 256:264])
                elif first:
                    k.memset(pool, mcar.v, 0.0)
                    k.memset(pool, hist.v, 0.0)
                    k.memset(pool, n_col.v, 0.0)
                    k.memset(pool, S_st.v, 0.0)
                    k.memset(pool, S_bf[0].v, 0.0)
                for b in range(NSB):
                    k.actf(junk.v, x_tm[:, b, :], AF.Square, accum=ssq[:, 0:1])
                    k.actf(ssq[:, 1:2], ssq[:, 0:1], AF.Ln, scale=1.0 / D, bias=EPS)
                    k.actf(ssq[:, 2:3], ssq[:, 1:2], AF.Exp, scale=-0.5)
                    k.ts(dve, xs_b.v, x_tm[:, b, :], ssq[:, 2:3], ALU.mult)
                    for c in range(8):
                        k.tr(ptrb[:, c * 128:(c + 1) * 128], xs_b[:, c * 128:(c + 1) * 128], identB.v)
                    k.tt(dve, xnT[:, :, blk(b)], ptrb.v.rr("p (c t) -> p c t", c=8),
                         V(cols, ng_col.h.unsqueeze(2).to_broadcast([128, 8, 128])), ALU.mult)
            k.barrier()

            with ExitStack() as pes:
                u_exts = [k.sb("u_ext%d" % i, [128, NSEG, SEG + 3], es=pes) for i in range(2)]
                cv = k.sb("cv", [128, NSEG, SEG], es=pes)
                gs = [k.sb("gs%d" % i, [128, 2], es=pes) for i in range(8)]
                g_ig, g_fg, g_e1, g_sp, g_csp, g_d4, g_d4b, g_wl = gs
                gr = [k.sb("gr%d" % i, [2, TS], es=pes) for i in range(5)]
                g_uT, g_M, g_mT, g_wiT, g_clT = gr
                BS = [dict(cT=k.sb("cT%d" % i, [128, 4, NT], BF16, es=pes), szT=k.sb("szT%d" % i, [128, 4, NT], BF16, es=pes),
                           qT=k.sb("qT%d" % i, [128, 4, NT], BF16, es=pes), kT=k.sb("kT%d" % i, [128, 4, NT], BF16, es=pes),
                           k_tm=k.sb("k_tm%d" % i, [128, NU, 512], BF16, es=pes), v_tm=k.sb("v_tm%d" % i, [128, NU, 512], BF16, es=pes),
                           hmT=k.sb("hmT%d" % i, [128, 4, NT], BF16, es=pes)) for i in range(2)]
                pdc2 = V(ptrb, ptrb.h[:].bitcast(F32))
                dcst = dict(i=0)

                def next_dc2():
                    dcst["i"] ^= 1
                    return pdc.v if dcst["i"] else pdc2
                NCT = 2 if sample else NU - 1
                CTtmp = [k.sb("CTtmp%d" % i, [128, 4, 512], BF16, es=pes) for i in range(NCT)]
                n_bfs = k.sb("n_bfs", [128, NU + 1, 4], BF16, es=pes)
                dT_sb = [k.sb("dT_sb%d" % i, [128, 128], es=pes) for i in range(2)]
                sT_sb = [k.sb("sT_sb%d" % i, [128, 128], BF16, es=pes) for i in range(2)]
                qwT = [k.sb("qwT%d" % i, [128, 4, 128], BF16, es=pes) for i in range(2)]
                hraw = [k.sb("hraw%d" % i, [128, 512], es=pes) for i in range(2)]
                hn_sb = [k.sb("hn_sb%d" % i, [128, 512], es=pes) for i in range(2)]
                tmp_sb = [k.sb("tmp_sb%d" % i, [128, 4, 128], es=pes) for i in range(2)]
                wv_all = k.sb("wv_all", [128, NU, 512], BF16, es=pes)
                dd = [[k.sb("dd%d_%d" % (i, j), [128, 8], es=pes) for i in range(6)] for j in range(2)]
                SC = float(HD) ** -0.5
                cnt = dict(u=0)

                tb = dict(i=0)

                def make_CT3(h, dst):
                    for dc in range(4):
                        tb["i"] = (tb["i"] + 1) % 3
                        bank = [ptr, pnum, pint][tb["i"]]
                        for ec in range(4):
                            k.tr(bank[:, ec * 128:(ec + 1) * 128], C_nat[:, h, ec, dc * 128:(dc + 1) * 128], identF.v)
                        k.cp(act, dst[:, dc, :], bank.v)
                        yield

                def gate_gen():
                    for u in range(NU):
                        slot = u if sample else 0
                        G = psm[0:TS, 384:388]
                        for kk in range(8):
                            k.mm(G, xnT[:, kk, utok(u)], wg_sb[:, kk, :], start=(kk == 0), stop=(kk == 7))
                        k.tt(dve, g_ig[0:TS, :], G[:, 0:2], big_bc[0:TS, :], ALU.add)
                        k.tt(dve, g_fg[0:TS, :], G[:, 2:4], bfg_bc[0:TS, :], ALU.add)
                        k.actf(g_e1[0:TS, :], g_fg[0:TS, :], AF.Exp, scale=-1.0)
                        k.actf(g_sp[0:TS, :], g_e1[0:TS, :], AF.Ln, bias=1.0)
                        yield
                        k.mm(psm[0:TS, 392:394], utri[0:TS, 0:TS], g_sp[0:TS, :])
                        k.cp(act, g_csp[0:TS, :], psm[0:TS, 392:394])
                        yield
                        k.tt(dve, u_tm[0:TS, u, :], g_ig[0:TS, :], g_csp[0:TS, :], ALU.add)
                        k.tr(psm[0:2, 256:256 + TS], u_tm[0:TS, u, :], identF[0:TS, 0:TS])
                        k.cp(dve, g_uT.v, psm[0:2, 256:256 + TS])
                        yield
                        k.tr(psm[0:2, 128:128 + TS], g_csp[0:TS, :], identF[0:TS, 0:TS])
                        k.op(dve, lambda: nc.vector.tensor_tensor_scan(out=g_M.h[:], data0=g_uT.h[:], data1=g_uT.h[:],
                                                                        initial=mcar.h[0:2, slot:slot + 1], op0=ALU.max, op1=ALU.max),
                             [g_uT.v, mcar.v], [g_M.v])
                        k.ts(dve, negM[0:2, u, :], g_M.v, -1.0, ALU.mult)
                        yield
                        k.tt(dve, g_mT.v, g_M.v, psm[0:2, 128:128 + TS], ALU.subtract)
                        k.actf(g_wiT.v, g_M.v, AF.Exp, scale=-1.0, bias=mcar[0:2, slot:slot + 1])
                        k.actf(g_clT.v, g_mT.v, AF.Exp, scale=-1.0)
                        yield
                        k.cp(dve, mcar[0:2, slot:slot + 1], g_mT[0:2, TS - 1:TS])
                        k.tr(psm[0:TS, 396:398], g_wiT.v, identF[0:2, 0:2])
                        k.cp(act, wi_tm[0:TS, u, :], psm[0:TS, 396:398])
                        k.tr(psm[0:TS, 400:402], g_clT.v, identF[0:2, 0:2])
                        k.cp(act, cl_tm[0:TS, u, :], psm[0:TS, 400:402])
                        yield
                        k.ts(dve, g_d4[0:2, 0:2], identF[0:2, 0:2], g_wiT[0:2, TS - 1:TS], ALU.mult)
                        k.mm(psm[:, 404:406], ones4.v, g_d4[0:2, 0:2])
                        k.cp(act, dec_bc[:, u, :], psm[:, 404:406])
                        yield
                        k.cp(dve, wiT_all[0:2, u, :], g_wiT.v)
                        k.ts(dve, g_d4b[0:2, 0:2], identF[0:2, 0:2], negM[0:2, u, TS - 1:TS], ALU.mult)
                        k.mm(psm[:, 408:410], ones4.v, g_d4b[0:2, 0:2])
                        k.tt(dve, g_wl[0:TS, :], u_tm[0:TS, u, :], psm[0:TS, 408:410], ALU.add)
                        k.actf(wl_all[0:TS, u, :], g_wl[0:TS, :], AF.Exp)
                        k.cp(dve, wlb_all[0:TS, u, :], wl_all[0:TS, u, :])
                        yield
                        if sample or (last and u == NU - 1):
                            k.dma(pool, O["m_s"][u] if sample else O["m_p"], mcar[0:2, slot:slot + 1])
                def proj_gen(h, B_):
                    cT, szT, qT, kT, k_tm, v_tm, hmT = (B_[n_] for n_ in ('cT', 'szT', 'qT', 'kT', 'k_tm', 'v_tm', 'hmT'))
                    W = w_get("U%d" % h)
                    W3 = w3(W, 8)
                    for cc in range(4):
                        c16 = 4 * h + cc
                        acc = next_pa()
                        for kk in range(8):
                            k.mm(acc[:, 0:NT], W3[:, kk, cc * 128:(cc + 1) * 128], xnT[:, kk, :], start=(kk == 0), stop=(kk == 7))
                        u_ext = u_exts[cc % 2]
                        k.cp(pool, u_ext[:, :, 0:3], hist[:, c16, 0:NSEG, :])
                        k.cp(act, u_ext[:, :, 3:3 + SEG], acc[:, 0:NT].rr("p (s t) -> p s t", s=NSEG))
                        k.cp(pool, hist[:, c16, 0:NSEG, :], u_ext[:, :, SEG:SEG + 3])
                        k.actf(cv.v, u_ext[:, :, 0:SEG], AF.Identity, scale=cw_col[:, c16, 0:1], bias=cb_col[:, c16:c16 + 1])
                        for j in range(1, 4):
                            k.stt(cv.v, u_ext[:, :, j:j + SEG], cw_col[:, c16, j:j + 1], cv.v, ALU.mult, ALU.add)
                        k.actf(cT[:, cc, :].rr("p (s t) -> p s t", s=NSEG), cv.v, AF.Silu)
                        yield
                    W = w_get("Z%d" % h)
                    W3 = w3(W, 8)
                    for cc in range(4):
                        acc = next_pa()
                        for kk in range(8):
                            k.mm(acc[:, 0:NT], W3[:, kk, cc * 128:(cc + 1) * 128], xnT[:, kk, :], start=(kk == 0), stop=(kk == 7))
                        k.actf(szT[:, cc, :], acc[:, 0:NT], AF.Silu)
                        yield
                    W = w_get("V%d" % h)
                    W3 = w3(W, 8)
                    for u in range(NU):
                        acc = next_pa()
                        for kk in range(8):
                            k.mm(acc[0:TS, :], xnT[:, kk, utok(u)], W3[:, kk, :], start=(kk == 0), stop=(kk == 7))
                        k.cp(act if u % 2 else dve, v_tm[0:TS, u, :], acc[0:TS, :])
                        yield
                    W = w_get("QK%d" % h)
                    Wq = w3(W, 4, 0, 2048)
                    Wk = w3(W, 4, 2048, 4096)
                    for ec in range(4):
                        acc = next_pa()
                        for dc in range(4):
                            k.mm(acc[:, 0:NT], Wq[:, dc, ec * 128:(ec + 1) * 128], cT[:, dc, :], start=(dc == 0), stop=(dc == 3))
                        k.cp(act, qT[:, ec, :], acc[:, 0:NT])
                        acc = next_pa()
                        for dc in range(4):
                            k.mm(acc[:, 0:NT], Wk[:, dc, ec * 128:(ec + 1) * 128], cT[:, dc, :], start=(dc == 0), stop=(dc == 3))
                        k.ts(dve, kT[:, ec, :], acc[:, 0:NT], SC, ALU.mult)
                        yield
                    for u in range(NU):
                        acc = next_pa()
                        for dc in range(4):
                            k.mm(acc[0:TS, :], cT[:, dc, utok(u)], Wk[:, dc, :], start=(dc == 0), stop=(dc == 3))
                        k.ts(dve, k_tm[0:TS, u, :], acc[0:TS, :], SC, ALU.mult)
                        yield
                    for ec in range(4):
                        c16 = 4 * h + ec
                        k.stt(cT[:, ec, :], cT[:, ec, :], sk_col[:, c16:c16 + 1], szT[:, ec, :], ALU.mult, ALU.mult)
                        k.actf(szT[:, ec, :], szT[:, ec, :], AF.Copy, scale=mg_col[:, c16:c16 + 1])
                        yield


                def passes_gen(h, B_):
                    cT, szT, qT, kT, k_tm, v_tm, hmT = (B_[n_] for n_ in ('cT', 'szT', 'qT', 'kT', 'k_tm', 'v_tm', 'hmT'))
                    for u in range(NU):
                        k.actf(wv_all[0:TS, u, :], v_tm[0:TS, u, :], AF.Copy, scale=wl_all[0:TS, u, h:h + 1])

                    def state_pass(u):
                        slot = u if sample else 0
                        if sample:
                            k.dma(sp, C_nat[:, h], I["sC"][u, h].rearrange("(ec p) d -> p ec d", p=128))
                            yield from make_CT3(h, CTtmp[u % 2])
                        elif first and u == 0:
                            k.memset(pool, C_nat[:, h], 0.0)
                            k.memset(pool, CT_cur[:, h], 0.0)
                        k.cp(dve, n_bfs[:, u, :], n_col[:, slot, 4 * h:4 * h + 4])
                        for ec in range(4):
                            dcp = next_dc2()
                            k.mm(dcp, wv_all[0:TS, u, ec * 128:(ec + 1) * 128], k_tm[0:TS, u, :])
                            k.stt(C_nat[:, h, ec, :], C_nat[:, h, ec, :], dec_bc[:, u, h:h + 1], dcp, ALU.mult, ALU.add)
                            yield
                        for dc in range(4):
                            k.mm(psm[:, 420 + dc:421 + dc], k_tm[0:TS, u, dc * 128:(dc + 1) * 128], wlb_all[0:TS, u, h:h + 1])
                        k.stt(n_col[:, slot, 4 * h:4 * h + 4], n_col[:, slot, 4 * h:4 * h + 4], dec_bc[:, u, h:h + 1],
                              psm[:, 420:424], ALU.mult, ALU.add)
                        if sample:
                            k.dma(pool, O["C_s"][u, h].rearrange("(ec p) d -> p ec d", p=128), C_nat[:, h])
                        elif last and u == NU - 1:
                            k.dma(pool, O["C_p"][h].rearrange("(ec p) d -> p ec d", p=128), C_nat[:, h])
                        else:
                            yield from make_CT3(h, CTtmp[u] if u < NU - 1 else CT_nxt[:, h])

                    def h_group(us):
                        R_ = [(u_ % 2, u_) for u_ in us]
                        CTs = {u: (CTtmp[u % 2] if sample else (CT_cur[:, h] if u == 0 else CTtmp[u - 1])) for u in us}
                        hb = [ptr, ptr]
                        yield
                        for i2, u in R_:
                            nm = psm[0:TS, 0:TS]
                            k.mm(nm, sel[0:2, h, 0:TS], negM[0:2, u, :], start=True, stop=False)
                            k.mm(nm, identF[0:TS, 0:TS], mnegT[0:TS, 0:TS], start=False, stop=True)
                            kq = psm[0:TS, 128:128 + TS]
                            for ec in range(4):
                                k.mm(kq, kT[:, ec, utok(u)], qT[:, ec, utok(u)], start=(ec == 0), stop=(ec == 3))
                            k.mm(psm[:, 256:256 + TS], sel[0:2, h, :], wiT_all[0:2, u, :])
                        yield
                        for i2, u in R_:
                            k.actf(dT_sb[i2][0:TS, 0:TS], psm[0:TS, 0:TS], AF.Exp, bias=u_tm[0:TS, u, h:h + 1])
                        yield
                        for i2, u in R_:
                            k.tt(dve, sT_sb[i2][0:TS, 0:TS], psm[0:TS, 128:128 + TS], dT_sb[i2][0:TS, 0:TS], ALU.mult)
                            k.tt(dve, qwT[i2][:, :, 0:TS], qT[:, :, utok(u)],
                                 V(psm, psm.h[:, 256:256 + TS].unsqueeze(1).to_broadcast([128, 4, TS])), ALU.mult)
                        yield
                        for i2, u in R_:
                            pn = pnum if i2 == 0 else pint
                            k.mm(pn[0:TS, :], sT_sb[i2][0:TS, 0:TS], v_tm[0:TS, u, :], start=True, stop=False)
                            for dc in range(4):
                                k.mm(pn[0:TS, :], qwT[i2][:, dc, 0:TS], CTs[u][:, dc, :], start=False, stop=(dc == 3))
                            dn_ = psm[0:TS, 416 + i2:417 + i2]
                            k.mm(dn_, sT_sb[i2][0:TS, 0:TS], onesb[0:TS, :], start=True, stop=False)
                            for dc in range(4):
                                k.mm(dn_, qwT[i2][:, dc, 0:TS], n_bfs[:, u, dc:dc + 1], start=False, stop=(dc == 3))
                        yield
                        for i2, u in R_:
                            d_rs, d_den, d_aden, d_rden, d_st, d_mv = dd[i2]
                            dn_ = psm[0:TS, 416 + i2:417 + i2]
                            k.ts(dve, d_den[0:TS, 0:1], dn_, -1.0, ALU.mult)
                            k.tt(dve, d_aden[0:TS, 0:1], d_den[0:TS, 0:1], dn_, ALU.max)
                            k.tt(dve, d_aden[0:TS, 0:1], d_aden[0:TS, 0:1], cl_tm[0:TS, u, h:h + 1], ALU.max)
                            k.op(dve, lambda d_rden=d_rden, d_aden=d_aden: nc.vector.reciprocal(out=d_rden.h[0:TS, 0:1], in_=d_aden.h[0:TS, 0:1]),
                                 [d_aden.v], [d_rden.v])
                        yield
                        for i2, u in R_:
                            pn = pnum if i2 == 0 else pint
                            k.actf(hraw[i2][0:TS, :], pn[0:TS, :], AF.Copy, scale=dd[i2][3][0:TS, 0:1])
                        yield
                        for i2, u in R_:
                            d_rs, d_den, d_aden, d_rden, d_st, d_mv = dd[i2]
                            k.op(dve, lambda d_st=d_st, i2=i2: nc.vector.bn_stats(out=d_st.h[0:TS, 0:6], in_=hraw[i2].h[0:TS, :]), [hraw[i2].v], [d_st.v])
                            k.op(dve, lambda d_st=d_st, d_mv=d_mv: nc.vector.bn_aggr(out=d_mv.h[0:TS, 0:2], in_=d_st.h[0:TS, 0:6]), [d_st.v], [d_mv.v])
                        yield
                        for i2, u in R_:
                            d_rs, d_den, d_aden, d_rden, d_st, d_mv = dd[i2]
                            k.actf(d_rs[0:TS, 0:1], d_mv[0:TS, 1:2], AF.Ln, bias=EPS)
                            k.actf(d_rs[0:TS, 1:2], d_rs[0:TS, 0:1], AF.Exp, scale=-0.5)
                        yield
                        for i2, u in R_:
                            d_rs, d_den, d_aden, d_rden, d_st, d_mv = dd[i2]
                            k.ts(dve, hn_sb[i2][0:TS, :], hraw[i2][0:TS, :], d_mv[0:TS, 0:1], ALU.subtract, d_rs[0:TS, 1:2], ALU.mult)
                        yield
                        for i2, u in R_:
                            for ec in range(4):
                                k.tr(hb[i2][:, ec * TS:(ec + 1) * TS], hn_sb[i2][0:TS, ec * 128:(ec + 1) * 128], identF[0:TS, 0:TS])
                        yield
                        for i2, u in R_:
                            k.tt(dve, tmp_sb[i2][:, :, 0:TS], hb[i2][:, 0:4 * TS].rr("p (e t) -> p e t", e=4), szT[:, :, utok(u)], ALU.mult)
                            k.tt(pool, hmT[:, :, utok(u)], tmp_sb[i2][:, :, 0:TS], cT[:, :, utok(u)], ALU.add)


                    if sample:
                        for u in range(NU):
                            yield from state_pass(u)
                            yield from h_group([u])
                    else:
                        for u in range(NU):
                            yield from state_pass(u)
                        for u in range(NU):
                            yield from h_group([u])
                def brm_gen(h, B_):
                    cT, szT, qT, kT, k_tm, v_tm, hmT = (B_[n_] for n_ in ('cT', 'szT', 'qT', 'kT', 'k_tm', 'v_tm', 'hmT'))
                    W = w_get("BRM%d" % h)
                    W3 = w3(W, 4)
                    for b in range(NSB):
                        for q in range(2):
                            acc = next_pa()
                            for ec in range(4):
                                k.mm(acc.v, hmT[:, ec, blk(b)], W3[:, ec, q * 512:(q + 1) * 512], start=(ec == 0), stop=(ec == 3))
                            dst = y_acc[:, b, q * 512:(q + 1) * 512]
                            if h == 0:
                                k.cp(act, dst, acc.v)
                                yield
                            else:
                                k.tt(dve, dst, dst, acc.v, ALU.add)
                                yield


                def run(ga, gb, ra=1, rb=1):
                    gens = [g_ for g_ in (ga, gb) if g_ is not None]
                    rate = {id(ga): ra, id(gb): rb}
                    while gens:
                        for g_ in list(gens):
                            for _ in range(rate[id(g_)]):
                                try:
                                    next(g_)
                                except StopIteration:
                                    gens.remove(g_)
                                    break
                run(gate_gen(), proj_gen(0, BS[0]), 1, 1)
                run(passes_gen(0, BS[0]), proj_gen(1, BS[1]), 3, 1)
                run(passes_gen(1, BS[1]), brm_gen(0, BS[0]), 3, 1)
                run(brm_gen(1, BS[1]), None)
                for slot in range(NU if sample else 1):
                    if sample or last:
                        k.tr(psm[0:8, 256:384], n_col[:, slot, :], identF.v)
                        k.cp(act, nout[0:8, slot, :], psm[0:8, 256:384])
                        k.dma(pool, O["n_s"][slot] if sample else O["n_p"], nout[0:8, slot, :])
                if sample or last:
                    for grp in range(2):
                        for cc in range(4):
                            c16 = grp * 4 + cc
                            k.tr(ptr[0:NSEG * 3, cc * 128:(cc + 1) * 128], hist[:, c16, 0:NSEG, :].rr("p s j -> p (s j)"), identF.v)
                        k.cp(act, cvo[0:NSEG * 3, grp * 512:(grp + 1) * 512], ptr[0:NSEG * 3, :])
                    k.dma(pool, O["conv_s"] if sample else O["conv_p"], cvo[0:NSEG * 3, :])
                evA, xdA = exchange(NT, "a", y_acc, NSB)
            k.barrier()

            yb_acc = k.sb("yb_acc", [128, NSB, D], es=tes)
            if True:
                with ExitStack() as pes:
                    ohT = k.sb("ohT", [128, 4, NT], BF16, es=pes)
                    qs_sb = k.sb("qs_sb", [128, 4, NT], es=pes)
                    sigf = k.sb("sigf", [128, NT], es=pes)
                    lfh = k.sb("lfh", [128, NT], es=pes)
                    kh_sb = k.sb("kh_sb", [128, NT], es=pes)
                    g_sb = k.sb("g_sb", [128, NT], es=pes)
                    eg_sb = k.sb("eg_sb", [128, NT], es=pes)
                    eng_sb = k.sb("eng_sb", [128, NT], es=pes)
                    qtT = k.sb("qtT", [128, 4, NT], BF16, es=pes)
                    ktT = k.sb("ktT", [128, 4, NT], BF16, es=pes)
                    sgT = k.sb("sgT", [128, 4, NT], BF16, es=pes)
                    egl = k.sb("egl", [128, 4, NT // L], es=pes)
                    vh_tm = k.sb("vh_tm", [128, NU, 512], BF16, es=pes)
                    kt_tm = k.sb("kt_tm", [128, 512], BF16, es=pes)
                    aT_sb = k.sb("aT_sb", [128, 4, 128], BF16, es=pes)
                    o_sb = k.sb("o_sb", [128, 512], es=pes)
                    sq_sb = k.sb("sq_sb", [128, 512], es=pes)
                    on_sb = k.sb("on_sb", [128, 4, 128], es=pes)
                    hs = k.sb("hs", [128, 12], es=pes)
                    for g in range(1):
                        hsl = slice(4 * g, 4 * g + 4)
                        set_pa([pa[0], pa[1], pdc, ptr])
                        W = w_get("QH%d" % g)
                        W3 = w3(W, 8)
                        for j in range(4):
                            acc = next_pa()
                            for kk in range(8):
                                k.mm(acc[:, 0:NT], W3[:, kk, j * 128:(j + 1) * 128], xnT[:, kk, :], start=(kk == 0), stop=(kk == 7))
                            k.actf(qs_sb[:, j, :], acc[:, 0:NT], AF.Silu)
                        W = w_get("FH%d" % g)
                        W3 = w3(W, 8)
                        for j in range(4):
                            hh = 4 * g + j
                            acc = next_pa()
                            for kk in range(8):
                                k.mm(acc[:, 0:NT], W3[:, kk, j * 128:(j + 1) * 128], xnT[:, kk, :], start=(kk == 0), stop=(kk == 7))
                            k.actf(sigf.v, acc[:, 0:NT], AF.Sigmoid)
                            k.actf(lfh.v, sigf.v, AF.Ln, scale=oml_col[:, hh:hh + 1], bias=lb_col[:, hh:hh + 1])
                            k.ts(dve, kh_sb.v, sigf.v, noml_col[:, hh:hh + 1], ALU.mult, oml_col[:, hh:hh + 1], ALU.add)
                            k.op(dve, lambda: nc.vector.tensor_tensor_scan(out=g_sb.h[:], data0=m01.h[:, 0:NT], data1=lfh.h[:],
                                                                            initial=0.0, op0=ALU.mult, op1=ALU.add),
                                 [m01.v, lfh.v], [g_sb.v])
                            k.actf(eg_sb.v, g_sb.v, AF.Exp)
                            k.actf(eng_sb.v, g_sb.v, AF.Exp, scale=-1.0)
                            k.tt(pool, qtT[:, j, :], qs_sb[:, j, :], eg_sb.v, ALU.mult)
                            k.tt(dve, ktT[:, j, :], kh_sb.v, eng_sb.v, ALU.mult)
                            k.cp(dve, egl[:, j, :], eg_sb.v.rr("p (c l) -> p c l", l=L)[:, :, L - 1])
                        W = w_get("IH%d" % g)
                        W3 = w3(W, 8)
                        for u in range(NU):
                            acc = next_pa()
                            for kk in range(8):
                                k.mm(acc[0:TS, :], xnT[:, kk, utok(u)], W3[:, kk, :], start=(kk == 0), stop=(kk == 7))
                            k.cp(act if u % 2 else dve, vh_tm[0:TS, u, :], acc[0:TS, :])
                        W = w_get("GH%d" % g)
                        W3 = w3(W, 8)
                        for j in range(4):
                            acc = next_pa()
                            for kk in range(8):
                                k.mm(acc[:, 0:NT], W3[:, kk, j * 128:(j + 1) * 128], xnT[:, kk, :], start=(kk == 0), stop=(kk == 7))
                            k.actf(sgT[:, j, :], acc[:, 0:NT], AF.Silu)
                        set_pa([pa[0], pa[1]])
                        for u in range(NU):
                            if sample:
                                k.dma(sp, S_st[:, hsl, :], I["sS"][u].rearrange("h c e -> c h e"))
                                k.cp(act, S_bf[0][:, hsl, :], S_st[:, hsl, :])
                            for j in range(4):
                                k.tr(ptrb[0:TS, j * 128:(j + 1) * 128], ktT[:, j, utok(u)], identB.v)
                            k.cp(act, kt_tm[0:TS, :], ptrb[0:TS, 0:512])
                            for j in range(4):
                                k.mm(pnum[0:TS, j * TS:(j + 1) * TS], ktT[:, j, utok(u)], qtT[:, j, utok(u)])
                            k.tt(dve, aT_sb[0:TS, :, 0:TS], pnum[0:TS, 0:4 * TS].rr("p (j t) -> p j t", j=4),
                                 V(hmask, hmask.h[0:TS, 0:TS].unsqueeze(1).to_broadcast([TS, 4, TS])), ALU.mult)
                            def s_update(c, dst_bf):
                                rows = slice(c * L, (c + 1) * L)
                                gci = u * NCH + c
                                for j in range(4):
                                    k.mm(pdc[:, j * 128:(j + 1) * 128], kt_tm[rows, j * 128:(j + 1) * 128],
                                         vh_tm[rows, u, j * 128:(j + 1) * 128])
                                k.tt(dve, S_st[:, hsl, :], S_st[:, hsl, :], pdc.v.rr("p (j e) -> p j e", j=4), ALU.add)
                                k.tt(dve, S_st[:, hsl, :], S_st[:, hsl, :],
                                     V(egl, egl.h[:, :, gci:gci + 1].to_broadcast([128, 4, 128])), ALU.mult)
                                if dst_bf is not None:
                                    k.cp(act, dst_bf[:, hsl, :], S_st[:, hsl, :])
                            for c in range(NCH - 1):
                                s_update(c, S_bf[c + 1])
                            for j in range(4):
                                hh = 4 * g + j
                                for c in range(NCH):
                                    rows = slice(c * L, (c + 1) * L)
                                    k.mm(pint[rows, j * 128:(j + 1) * 128], qtT[:, j, u * TS + c * L:u * TS + (c + 1) * L],
                                         S_bf[c][:, hh, :], start=True, stop=False, skip_group_check=True)
                                k.mm(pint[0:TS, j * 128:(j + 1) * 128], aT_sb[0:TS, j, 0:TS], vh_tm[0:TS, u, j * 128:(j + 1) * 128],
                                     start=False, stop=True, skip_group_check=True)
                            s_update(NCH - 1, None if sample else S_bf[0])
                            k.cp(act, o_sb[0:TS, :], pint[0:TS, :])
                            k.tt(pool, sq_sb[0:TS, :], o_sb[0:TS, :], o_sb[0:TS, :], ALU.mult)
                            k.op(dve, lambda: nc.vector.tensor_reduce(out=hs.h[0:TS, 0:4],
                                                                       in_=sq_sb.h[0:TS, :].rearrange("p (j e) -> p j e", j=4),
                                                                       axis=AX.X, op=ALU.add), [sq_sb.v], [hs.v])
                            k.actf(hs[0:TS, 4:8], hs[0:TS, 0:4], AF.Ln, scale=1.0 / HE, bias=EPS)
                            k.actf(hs[0:TS, 8:12], hs[0:TS, 4:8], AF.Exp, scale=-0.5)
                            k.tt(dve, on_sb[0:TS, :, :], o_sb[0:TS, :].rr("p (j e) -> p j e", j=4),
                                 V(hs, hs.h[0:TS, 8:12].unsqueeze(2).to_broadcast([TS, 4, 128])), ALU.mult)
                            for j in range(4):
                                k.tr(ptr[:, j * TS:(j + 1) * TS], on_sb[0:TS, j, :], identF[0:TS, 0:TS])
                            for j in range(4):
                                hh = 4 * g + j
                                k.stt(ohT[:, hh, utok(u)], ptr[:, j * TS:(j + 1) * TS], hg_col[:, hh:hh + 1], sgT[:, j, utok(u)],
                                      ALU.mult, ALU.mult)
                            if sample:
                                k.dma(pool, O["S_s"][u].rearrange("h c e -> c h e"), S_st[:, hsl, :])
                            elif last and u == NU - 1:
                                k.dma(pool, O["S_p"].rearrange("h c e -> c h e"), S_st[:, hsl, :])
                    W = w_get("BRH")
                    W3 = w3(W, 4)
                    for b in range(NSB):
                        for q in range(2):
                            acc = next_pa()
                            for hh in range(4):
                                k.mm(acc.v, ohT[:, hh, blk(b)], W3[:, hh, q * 512:(q + 1) * 512], start=(hh == 0), stop=(hh == 3))
                            k.cp(act, yb_acc[:, b, q * 512:(q + 1) * 512], acc.v)
                    evB, xdB = exchange(NT, "b", yb_acc, NSB)
            k.barrier()

            with ExitStack() as pes:
                x2 = k.sb("x2", [128, NSB, D], es=pes)
                tT = k.sb("tT", [128, 8, NT], BF16, es=pes)
                pT = k.sb("pT", [128, 2, NT], BF16, es=pes)
                p_tm = k.sb("p_tm", [128, NSB, PLE], es=pes)
                bf_sb = k.sb("bf_sb", [128, D], BF16, es=pes)
                sg_sb = k.sb("sg2_sb", [128, 512], es=pes)
                sga = k.sb("sga", [128, NSB, D], es=pes)
                sgb = k.sb("sgb", [128, NSB, D], es=pes)
                outb = [k.sb("outb%d" % i, [128, D], es=pes) for i in range(2)]
                ssq = k.sb("ssq2", [128, 4], es=pes)
                set_pa([pa[0], pa[1], pnum, pint])
                for b in range(NSB):
                    k.dma(sp, x2[:, b, :], x_dram[b * 128:(b + 1) * 128, :])
                    k.dma(sp, p_tm[:, b, :], p_dram[b * 128:(b + 1) * 128, :])

                def to_T(dst, src_fn, nchunk):
                    for b in range(NSB):
                        k.cp(act, bf_sb[:, 0:nchunk * 128], src_fn(b))
                        for c in range(nchunk):
                            k.tr(ptrb[:, c * 128:(c + 1) * 128], bf_sb[:, c * 128:(c + 1) * 128], identB.v)
                        k.cp(dve, dst[:, 0:nchunk, blk(b)], ptrb[:, 0:nchunk * 128].rr("p (c t) -> p c t", c=nchunk))
                to_T(pT, lambda b: p_tm[:, b, :], 2)
                def gate_proj(nm_, sgt):
                    for q in range(2):
                        W = w_get("%s%d" % (nm_, q))
                        W3 = w3(W, 8)
                        for b in range(NSB):
                            acc = next_pa()
                            for kk in range(8):
                                k.mm(acc.v, xnT[:, kk, blk(b)], W3[:, kk, :], start=(kk == 0), stop=(kk == 7))
                            k.actf(sgt[:, b, q * 512:(q + 1) * 512], acc.v, AF.Sigmoid)
                gate_proj("GA", sga)
                sp.wait(evA)
                for b in range(NSB):
                    k.dma(sp, y_acc[:, b, :], xdA[b * 128:(b + 1) * 128, :])
                for b in range(NSB):
                    k.tt(dve, y_acc[:, b, :], y_acc[:, b, :], sga[:, b, :], ALU.mult)
                gate_proj("GB", sgb)
                sp.wait(evB)
                for b in range(NSB):
                    k.dma(sp, yb_acc[:, b, :], xdB[b * 128:(b + 1) * 128, :])
                for b in range(NSB):
                    k.tt(dve, yb_acc[:, b, :], yb_acc[:, b, :], sgb[:, b, :], ALU.mult)
                    k.tt(dve, y_acc[:, b, :], y_acc[:, b, :], yb_acc[:, b, :], ALU.add)
                to_T(tT, lambda b: y_acc[:, b, :], 8)
                for q in range(2):
                    W = w_get("OUT%d" % q)
                    W3 = w3(W, 8)
                    for b in range(NSB):
                        acc = next_pa()
                        for kk in range(8):
                            k.mm(acc.v, tT[:, kk, blk(b)], W3[:, kk, :], start=(kk == 0), stop=(kk == 7))
                        dst = x2[:, b, q * 512:(q + 1) * 512]
                        k.tt(dve, dst, dst, acc.v, ALU.add)
                to_T(tT, lambda b: x2[:, b, :], 8)
                Wp = [w_get("PG0"), w_get("PG1", hold=1), w_get("PLE", hold=2)]
                Wple = w3(Wp[2], 2, 0, 2048)
                for b in range(NSB):
                    for q in range(2):
                        W3 = w3(Wp[q], 8)
                        acc = next_pa()
                        for kk in range(8):
                            k.mm(acc.v, tT[:, kk, blk(b)], W3[:, kk, :], start=(kk == 0), stop=(kk == 7))
                        k.actf(sg_sb.v, acc.v, AF.Sigmoid)
                        acc2 = next_pa()
                        for kk in range(2):
                            k.mm(acc2.v, pT[:, kk, blk(b)], Wple[:, kk, q * 512:(q + 1) * 512], start=(kk == 0), stop=(kk == 1))
                        k.tt(dve, sg_sb.v, sg_sb.v, acc2.v, ALU.mult)
                        dst = x2[:, b, q * 512:(q + 1) * 512]
                        k.tt(pool, dst, dst, sg_sb.v, ALU.add)
                    ob = outb[b % 2]
                    k.actf(ob.v, x2[:, b, :], AF.Square, accum=ssq[:, 0:1])
                    k.actf(ssq[:, 1:2], ssq[:, 0:1], AF.Ln, scale=1.0 / D, bias=EPS)
                    k.actf(ssq[:, 2:3], ssq[:, 1:2], AF.Exp, scale=-0.5)
                    k.stt(ob.v, x2[:, b, :], ssq[:, 2:3], fg_bc.v, ALU.mult, ALU.mult)
                    k.dma(pool, y_dram[b * 128:(b + 1) * 128, :], ob.v)
                set_pa([pa[0], pa[1]])
            k.barrier()

    NTP = min(512, TP)
    ntile = TP // NTP if do_prompt else 0
    for ti in range(ntile):
        seq.extend(range(NW))
    if do_sample:
        seq.extend(range(NW))
    for ti in range(ntile):
        tile(I["xp"][ti * NTP:(ti + 1) * NTP, :], I["pp"][ti * NTP:(ti + 1) * NTP, :], O["yp"][ti * NTP:(ti + 1) * NTP, :],
             NTP, 128, NTP // 128, 1, ti == 0, ti == ntile - 1, False, tix=ti)
    if do_sample:
        tile(I["xs"], I["ps"], O["ys"], NS * 32, 32, NS, NS, True, True, True)
    for ev in k.out_events:
        sp.wait(ev)


_NC_CACHE = {}


def _in_map(core, inputs, NS):
    b = core // 2
    hf = core % 2
    s0 = (core // 2) * NS
    f = lambda a: np.ascontiguousarray(a, dtype=np.float32)
    hs = slice(2 * hf, 2 * hf + 2)
    cs = slice(hf * MIL, (hf + 1) * MIL)
    gs = slice(hf * HWL, (hf + 1) * HWL)
    w = inputs["w_in"][0]
    w_in = np.concatenate([
        w[:, 0 + hf * MIL:0 + (hf + 1) * MIL], w[:, 4096 + hf * MIL:4096 + (hf + 1) * MIL], w[:, 2048 + hf * MIL:2048 + (hf + 1) * MIL],
        w[:, 6144 + 2 * hf:6144 + 2 * hf + 2], w[:, 6148 + 2 * hf:6148 + 2 * hf + 2],
        w[:, 6152 + hf * HWL:6152 + (hf + 1) * HWL], w[:, 7176 + hf * HWL:7176 + (hf + 1) * HWL],
        w[:, 8200 + hf * HWL:8200 + (hf + 1) * HWL], w[:, 9224 + hf * HWL:9224 + (hf + 1) * HWL],
        w[:, 10248:12296]], axis=1)
    assert w_in.shape[1] == NINL
    m = dict(
        xp=f(inputs["x_prompt"][b]), pp=f(inputs["p_prompt"][0, b]),
        xs=f(inputs["x_sample"][s0:s0 + NS].reshape(NS * 32, D)), ps=f(inputs["p_sample"][0, s0:s0 + NS].reshape(NS * 32, PLE)),
        sconv=f(inputs["state_conv"][0, s0:s0 + NS][:, :, cs].reshape(NS * 3, MIL)), sC=f(inputs["state_mlstm_C"][0, s0:s0 + NS, hs]),
        sn=f(inputs["state_mlstm_n"][0, s0:s0 + NS, hs].reshape(NS, MHL * 4, 128)), sm=f(inputs["state_mlstm_m"][0, s0:s0 + NS, hs]),
        sS=f(inputs["state_hgrn"][0, s0:s0 + NS, 4 * hf:4 * hf + 4]),
        norm_g=f(inputs["norm_g"][0]), w_in=f(w_in), b_ig=f(inputs["b_ig"][0, hs]), b_fg=f(inputs["b_fg"][0, hs]),
        conv_w=f(inputs["conv_w"][0][:, cs]), conv_b=f(inputs["conv_b"][0, cs]), w_qm=f(inputs["w_qm"][0, hs]), w_km=f(inputs["w_km"][0, hs]),
        mnorm_g=f(inputs["mnorm_g"][0, cs]), m_skip=f(inputs["m_skip"][0, cs]), w_brm=f(inputs["w_brm"][0, cs]),
        hgrn_lb=f(inputs["hgrn_lb"][:, gs]), hnorm_g=f(inputs["hnorm_g"][0, gs]), w_brh=f(inputs["w_brh"][0, gs]), w_out=f(inputs["w_out"][0]),
        w_ple=f(inputs["w_ple"][0]), w_pg=f(inputs["w_pg"][0]), final_g=f(inputs["final_g"]),
    )
    return m


def kernel(**inputs):
    inputs = {k_: np.asarray(v) for k_, v in inputs.items()}
    B, TP = inputs["x_prompt"].shape[:2]
    NSEQ = inputs["x_sample"].shape[0]
    NCORE = 8
    NS = NSEQ // (NCORE // 2)
    key = (TP, NS)
    if key not in _NC_CACHE:
        _NC_CACHE[key] = build(TP, NS)
    nc = _NC_CACHE[key]
    in_maps = [_in_map(c, inputs, NS) for c in range(NCORE)]
    res = run_bass_kernel_spmd(nc, in_maps, core_ids=list(range(NCORE)))
    R = res.results
    f32 = np.float32
    NP = NCORE // 2
    cat2 = lambda name, fn, ax: [np.concatenate([fn(R[2 * p][name]), fn(R[2 * p + 1][name])], axis=ax) for p in range(NP)]
    y_prompt = np.stack([R[2 * b]["yp"] for b in range(B)]).astype(f32)
    y_sample = np.concatenate([R[2 * p]["ys"].reshape(NS, 32, D) for p in range(NP)]).astype(f32)
    conv_p = np.stack(cat2("conv_p", lambda a: a, 1))[None].astype(f32)
    C_p = np.stack(cat2("C_p", lambda a: a, 0))[None].astype(f32)
    n_p = np.stack(cat2("n_p", lambda a: a.reshape(MHL, HD), 0))[None].astype(f32)
    m_p = np.stack(cat2("m_p", lambda a: a.reshape(MHL), 0))[None].astype(f32)
    S_p = np.stack(cat2("S_p", lambda a: a, 0))[None].astype(f32)
    conv_s = np.concatenate(cat2("conv_s", lambda a: a.reshape(NS, 3, MIL), 2))[None].astype(f32)
    C_s = np.concatenate(cat2("C_s", lambda a: a, 1))[None].astype(f32)
    n_s = np.concatenate(cat2("n_s", lambda a: a.reshape(NS, MHL, HD), 1))[None].astype(f32)
    m_s = np.concatenate(cat2("m_s", lambda a: a.reshape(NS, MHL), 1))[None].astype(f32)
    S_s = np.concatenate(cat2("S_s", lambda a: a, 1))[None].astype(f32)
    return (y_prompt, y_sample, conv_p, C_p, n_p, m_p, S_p, conv_s, C_s, n_s, m_s, S_s)
```
